# Optimizing a Trainium2 kernel written in Bass

```python
import math
import jax, jax.numpy as jnp
from jax import lax
import numpy as np

D_MODEL = 1024
BATCH = 1
SEQ = 16384
DEPTH = 2
DEC_BATCH = 8
DEC_SEQ = 4096
PAST_LEN = 128

HEAD_DIM = 64
BLOCK = 128
EPS = 1e-6
SUBLN_EPS = 1e-5
NEG = -1e30
A_HEADS = 4
A_KV_HEADS = 2
A_WINDOW = 128
B_HEADS = 4
B_VDIM = 2 * HEAD_DIM
C_PAIRS = ((128, 1), (512, 4), (2048, 16))
C_GROUPS = 3
C_HEADS = 4
C_SIDE = 64
C_REACH = 1024
D_FF = 2816
CONV_W = 3
A_Q = A_HEADS * HEAD_DIM
A_KV = A_KV_HEADS * HEAD_DIM
B_QK = B_HEADS * 2 * HEAD_DIM
B_V = B_HEADS * B_VDIM
C_QKV = C_GROUPS * C_HEADS * HEAD_DIM
D_IN = A_Q + 2 * A_KV + 2 * B_QK + B_V + 3 * C_QKV
D_MIX = A_Q + B_V + C_HEADS * HEAD_DIM

kernel_name = "hymba_style_hybrid_encoder"


def rmsnorm(x, g, eps=EPS):
    xf = x.astype(jnp.float32)
    y = xf * lax.rsqrt(jnp.mean(xf * xf, axis=-1, keepdims=True) + eps)
    return (y * g.astype(jnp.float32)).astype(x.dtype)


def alibi_slopes(n):
    return jnp.asarray(2.0 ** (-8.0 * np.arange(1, n + 1) / n), dtype=jnp.float32)


def c_offsets():
    return jnp.asarray(np.stack([d * np.arange(-(w // (2 * d)), w // (2 * d) + 1) for (w, d) in C_PAIRS]), dtype=jnp.int32)


def window_gqa_sink(q, k, v, sink):
    B, S = q.shape[0], q.shape[1]
    nb = S // BLOCK
    g = A_HEADS // A_KV_HEADS
    scale = HEAD_DIM ** -0.5
    qb = q.astype(jnp.float32).reshape(B, nb, BLOCK, A_KV_HEADS, g, HEAD_DIM)
    pad = ((0, 0), (BLOCK, BLOCK), (0, 0), (0, 0))

    def band(t):
        tb = jnp.pad(t.astype(jnp.float32), pad).reshape(B, nb + 2, BLOCK, A_KV_HEADS, HEAD_DIM)
        return jnp.concatenate([tb[:, :-2], tb[:, 1:-1], tb[:, 2:]], axis=2)

    kb, vb = band(k), band(v)
    qi = jnp.arange(BLOCK)[:, None]
    kc = jnp.arange(3 * BLOCK)[None, :]
    dist = kc - BLOCK - qi
    s_pos = jnp.arange(nb)[:, None, None] * BLOCK + kc[None] - BLOCK
    valid = (jnp.abs(dist)[None] <= A_WINDOW) & (s_pos >= 0) & (s_pos < S)
    slopes = alibi_slopes(A_HEADS).reshape(A_KV_HEADS, g)
    logits = jnp.einsum('bnqhgd,bnkhd->bnhgqk', qb, kb) * scale
    logits = logits - slopes[:, :, None, None] * jnp.abs(dist).astype(jnp.float32)
    logits = jnp.where(valid[None, :, None, None], logits, NEG)
    sink_b = sink.astype(jnp.float32).reshape(A_KV_HEADS, g)[:, :, None, None]
    m = jnp.maximum(jnp.max(logits, axis=-1, keepdims=True), sink_b)
    p = jnp.exp(logits - m)
    den = jnp.sum(p, axis=-1, keepdims=True) + jnp.exp(sink_b - m)
    o = jnp.einsum('bnhgqk,bnkhd->bnqhgd', p / den, vb)
    return o.reshape(B, S, A_HEADS * HEAD_DIM).astype(q.dtype)


def diff_attention(q, k, v, lam, lam_init, subln_g):
    B, S = q.shape[0], q.shape[1]
    nb = S // BLOCK
    scale = HEAD_DIM ** -0.5
    slopes = alibi_slopes(B_HEADS)
    kf = k.astype(jnp.float32)
    vf = v.astype(jnp.float32)
    qb = jnp.moveaxis(q.astype(jnp.float32).reshape(B, nb, BLOCK, B_HEADS, 2, HEAD_DIM), 1, 0)
    kpos = jnp.arange(S)

    def one_block(args):
        qblk, i = args
        qpos = i * BLOCK + jnp.arange(BLOCK)
        logits = jnp.einsum('bqhmd,bkhmd->bhmqk', qblk, kf) * scale
        logits = logits - slopes[:, None, None, None] * jnp.abs(qpos[:, None] - kpos[None, :]).astype(jnp.float32)
        a = jax.nn.softmax(logits, axis=-1)
        w = a[:, :, 0] - lam * a[:, :, 1]
        return jnp.einsum('bhqk,bkhe->bqhe', w, vf)

    o = lax.map(one_block, (qb, jnp.arange(nb)))
    o = jnp.moveaxis(o, 0, 1).reshape(B, S, B_HEADS, B_VDIM)
    o = rmsnorm(o, subln_g, SUBLN_EPS) * (1.0 - lam_init)
    return o.reshape(B, S, B_HEADS * B_VDIM).astype(q.dtype)


def dilated_attention(q, k, v):
    B, S = q.shape[0], q.shape[1]
    nb = S // BLOCK
    scale = HEAD_DIM ** -0.5
    span = BLOCK + 2 * C_REACH
    offs = c_offsets()
    slopes = alibi_slopes(C_GROUPS * C_HEADS).reshape(C_GROUPS, C_HEADS)
    pad = ((0, 0), (C_REACH, C_REACH), (0, 0), (0, 0), (0, 0))
    kp = jnp.pad(k.astype(jnp.float32), pad)
    vp = jnp.pad(v.astype(jnp.float32), pad)
    qb = jnp.moveaxis(q.astype(jnp.float32).reshape(B, nb, BLOCK, C_GROUPS, C_HEADS, HEAD_DIM), 1, 0)
    loc = jnp.arange(BLOCK)[:, None, None] + C_REACH + offs[None]
    gidx = jnp.arange(C_GROUPS)[None, :, None]
    dpen = jnp.abs(offs).astype(jnp.float32)[:, None, :]

    def one_block(args):
        qblk, i = args
        ks = lax.dynamic_slice_in_dim(kp, i * BLOCK, span, axis=1)
        vs = lax.dynamic_slice_in_dim(vp, i * BLOCK, span, axis=1)
        kg = ks[:, loc, gidx]
        vg = vs[:, loc, gidx]
        spos = i * BLOCK + jnp.arange(BLOCK)[:, None, None] + offs[None]
        valid = (spos >= 0) & (spos < S)
        logits = jnp.einsum('bqghd,bqgjhd->bqghj', qblk, kg) * scale
        logits = logits - slopes[:, :, None] * dpen
        logits = jnp.where(valid[None, :, :, None, :], logits, NEG)
        lse = jax.nn.logsumexp(logits, axis=-1)
        og = jnp.einsum('bqghj,bqgjhd->bqghd', jnp.exp(logits - lse[..., None]), vg)
        wg = jax.nn.softmax(lse, axis=2)
        return jnp.einsum('bqgh,bqghd->bqhd', wg, og)

    o = lax.map(one_block, (qb, jnp.arange(nb)))
    return jnp.moveaxis(o, 0, 1).reshape(B, S, C_HEADS * HEAD_DIM).astype(q.dtype)


def conv_gated_mlp(h, w_up, conv_w, conv_b, w_down):
    u = h @ w_up
    up = jnp.pad(u, ((0, 0), (1, 1), (0, 0)))
    u = up[:, :-2] * conv_w[0] + up[:, 1:-1] * conv_w[1] + up[:, 2:] * conv_w[2] + conv_b
    gate, val = jnp.split(u, 2, axis=-1)
    return (jax.nn.silu(gate) * val) @ w_down


def trunk(x, ln1, w_in, a_sink, lam_q1, lam_k1, lam_q2, lam_k2, subln, w_out,
          ln2, w_up, conv_w, conv_b, w_down, ln_f):
    B, S = x.shape[0], x.shape[1]
    cuts = np.cumsum([A_Q, A_KV, A_KV, B_QK, B_QK, B_V, C_QKV, C_QKV])
    for l in range(DEPTH):
        h = rmsnorm(x, ln1[l])
        proj = h @ w_in[l]
        aq, ak, av, bq, bk, bv, cq, ck, cv = jnp.split(proj, [int(c) for c in cuts], axis=-1)
        o_a = window_gqa_sink(aq.reshape(B, S, A_HEADS, HEAD_DIM),
                              ak.reshape(B, S, A_KV_HEADS, HEAD_DIM),
                              av.reshape(B, S, A_KV_HEADS, HEAD_DIM), a_sink[l])
        lam_init = 0.8 - 0.6 * math.exp(-0.3 * l)
        lam = (jnp.exp(jnp.sum(lam_q1[l].astype(jnp.float32) * lam_k1[l].astype(jnp.float32)))
               - jnp.exp(jnp.sum(lam_q2[l].astype(jnp.float32) * lam_k2[l].astype(jnp.float32))) + lam_init)
        o_b = diff_attention(bq.reshape(B, S, B_HEADS, 2, HEAD_DIM),
                             bk.reshape(B, S, B_HEADS, 2, HEAD_DIM),
                             bv.reshape(B, S, B_HEADS, B_VDIM), lam, lam_init, subln[l])
        cshape = (B, S, C_GROUPS, C_HEADS, HEAD_DIM)
        o_c = dilated_attention(cq.reshape(cshape), ck.reshape(cshape), cv.reshape(cshape))
        x = x + jnp.concatenate([o_a, o_b, o_c], axis=-1) @ w_out[l]
        x = x + conv_gated_mlp(rmsnorm(x, ln2[l]), w_up[l], conv_w[l], conv_b[l], w_down[l])
    return rmsnorm(x, ln_f)


def setup_inputs(seed: int = 0) -> dict:
    key = jax.random.key(seed)
    ks = jax.random.split(key, 20)
    f32 = jnp.float32
    nrm = lambda k, s, sc: jax.random.normal(k, s, f32) * sc
    return {
        "x_prompt": nrm(ks[0], (BATCH, SEQ, D_MODEL), 1.0),
        "x_sample": nrm(ks[1], (DEC_BATCH, DEC_SEQ, D_MODEL), 1.0),
        "ln1": 1.0 + nrm(ks[2], (DEPTH, D_MODEL), 0.02),
        "w_in": nrm(ks[3], (DEPTH, D_MODEL, D_IN), D_MODEL ** -0.5),
        "a_sink": nrm(ks[4], (DEPTH, A_HEADS), 0.5),
        "lam_q1": nrm(ks[5], (DEPTH, HEAD_DIM), 0.1),
        "lam_k1": nrm(ks[6], (DEPTH, HEAD_DIM), 0.1),
        "lam_q2": nrm(ks[7], (DEPTH, HEAD_DIM), 0.1),
        "lam_k2": nrm(ks[8], (DEPTH, HEAD_DIM), 0.1),
        "subln": 1.0 + nrm(ks[9], (DEPTH, B_VDIM), 0.02),
        "w_out": nrm(ks[10], (DEPTH, D_MIX, D_MODEL), 0.5 * D_MIX ** -0.5),
        "ln2": 1.0 + nrm(ks[11], (DEPTH, D_MODEL), 0.02),
        "w_up": nrm(ks[12], (DEPTH, D_MODEL, 2 * D_FF), D_MODEL ** -0.5),
        "conv_w": nrm(ks[13], (DEPTH, CONV_W, 2 * D_FF), CONV_W ** -0.5),
        "conv_b": nrm(ks[14], (DEPTH, 2 * D_FF), 0.02),
        "w_down": nrm(ks[15], (DEPTH, D_FF, D_MODEL), 0.5 * D_FF ** -0.5),
        "ln_f": 1.0 + nrm(ks[16], (D_MODEL,), 0.02),
    }


def reference(x_prompt, x_sample, ln1, w_in, a_sink, lam_q1, lam_k1, lam_q2, lam_k2, subln,
              w_out, ln2, w_up, conv_w, conv_b, w_down, ln_f):
    y_prompt = trunk(x_prompt, ln1, w_in, a_sink, lam_q1, lam_k1, lam_q2, lam_k2, subln, w_out,
                     ln2, w_up, conv_w, conv_b, w_down, ln_f)
    y_sample = trunk(x_sample, ln1, w_in, a_sink, lam_q1, lam_k1, lam_q2, lam_k2, subln, w_out,
                     ln2, w_up, conv_w, conv_b, w_down, ln_f)
    return (y_prompt, y_sample)
```

```python
import contextlib
import numpy as np
import ml_dtypes
import concourse.bass as bass
import concourse.mybir as mybir
from concourse.bass_utils import run_bass_kernel_spmd

F32 = mybir.dt.float32
BF16 = mybir.dt.bfloat16
AF = mybir.ActivationFunctionType
ALU = mybir.AluOpType

D = 1024
DEPTH = 2
HD = 64
D_IN = 4352
D_FF = 2816
EPS = 1e-6
SUBLN_EPS = 1e-5
BIG = float(2 ** 20)
B_SKIP = 110.0
SAME_ENG_SYNC = True

W_COLS_T = [(0, 256), (256, 128), (512, 512), (1024, 512), (2048, 768), (2816, 768)]
QT_AQ, QT_AK, QT_BQ, QT_BK, QT_CQ, QT_CK = 0, 256, 384, 896, 1408, 2176
QT_ROWS = 2944
W_COLS_V = [(384, 128), (1536, 512), (3584, 768)]
VT_AV, VT_BV, VT_CV = 0, 128, 640
VT_COLS = 1408


class _Op:
    __slots__ = ("eng", "fn", "deps", "dma_key", "needs_inc", "sem", "val", "idx")

    def __init__(self, eng, fn, dma_key):
        self.eng = eng
        self.fn = fn
        self.deps = []
        self.dma_key = dma_key
        self.needs_inc = dma_key is not None
        self.sem = None
        self.val = 0


class Phase:
    ENGS = ("pe", "act", "dve", "pool", "sp")

    def __init__(self, nc):
        self.nc = nc
        self.ops = {e: [] for e in self.ENGS}
        self.lastw = {}
        self.readers = {}
        self.last_dma = {}

    def op(self, eng, fn, reads=(), writes=(), dma=None):
        o = _Op(eng, fn, dma)
        deps = []
        for r in reads:
            w = self.lastw.get(r)
            if w is not None:
                deps.append(w)
        for r in writes:
            w = self.lastw.get(r)
            if w is not None:
                deps.append(w)
            deps.extend(self.readers.get(r, ()))
        if dma is not None:
            prev = self.last_dma.get(dma)
            if prev is not None:
                deps.append(prev)
            self.last_dma[dma] = o
        seen = set()
        for d in deps:
            if d is o or id(d) in seen:
                continue
            seen.add(id(d))
            if d.dma_key is None and d.eng == eng and (eng == "pe" or not SAME_ENG_SYNC):
                continue
            o.deps.append(d)
            d.needs_inc = True
        for r in reads:
            self.readers.setdefault(r, []).append(o)
        for r in writes:
            self.lastw[r] = o
            self.readers[r] = []
        self.ops[eng].append(o)
        return o

    def emit(self):
        nc = self.nc
        reg = SEMREG
        LIM = 30000

        def assign(key, o, inc):
            ent = reg.get(key)
            if ent is None or ent[1] + inc > LIM:
                ent = [nc.alloc_semaphore("k%d" % len(reg.setdefault("_all", []))), 0]
                reg["_all"].append(ent[0])
                reg[key] = ent
            ent[1] += inc
            o.sem = ent[0]
            o.val = ent[1]

        for e in self.ENGS:
            for o in self.ops[e]:
                if o.dma_key is None:
                    if o.needs_inc:
                        assign(("eng", e), o, 1)
                else:
                    assign(("dma", o.dma_key), o, 16)
        final = {}
        for e in self.ENGS:
            for o in self.ops[e]:
                if o.dma_key is not None:
                    final[id(o.sem)] = (o.sem, o.val)
        with nc.Block() as block:
            def make(e):
                def body(eng):
                    waited = {}
                    for o in self.ops[e]:
                        for d in o.deps:
                            if waited.get(id(d.sem), 0) >= d.val:
                                continue
                            waited[id(d.sem)] = d.val
                            eng.wait_ge(d.sem, d.val)
                        ins = o.fn(eng)
                        if o.needs_inc:
                            ins.then_inc(o.sem, 16 if o.dma_key is not None else 1)
                    if e == "sp":
                        for k, (sm, v) in final.items():
                            if waited.get(k, 0) < v:
                                eng.wait_ge(sm, v)
                return body

            block.tensor(make("pe"))
            block.scalar(make("act"))
            block.vector(make("dve"))
            block.gpsimd(make("pool"))
            block.sync(make("sp"))


SEMREG = {}


def _bf(a):
    return np.asarray(a, np.float32).astype(ml_dtypes.bfloat16)


class Builder:
    def __init__(self, seqs, depth=DEPTH):
        self.seqs = seqs
        self.depth = depth
        self.nc = bass.Bass("TRN2", target_bir_lowering=False)
        nc = self.nc
        SEMREG.clear()
        self.dr = {}

        def din(name, shape, dt=F32):
            self.dr[name] = nc.dram_tensor(name, list(shape), dt, kind="ExternalInput").ap()

        def dout(name, shape):
            self.dr[name] = nc.dram_tensor(name, list(shape), F32, kind="ExternalOutput").ap()

        def dtmp(name, shape, dt):
            self.dr[name] = nc.dram_tensor(name, list(shape), dt).ap()

        for (s, L) in seqs:
            din("x_" + s, (L, D))
            dout("y_" + s, (L, D))
            dtmp("XT_" + s, (D, L), F32)
            dtmp("XU_" + s, (D, L), F32)
            dtmp("QT_" + s, (QT_ROWS, L), BF16)
            dtmp("VT_" + s, (L, VT_COLS), BF16)
            dtmp("OT_" + s, (D, L), BF16)
        Lmax = max(L for _, L in seqs)
        self.Lmax = Lmax
        din("w_in", (depth, D, D_IN))
        din("w_out", (depth, D, D))
        din("w_up", (depth, D, 2 * D_FF))
        din("w_down", (depth, D_FF, D))
        din("ln1", (128, depth * 8))
        din("ln2", (128, depth * 8))
        din("lnf", (128, 8))
        din("subln", (128, depth))
        din("sink", (128, depth * 4))
        din("lam", (128, depth * 4 * 64))
        din("convw", (128, depth * 3 * 44))
        din("convb", (128, depth * 44))
        din("c_ident", (128, 128))
        din("c_diagA", (128, 4 * 128), BF16)
        din("c_diagB", (128, 4 * 128), BF16)
        din("c_diagC", (128, 24 * 128), BF16)
        din("c_MA", (128, 1152), BF16)
        din("c_MC0", (128, 1152), BF16)
        din("c_MC1", (128, 1408), BF16)
        din("c_MC2", (128, 2944), BF16)
        din("c_MBh", (128, 896), BF16)
        din("c_MBl", (128, 896), BF16)
        din("c_kaug", (4, 4, Lmax), BF16)
        din("c_qaug", (4, 2, 4, Lmax), BF16)

    def phase_transpose_in(self, s, L):
        nc, dr = self.nc, self.dr
        x = dr["x_" + s]
        XT = dr["XT_" + s].rearrange("(c p) t -> p c t", p=128)
        nt = L // 512
        with contextlib.ExitStack() as st:
            ident = st.enter_context(nc.sbuf_tensor(self._nm("p0_id"), [128, 128], F32))
            xin = st.enter_context(nc.sbuf_tensor(self._nm("p0_xin"), [128, 2, 4, D], F32))
            xt = st.enter_context(nc.sbuf_tensor(self._nm("p0_xt"), [128, 2, 8, 512], F32))
            ps = st.enter_context(nc.psum_tensor(self._nm("p0_ps"), [128, 4, 512], F32))
            ph = Phase(nc)
            ph.op("sp", lambda e: e.dma_start(out=ident[:], in_=dr["c_ident"][:, :]), writes=["ident"], dma="ld_id")
            for t in range(nt):
                b = t % 2
                src = x[t * 512:(t + 1) * 512, :].rearrange("(s p) f -> p s f", p=128)
                ph.op("sp", lambda e, b=b, src=src: e.dma_start(out=xin[:, b], in_=src),
                      writes=[("xin", b)], dma="ld_xin%d" % b)
                for c in range(8):
                    pb = (t * 8 + c) % 4
                    for sub in range(4):
                        ph.op("pe", lambda e, pb=pb, sub=sub, b=b, c=c: e.transpose(
                            ps[:, pb, sub * 128:(sub + 1) * 128], xin[:, b, sub, c * 128:(c + 1) * 128], ident[:]),
                            reads=[("xin", b), "ident"], writes=[("ps", pb, sub)])
                    if c % 2 == 0:
                        ph.op("act", lambda e, pb=pb, b=b, c=c: e.copy(out=xt[:, b, c, :], in_=ps[:, pb, :]),
                              reads=[("ps", pb, q) for q in range(4)], writes=[("xt", b, c)])
                    else:
                        ph.op("dve", lambda e, pb=pb, b=b, c=c: e.tensor_copy(out=xt[:, b, c, :], in_=ps[:, pb, :]),
                              reads=[("ps", pb, q) for q in range(4)], writes=[("xt", b, c)])
                ph.op("pool", lambda e, b=b, t=t: e.dma_start(out=XT[:, :, t * 512:(t + 1) * 512], in_=xt[:, b]),
                      reads=[("xt", b, c) for c in range(8)], dma="st_xt%d" % b)
            ph.emit()

    def _rmsnorm_tile(self, ph, xt_ap, xt_res, ncols, sq, ones_bf, ss_ps, ss_res, r_ap, r_res, h_ap_fn, h_res_fn,
                      g_ap_fn, nfeat, eps):
        ph.op("act", lambda e: e.activation(out=sq, in_=xt_ap, func=AF.Square),
              reads=[xt_res], writes=["sq"])
        for c in range(8):
            ph.op("pe", lambda e, c=c: e.matmul(ss_ps, ones_bf, sq[:, c, :], start=(c == 0), stop=(c == 7)),
                  reads=["sq", "ones"], writes=[ss_res])
        ph.op("dve", lambda e: e.tensor_scalar(r_ap, ss_ps, 1.0 / nfeat, eps, ALU.mult, ALU.add),
              reads=[ss_res], writes=[r_res])
        ph.op("act", lambda e: e.activation(out=r_ap, in_=r_ap, func=AF.Sqrt), reads=[r_res], writes=[r_res])
        ph.op("dve", lambda e: e.reciprocal(r_ap, r_ap), reads=[r_res], writes=[r_res])
        for c in range(8):
            eng = "dve"
            ph.op(eng, lambda e, c=c: e.scalar_tensor_tensor(h_ap_fn(c), xt_ap[:, c, :], g_ap_fn(c), r_ap,
                                                              ALU.mult, ALU.mult),
                  reads=[xt_res, r_res, "params"], writes=[h_res_fn(c)])

    def phase_proj(self, l, s, L):
        nc, dr = self.nc, self.dr
        XT = dr[("XT_", "XU_")[l % 2] + s].rearrange("(c p) t -> p c t", p=128)
        QT = dr["QT_" + s].rearrange("(j p) t -> p j t", p=128)
        VT = dr["VT_" + s]
        w_in = dr["w_in"]
        nt = L // 512
        with contextlib.ExitStack() as st:
            w = st.enter_context(nc.sbuf_tensor(self._nm("p1_w"), [128, 8, D_IN], BF16))
            g = st.enter_context(nc.sbuf_tensor(self._nm("p1_g"), [128, 8], F32))
            ones = st.enter_context(nc.sbuf_tensor(self._nm("p1_ones"), [128, 128], BF16))
            xt = st.enter_context(nc.sbuf_tensor(self._nm("p1_xt"), [128, 2, 8, 512], F32))
            sq = st.enter_context(nc.sbuf_tensor(self._nm("p1_sq"), [128, 8, 512], BF16))
            r = st.enter_context(nc.sbuf_tensor(self._nm("p1_r"), [128, 512], F32))
            h = st.enter_context(nc.sbuf_tensor(self._nm("p1_h"), [128, 2, 8, 512], BF16))
            qt = st.enter_context(nc.sbuf_tensor(self._nm("p1_qt"), [128, 2, 23, 512], BF16))
            vt = st.enter_context(nc.sbuf_tensor(self._nm("p1_vt"), [128, 2, 4, VT_COLS], BF16))
            ss_ps = st.enter_context(nc.psum_tensor(self._nm("p1_ss"), [128, 512], F32))
            ps = st.enter_context(nc.psum_tensor(self._nm("p1_ps"), [128, 6, 512], F32))
            ph = Phase(nc)
            ph.op("pool", lambda e: e.memset(ones[:], 1.0), writes=["ones"])
            ph.op("sp", lambda e: e.dma_start(out=g[:], in_=dr["ln1"][:, l * 8:(l + 1) * 8]), writes=["params"], dma="ld_g")
            wsrc = w_in[l].rearrange("(c p) n -> p c n", p=128)
            for c in range(8):
                for (n0, n1) in ((0, 1536), (1536, 3072), (3072, D_IN)):
                    ph.op("pool", lambda e, c=c, n0=n0, n1=n1: e.dma_start(out=w[:, c, n0:n1], in_=wsrc[:, c, n0:n1]),
                          writes=[("w", c, n0)], dma="ld_w%d" % (c % 4))
            wres = [("w", c, n0) for c in range(8) for n0 in (0, 1536, 3072)]
            pcount = [0]

            def next_ps():
                pcount[0] += 1
                return pcount[0] % 6

            ecount = [0]

            def evac(ph, dst, src, reads, writes):
                ecount[0] += 1
                if ecount[0] % 2 == 0:
                    ph.op("act", lambda e: e.copy(out=dst, in_=src), reads=reads, writes=writes)
                else:
                    ph.op("dve", lambda e: e.tensor_copy(out=dst, in_=src), reads=reads, writes=writes)

            for t in range(nt):
                b = t % 2
                ph.op("sp", lambda e, b=b, t=t: e.dma_start(out=xt[:, b], in_=XT[:, :, t * 512:(t + 1) * 512]),
                      writes=[("xt", b)], dma="ld_xt%d" % b)
                self._rmsnorm_tile(ph, xt[:, b], ("xt", b), 512, sq[:], ones[:], ss_ps[:], "ss", r[:], "r",
                                   lambda c, b=b: h[:, b, c, :], lambda c, b=b: ("h", b, c),
                                   lambda c: g[:, c:c + 1], D, EPS)
                hres = [("h", b, c) for c in range(8)]
                j = 0
                for (c0, n) in W_COLS_T:
                    for jj in range(n // 128):
                        pb = next_ps()
                        col = c0 + jj * 128
                        for c in range(8):
                            ph.op("pe", lambda e, pb=pb, c=c, col=col, b=b: e.matmul(
                                ps[:, pb, :], w[:, c, col:col + 128], h[:, b, c, :], start=(c == 0), stop=(c == 7)),
                                reads=hres + wres if c == 0 else [], writes=[("ps", pb)])
                        evac(ph, qt[:, b, j, :], ps[:, pb, :], [("ps", pb)], [("qt", b, j)])
                        j += 1
                ph.op("pool", lambda e, b=b, t=t: e.dma_start(out=QT[:, :, t * 512:(t + 1) * 512], in_=qt[:, b]),
                      reads=[("qt", b, j) for j in range(23)], dma="st_qt%d" % b)
                for sub in range(4):
                    for (c0, n), v0 in zip(W_COLS_V, (VT_AV, VT_BV, VT_CV)):
                        for n0 in range(0, n, 512):
                            nn = min(512, n - n0)
                            pb = next_ps()
                            for c in range(8):
                                ph.op("pe", lambda e, pb=pb, c=c, col=c0 + n0, nn=nn, b=b, sub=sub: e.matmul(
                                    ps[:, pb, 0:nn], h[:, b, c, sub * 128:(sub + 1) * 128], w[:, c, col:col + nn],
                                    start=(c == 0), stop=(c == 7)),
                                    reads=hres + wres if c == 0 else [], writes=[("ps", pb)])
                            evac(ph, vt[:, b, sub, v0 + n0:v0 + n0 + nn], ps[:, pb, 0:nn], [("ps", pb)],
                                 [("vt", b, sub, v0 + n0)])
                vsrc = VT[t * 512:(t + 1) * 512, :].rearrange("(s p) n -> p s n", p=128)
                ph.op("pool", lambda e, b=b, vsrc=vsrc: e.dma_start(out=vsrc, in_=vt[:, b]),
                      reads=[("vt", b, sub, v) for sub in range(4) for v in (0, 128, 640, 1152)], dma="st_vt%d" % b)
            ph.emit()

    def _attend(self, ph, items, nout, S_ps, O_ps, pt, acc, dv, tag):
        n = len(items)
        first = {}
        last = {}
        for i, it in enumerate(items):
            first.setdefault(it["out"], i)
            last[it["out"]] = i
        NS = S_ps.shape[1]
        NP = pt.shape[1]

        def do_s(i):
            it = items[i]
            sb = i % NS
            nm = len(it["masks"])
            ph.op("pe", lambda e: e.matmul(S_ps[:, sb, :], it["k"], it["q"], start=True, stop=(nm == 0)),
                  reads=it["reads"], writes=[("S", sb)])
            for mi, (ml, mr) in enumerate(it["masks"]):
                ph.op("pe", lambda e, ml=ml, mr=mr, mi=mi: e.matmul(S_ps[:, sb, :], ml, mr, start=False,
                                                                    stop=(mi == nm - 1)),
                      reads=["consts"], writes=[("S", sb)])
            ph.op("act", lambda e: e.activation(out=pt[:, i % NP, :], in_=S_ps[:, sb, :], func=AF.Exp, scale=0.125),
                  reads=[("S", sb)], writes=[("pt", i % NP)])

        cnt = {}

        def do_pv(i):
            it = items[i]
            o = it["out"]
            par = cnt.get(o, 0) % 2
            fresh = cnt.get(o, 0) < 2
            cnt[o] = cnt.get(o, 0) + 1
            if fresh:
                ph.op("dve", lambda e: e.tensor_copy(out=acc[:, o, par, :], in_=pt[:, i % NP, :]),
                      reads=[("pt", i % NP)], writes=[("acc", o, par)])
            else:
                ph.op("dve", lambda e: e.tensor_tensor(out=acc[:, o, par, :], in0=acc[:, o, par, :],
                                                      in1=pt[:, i % NP, :], op=ALU.add),
                      reads=[("pt", i % NP)], writes=[("acc", o, par)])
            ph.op("pe", lambda e: e.matmul(O_ps[0:dv, o, :], it["v"], pt[:, i % NP, :], start=(first[o] == i),
                                           stop=(last[o] == i)),
                  reads=[("pt", i % NP)] + it["reads"], writes=[("O", o)])

        for i in range(n):
            do_s(i)
            if i >= 1:
                do_pv(i - 1)
        do_pv(n - 1)
        for o, c in cnt.items():
            if c >= 2:
                ph.op("dve", lambda e, o=o: e.tensor_tensor(out=acc[:, o, 0, :], in0=acc[:, o, 0, :],
                                                            in1=acc[:, o, 1, :], op=ALU.add),
                      reads=[("acc", o, 1)], writes=[("acc", o, 0)])

    def _attend_b(self, ph, pairs, S_ps, O_ps, pt, acc, NPB):
        n = len(pairs)

        def do_s(i):
            it = pairs[i]
            sb = 2 * (i % 2)
            nm = len(it["masks"])
            for m in range(2):
                ph.op("pe", lambda e, m=m: e.matmul(S_ps[:, sb + m, :], it["k"][m], it["q"][m], start=True, stop=(nm == 0)),
                      reads=it["reads"], writes=[("S", i % 2)])
                for mi, (ml, mr) in enumerate(it["masks"]):
                    ph.op("pe", lambda e, m=m, ml=ml, mr=mr, mi=mi: e.matmul(S_ps[:, sb + m, :], ml, mr, start=False,
                                                                         stop=(mi == nm - 1)),
                          reads=["consts"], writes=[("S", i % 2)])
            ps0 = 2 * (i % NPB)
            ph.op("act", lambda e: e.activation(out=pt[:, ps0:ps0 + 2, :], in_=S_ps[:, sb:sb + 2, :], func=AF.Exp,
                                                scale=0.125),
                  reads=[("S", i % 2)], writes=[("pt", i % NPB)])

        def do_pv(i):
            it = pairs[i]
            ps0 = 2 * (i % NPB)
            par = i % 2
            if i < 2:
                ph.op("dve", lambda e: e.tensor_copy(out=acc[:, par], in_=pt[:, ps0:ps0 + 2, :]),
                      reads=[("pt", i % NPB)], writes=[("acc", par)])
            else:
                ph.op("dve", lambda e: e.tensor_tensor(out=acc[:, par], in0=acc[:, par], in1=pt[:, ps0:ps0 + 2, :],
                                                      op=ALU.add),
                      reads=[("pt", i % NPB)], writes=[("acc", par)])
            for m in range(2):
                ph.op("pe", lambda e, m=m: e.matmul(O_ps[:, m, :], it["v"], pt[:, ps0 + m, :], start=(i == 0),
                                                    stop=(i == n - 1)),
                      reads=[("pt", i % NPB)] + it["reads"], writes=[("O", m)])

        for i in range(n):
            do_s(i)
            if i >= 1:
                do_pv(i - 1)
        do_pv(n - 1)
        if n >= 2:
            ph.op("dve", lambda e: e.tensor_tensor(out=acc[:, 0], in0=acc[:, 0], in1=acc[:, 1], op=ALU.add),
                  reads=[("acc", 1)], writes=[("acc", 0)])

    def _finalize_den(self, ph, acc_ap, acc_res, ones_f, den_ps, den_res, rden_ap, rden_res, rows, extra=None):
        ph.op("pe", lambda e: e.matmul(den_ps, ones_f, acc_ap, start=True, stop=True),
              reads=[acc_res, "ones"], writes=[den_res])
        if extra is not None:
            ph.op("dve", lambda e: e.tensor_scalar(rden_ap, den_ps[0:rows], extra, None, ALU.add),
                  reads=[den_res, "params"], writes=[rden_res])
            ph.op("dve", lambda e: e.reciprocal(rden_ap, rden_ap), reads=[rden_res], writes=[rden_res])
        else:
            ph.op("dve", lambda e: e.reciprocal(rden_ap, den_ps[0:rows]), reads=[den_res], writes=[rden_res])

    def phase_attn_ac(self, l, s, L):
        nc, dr = self.nc, self.dr
        QT = dr["QT_" + s]
        VT = dr["VT_" + s]
        OT = dr["OT_" + s]
        nt = L // 512
        nblk = L // 128
        GW = (128, 256, 1024)
        GN = (6, 8, 20)
        with contextlib.ExitStack() as st:
            ones_f = st.enter_context(nc.sbuf_tensor(self._nm("pa_onesf"), [128, 128], F32))
            esink = st.enter_context(nc.sbuf_tensor(self._nm("pa_sink"), [128, 4], F32))
            dA = st.enter_context(nc.sbuf_tensor(self._nm("pa_dA"), [128, 4 * 128], BF16))
            dC = st.enter_context(nc.sbuf_tensor(self._nm("pa_dC"), [128, 24 * 128], BF16))
            MA = st.enter_context(nc.sbuf_tensor(self._nm("pa_MA"), [128, 1152], BF16))
            MC0 = st.enter_context(nc.sbuf_tensor(self._nm("pa_MC0"), [128, 1152], BF16))
            MC1 = st.enter_context(nc.sbuf_tensor(self._nm("pa_MC1"), [128, 1408], BF16))
            MC2 = st.enter_context(nc.sbuf_tensor(self._nm("pa_MC2"), [128, 2944], BF16))
            kA = st.enter_context(nc.sbuf_tensor(self._nm("pa_kA"), [128, 2, 768], BF16))
            vA = st.enter_context(nc.sbuf_tensor(self._nm("pa_vA"), [128, 2, 6, 128], BF16))
            qA = st.enter_context(nc.sbuf_tensor(self._nm("pa_qA"), [128, 2, 2, 512], BF16))
            kC = st.enter_context(nc.sbuf_tensor(self._nm("pa_kC"), [128, 2, 2, 34 * 128], BF16))
            vC = st.enter_context(nc.sbuf_tensor(self._nm("pa_vC"), [128, 2, 34, 256], BF16))
            qC = st.enter_context(nc.sbuf_tensor(self._nm("pa_qC"), [128, 2, 3, 2, 512], BF16))
            pt = st.enter_context(nc.sbuf_tensor(self._nm("pa_pt"), [128, 3, 512], BF16))
            acc = st.enter_context(nc.sbuf_tensor(self._nm("pa_acc"), [128, 2, 2, 512], F32))
            rden = st.enter_context(nc.sbuf_tensor(self._nm("pa_rden"), [64, 512], F32))
            ot = st.enter_context(nc.sbuf_tensor(self._nm("pa_ot"), [64, 2, 8, 512], BF16))
            S_ps = st.enter_context(nc.psum_tensor(self._nm("pa_S"), [128, 3, 512], F32))
            O_ps = st.enter_context(nc.psum_tensor(self._nm("pa_O"), [128, 2, 512], F32))
            den_ps = st.enter_context(nc.psum_tensor(self._nm("pa_den"), [128, 512], F32))
            ph = Phase(nc)
            MC = (MC0, MC1, MC2)
            ph.op("pool", lambda e: e.memset(ones_f[:], 1.0), writes=["ones"])
            ph.op("sp", lambda e: e.dma_start(out=esink[:], in_=dr["sink"][:, l * 4:(l + 1) * 4]), writes=["params"], dma="ld_c0")
            ph.op("act", lambda e: e.activation(out=esink[:], in_=esink[:], func=AF.Exp), reads=["params"], writes=["params"])
            for nm, tl in (("c_diagA", dA), ("c_diagC", dC), ("c_MA", MA), ("c_MC0", MC0), ("c_MC1", MC1), ("c_MC2", MC2)):
                ph.op("sp", lambda e, nm=nm, tl=tl: e.dma_start(out=tl[:], in_=dr[nm][:, :]), writes=["consts"], dma="ld_c1")
            goff = (0, 6, 14)
            oc = [0]
            for c in range(nt):
                b = c % 2
                a = c * 512
                u_lo = 1 if c == 0 else 0
                u_hi = 5 if c == nt - 1 else 6
                k0 = a - 128 + 128 * u_lo
                k1 = a - 128 + 128 * u_hi
                ph.op("sp", lambda e, b=b, k0=k0, k1=k1, u_lo=u_lo, u_hi=u_hi: e.dma_start(
                    out=kA[:, b, u_lo * 128:u_hi * 128], in_=QT[QT_AK:QT_AK + 128, k0:k1]),
                    writes=[("kA", b)], dma="ld_kA%d" % b)
                ph.op("sp", lambda e, b=b, k0=k0, k1=k1, u_lo=u_lo, u_hi=u_hi: e.dma_start(
                    out=vA[:, b, u_lo:u_hi, :],
                    in_=VT[k0:k1, VT_AV:VT_AV + 128].rearrange("(u p) n -> p u n", p=128)),
                    writes=[("vA", b)], dma="ld_vA%d" % b)
                for kvh in range(2):
                    for j in range(2):
                        r0 = QT_AQ + (2 * kvh + j) * 64
                        ph.op("sp", lambda e, b=b, kvh=kvh, j=j, r0=r0, a=a: e.dma_start(
                            out=qA[kvh * 64:(kvh + 1) * 64, b, j, :], in_=QT[r0:r0 + 64, a:a + 512]),
                            writes=[("qA", b, kvh, j)], dma="ld_qA%d" % b)
                cval = []
                for g in range(3):
                    ulo = max(0, -((a - GW[g]) // 128))
                    uhi = min(GN[g], (L - (a - GW[g])) // 128)
                    cval.append((ulo, uhi))
                    k0 = a - GW[g] + 128 * ulo
                    k1 = a - GW[g] + 128 * uhi
                    for pr in range(2):
                        r0 = QT_CK + g * 256 + pr * 128
                        ph.op("sp", lambda e, b=b, g=g, pr=pr, r0=r0, k0=k0, k1=k1, ulo=ulo, uhi=uhi: e.dma_start(
                            out=kC[:, b, pr, (goff[g] + ulo) * 128:(goff[g] + uhi) * 128], in_=QT[r0:r0 + 128, k0:k1]),
                            writes=[("kC", b, g, pr)], dma="ld_kC%d" % b)
                        r1 = QT_CQ + g * 256 + pr * 128
                        ph.op("sp", lambda e, b=b, g=g, pr=pr, r1=r1, a=a: e.dma_start(
                            out=qC[:, b, g, pr, :], in_=QT[r1:r1 + 128, a:a + 512]),
                            writes=[("qC", b, g, pr)], dma="ld_qC%d" % b)
                    ph.op("sp", lambda e, b=b, g=g, k0=k0, k1=k1, ulo=ulo, uhi=uhi: e.dma_start(
                        out=vC[:, b, goff[g] + ulo:goff[g] + uhi, :],
                        in_=VT[k0:k1, VT_CV + g * 256:VT_CV + (g + 1) * 256].rearrange("(u p) n -> p u n", p=128)),
                        writes=[("vC", b, g)], dma="ld_vC%d" % b)
                for hq in range(4):
                    kvh, j = hq // 2, hq % 2
                    items = []
                    for u in range(u_lo, u_hi):
                        items.append(dict(
                            q=qA[kvh * 64:(kvh + 1) * 64, b, j, :],
                            k=kA[kvh * 64:(kvh + 1) * 64, b, u * 128:(u + 1) * 128],
                            v=vA[:, b, u, kvh * 64:(kvh + 1) * 64],
                            masks=[(dA[:, hq * 128:(hq + 1) * 128], MA[:, 640 - 128 * u:640 - 128 * u + 512])],
                            out=oc[0] % 2, reads=[("kA", b), ("vA", b), ("qA", b, kvh, j), "consts"]))
                    o = oc[0] % 2
                    oc[0] += 1
                    self._attend(ph, items, 1, S_ps, O_ps, pt, acc, 64, "A")
                    self._finalize_den(ph, acc[:, o, 0, :], ("acc", o, 0), ones_f[:], den_ps[:], "den", rden[:], "rden", 64,
                                       extra=esink[0:64, hq:hq + 1])
                    ph.op("dve", lambda e, o=o, b=b, hq=hq: e.tensor_tensor(out=ot[:, b, hq, :], in0=O_ps[0:64, o, :],
                                                                           in1=rden[:], op=ALU.mult),
                          reads=[("O", o), "rden"], writes=[("ot", b, hq)])
                for h in range(4):
                    pr, hp = h // 2, h % 2
                    items = []
                    for g in range(3):
                        ulo, uhi = cval[g]
                        for u in range(ulo, uhi):
                            off = 128 * (GN[g] - 1) - 128 * u
                            gi = (g * 4 + h) * 2
                            items.append(dict(
                                q=qC[hp * 64:(hp + 1) * 64, b, g, pr, :],
                                k=kC[hp * 64:(hp + 1) * 64, b, pr, (goff[g] + u) * 128:(goff[g] + u + 1) * 128],
                                v=vC[:, b, goff[g] + u, h * 64:(h + 1) * 64],
                                masks=[(dC[:, gi * 128:(gi + 1) * 128], MC[g][:, off:off + 512])],
                                out=oc[0] % 2,
                                reads=[("kC", b, g, pr), ("vC", b, g), ("qC", b, g, pr), "consts"]))
                    o = oc[0] % 2
                    oc[0] += 1
                    self._attend(ph, items, 1, S_ps, O_ps, pt, acc, 64, "C")
                    self._finalize_den(ph, acc[:, o, 0, :], ("acc", o, 0), ones_f[:], den_ps[:], "den", rden[:], "rden", 64)
                    ph.op("dve", lambda e, o=o, b=b, h=h: e.tensor_tensor(out=ot[:, b, 4 + h, :], in0=O_ps[0:64, o, :],
                                                                         in1=rden[:], op=ALU.mult),
                          reads=[("O", o), "rden"], writes=[("ot", b, 4 + h)])
                ph.op("pool", lambda e, b=b, a=a: e.dma_start(
                    out=OT[0:256, a:a + 512].rearrange("(h p) t -> p h t", p=64), in_=ot[:, b, 0:4, :]),
                    reads=[("ot", b, hh) for hh in range(4)], dma="st_oA%d" % b)
                ph.op("pool", lambda e, b=b, a=a: e.dma_start(
                    out=OT[768:1024, a:a + 512].rearrange("(h p) t -> p h t", p=64), in_=ot[:, b, 4:8, :]),
                    reads=[("ot", b, 4 + hh) for hh in range(4)], dma="st_oC%d" % b)
            ph.emit()

    def phase_attn_b(self, l, s, L):
        nc, dr = self.nc, self.dr
        QT = dr["QT_" + s]
        VT = dr["VT_" + s]
        OT = dr["OT_" + s]
        nt = L // 512
        nblk = L // 128
        lam_init = 0.8 - 0.6 * float(np.exp(-0.3 * l))
        with contextlib.ExitStack() as st:
            ones_f = st.enter_context(nc.sbuf_tensor(self._nm("pb_onesf"), [128, 128], F32))
            ones_b = st.enter_context(nc.sbuf_tensor(self._nm("pb_onesb"), [128, 128], BF16))
            lamt = st.enter_context(nc.sbuf_tensor(self._nm("pb_lam"), [128, 4 * 64], F32))
            lt = st.enter_context(nc.sbuf_tensor(self._nm("pb_lt"), [128, 2 * 64], F32))
            ls = st.enter_context(nc.sbuf_tensor(self._nm("pb_ls"), [128, 4], F32))
            gs = st.enter_context(nc.sbuf_tensor(self._nm("pb_gs"), [128, 1], F32))
            dB = st.enter_context(nc.sbuf_tensor(self._nm("pb_dB"), [128, 4 * 128], BF16))
            MBh = st.enter_context(nc.sbuf_tensor(self._nm("pb_MBh"), [128, 896], BF16))
            MBl = st.enter_context(nc.sbuf_tensor(self._nm("pb_MBl"), [128, 896], BF16))
            kB = st.enter_context(nc.sbuf_tensor(self._nm("pb_kB"), [68, 2, L], BF16))
            vB = st.enter_context(nc.sbuf_tensor(self._nm("pb_vB"), [128, nblk, 128], BF16))
            qB = st.enter_context(nc.sbuf_tensor(self._nm("pb_qB"), [68, 2, 2, 3, 512], BF16))
            pt = st.enter_context(nc.sbuf_tensor(self._nm("pb_pt"), [128, 6, 512], BF16))
            acc = st.enter_context(nc.sbuf_tensor(self._nm("pb_acc"), [128, 2, 2, 512], F32))
            rden = st.enter_context(nc.sbuf_tensor(self._nm("pb_rden"), [128, 2, 512], F32))
            tt = st.enter_context(nc.sbuf_tensor(self._nm("pb_t"), [128, 2, 512], F32))
            sq = st.enter_context(nc.sbuf_tensor(self._nm("pb_sq"), [128, 512], BF16))
            ot = st.enter_context(nc.sbuf_tensor(self._nm("pb_ot"), [128, 2, 512], BF16))
            S_ps = st.enter_context(nc.psum_tensor(self._nm("pb_S"), [128, 4, 512], F32))
            O_ps = st.enter_context(nc.psum_tensor(self._nm("pb_O"), [128, 2, 512], F32))
            den_ps = st.enter_context(nc.psum_tensor(self._nm("pb_den"), [128, 2, 512], F32))
            ph = Phase(nc)
            ph.op("pool", lambda e: e.memset(ones_f[:], 1.0), writes=["ones"])
            ph.op("pool", lambda e: e.memset(ones_b[:], 1.0), writes=["ones"])
            ph.op("pool", lambda e: e.memset(qB[:], 0.0), writes=["qinit"])
            for nm, tl in (("c_diagB", dB), ("c_MBh", MBh), ("c_MBl", MBl)):
                ph.op("sp", lambda e, nm=nm, tl=tl: e.dma_start(out=tl[:], in_=dr[nm][:, :]), writes=["consts"], dma="ld_c1")
            ph.op("sp", lambda e: e.dma_start(out=lamt[:], in_=dr["lam"][:, l * 256:(l + 1) * 256]), writes=["lamt"], dma="ld_c0")
            ph.op("sp", lambda e: e.dma_start(out=gs[:], in_=dr["subln"][:, l:l + 1], allow_slow_non_contiguous=True), writes=["gs"], dma="ld_c0")
            ph.op("dve", lambda e: e.tensor_tensor(out=lt[:, 0:64], in0=lamt[:, 0:64], in1=lamt[:, 64:128], op=ALU.mult),
                  reads=["lamt"], writes=["lt"])
            ph.op("dve", lambda e: e.tensor_tensor(out=lt[:, 64:128], in0=lamt[:, 128:192], in1=lamt[:, 192:256], op=ALU.mult),
                  reads=["lamt"], writes=["lt"])
            ph.op("dve", lambda e: e.reduce_sum(ls[:, 0:1], lt[:, 0:64], mybir.AxisListType.X), reads=["lt"], writes=["ls"])
            ph.op("dve", lambda e: e.reduce_sum(ls[:, 1:2], lt[:, 64:128], mybir.AxisListType.X), reads=["lt"], writes=["ls"])
            ph.op("act", lambda e: e.activation(out=ls[:, 0:2], in_=ls[:, 0:2], func=AF.Exp), reads=["ls"], writes=["ls"])
            ph.op("dve", lambda e: e.tensor_tensor(out=ls[:, 2:3], in0=ls[:, 1:2], in1=ls[:, 0:1], op=ALU.subtract),
                  reads=["ls"], writes=["ls"])
            ph.op("dve", lambda e: e.tensor_scalar(ls[:, 2:3], ls[:, 2:3], -lam_init, None, ALU.add),
                  reads=["ls"], writes=["ls"])
            ph.op("dve", lambda e: e.tensor_scalar(gs[:], gs[:], 1.0 - lam_init, None, ALU.mult),
                  reads=["gs"], writes=["gs"])
            HCH = min(4096, L)
            VCH = min(32, nblk)
            for h in range(4):
                for m in range(2):
                    r0 = QT_BK + h * 128 + m * 64
                    for c0 in range(0, L, HCH):
                        ph.op("sp", lambda e, m=m, r0=r0, c0=c0: e.dma_start(out=kB[0:64, m, c0:c0 + HCH],
                                                                         in_=QT[r0:r0 + 64, c0:c0 + HCH]),
                              writes=[("kB", m)], dma="ld_kB%d" % m)
                    ph.op("sp", lambda e, m=m, h=h: e.dma_start(out=kB[64:68, m, :], in_=dr["c_kaug"][h, :, 0:L]),
                          writes=[("kB", m)], dma="ld_kB%d" % m)
                for c0 in range(0, nblk, VCH):
                    ph.op("sp", lambda e, h=h, c0=c0: e.dma_start(
                        out=vB[:, c0:c0 + VCH, :],
                        in_=VT[c0 * 128:(c0 + VCH) * 128, VT_BV + h * 128:VT_BV + (h + 1) * 128].rearrange(
                            "(u p) n -> p u n", p=128)),
                        writes=["vB"], dma="ld_vB")
                for c in range(nt):
                    b = c % 2
                    a = c * 512
                    for m in range(2):
                        r0 = QT_BQ + h * 128 + m * 64
                        for var in range(3):
                            ph.op("sp", lambda e, b=b, m=m, var=var, r0=r0, a=a: e.dma_start(
                                out=qB[0:64, b, m, var, :], in_=QT[r0:r0 + 64, a:a + 512]),
                                reads=["qinit"], writes=[("qB", b, m)], dma="ld_qB%d" % b)
                        for var in range(2):
                            ph.op("sp", lambda e, b=b, m=m, var=var, h=h, a=a: e.dma_start(
                                out=qB[64:68, b, m, var, :], in_=dr["c_qaug"][h, var, :, a:a + 512]),
                                reads=["qinit"], writes=[("qB", b, m)], dma="ld_qB%d" % b)
                    pairs = []
                    slope = 2.0 ** (-2.0 * (h + 1))
                    for kb in range(nblk):
                        k0 = kb * 128
                        dmin = max(0, k0 - (a + 511), a - (k0 + 127))
                        if slope * dmin >= B_SKIP:
                            continue
                        if kb < 4 * c:
                            var, masks = 0, []
                        elif kb > 4 * c + 3:
                            var, masks = 1, []
                        else:
                            u = kb - 4 * c
                            off = 384 - 128 * u
                            var = 2
                            masks = [(dB[:, h * 128:(h + 1) * 128], MBh[:, off:off + 512]),
                                     (dB[:, h * 128:(h + 1) * 128], MBl[:, off:off + 512])]
                        pairs.append(dict(q=[qB[:, b, 0, var, :], qB[:, b, 1, var, :]],
                                          k=[kB[:, 0, k0:k0 + 128], kB[:, 1, k0:k0 + 128]],
                                          v=vB[:, kb, :], masks=masks,
                                          reads=[("kB", 0), ("kB", 1), "vB", ("qB", b, 0), ("qB", b, 1), "consts"]))
                    self._attend_b(ph, pairs, S_ps, O_ps, pt, acc, 3)
                    for m in range(2):
                        self._finalize_den(ph, acc[:, 0, m, :], ("acc", 0), ones_f[:], den_ps[:, m, :], ("den", m),
                                           rden[:, m, :], ("rden", m), 128)
                        ph.op("dve", lambda e, m=m: e.tensor_tensor(out=tt[:, m, :], in0=O_ps[:, m, :], in1=rden[:, m, :],
                                                                    op=ALU.mult),
                              reads=[("O", m), ("rden", m)], writes=[("tt", m)])
                    ph.op("dve", lambda e: e.scalar_tensor_tensor(tt[:, 0, :], tt[:, 1, :], ls[:, 2:3], tt[:, 0, :],
                                                                  ALU.mult, ALU.add),
                          reads=[("tt", 0), ("tt", 1), "ls"], writes=[("tt", 0)])
                    ph.op("act", lambda e: e.activation(out=sq[:], in_=tt[:, 0, :], func=AF.Square),
                          reads=[("tt", 0)], writes=["sq"])
                    ph.op("pe", lambda e: e.matmul(den_ps[:, 0, :], ones_b[:], sq[:], start=True, stop=True),
                          reads=["sq", "ones"], writes=[("den", 0)])
                    ph.op("dve", lambda e: e.tensor_scalar(rden[:, 0, :], den_ps[:, 0, :], 1.0 / 128, SUBLN_EPS,
                                                           ALU.mult, ALU.add),
                          reads=[("den", 0)], writes=[("rden", 0)])
                    ph.op("act", lambda e: e.activation(out=rden[:, 0, :], in_=rden[:, 0, :], func=AF.Sqrt),
                          reads=[("rden", 0)], writes=[("rden", 0)])
                    ph.op("dve", lambda e: e.reciprocal(rden[:, 0, :], rden[:, 0, :]),
                          reads=[("rden", 0)], writes=[("rden", 0)])
                    ph.op("dve", lambda e, b=b: e.scalar_tensor_tensor(ot[:, b, :], tt[:, 0, :], gs[:, 0:1], rden[:, 0, :],
                                                                       ALU.mult, ALU.mult),
                          reads=[("tt", 0), ("rden", 0), "gs"], writes=[("ot", b)])
                    ph.op("pool", lambda e, b=b, a=a, h=h: e.dma_start(
                        out=OT[256 + h * 128:256 + (h + 1) * 128, a:a + 512], in_=ot[:, b, :]),
                        reads=[("ot", b)], dma="st_oB%d" % b)
            ph.emit()

    def phase_wout(self, l, s, L):
        nc, dr = self.nc, self.dr
        XT = dr[("XT_", "XU_")[l % 2] + s].rearrange("(c p) t -> p c t", p=128)
        OT = dr["OT_" + s].rearrange("(c p) t -> p c t", p=128)
        nt = L // 512
        with contextlib.ExitStack() as st:
            w = st.enter_context(nc.sbuf_tensor(self._nm("pw_w"), [128, 8, D], BF16))
            xt = st.enter_context(nc.sbuf_tensor(self._nm("pw_xt"), [128, 2, 8, 512], F32))
            ot = st.enter_context(nc.sbuf_tensor(self._nm("pw_ot"), [128, 2, 8, 512], BF16))
            ps = st.enter_context(nc.psum_tensor(self._nm("pw_ps"), [128, 4, 512], F32))
            ph = Phase(nc)
            wsrc = dr["w_out"][l].rearrange("(c p) n -> p c n", p=128)
            for c in range(8):
                ph.op("pool", lambda e, c=c: e.dma_start(out=w[:, c, :], in_=wsrc[:, c, :]), writes=[("w", c)],
                      dma="ld_w%d" % (c % 4))
            wres = [("w", c) for c in range(8)]
            k = 0
            for t in range(nt):
                b = t % 2
                ph.op("sp", lambda e, b=b, t=t: e.dma_start(out=xt[:, b], in_=XT[:, :, t * 512:(t + 1) * 512]),
                      writes=[("xt", b, c) for c in range(8)], dma="ld_xt%d" % b)
                ph.op("sp", lambda e, b=b, t=t: e.dma_start(out=ot[:, b], in_=OT[:, :, t * 512:(t + 1) * 512]),
                      writes=[("ot", b)], dma="ld_ot%d" % b)
                for oc in range(8):
                    pb = k % 4
                    k += 1
                    for c in range(8):
                        ph.op("pe", lambda e, pb=pb, c=c, oc=oc, b=b: e.matmul(
                            ps[:, pb, :], w[:, c, oc * 128:(oc + 1) * 128], ot[:, b, c, :], start=(c == 0), stop=(c == 7)),
                            reads=[("ot", b)] + wres if c == 0 else [], writes=[("ps", pb)])
                    ph.op("dve", lambda e, pb=pb, b=b, oc=oc: e.tensor_tensor(out=xt[:, b, oc, :], in0=ps[:, pb, :],
                                                                          in1=xt[:, b, oc, :], op=ALU.add),
                          reads=[("ps", pb)], writes=[("xt", b, oc)])
                ph.op("pool", lambda e, b=b, t=t: e.dma_start(out=XT[:, :, t * 512:(t + 1) * 512], in_=xt[:, b]),
                      reads=[("xt", b, c) for c in range(8)], dma="st_xt%d" % b)
            ph.emit()

    def phase_mlp(self, l, s, L):
        nc, dr = self.nc, self.dr
        XT = dr[("XT_", "XU_")[l % 2] + s].rearrange("(c p) t -> p c t", p=128)
        XO = dr[("XT_", "XU_")[(l + 1) % 2] + s].rearrange("(c p) t -> p c t", p=128)
        NT = 256
        NC = NT + 2
        nt = L // NT
        with contextlib.ExitStack() as st:
            wu = st.enter_context(nc.sbuf_tensor(self._nm("pm_wu"), [128, 8, 2 * D_FF], BF16))
            wd = st.enter_context(nc.sbuf_tensor(self._nm("pm_wd"), [128, 22, D], BF16))
            g = st.enter_context(nc.sbuf_tensor(self._nm("pm_g"), [128, 8], F32))
            cw = st.enter_context(nc.sbuf_tensor(self._nm("pm_cw"), [128, 3 * 44], F32))
            cb = st.enter_context(nc.sbuf_tensor(self._nm("pm_cb"), [128, 44], F32))
            ones = st.enter_context(nc.sbuf_tensor(self._nm("pm_ones"), [128, 128], BF16))
            xt = st.enter_context(nc.sbuf_tensor(self._nm("pm_xt"), [128, 2, 8, NC], F32))
            sq = st.enter_context(nc.sbuf_tensor(self._nm("pm_sq"), [128, 8, NC], BF16))
            r = st.enter_context(nc.sbuf_tensor(self._nm("pm_r"), [128, NC], F32))
            h = st.enter_context(nc.sbuf_tensor(self._nm("pm_h"), [128, 8, NC], BF16))
            tmp = st.enter_context(nc.sbuf_tensor(self._nm("pm_tmp"), [128, 2, 2, NT], F32))
            gt = st.enter_context(nc.sbuf_tensor(self._nm("pm_gt"), [128, 22, NT], BF16))
            xo = st.enter_context(nc.sbuf_tensor(self._nm("pm_xo"), [128, 2, 8, NT], F32))
            ss_ps = st.enter_context(nc.psum_tensor(self._nm("pm_ss"), [128, 512], F32))
            u_ps = st.enter_context(nc.psum_tensor(self._nm("pm_u"), [128, 4, 512], F32))
            o_ps = st.enter_context(nc.psum_tensor(self._nm("pm_o"), [128, 2, 512], F32))
            ph = Phase(nc)
            ph.op("pool", lambda e: e.memset(ones[:], 1.0), writes=["ones"])
            ph.op("sp", lambda e: e.dma_start(out=g[:], in_=dr["ln2"][:, l * 8:(l + 1) * 8]), writes=["params"], dma="ld_g")
            ph.op("sp", lambda e: e.dma_start(out=cw[:], in_=dr["convw"][:, l * 132:(l + 1) * 132]), writes=["params"], dma="ld_g")
            ph.op("sp", lambda e: e.dma_start(out=cb[:], in_=dr["convb"][:, l * 44:(l + 1) * 44]), writes=["params"], dma="ld_g")
            wsrc = dr["w_up"][l].rearrange("(c p) n -> p c n", p=128)
            for c in range(8):
                for n0 in range(0, 2 * D_FF, 1408):
                    ph.op("pool", lambda e, c=c, n0=n0: e.dma_start(out=wu[:, c, n0:n0 + 1408], in_=wsrc[:, c, n0:n0 + 1408]),
                          writes=[("wu", c, n0)], dma="ld_w%d" % (c % 4))
            wures = [("wu", c, n0) for c in range(8) for n0 in range(0, 2 * D_FF, 1408)]
            wdsrc = dr["w_down"][l].rearrange("(c p) n -> p c n", p=128)
            for c in range(22):
                ph.op("pool", lambda e, c=c: e.dma_start(out=wd[:, c, :], in_=wdsrc[:, c, :]), writes=[("wd", c)],
                      dma="ld_w%d" % (c % 4))
            wdres = [("wd", c) for c in range(22)]
            uk = 0
            ok = 0
            for t in range(nt):
                b = t % 2
                t0 = t * NT
                lo = 1 if t == 0 else 0
                hi = NC - 1 if t == nt - 1 else NC
                if lo:
                    ph.op("pool", lambda e, b=b: e.memset(xt[:, b, :, 0:1], 0.0), writes=[("xt", b)])
                if hi != NC:
                    ph.op("pool", lambda e, b=b: e.memset(xt[:, b, :, NC - 1:NC], 0.0), writes=[("xt", b)])
                ph.op("sp", lambda e, b=b, t0=t0, lo=lo, hi=hi: e.dma_start(out=xt[:, b, :, lo:hi],
                                                                       in_=XT[:, :, t0 - 1 + lo:t0 - 1 + hi]),
                      writes=[("xt", b)] if not (lo or hi != NC) else [("xt", b), ("xtedge", b)], dma="ld_xt%d" % b)
                self._rmsnorm_tile(ph, xt[:, b], ("xt", b), NC, sq[:], ones[:], ss_ps[:, 0:NC], "ss", r[:], "r",
                                   lambda c: h[:, c, :], lambda c: ("h", c), lambda c: g[:, c:c + 1], D, EPS)
                hres = [("h", c) for c in range(8)]
                for p in range(22):
                    for part in range(2):
                        j = p + 22 * part
                        ub = uk % 4
                        uk += 1
                        for c in range(8):
                            ph.op("pe", lambda e, ub=ub, c=c, j=j: e.matmul(
                                u_ps[:, ub, 0:NC], wu[:, c, j * 128:(j + 1) * 128], h[:, c, :], start=(c == 0), stop=(c == 7)),
                                reads=hres + wures if c == 0 else [], writes=[("u", ub)])
                        tb = p % 2
                        tm = tmp[:, tb, part, :]
                        tres = ("tmp", tb, part)
                        ph.op("act", lambda e, tm=tm, ub=ub, j=j: e.activation(
                            out=tm, in_=u_ps[:, ub, 1:NT + 1], func=AF.Identity, bias=cb[:, j:j + 1],
                            scale=cw[:, 44 + j:45 + j]), reads=[("u", ub), "params"], writes=[tres])
                        ph.op("dve", lambda e, tm=tm, ub=ub, j=j: e.scalar_tensor_tensor(
                            tm, u_ps[:, ub, 0:NT], cw[:, j:j + 1], tm, ALU.mult, ALU.add),
                            reads=[("u", ub), "params"], writes=[tres])
                        ph.op("dve", lambda e, tm=tm, ub=ub, j=j: e.scalar_tensor_tensor(
                            tm, u_ps[:, ub, 2:NT + 2], cw[:, 88 + j:89 + j], tm, ALU.mult, ALU.add),
                            reads=[("u", ub), "params"], writes=[tres])
                    tb = p % 2
                    ph.op("act", lambda e, tb=tb: e.activation(out=tmp[:, tb, 0, :], in_=tmp[:, tb, 0, :], func=AF.Silu),
                          reads=[("tmp", tb, 0)], writes=[("tmp", tb, 0)])
                    ph.op("pool", lambda e, tb=tb, p=p: e.tensor_tensor(out=gt[:, p, :], in0=tmp[:, tb, 0, :],
                                                                      in1=tmp[:, tb, 1, :], op=ALU.mult),
                          reads=[("tmp", tb, 0), ("tmp", tb, 1)], writes=[("gt", p)])
                gres = [("gt", p) for p in range(22)]
                for oc in range(8):
                    ob = ok % 2
                    ok += 1
                    for c in range(22):
                        ph.op("pe", lambda e, ob=ob, c=c, oc=oc: e.matmul(
                            o_ps[:, ob, 0:NT], wd[:, c, oc * 128:(oc + 1) * 128], gt[:, c, :], start=(c == 0), stop=(c == 21)),
                            reads=gres + wdres if c == 0 else [], writes=[("o", ob)])
                    ph.op("dve", lambda e, ob=ob, b=b, oc=oc: e.tensor_tensor(out=xo[:, b, oc, :], in0=o_ps[:, ob, 0:NT],
                                                                          in1=xt[:, b, oc, 1:NT + 1], op=ALU.add),
                          reads=[("o", ob), ("xt", b)], writes=[("xo", b, oc)])
                ph.op("pool", lambda e, b=b, t0=t0: e.dma_start(out=XO[:, :, t0:t0 + NT], in_=xo[:, b]),
                      reads=[("xo", b, c) for c in range(8)], dma="st_xo%d" % b)
            ph.emit()

    def phase_final(self, s, L):
        nc, dr = self.nc, self.dr
        XT = dr[("XT_", "XU_")[self.depth % 2] + s].rearrange("(c p) t -> p c t", p=128)
        y = dr["y_" + s]
        nt = L // 512
        with contextlib.ExitStack() as st:
            ident = st.enter_context(nc.sbuf_tensor(self._nm("pf_id"), [128, 128], F32))
            g = st.enter_context(nc.sbuf_tensor(self._nm("pf_g"), [128, 8], F32))
            ones = st.enter_context(nc.sbuf_tensor(self._nm("pf_ones"), [128, 128], BF16))
            xt = st.enter_context(nc.sbuf_tensor(self._nm("pf_xt"), [128, 2, 8, 512], F32))
            sq = st.enter_context(nc.sbuf_tensor(self._nm("pf_sq"), [128, 8, 512], BF16))
            r = st.enter_context(nc.sbuf_tensor(self._nm("pf_r"), [128, 512], F32))
            h = st.enter_context(nc.sbuf_tensor(self._nm("pf_h"), [128, 8, 512], F32))
            yo = st.enter_context(nc.sbuf_tensor(self._nm("pf_yo"), [128, 2, 4, D], F32))
            ss_ps = st.enter_context(nc.psum_tensor(self._nm("pf_ss"), [128, 512], F32))
            ps = st.enter_context(nc.psum_tensor(self._nm("pf_ps"), [128, 3, 2, 512], F32))
            ph = Phase(nc)
            ph.op("pool", lambda e: e.memset(ones[:], 1.0), writes=["ones"])
            ph.op("sp", lambda e: e.dma_start(out=ident[:], in_=dr["c_ident"][:, :]), writes=["ident"], dma="ld_g")
            ph.op("sp", lambda e: e.dma_start(out=g[:], in_=dr["lnf"][:, :]), writes=["params"], dma="ld_g")
            k = 0
            for t in range(nt):
                b = t % 2
                ph.op("sp", lambda e, b=b, t=t: e.dma_start(out=xt[:, b], in_=XT[:, :, t * 512:(t + 1) * 512]),
                      writes=[("xt", b)], dma="ld_xt%d" % b)
                self._rmsnorm_tile(ph, xt[:, b], ("xt", b), 512, sq[:], ones[:], ss_ps[:], "ss", r[:], "r",
                                   lambda c: h[:, c, :], lambda c: ("h", c), lambda c: g[:, c:c + 1], D, EPS)
                for sub in range(4):
                    pb = k % 3
                    k += 1
                    for c in range(8):
                        ph.op("pe", lambda e, pb=pb, c=c, sub=sub: e.transpose(
                            ps[:, pb, c // 4, (c % 4) * 128:(c % 4 + 1) * 128], h[:, c, sub * 128:(sub + 1) * 128], ident[:]),
                            reads=[("h", c), "ident"], writes=[("ps", pb)])
                    if sub % 2 == 0:
                        ph.op("act", lambda e, pb=pb, b=b, sub=sub: e.copy(
                            out=yo[:, b, sub, :], in_=ps[:, pb].rearrange("p a n -> p (a n)")),
                            reads=[("ps", pb)], writes=[("yo", b, sub)])
                    else:
                        ph.op("dve", lambda e, pb=pb, b=b, sub=sub: e.tensor_copy(
                            out=yo[:, b, sub, :], in_=ps[:, pb].rearrange("p a n -> p (a n)")),
                            reads=[("ps", pb)], writes=[("yo", b, sub)])
                dst = y[t * 512:(t + 1) * 512, :].rearrange("(s p) f -> p s f", p=128)
                ph.op("pool", lambda e, b=b, dst=dst: e.dma_start(out=dst, in_=yo[:, b]),
                      reads=[("yo", b, sub) for sub in range(4)], dma="st_y%d" % b)
            ph.emit()

    def _nm(self, base):
        self._cnt = getattr(self, "_cnt", 0) + 1
        return "%s_%d" % (base, self._cnt)

    def build(self):
        import os
        lim = int(os.environ.get("KPH", "1000"))
        plist = []
        for (s, L) in self.seqs:
            plist.append(lambda s=s, L=L: self.phase_transpose_in(s, L))
            for l in range(self.depth):
                plist.append(lambda s=s, L=L, l=l: self.phase_proj(l, s, L))
                plist.append(lambda s=s, L=L, l=l: self.phase_attn_ac(l, s, L))
                plist.append(lambda s=s, L=L, l=l: self.phase_attn_b(l, s, L))
                plist.append(lambda s=s, L=L, l=l: self.phase_wout(l, s, L))
                plist.append(lambda s=s, L=L, l=l: self.phase_mlp(l, s, L))
            plist.append(lambda s=s, L=L: self.phase_final(s, L))
        for f in plist[:lim]:
            f()
        return self.nc


def _toeplitz(width, X, f):
    kk = np.arange(128)[:, None]
    col = np.arange(width)[None, :]
    return f(col - kk - X)


def make_consts(Lmax):
    c = {}
    c["c_ident"] = np.eye(128, dtype=np.float32)
    slA = 2.0 ** (-8.0 * np.arange(1, 5) / 4)
    slC = (2.0 ** (-8.0 * np.arange(1, 13) / 12)).astype(np.float32).astype(np.float64)
    eye = np.eye(128)
    c["c_diagA"] = _bf(np.concatenate([eye * (-8.0 * s) for s in slA], 1))
    c["c_diagB"] = _bf(np.concatenate([eye * (-8.0 * s) for s in slA], 1))
    dC = []
    for gh in range(12):
        v = -8.0 * slC[gh]
        hi = float(_bf(v).astype(np.float32))
        lo = float(_bf(v - hi).astype(np.float32))
        dC += [eye * hi, eye * lo]
    c["c_diagC"] = _bf(np.concatenate(dC, 1))

    def band(W, d):
        def f(delta):
            ad = np.abs(delta)
            ok = (ad <= W) & (delta % d == 0)
            return np.where(ok, ad, BIG).astype(np.float32)
        return f
    c["c_MA"] = _bf(_toeplitz(1152, 512, band(128, 1)))
    c["c_MC0"] = _bf(_toeplitz(1152, 128 * 5 - 128, band(64, 1)))
    c["c_MC1"] = _bf(_toeplitz(1408, 128 * 7 - 256, band(256, 4)))
    c["c_MC2"] = _bf(_toeplitz(2944, 128 * 19 - 1024, band(1024, 16)))
    c["c_MBh"] = _bf(_toeplitz(896, 384, lambda dl: (2 * (np.abs(dl) // 2)).astype(np.float32)))
    c["c_MBl"] = _bf(_toeplitz(896, 384, lambda dl: (np.abs(dl) % 2).astype(np.float32)))
    pos = np.arange(Lmax)
    hi = (pos // 128) * 128.0
    lo = (pos % 128) * 1.0
    kaug = np.zeros((4, 4, Lmax), np.float32)
    qaug = np.zeros((4, 2, 4, Lmax), np.float32)
    for h in range(4):
        s8 = 8.0 * slA[h]
        kaug[h] = np.stack([np.ones(Lmax), np.ones(Lmax), s8 * hi, s8 * lo])
        base = np.stack([-s8 * hi, -s8 * lo, np.ones(Lmax), np.ones(Lmax)])
        qaug[h, 0] = base
        qaug[h, 1] = -base
    c["c_kaug"] = _bf(kaug)
    c["c_qaug"] = _bf(qaug)
    return c


def layout_params(p, depth):
    out = {}

    def chunks(v, n):
        v = np.asarray(v, np.float32).reshape(-1, n, 128)
        return np.ascontiguousarray(v.transpose(2, 0, 1).reshape(128, -1))
    out["ln1"] = chunks(p["ln1"], 8)
    out["ln2"] = chunks(p["ln2"], 8)
    out["lnf"] = chunks(np.asarray(p["ln_f"])[None], 8)
    out["subln"] = np.ascontiguousarray(np.asarray(p["subln"], np.float32).T)
    out["sink"] = np.ascontiguousarray(np.broadcast_to(np.asarray(p["a_sink"], np.float32).reshape(1, -1), (128, depth * 4)))
    lam = np.stack([np.asarray(p[k], np.float32) for k in ("lam_q1", "lam_k1", "lam_q2", "lam_k2")], 1)
    out["lam"] = np.ascontiguousarray(np.broadcast_to(lam.reshape(1, -1), (128, depth * 256)))
    cw = np.asarray(p["conv_w"], np.float32).reshape(depth, 3, 44, 128)
    out["convw"] = np.ascontiguousarray(cw.transpose(3, 0, 1, 2).reshape(128, -1))
    cb = np.asarray(p["conv_b"], np.float32).reshape(depth, 44, 128)
    out["convb"] = np.ascontiguousarray(cb.transpose(2, 0, 1).reshape(128, -1))
    for k in ("w_in", "w_out", "w_up", "w_down"):
        out[k] = np.ascontiguousarray(np.asarray(p[k], np.float32))
    return out


_CACHE = {}


def run(seq_inputs, params, depth=DEPTH, n_cores=8):
    seqs = [(k, v.shape[0]) for k, v in seq_inputs[0].items()]
    key = (tuple(seqs), depth)
    if key not in _CACHE:
        _CACHE[key] = Builder(seqs, depth).build()
    nc = _CACHE[key]
    Lmax = max(L for _, L in seqs)
    shared = dict(make_consts(Lmax))
    shared.update(layout_params(params, depth))
    in_maps = []
    for c in range(n_cores):
        m = dict(shared)
        for k, v in seq_inputs[c].items():
            m["x_" + k] = np.ascontiguousarray(np.asarray(v, np.float32))
        in_maps.append(m)
    import os
    if os.environ.get("KTRACE"):
        res = run_bass_kernel_spmd(nc, in_maps, core_ids=list(range(n_cores)), trace=True)
        print("EXEC_TIME_NS", res.exec_time_ns, flush=True)
    else:
        res = run_bass_kernel_spmd(nc, in_maps, core_ids=list(range(n_cores)))
    return res.results


def kernel(x_prompt, x_sample, ln1, w_in, a_sink, lam_q1, lam_k1, lam_q2, lam_k2, subln,
           w_out, ln2, w_up, conv_w, conv_b, w_down, ln_f):
    params = dict(ln1=ln1, w_in=w_in, a_sink=a_sink, lam_q1=lam_q1, lam_k1=lam_k1, lam_q2=lam_q2, lam_k2=lam_k2,
                  subln=subln, w_out=w_out, ln2=ln2, w_up=w_up, conv_w=conv_w, conv_b=conv_b, w_down=w_down, ln_f=ln_f)
    x_prompt = np.asarray(x_prompt, np.float32)
    x_sample = np.asarray(x_sample, np.float32)
    seq_inputs = [{"s": x_sample[c], "p": x_prompt[0]} for c in range(8)]
    res = run(seq_inputs, params)
    y_sample = np.stack([res[c]["y_s"] for c in range(8)], 0)
    y_prompt = res[0]["y_p"][None]
    return (y_prompt.astype(np.float32), y_sample.astype(np.float32))
```

```python
import contextlib
import numpy as np
import ml_dtypes
import concourse.bass as bass
import concourse.mybir as mybir
from concourse.bass_utils import run_bass_kernel_spmd

F32 = mybir.dt.float32
BF16 = mybir.dt.bfloat16
AF = mybir.ActivationFunctionType
ALU = mybir.AluOpType

D = 1024
DEPTH = 2
HD = 64
D_IN = 4352
D_FF = 2816
EPS = 1e-6
SUBLN_EPS = 1e-5
BIG = float(2 ** 20)
B_SKIP = 110.0
QM = 512
KM = 1536
KBIAS_NEG = -30000.0
SAME_ENG_SYNC = True

W_COLS_T = [(0, 256), (256, 128), (512, 512), (1024, 512), (2048, 768), (2816, 768)]
QT_AQ, QT_AK, QT_BQ, QT_BK, QT_CQ, QT_CK = 0, 256, 384, 896, 1408, 2176
QT_ROWS = 2944
W_COLS_V = [(384, 128), (1536, 512), (3584, 768)]
VT_AV, VT_BV, VT_CV = 0, 128, 640
VT_COLS = 1408


class _Op:
    __slots__ = ("eng", "fn", "deps", "dma_key", "needs_inc", "sem", "val", "idx")

    def __init__(self, eng, fn, dma_key):
        self.eng = eng
        self.fn = fn
        self.deps = []
        self.dma_key = dma_key
        self.needs_inc = dma_key is not None
        self.sem = None
        self.val = 0


class Phase:
    ENGS = ("pe", "act", "dve", "pool", "sp")

    def __init__(self, nc):
        self.nc = nc
        self.ops = {e: [] for e in self.ENGS}
        self.lastw = {}
        self.readers = {}
        self.last_dma = {}

    def pid(self, e):
        c = self.__dict__.setdefault("_pidc", {})
        if id(e) not in c:
            c[id(e)] = e.partition_id()
        return c[id(e)]

    def op(self, eng, fn, reads=(), writes=(), dma=None):
        o = _Op(eng, fn, dma)
        deps = []
        for r in reads:
            w = self.lastw.get(r)
            if w is not None:
                deps.append(w)
        for r in writes:
            w = self.lastw.get(r)
            if w is not None:
                deps.append(w)
            deps.extend(self.readers.get(r, ()))
        if dma is not None:
            prev = self.last_dma.get(dma)
            if prev is not None:
                deps.append(prev)
            self.last_dma[dma] = o
        seen = set()
        for d in deps:
            if d is o or id(d) in seen:
                continue
            seen.add(id(d))
            if d.dma_key is None and d.eng == eng and (eng == "pe" or not SAME_ENG_SYNC):
                continue
            o.deps.append(d)
            d.needs_inc = True
        for r in reads:
            self.readers.setdefault(r, []).append(o)
        for r in writes:
            self.lastw[r] = o
            self.readers[r] = []
        self.ops[eng].append(o)
        return o

    def emit(self):
        nc = self.nc
        reg = SEMREG
        LIM = 30000

        def assign(key, o, inc):
            ent = reg.get(key)
            if ent is None or ent[1] + inc > LIM:
                ent = [nc.alloc_semaphore("k%d" % len(reg.setdefault("_all", []))), 0]
                reg["_all"].append(ent[0])
                reg[key] = ent
            ent[1] += inc
            o.sem = ent[0]
            o.val = ent[1]

        for e in self.ENGS:
            for o in self.ops[e]:
                if o.dma_key is None:
                    if o.needs_inc:
                        assign(("eng", e), o, 1)
                else:
                    assign(("dma", o.dma_key), o, 16)
        final = {}
        for e in self.ENGS:
            for o in self.ops[e]:
                if o.dma_key is not None:
                    final[id(o.sem)] = (o.sem, o.val)
        with nc.Block() as block:
            def make(e):
                def body(eng):
                    waited = {}
                    for o in self.ops[e]:
                        for d in o.deps:
                            if waited.get(id(d.sem), 0) >= d.val:
                                continue
                            waited[id(d.sem)] = d.val
                            eng.wait_ge(d.sem, d.val)
                        ins = o.fn(eng)
                        if o.needs_inc:
                            ins.then_inc(o.sem, 16 if o.dma_key is not None else 1)
                    if e == "sp":
                        for k, (sm, v) in final.items():
                            if waited.get(k, 0) < v:
                                eng.wait_ge(sm, v)
                return body

            block.tensor(make("pe"))
            block.scalar(make("act"))
            block.vector(make("dve"))
            block.gpsimd(make("pool"))
            block.sync(make("sp"))


SEMREG = {}


def _bf(a):
    return np.asarray(a, np.float32).astype(ml_dtypes.bfloat16)


class Builder:
    def __init__(self, seqs, depth=DEPTH):
        self.seqs = seqs
        self.depth = depth
        self.nc = bass.Bass("TRN2", target_bir_lowering=False)
        nc = self.nc
        SEMREG.clear()
        self.dr = {}

        def din(name, shape, dt=F32):
            self.dr[name] = nc.dram_tensor(name, list(shape), dt, kind="ExternalInput").ap()

        def dout(name, shape):
            self.dr[name] = nc.dram_tensor(name, list(shape), F32, kind="ExternalOutput").ap()

        def dtmp(name, shape, dt):
            self.dr[name] = nc.dram_tensor(name, list(shape), dt).ap()

        self.shard = None
        for (s, L) in seqs:
            din("x_" + s, (L, D))
            if s == "p":
                self.shard = L // 8
            else:
                dout("y_" + s, (L, D))
            dtmp("XT_" + s, (D, L + 2 * QM), F32)
            dtmp("XU_" + s, (D, L + 2 * QM), F32)
            dtmp("QT_" + s, (QT_ROWS, L + 2 * KM), BF16)
            dtmp("VT_" + s, (L + 2 * KM, VT_COLS), BF16)
            dtmp("OT_" + s, (D, L), BF16)
        if self.shard:
            SH = self.shard
            self.LQ = SH + 2 * QM
            self.LK = SH + 2 * KM
            dout("y_o", (SH, D))
            dtmp("XL_o", (D, self.LQ), F32)
            dtmp("XW_o", (D, SH), F32)
            dtmp("QT_o", (QT_ROWS, self.LK), BF16)
            dtmp("VT_o", (self.LK, VT_COLS), BF16)
            dtmp("OT_o", (D, self.LQ), BF16)
        Lmax = max(L for _, L in seqs)
        self.Lmax = Lmax
        din("w_in", (depth, D, D_IN))
        din("w_out", (depth, D, D))
        din("w_up", (depth, D, 2 * D_FF))
        din("w_down", (depth, D_FF, D))
        din("ln1", (128, depth * 8))
        din("ln2", (128, depth * 8))
        din("lnf", (128, 8))
        din("subln", (128, depth))
        din("sink", (128, depth * 4))
        din("lam", (128, depth * 4 * 64))
        din("convw", (128, depth * 3 * 44))
        din("convb", (128, depth * 44))
        din("c_ident", (128, 128))
        din("c_diagA", (128, 4 * 128), BF16)
        din("c_diagB", (128, 4 * 128), BF16)
        din("c_diagC", (128, 24 * 128), BF16)
        din("c_MA", (128, 1152), BF16)
        din("c_MC0", (128, 1152), BF16)
        din("c_MC1", (128, 1408), BF16)
        din("c_MC2", (128, 2944), BF16)
        din("c_MBh", (128, 896), BF16)
        din("c_MBl", (128, 896), BF16)
        din("c_kaug", (4, 4, Lmax), BF16)
        din("c_qaug", (4, 2, 4, Lmax), BF16)
        if self.shard:
            din("c_kbias", (128, self.LK // 128))
            din("c_kaugL", (self.LQ // 512, 4, 4, Lmax), BF16)
            din("c_qaugL", (4, 4, self.LQ), BF16)
            din("c_flag", (128, 2))

    def X(self, which, s):
        t = self.dr[("XT_", "XU_")[which % 2] + s]
        return t[:, QM:t.shape[1] - QM]

    def QTv(self, s):
        t = self.dr["QT_" + s]
        return t[:, KM:t.shape[1] - KM]

    def VTv(self, s):
        t = self.dr["VT_" + s]
        return t[KM:t.shape[0] - KM, :]

    def phase_init_pads(self, s, L, which):
        nc, dr = self.nc, self.dr
        with contextlib.ExitStack() as st:
            zb = st.enter_context(nc.sbuf_tensor(self._nm("pi_zb"), [128, 12, VT_COLS], BF16))
            zf = st.enter_context(nc.sbuf_tensor(self._nm("pi_zf"), [128, 8, QM], F32))
            zq = st.enter_context(nc.sbuf_tensor(self._nm("pi_zq"), [128, KM], BF16))
            ph = Phase(nc)
            ph.op("pool", lambda e: e.memset(zq[:], 0.0), writes=["zb"])
            ph.op("pool", lambda e: e.memset(zb[:], 0.0), writes=["zb"])
            ph.op("pool", lambda e: e.memset(zf[:], 0.0), writes=["zf"])
            QTf = dr["QT_" + s]
            VTf = dr["VT_" + s]
            Xf = dr[("XT_", "XU_")[which % 2] + s]
            k = 0
            for c0 in (0, KM + L):
                for j in range(QT_ROWS // 128):
                    ph.op("sp", lambda e, c0=c0, j=j: e.dma_start(out=QTf[j * 128:(j + 1) * 128, c0:c0 + KM],
                                                               in_=zq[:]),
                          reads=["zb"], dma="st_z%d" % (k % 4))
                    k += 1
                ph.op("sp", lambda e, c0=c0: e.dma_start(
                    out=VTf[c0:c0 + KM, :].rearrange("(u p) n -> p u n", p=128), in_=zb[:]), reads=["zb"], dma="st_z%d" % (k % 4))
                k += 1
            for c0 in (0, QM + L):
                ph.op("sp", lambda e, c0=c0: e.dma_start(
                    out=Xf[:, c0:c0 + QM].rearrange("(c p) t -> p c t", p=128), in_=zf[:]), reads=["zf"], dma="st_z%d" % (k % 4))
                k += 1
            ph.emit()

    def phase_localize(self, s, L, which):
        nc, dr = self.nc, self.dr
        SH, LQ, LK = self.shard, self.LQ, self.LK
        QTf = dr["QT_" + s]
        VTf = dr["VT_" + s]
        Xf = dr[("XT_", "XU_")[which % 2] + s]
        ph = Phase(nc)
        k = 0
        RQ = QT_ROWS // 4
        for j in range(4):
            ph.op("sp", lambda e, j=j: e.dma_start(out=dr["QT_o"][j * RQ:(j + 1) * RQ, :],
                                                   in_=QTf[j * RQ:(j + 1) * RQ, bass.ds(ph.pid(e) * SH, LK)]),
                  dma="cp%d" % (k % 4))
            k += 1
        for c0 in range(0, VT_COLS, VT_COLS // 4):
            ph.op("sp", lambda e, c0=c0: e.dma_start(out=dr["VT_o"][:, c0:c0 + VT_COLS // 4],
                                                     in_=VTf[bass.ds(ph.pid(e) * SH, LK), c0:c0 + VT_COLS // 4]),
                  dma="cp%d" % (k % 4))
            k += 1
        for j in range(2):
            ph.op("sp", lambda e, j=j: e.dma_start(out=dr["XL_o"][j * 512:(j + 1) * 512, :],
                                                   in_=Xf[j * 512:(j + 1) * 512, bass.ds(ph.pid(e) * SH, LQ)]),
                  dma="cp%d" % (k % 4))
            k += 1
        ph.emit()

    def phase_transpose_in(self, s, L):
        nc, dr = self.nc, self.dr
        x = dr["x_" + s]
        XT = self.X(0, s).rearrange("(c p) t -> p c t", p=128)
        nt = L // 512
        with contextlib.ExitStack() as st:
            ident = st.enter_context(nc.sbuf_tensor(self._nm("p0_id"), [128, 128], F32))
            xin = st.enter_context(nc.sbuf_tensor(self._nm("p0_xin"), [128, 2, 4, D], F32))
            xt = st.enter_context(nc.sbuf_tensor(self._nm("p0_xt"), [128, 2, 8, 512], F32))
            ps = st.enter_context(nc.psum_tensor(self._nm("p0_ps"), [128, 4, 512], F32))
            ph = Phase(nc)
            ph.op("sp", lambda e: e.dma_start(out=ident[:], in_=dr["c_ident"][:, :]), writes=["ident"], dma="ld_id")
            for t in range(nt):
                b = t % 2
                src = x[t * 512:(t + 1) * 512, :].rearrange("(s p) f -> p s f", p=128)
                ph.op("sp", lambda e, b=b, src=src: e.dma_start(out=xin[:, b], in_=src),
                      writes=[("xin", b)], dma="ld_xin%d" % b)
                for c in range(8):
                    pb = (t * 8 + c) % 4
                    for sub in range(4):
                        ph.op("pe", lambda e, pb=pb, sub=sub, b=b, c=c: e.transpose(
                            ps[:, pb, sub * 128:(sub + 1) * 128], xin[:, b, sub, c * 128:(c + 1) * 128], ident[:]),
                            reads=[("xin", b), "ident"], writes=[("ps", pb, sub)])
                    if c % 2 == 0:
                        ph.op("act", lambda e, pb=pb, b=b, c=c: e.copy(out=xt[:, b, c, :], in_=ps[:, pb, :]),
                              reads=[("ps", pb, q) for q in range(4)], writes=[("xt", b, c)])
                    else:
                        ph.op("dve", lambda e, pb=pb, b=b, c=c: e.tensor_copy(out=xt[:, b, c, :], in_=ps[:, pb, :]),
                              reads=[("ps", pb, q) for q in range(4)], writes=[("xt", b, c)])
                ph.op("pool", lambda e, b=b, t=t: e.dma_start(out=XT[:, :, t * 512:(t + 1) * 512], in_=xt[:, b]),
                      reads=[("xt", b, c) for c in range(8)], dma="st_xt%d" % b)
            ph.emit()

    def _rmsnorm_tile(self, ph, xt_ap, xt_res, ncols, sq, ones_bf, ss_ps, ss_res, r_ap, r_res, h_ap_fn, h_res_fn,
                      g_ap_fn, nfeat, eps):
        ph.op("act", lambda e: e.activation(out=sq, in_=xt_ap, func=AF.Square),
              reads=[xt_res], writes=["sq"])
        for c in range(8):
            ph.op("pe", lambda e, c=c: e.matmul(ss_ps, ones_bf, sq[:, c, :], start=(c == 0), stop=(c == 7)),
                  reads=["sq", "ones"], writes=[ss_res])
        ph.op("dve", lambda e: e.tensor_scalar(r_ap, ss_ps, 1.0 / nfeat, eps, ALU.mult, ALU.add),
              reads=[ss_res], writes=[r_res])
        ph.op("act", lambda e: e.activation(out=r_ap, in_=r_ap, func=AF.Sqrt), reads=[r_res], writes=[r_res])
        ph.op("dve", lambda e: e.reciprocal(r_ap, r_ap), reads=[r_res], writes=[r_res])
        for c in range(8):
            eng = "dve"
            ph.op(eng, lambda e, c=c: e.scalar_tensor_tensor(h_ap_fn(c), xt_ap[:, c, :], g_ap_fn(c), r_ap,
                                                              ALU.mult, ALU.mult),
                  reads=[xt_res, r_res, "params"], writes=[h_res_fn(c)])

    def phase_proj(self, l, jobs):
        nc, dr = self.nc, self.dr
        w_in = dr["w_in"]
        with contextlib.ExitStack() as st:
            w = st.enter_context(nc.sbuf_tensor(self._nm("p1_w"), [128, 8, D_IN], BF16))
            g = st.enter_context(nc.sbuf_tensor(self._nm("p1_g"), [128, 8], F32))
            ones = st.enter_context(nc.sbuf_tensor(self._nm("p1_ones"), [128, 128], BF16))
            xt = st.enter_context(nc.sbuf_tensor(self._nm("p1_xt"), [128, 2, 8, 512], F32))
            sq = st.enter_context(nc.sbuf_tensor(self._nm("p1_sq"), [128, 8, 512], BF16))
            r = st.enter_context(nc.sbuf_tensor(self._nm("p1_r"), [128, 512], F32))
            h = st.enter_context(nc.sbuf_tensor(self._nm("p1_h"), [128, 2, 8, 512], BF16))
            qt = st.enter_context(nc.sbuf_tensor(self._nm("p1_qt"), [128, 2, 23, 512], BF16))
            vt = st.enter_context(nc.sbuf_tensor(self._nm("p1_vt"), [128, 2, 4, VT_COLS], BF16))
            ss_ps = st.enter_context(nc.psum_tensor(self._nm("p1_ss"), [128, 512], F32))
            ps = st.enter_context(nc.psum_tensor(self._nm("p1_ps"), [128, 6, 512], F32))
            ph = Phase(nc)
            ph.op("pool", lambda e: e.memset(ones[:], 1.0), writes=["ones"])
            ph.op("sp", lambda e: e.dma_start(out=g[:], in_=dr["ln1"][:, l * 8:(l + 1) * 8]), writes=["params"], dma="ld_g")
            wsrc = w_in[l].rearrange("(c p) n -> p c n", p=128)
            for c in range(8):
                for (n0, n1) in ((0, 1536), (1536, 3072), (3072, D_IN)):
                    ph.op("pool", lambda e, c=c, n0=n0, n1=n1: e.dma_start(out=w[:, c, n0:n1], in_=wsrc[:, c, n0:n1]),
                          writes=[("w", c, n0)], dma="ld_w%d" % (c % 4))
            wres = [("w", c, n0) for c in range(8) for n0 in (0, 1536, 3072)]
            pcount = [0]

            def next_ps():
                pcount[0] += 1
                return pcount[0] % 6

            ecount = [0]

            def evac(ph, dst, src, reads, writes):
                ecount[0] += 1
                if ecount[0] % 2 == 0:
                    ph.op("act", lambda e: e.copy(out=dst, in_=src), reads=reads, writes=writes)
                else:
                    ph.op("dve", lambda e: e.tensor_copy(out=dst, in_=src), reads=reads, writes=writes)

            for (s, L) in jobs:
                XT = self.X(l, s).rearrange("(c p) t -> p c t", p=128)
                QT = self.QTv(s).rearrange("(j p) t -> p j t", p=128)
                VT = self.VTv(s)
                nt = L // 512
                for t in range(nt):
                    b = t % 2
                    ph.op("sp", lambda e, b=b, t=t, XT=XT: e.dma_start(out=xt[:, b], in_=XT[:, :, t * 512:(t + 1) * 512]),
                          writes=[("xt", b)], dma="ld_xt%d" % b)
                    self._rmsnorm_tile(ph, xt[:, b], ("xt", b), 512, sq[:], ones[:], ss_ps[:], "ss", r[:], "r",
                                       lambda c, b=b: h[:, b, c, :], lambda c, b=b: ("h", b, c),
                                       lambda c: g[:, c:c + 1], D, EPS)
                    hres = [("h", b, c) for c in range(8)]
                    j = 0
                    for (c0, n) in W_COLS_T:
                        for jj in range(n // 128):
                            pb = next_ps()
                            col = c0 + jj * 128
                            for c in range(8):
                                ph.op("pe", lambda e, pb=pb, c=c, col=col, b=b: e.matmul(
                                    ps[:, pb, :], w[:, c, col:col + 128], h[:, b, c, :], start=(c == 0), stop=(c == 7)),
                                    reads=hres + wres if c == 0 else [], writes=[("ps", pb)])
                            evac(ph, qt[:, b, j, :], ps[:, pb, :], [("ps", pb)], [("qt", b, j)])
                            j += 1
                    ph.op("pool", lambda e, b=b, t=t, QT=QT: e.dma_start(out=QT[:, :, t * 512:(t + 1) * 512], in_=qt[:, b]),
                          reads=[("qt", b, j) for j in range(23)], dma="st_qt%d" % b)
                    for sub in range(4):
                        for (c0, n), v0 in zip(W_COLS_V, (VT_AV, VT_BV, VT_CV)):
                            for n0 in range(0, n, 512):
                                nn = min(512, n - n0)
                                pb = next_ps()
                                for c in range(8):
                                    ph.op("pe", lambda e, pb=pb, c=c, col=c0 + n0, nn=nn, b=b, sub=sub: e.matmul(
                                        ps[:, pb, 0:nn], h[:, b, c, sub * 128:(sub + 1) * 128], w[:, c, col:col + nn],
                                        start=(c == 0), stop=(c == 7)),
                                        reads=hres + wres if c == 0 else [], writes=[("ps", pb)])
                                evac(ph, vt[:, b, sub, v0 + n0:v0 + n0 + nn], ps[:, pb, 0:nn], [("ps", pb)],
                                     [("vt", b, sub, v0 + n0)])
                    vsrc = VT[t * 512:(t + 1) * 512, :].rearrange("(s p) n -> p s n", p=128)
                    ph.op("pool", lambda e, b=b, vsrc=vsrc: e.dma_start(out=vsrc, in_=vt[:, b]),
                          reads=[("vt", b, sub, v) for sub in range(4) for v in (0, 128, 640, 1152)], dma="st_vt%d" % b)
            ph.emit()

    def _attend(self, ph, items, nout, S_ps, O_ps, pt, acc, dv, tag):
        n = len(items)
        first = {}
        last = {}
        for i, it in enumerate(items):
            first.setdefault(it["out"], i)
            last[it["out"]] = i
        NS = S_ps.shape[1]
        NP = pt.shape[1]

        def do_s(i):
            it = items[i]
            sb = i % NS
            nm = len(it["masks"])
            ph.op("pe", lambda e: e.matmul(S_ps[:, sb, :], it["k"], it["q"], start=True, stop=(nm == 0)),
                  reads=it["reads"], writes=[("S", sb)])
            for mi, (ml, mr) in enumerate(it["masks"]):
                ph.op("pe", lambda e, ml=ml, mr=mr, mi=mi: e.matmul(S_ps[:, sb, :], ml, mr, start=False,
                                                                    stop=(mi == nm - 1)),
                      reads=["consts"], writes=[("S", sb)])
            if it.get("bias") is not None:
                ph.op("act", lambda e: e.activation(out=pt[:, i % NP, :], in_=S_ps[:, sb, :], func=AF.Exp, scale=0.125,
                                                    bias=it["bias"]),
                      reads=[("S", sb), "consts"], writes=[("pt", i % NP)])
            else:
                ph.op("act", lambda e: e.activation(out=pt[:, i % NP, :], in_=S_ps[:, sb, :], func=AF.Exp, scale=0.125),
                      reads=[("S", sb)], writes=[("pt", i % NP)])

        cnt = {}

        def do_pv(i):
            it = items[i]
            o = it["out"]
            par = cnt.get(o, 0) % 2
            fresh = cnt.get(o, 0) < 2
            cnt[o] = cnt.get(o, 0) + 1
            if fresh:
                ph.op("dve", lambda e: e.tensor_copy(out=acc[:, o, par, :], in_=pt[:, i % NP, :]),
                      reads=[("pt", i % NP)], writes=[("acc", o, par)])
            else:
                ph.op("dve", lambda e: e.tensor_tensor(out=acc[:, o, par, :], in0=acc[:, o, par, :],
                                                      in1=pt[:, i % NP, :], op=ALU.add),
                      reads=[("pt", i % NP)], writes=[("acc", o, par)])
            ph.op("pe", lambda e: e.matmul(O_ps[0:dv, o, :], it["v"], pt[:, i % NP, :], start=(first[o] == i),
                                           stop=(last[o] == i)),
                  reads=[("pt", i % NP)] + it["reads"], writes=[("O", o)])

        for i in range(n):
            do_s(i)
            if i >= 1:
                do_pv(i - 1)
        do_pv(n - 1)
        for o, c in cnt.items():
            if c >= 2:
                ph.op("dve", lambda e, o=o: e.tensor_tensor(out=acc[:, o, 0, :], in0=acc[:, o, 0, :],
                                                            in1=acc[:, o, 1, :], op=ALU.add),
                      reads=[("acc", o, 1)], writes=[("acc", o, 0)])

    def _attend_b(self, ph, pairs, S_ps, O_ps, pt, acc, NPB, den_ps=None, ones_b=None, pe_every=4):
        n = len(pairs)

        def do_s(i):
            it = pairs[i]
            sb = 2 * (i % 2)
            nm = len(it["masks"])
            for m in range(2):
                ph.op("pe", lambda e, m=m: e.matmul(S_ps[:, sb + m, :], it["k"][m], it["q"][m], start=True, stop=(nm == 0)),
                      reads=it["reads"], writes=[("S", i % 2)])
                for mi, (ml, mr) in enumerate(it["masks"]):
                    ph.op("pe", lambda e, m=m, ml=ml, mr=mr, mi=mi: e.matmul(S_ps[:, sb + m, :], ml, mr, start=False,
                                                                         stop=(mi == nm - 1)),
                          reads=["consts"], writes=[("S", i % 2)])
            ps0 = 2 * (i % NPB)
            ph.op("act", lambda e: e.activation(out=pt[:, ps0:ps0 + 2, :], in_=S_ps[:, sb:sb + 2, :], func=AF.Exp,
                                                scale=0.125),
                  reads=[("S", i % 2)], writes=[("pt", i % NPB)])

        st8 = {"dve": 0, "pe": 0}

        def do_pv(i):
            it = pairs[i]
            ps0 = 2 * (i % NPB)
            if den_ps is not None and (i % pe_every) == pe_every - 1:
                for m in range(2):
                    ph.op("pe", lambda e, m=m, first=(st8["pe"] == 0): e.matmul(den_ps[:, m, :], ones_b, pt[:, ps0 + m, :],
                                                                               start=first, stop=False),
                          reads=[("pt", i % NPB), "ones"], writes=[("den", m)])
                st8["pe"] += 1
            else:
                par = st8["dve"] % 2
                if st8["dve"] < 2:
                    ph.op("dve", lambda e: e.tensor_copy(out=acc[:, par], in_=pt[:, ps0:ps0 + 2, :]),
                          reads=[("pt", i % NPB)], writes=[("acc", par)])
                else:
                    ph.op("dve", lambda e: e.tensor_tensor(out=acc[:, par], in0=acc[:, par], in1=pt[:, ps0:ps0 + 2, :],
                                                          op=ALU.add),
                          reads=[("pt", i % NPB)], writes=[("acc", par)])
                st8["dve"] += 1
            for m in range(2):
                ph.op("pe", lambda e, m=m: e.matmul(O_ps[:, m, :], it["v"], pt[:, ps0 + m, :], start=(i == 0),
                                                    stop=(i == n - 1)),
                      reads=[("pt", i % NPB)] + it["reads"], writes=[("O", m)])

        for i in range(n):
            do_s(i)
            if i >= 1:
                do_pv(i - 1)
        do_pv(n - 1)
        if st8["dve"] >= 2:
            ph.op("dve", lambda e: e.tensor_tensor(out=acc[:, 0], in0=acc[:, 0], in1=acc[:, 1], op=ALU.add),
                  reads=[("acc", 1)], writes=[("acc", 0)])
        return st8["pe"] > 0

    def _finalize_den(self, ph, acc_ap, acc_res, ones_f, den_ps, den_res, rden_ap, rden_res, rows, extra=None,
                      start=True):
        ph.op("pe", lambda e: e.matmul(den_ps, ones_f, acc_ap, start=start, stop=True),
              reads=[acc_res, "ones"], writes=[den_res])
        if extra is not None:
            ph.op("dve", lambda e: e.tensor_scalar(rden_ap, den_ps[0:rows], extra, None, ALU.add),
                  reads=[den_res, "params"], writes=[rden_res])
            ph.op("dve", lambda e: e.reciprocal(rden_ap, rden_ap), reads=[rden_res], writes=[rden_res])
        else:
            ph.op("dve", lambda e: e.reciprocal(rden_ap, den_ps[0:rows]), reads=[den_res], writes=[rden_res])

    def phase_attn_ac(self, l, s, L, own=False):
        nc, dr = self.nc, self.dr
        if own:
            QT, VT, OT = dr["QT_o"], dr["VT_o"], dr["OT_o"]
            nt = self.LQ // 512
            kshift = KM - QM
            L = self.LK
        else:
            QT = self.QTv(s)
            VT = self.VTv(s)
            OT = dr["OT_" + s]
            nt = L // 512
            kshift = 0
        nblk = L // 128
        GW = (128, 256, 1024)
        GN = (6, 8, 20)
        with contextlib.ExitStack() as st:
            ones_f = st.enter_context(nc.sbuf_tensor(self._nm("pa_onesf"), [128, 128], F32))
            esink = st.enter_context(nc.sbuf_tensor(self._nm("pa_sink"), [128, 4], F32))
            dA = st.enter_context(nc.sbuf_tensor(self._nm("pa_dA"), [128, 4 * 128], BF16))
            dC = st.enter_context(nc.sbuf_tensor(self._nm("pa_dC"), [128, 24 * 128], BF16))
            MA = st.enter_context(nc.sbuf_tensor(self._nm("pa_MA"), [128, 1152], BF16))
            MC0 = st.enter_context(nc.sbuf_tensor(self._nm("pa_MC0"), [128, 1152], BF16))
            MC1 = st.enter_context(nc.sbuf_tensor(self._nm("pa_MC1"), [128, 1408], BF16))
            MC2 = st.enter_context(nc.sbuf_tensor(self._nm("pa_MC2"), [128, 2944], BF16))
            kA = st.enter_context(nc.sbuf_tensor(self._nm("pa_kA"), [128, 2, 768], BF16))
            vA = st.enter_context(nc.sbuf_tensor(self._nm("pa_vA"), [128, 2, 6, 128], BF16))
            qA = st.enter_context(nc.sbuf_tensor(self._nm("pa_qA"), [128, 2, 2, 512], BF16))
            kC = st.enter_context(nc.sbuf_tensor(self._nm("pa_kC"), [128, 2, 2, 34 * 128], BF16))
            vC = st.enter_context(nc.sbuf_tensor(self._nm("pa_vC"), [128, 2, 34, 256], BF16))
            qC = st.enter_context(nc.sbuf_tensor(self._nm("pa_qC"), [128, 2, 3, 2, 512], BF16))
            pt = st.enter_context(nc.sbuf_tensor(self._nm("pa_pt"), [128, 3, 512], BF16))
            acc = st.enter_context(nc.sbuf_tensor(self._nm("pa_acc"), [128, 2, 2, 512], F32))
            rden = st.enter_context(nc.sbuf_tensor(self._nm("pa_rden"), [64, 512], F32))
            ot = st.enter_context(nc.sbuf_tensor(self._nm("pa_ot"), [64, 2, 8, 512], BF16))
            S_ps = st.enter_context(nc.psum_tensor(self._nm("pa_S"), [128, 3, 512], F32))
            O_ps = st.enter_context(nc.psum_tensor(self._nm("pa_O"), [128, 2, 512], F32))
            den_ps = st.enter_context(nc.psum_tensor(self._nm("pa_den"), [128, 512], F32))
            ph = Phase(nc)
            kbias = None
            if own:
                kbias = st.enter_context(nc.sbuf_tensor(self._nm("pa_kbias"), [128, self.LK // 128], F32))
                ph.op("sp", lambda e: e.dma_start(out=kbias[:], in_=dr["c_kbias"][:, :]), writes=["consts"], dma="ld_c1")
            MC = (MC0, MC1, MC2)
            ph.op("pool", lambda e: e.memset(ones_f[:], 1.0), writes=["ones"])
            ph.op("sp", lambda e: e.dma_start(out=esink[:], in_=dr["sink"][:, l * 4:(l + 1) * 4]), writes=["params"], dma="ld_c0")
            ph.op("act", lambda e: e.activation(out=esink[:], in_=esink[:], func=AF.Exp), reads=["params"], writes=["params"])
            for nm, tl in (("c_diagA", dA), ("c_diagC", dC), ("c_MA", MA), ("c_MC0", MC0), ("c_MC1", MC1), ("c_MC2", MC2)):
                ph.op("sp", lambda e, nm=nm, tl=tl: e.dma_start(out=tl[:], in_=dr[nm][:, :]), writes=["consts"], dma="ld_c1")
            goff = (0, 6, 14)
            oc = [0]
            for c in range(nt):
                b = c % 2
                a = c * 512 + kshift
                ao = c * 512
                u_lo = max(0, -((a - 128) // 128))
                u_hi = min(6, (L - (a - 128)) // 128)
                k0 = a - 128 + 128 * u_lo
                k1 = a - 128 + 128 * u_hi
                ph.op("sp", lambda e, b=b, k0=k0, k1=k1, u_lo=u_lo, u_hi=u_hi: e.dma_start(
                    out=kA[:, b, u_lo * 128:u_hi * 128], in_=QT[QT_AK:QT_AK + 128, k0:k1]),
                    writes=[("kA", b)], dma="ld_kA%d" % b)
                ph.op("sp", lambda e, b=b, k0=k0, k1=k1, u_lo=u_lo, u_hi=u_hi: e.dma_start(
                    out=vA[:, b, u_lo:u_hi, :],
                    in_=VT[k0:k1, VT_AV:VT_AV + 128].rearrange("(u p) n -> p u n", p=128)),
                    writes=[("vA", b)], dma="ld_vA%d" % b)
                for kvh in range(2):
                    for j in range(2):
                        r0 = QT_AQ + (2 * kvh + j) * 64
                        ph.op("sp", lambda e, b=b, kvh=kvh, j=j, r0=r0, a=a: e.dma_start(
                            out=qA[kvh * 64:(kvh + 1) * 64, b, j, :], in_=QT[r0:r0 + 64, a:a + 512]),
                            writes=[("qA", b, kvh, j)], dma="ld_qA%d" % b)
                cval = []
                for g in range(3):
                    ulo = max(0, -((a - GW[g]) // 128))
                    uhi = min(GN[g], (L - (a - GW[g])) // 128)
                    cval.append((ulo, uhi))
                    k0 = a - GW[g] + 128 * ulo
                    k1 = a - GW[g] + 128 * uhi
                    for pr in range(2):
                        r0 = QT_CK + g * 256 + pr * 128
                        ph.op("sp", lambda e, b=b, g=g, pr=pr, r0=r0, k0=k0, k1=k1, ulo=ulo, uhi=uhi: e.dma_start(
                            out=kC[:, b, pr, (goff[g] + ulo) * 128:(goff[g] + uhi) * 128], in_=QT[r0:r0 + 128, k0:k1]),
                            writes=[("kC", b, g, pr)], dma="ld_kC%d" % b)
                        r1 = QT_CQ + g * 256 + pr * 128
                        ph.op("sp", lambda e, b=b, g=g, pr=pr, r1=r1, a=a: e.dma_start(
                            out=qC[:, b, g, pr, :], in_=QT[r1:r1 + 128, a:a + 512]),
                            writes=[("qC", b, g, pr)], dma="ld_qC%d" % b)
                    ph.op("sp", lambda e, b=b, g=g, k0=k0, k1=k1, ulo=ulo, uhi=uhi: e.dma_start(
                        out=vC[:, b, goff[g] + ulo:goff[g] + uhi, :],
                        in_=VT[k0:k1, VT_CV + g * 256:VT_CV + (g + 1) * 256].rearrange("(u p) n -> p u n", p=128)),
                        writes=[("vC", b, g)], dma="ld_vC%d" % b)
                for hq in range(4):
                    kvh, j = hq // 2, hq % 2
                    items = []
                    for u in range(u_lo, u_hi):
                        items.append(dict(
                            q=qA[kvh * 64:(kvh + 1) * 64, b, j, :],
                            k=kA[kvh * 64:(kvh + 1) * 64, b, u * 128:(u + 1) * 128],
                            v=vA[:, b, u, kvh * 64:(kvh + 1) * 64],
                            masks=[(dA[:, hq * 128:(hq + 1) * 128], MA[:, 640 - 128 * u:640 - 128 * u + 512])],
                            bias=(kbias[:, (a - 128) // 128 + u:(a - 128) // 128 + u + 1] if own else None),
                            out=oc[0] % 2, reads=[("kA", b), ("vA", b), ("qA", b, kvh, j), "consts"]))
                    o = oc[0] % 2
                    oc[0] += 1
                    self._attend(ph, items, 1, S_ps, O_ps, pt, acc, 64, "A")
                    self._finalize_den(ph, acc[:, o, 0, :], ("acc", o, 0), ones_f[:], den_ps[:], "den", rden[:], "rden", 64,
                                       extra=esink[0:64, hq:hq + 1])
                    ph.op("dve", lambda e, o=o, b=b, hq=hq: e.tensor_tensor(out=ot[:, b, hq, :], in0=O_ps[0:64, o, :],
                                                                           in1=rden[:], op=ALU.mult),
                          reads=[("O", o), "rden"], writes=[("ot", b, hq)])
                for h in range(4):
                    pr, hp = h // 2, h % 2
                    items = []
                    for g in range(3):
                        ulo, uhi = cval[g]
                        for u in range(ulo, uhi):
                            off = 128 * (GN[g] - 1) - 128 * u
                            gi = (g * 4 + h) * 2
                            items.append(dict(
                                q=qC[hp * 64:(hp + 1) * 64, b, g, pr, :],
                                k=kC[hp * 64:(hp + 1) * 64, b, pr, (goff[g] + u) * 128:(goff[g] + u + 1) * 128],
                                v=vC[:, b, goff[g] + u, h * 64:(h + 1) * 64],
                                masks=[(dC[:, gi * 128:(gi + 1) * 128], MC[g][:, off:off + 512])],
                                bias=(kbias[:, (a - GW[g]) // 128 + u:(a - GW[g]) // 128 + u + 1] if own else None),
                                out=oc[0] % 2,
                                reads=[("kC", b, g, pr), ("vC", b, g), ("qC", b, g, pr), "consts"]))
                    o = oc[0] % 2
                    oc[0] += 1
                    self._attend(ph, items, 1, S_ps, O_ps, pt, acc, 64, "C")
                    self._finalize_den(ph, acc[:, o, 0, :], ("acc", o, 0), ones_f[:], den_ps[:], "den", rden[:], "rden", 64)
                    ph.op("dve", lambda e, o=o, b=b, h=h: e.tensor_tensor(out=ot[:, b, 4 + h, :], in0=O_ps[0:64, o, :],
                                                                         in1=rden[:], op=ALU.mult),
                          reads=[("O", o), "rden"], writes=[("ot", b, 4 + h)])
                ph.op("pool", lambda e, b=b, a=ao: e.dma_start(
                    out=OT[0:256, a:a + 512].rearrange("(h p) t -> p h t", p=64), in_=ot[:, b, 0:4, :]),
                    reads=[("ot", b, hh) for hh in range(4)], dma="st_oA%d" % b)
                ph.op("pool", lambda e, b=b, a=ao: e.dma_start(
                    out=OT[768:1024, a:a + 512].rearrange("(h p) t -> p h t", p=64), in_=ot[:, b, 4:8, :]),
                    reads=[("ot", b, 4 + hh) for hh in range(4)], dma="st_oC%d" % b)
            ph.emit()

    def phase_attn_b(self, l, s, L, own=False):
        nc, dr = self.nc, self.dr
        QT = self.QTv(s)
        VT = self.VTv(s)
        OT = dr["OT_" + s]
        nt = L // 512
        nblk = L // 128
        lam_init = 0.8 - 0.6 * float(np.exp(-0.3 * l))
        with contextlib.ExitStack() as st:
            ones_f = st.enter_context(nc.sbuf_tensor(self._nm("pb_onesf"), [128, 128], F32))
            ones_b = st.enter_context(nc.sbuf_tensor(self._nm("pb_onesb"), [128, 128], BF16))
            lamt = st.enter_context(nc.sbuf_tensor(self._nm("pb_lam"), [128, 4 * 64], F32))
            lt = st.enter_context(nc.sbuf_tensor(self._nm("pb_lt"), [128, 2 * 64], F32))
            ls = st.enter_context(nc.sbuf_tensor(self._nm("pb_ls"), [128, 4], F32))
            gs = st.enter_context(nc.sbuf_tensor(self._nm("pb_gs"), [128, 1], F32))
            dB = st.enter_context(nc.sbuf_tensor(self._nm("pb_dB"), [128, 4 * 128], BF16))
            MBh = st.enter_context(nc.sbuf_tensor(self._nm("pb_MBh"), [128, 896], BF16))
            MBl = st.enter_context(nc.sbuf_tensor(self._nm("pb_MBl"), [128, 896], BF16))
            kB = st.enter_context(nc.sbuf_tensor(self._nm("pb_kB"), [68, 2, L], BF16))
            vB = st.enter_context(nc.sbuf_tensor(self._nm("pb_vB"), [128, nblk, 128], BF16))
            qB = st.enter_context(nc.sbuf_tensor(self._nm("pb_qB"), [68, 2, 2, 3, 512], BF16))
            pt = st.enter_context(nc.sbuf_tensor(self._nm("pb_pt"), [128, 6, 512], BF16))
            acc = st.enter_context(nc.sbuf_tensor(self._nm("pb_acc"), [128, 2, 2, 512], F32))
            rden = st.enter_context(nc.sbuf_tensor(self._nm("pb_rden"), [128, 2, 512], F32))
            tt = st.enter_context(nc.sbuf_tensor(self._nm("pb_t"), [128, 2, 512], F32))
            sq = st.enter_context(nc.sbuf_tensor(self._nm("pb_sq"), [128, 512], BF16))
            ot = st.enter_context(nc.sbuf_tensor(self._nm("pb_ot"), [128, 2, 512], BF16))
            S_ps = st.enter_context(nc.psum_tensor(self._nm("pb_S"), [128, 4, 512], F32))
            O_ps = st.enter_context(nc.psum_tensor(self._nm("pb_O"), [128, 2, 512], F32))
            den_ps = st.enter_context(nc.psum_tensor(self._nm("pb_den"), [128, 2, 512], F32))
            ph = Phase(nc)
            ph.op("pool", lambda e: e.memset(ones_f[:], 1.0), writes=["ones"])
            ph.op("pool", lambda e: e.memset(ones_b[:], 1.0), writes=["ones"])
            ph.op("pool", lambda e: e.memset(qB[:], 0.0), writes=["qinit"])
            for nm, tl in (("c_diagB", dB), ("c_MBh", MBh), ("c_MBl", MBl)):
                ph.op("sp", lambda e, nm=nm, tl=tl: e.dma_start(out=tl[:], in_=dr[nm][:, :]), writes=["consts"], dma="ld_c1")
            ph.op("sp", lambda e: e.dma_start(out=lamt[:], in_=dr["lam"][:, l * 256:(l + 1) * 256]), writes=["lamt"], dma="ld_c0")
            ph.op("sp", lambda e: e.dma_start(out=gs[:], in_=dr["subln"][:, l:l + 1], allow_slow_non_contiguous=True), writes=["gs"], dma="ld_c0")
            ph.op("dve", lambda e: e.tensor_tensor(out=lt[:, 0:64], in0=lamt[:, 0:64], in1=lamt[:, 64:128], op=ALU.mult),
                  reads=["lamt"], writes=["lt"])
            ph.op("dve", lambda e: e.tensor_tensor(out=lt[:, 64:128], in0=lamt[:, 128:192], in1=lamt[:, 192:256], op=ALU.mult),
                  reads=["lamt"], writes=["lt"])
            ph.op("dve", lambda e: e.reduce_sum(ls[:, 0:1], lt[:, 0:64], mybir.AxisListType.X), reads=["lt"], writes=["ls"])
            ph.op("dve", lambda e: e.reduce_sum(ls[:, 1:2], lt[:, 64:128], mybir.AxisListType.X), reads=["lt"], writes=["ls"])
            ph.op("act", lambda e: e.activation(out=ls[:, 0:2], in_=ls[:, 0:2], func=AF.Exp), reads=["ls"], writes=["ls"])
            ph.op("dve", lambda e: e.tensor_tensor(out=ls[:, 2:3], in0=ls[:, 1:2], in1=ls[:, 0:1], op=ALU.subtract),
                  reads=["ls"], writes=["ls"])
            ph.op("dve", lambda e: e.tensor_scalar(ls[:, 2:3], ls[:, 2:3], -lam_init, None, ALU.add),
                  reads=["ls"], writes=["ls"])
            ph.op("dve", lambda e: e.tensor_scalar(gs[:], gs[:], 1.0 - lam_init, None, ALU.mult),
                  reads=["gs"], writes=["gs"])
            HCH = min(4096, L)
            VCH = min(32, nblk)
            for h in range(4):
                for m in range(2):
                    r0 = QT_BK + h * 128 + m * 64
                    for c0 in range(0, L, HCH):
                        ph.op("sp", lambda e, m=m, r0=r0, c0=c0: e.dma_start(out=kB[0:64, m, c0:c0 + HCH],
                                                                         in_=QT[r0:r0 + 64, c0:c0 + HCH]),
                              writes=[("kB", m)], dma="ld_kB%d" % m)
                    ph.op("sp", lambda e, m=m, h=h: e.dma_start(out=kB[64:68, m, :], in_=dr["c_kaug"][h, :, 0:L]),
                          writes=[("kB", m)], dma="ld_kB%d" % m)
                for c0 in range(0, nblk, VCH):
                    ph.op("sp", lambda e, h=h, c0=c0: e.dma_start(
                        out=vB[:, c0:c0 + VCH, :],
                        in_=VT[c0 * 128:(c0 + VCH) * 128, VT_BV + h * 128:VT_BV + (h + 1) * 128].rearrange(
                            "(u p) n -> p u n", p=128)),
                        writes=["vB"], dma="ld_vB")
                for c in range(nt):
                    b = c % 2
                    a = c * 512
                    for m in range(2):
                        r0 = QT_BQ + h * 128 + m * 64
                        for var in range(3):
                            ph.op("sp", lambda e, b=b, m=m, var=var, r0=r0, a=a: e.dma_start(
                                out=qB[0:64, b, m, var, :], in_=QT[r0:r0 + 64, a:a + 512]),
                                reads=["qinit"], writes=[("qB", b, m)], dma="ld_qB%d" % b)
                        for var in range(2):
                            ph.op("sp", lambda e, b=b, m=m, var=var, h=h, a=a: e.dma_start(
                                out=qB[64:68, b, m, var, :], in_=dr["c_qaug"][h, var, :, a:a + 512]),
                                reads=["qinit"], writes=[("qB", b, m)], dma="ld_qB%d" % b)
                    pairs = []
                    slope = 2.0 ** (-2.0 * (h + 1))
                    for kb in range(nblk):
                        k0 = kb * 128
                        dmin = max(0, k0 - (a + 511), a - (k0 + 127))
                        if slope * dmin >= B_SKIP:
                            continue
                        if kb < 4 * c:
                            var, masks = 0, []
                        elif kb > 4 * c + 3:
                            var, masks = 1, []
                        else:
                            u = kb - 4 * c
                            off = 384 - 128 * u
                            var = 2
                            masks = [(dB[:, h * 128:(h + 1) * 128], MBh[:, off:off + 512]),
                                     (dB[:, h * 128:(h + 1) * 128], MBl[:, off:off + 512])]
                        pairs.append(dict(q=[qB[:, b, 0, var, :], qB[:, b, 1, var, :]],
                                          k=[kB[:, 0, k0:k0 + 128], kB[:, 1, k0:k0 + 128]],
                                          v=vB[:, kb, :], masks=masks,
                                          reads=[("kB", 0), ("kB", 1), "vB", ("qB", b, 0), ("qB", b, 1), "consts"]))
                    pe_used = self._attend_b(ph, pairs, S_ps, O_ps, pt, acc, 3, den_ps=den_ps, ones_b=ones_b[:])
                    for m in range(2):
                        self._finalize_den(ph, acc[:, 0, m, :], ("acc", 0), ones_f[:], den_ps[:, m, :], ("den", m),
                                           rden[:, m, :], ("rden", m), 128, start=(not pe_used))
                        ph.op("dve", lambda e, m=m: e.tensor_tensor(out=tt[:, m, :], in0=O_ps[:, m, :], in1=rden[:, m, :],
                                                                    op=ALU.mult),
                              reads=[("O", m), ("rden", m)], writes=[("tt", m)])
                    ph.op("dve", lambda e: e.scalar_tensor_tensor(tt[:, 0, :], tt[:, 1, :], ls[:, 2:3], tt[:, 0, :],
                                                                  ALU.mult, ALU.add),
                          reads=[("tt", 0), ("tt", 1), "ls"], writes=[("tt", 0)])
                    ph.op("act", lambda e: e.activation(out=sq[:], in_=tt[:, 0, :], func=AF.Square),
                          reads=[("tt", 0)], writes=["sq"])
                    ph.op("pe", lambda e: e.matmul(den_ps[:, 0, :], ones_b[:], sq[:], start=True, stop=True),
                          reads=["sq", "ones"], writes=[("den", 0)])
                    ph.op("dve", lambda e: e.tensor_scalar(rden[:, 0, :], den_ps[:, 0, :], 1.0 / 128, SUBLN_EPS,
                                                           ALU.mult, ALU.add),
                          reads=[("den", 0)], writes=[("rden", 0)])
                    ph.op("act", lambda e: e.activation(out=rden[:, 0, :], in_=rden[:, 0, :], func=AF.Sqrt),
                          reads=[("rden", 0)], writes=[("rden", 0)])
                    ph.op("dve", lambda e: e.reciprocal(rden[:, 0, :], rden[:, 0, :]),
                          reads=[("rden", 0)], writes=[("rden", 0)])
                    ph.op("dve", lambda e, b=b: e.scalar_tensor_tensor(ot[:, b, :], tt[:, 0, :], gs[:, 0:1], rden[:, 0, :],
                                                                       ALU.mult, ALU.mult),
                          reads=[("tt", 0), ("rden", 0), "gs"], writes=[("ot", b)])
                    ph.op("pool", lambda e, b=b, a=a, h=h: e.dma_start(
                        out=OT[256 + h * 128:256 + (h + 1) * 128, a:a + 512], in_=ot[:, b, :]),
                        reads=[("ot", b)], dma="st_oB%d" % b)
            ph.emit()

    def phase_attn_b_own(self, l, s, L):
        nc, dr = self.nc, self.dr
        QT = self.QTv(s)
        VT = self.VTv(s)
        QTo, VTo, OT = dr["QT_o"], dr["VT_o"], dr["OT_o"]
        nt = self.LQ // 512
        kshift = KM - QM
        nblk = L // 128
        lam_init = 0.8 - 0.6 * float(np.exp(-0.3 * l))
        with contextlib.ExitStack() as st:
            def sb(name, shape, dt):
                return st.enter_context(nc.sbuf_tensor(self._nm(name), shape, dt))
            ones_f = sb("po_onesf", [128, 128], F32)
            ones_b = sb("po_onesb", [128, 128], BF16)
            lamt = sb("po_lam", [128, 4 * 64], F32)
            lt = sb("po_lt", [128, 2 * 64], F32)
            ls = sb("po_ls", [128, 4], F32)
            gs = sb("po_gs", [128, 1], F32)
            dB = sb("po_dB", [128, 4 * 128], BF16)
            MBh = sb("po_MBh", [128, 896], BF16)
            MBl = sb("po_MBl", [128, 896], BF16)
            kB = sb("po_kB", [68, 2, L], BF16)
            vB = sb("po_vB", [128, nblk, 128], BF16)
            kN = sb("po_kN", [68, 2, 2, 512], BF16)
            vN = sb("po_vN", [128, 2, 4, 128], BF16)
            qB = sb("po_qB", [68, 2, 2, 2, 512], BF16)
            pt = sb("po_pt", [128, 6, 512], BF16)
            acc = sb("po_acc", [128, 2, 2, 512], F32)
            rden = sb("po_rden", [128, 2, 512], F32)
            tt = sb("po_t", [128, 2, 512], F32)
            sq = sb("po_sq", [128, 512], BF16)
            ot = sb("po_ot", [128, 2, 512], BF16)
            S_ps = st.enter_context(nc.psum_tensor(self._nm("po_S"), [128, 4, 512], F32))
            O_ps = st.enter_context(nc.psum_tensor(self._nm("po_O"), [128, 2, 512], F32))
            den_ps = st.enter_context(nc.psum_tensor(self._nm("po_den"), [128, 2, 512], F32))
            ph = Phase(nc)
            ph.op("pool", lambda e: e.memset(ones_f[:], 1.0), writes=["ones"])
            ph.op("pool", lambda e: e.memset(ones_b[:], 1.0), writes=["ones"])
            ph.op("pool", lambda e: e.memset(qB[:], 0.0), writes=["qinit"])
            ph.op("pool", lambda e: e.memset(kN[:], 0.0), writes=["qinit"])
            for nm, tl in (("c_diagB", dB), ("c_MBh", MBh), ("c_MBl", MBl)):
                ph.op("sp", lambda e, nm=nm, tl=tl: e.dma_start(out=tl[:], in_=dr[nm][:, :]), writes=["consts"], dma="ld_c1")
            ph.op("sp", lambda e: e.dma_start(out=lamt[:], in_=dr["lam"][:, l * 256:(l + 1) * 256]), writes=["lamt"], dma="ld_c0")
            ph.op("sp", lambda e: e.dma_start(out=gs[:], in_=dr["subln"][:, l:l + 1], allow_slow_non_contiguous=True),
                  writes=["gs"], dma="ld_c0")
            ph.op("dve", lambda e: e.tensor_tensor(out=lt[:, 0:64], in0=lamt[:, 0:64], in1=lamt[:, 64:128], op=ALU.mult),
                  reads=["lamt"], writes=["lt"])
            ph.op("dve", lambda e: e.tensor_tensor(out=lt[:, 64:128], in0=lamt[:, 128:192], in1=lamt[:, 192:256], op=ALU.mult),
                  reads=["lamt"], writes=["lt"])
            ph.op("dve", lambda e: e.reduce_sum(ls[:, 0:1], lt[:, 0:64], mybir.AxisListType.X), reads=["lt"], writes=["ls"])
            ph.op("dve", lambda e: e.reduce_sum(ls[:, 1:2], lt[:, 64:128], mybir.AxisListType.X), reads=["lt"], writes=["ls"])
            ph.op("act", lambda e: e.activation(out=ls[:, 0:2], in_=ls[:, 0:2], func=AF.Exp), reads=["ls"], writes=["ls"])
            ph.op("dve", lambda e: e.tensor_tensor(out=ls[:, 2:3], in0=ls[:, 1:2], in1=ls[:, 0:1], op=ALU.subtract),
                  reads=["ls"], writes=["ls"])
            ph.op("dve", lambda e: e.tensor_scalar(ls[:, 2:3], ls[:, 2:3], -lam_init, None, ALU.add),
                  reads=["ls"], writes=["ls"])
            ph.op("dve", lambda e: e.tensor_scalar(gs[:], gs[:], 1.0 - lam_init, None, ALU.mult),
                  reads=["gs"], writes=["gs"])
            HCH = min(4096, L)
            VCH = min(32, nblk)
            for h in range(4):
                for m in range(2):
                    r0 = QT_BK + h * 128 + m * 64
                    for c0 in range(0, L, HCH):
                        ph.op("sp", lambda e, m=m, r0=r0, c0=c0: e.dma_start(out=kB[0:64, m, c0:c0 + HCH],
                                                                         in_=QT[r0:r0 + 64, c0:c0 + HCH]),
                              writes=[("kB", m)], dma="ld_kB%d" % m)
                for c0 in range(0, nblk, VCH):
                    ph.op("sp", lambda e, h=h, c0=c0: e.dma_start(
                        out=vB[:, c0:c0 + VCH, :],
                        in_=VT[c0 * 128:(c0 + VCH) * 128, VT_BV + h * 128:VT_BV + (h + 1) * 128].rearrange(
                            "(u p) n -> p u n", p=128)),
                        writes=["vB"], dma="ld_vB")
                for c in range(nt):
                    b = c % 2
                    ak = c * 512 + kshift
                    ao = c * 512
                    for m in range(2):
                        ph.op("sp", lambda e, m=m, h=h, c=c: e.dma_start(out=kB[64:68, m, :],
                                                                       in_=dr["c_kaugL"][c, h, :, 0:L]),
                              writes=[("kBa", m)], dma="ld_kBa%d" % m)
                        r0 = QT_BQ + h * 128 + m * 64
                        for var in range(2):
                            ph.op("sp", lambda e, b=b, m=m, var=var, r0=r0, ak=ak: e.dma_start(
                                out=qB[0:64, b, m, var, :], in_=QTo[r0:r0 + 64, ak:ak + 512]),
                                reads=["qinit"], writes=[("qB", b, m)], dma="ld_qB%d" % b)
                        ph.op("sp", lambda e, b=b, m=m, h=h, ao=ao: e.dma_start(
                            out=qB[64:68, b, m, 0, :], in_=dr["c_qaugL"][h, :, ao:ao + 512]),
                            reads=["qinit"], writes=[("qB", b, m)], dma="ld_qB%d" % b)
                        r1 = QT_BK + h * 128 + m * 64
                        ph.op("sp", lambda e, b=b, m=m, r1=r1, ak=ak: e.dma_start(
                            out=kN[0:64, b, m, :], in_=QTo[r1:r1 + 64, ak:ak + 512]),
                            reads=["qinit"], writes=[("kN", b)], dma="ld_kN%d" % b)
                    ph.op("sp", lambda e, b=b, h=h, ak=ak: e.dma_start(
                        out=vN[:, b, :, :],
                        in_=VTo[ak:ak + 512, VT_BV + h * 128:VT_BV + (h + 1) * 128].rearrange("(u p) n -> p u n", p=128)),
                        writes=[("vN", b)], dma="ld_vN%d" % b)
                    pairs = []
                    for kb in range(nblk):
                        k0 = kb * 128
                        pairs.append(dict(q=[qB[:, b, 0, 0, :], qB[:, b, 1, 0, :]],
                                          k=[kB[:, 0, k0:k0 + 128], kB[:, 1, k0:k0 + 128]],
                                          v=vB[:, kb, :], masks=[],
                                          reads=[("kB", 0), ("kB", 1), ("kBa", 0), ("kBa", 1), "vB", ("qB", b, 0),
                                                 ("qB", b, 1)]))
                    for u in range(4):
                        off = 384 - 128 * u
                        masks = [(dB[:, h * 128:(h + 1) * 128], MBh[:, off:off + 512]),
                                 (dB[:, h * 128:(h + 1) * 128], MBl[:, off:off + 512])]
                        pairs.append(dict(q=[qB[:, b, 0, 1, :], qB[:, b, 1, 1, :]],
                                          k=[kN[:, b, 0, u * 128:(u + 1) * 128], kN[:, b, 1, u * 128:(u + 1) * 128]],
                                          v=vN[:, b, u, :], masks=masks,
                                          reads=[("kN", b), ("vN", b), ("qB", b, 0), ("qB", b, 1), "consts"]))
                    pe_used = self._attend_b(ph, pairs, S_ps, O_ps, pt, acc, 3, den_ps=den_ps, ones_b=ones_b[:])
                    for m in range(2):
                        self._finalize_den(ph, acc[:, 0, m, :], ("acc", 0), ones_f[:], den_ps[:, m, :], ("den", m),
                                           rden[:, m, :], ("rden", m), 128, start=(not pe_used))
                        ph.op("dve", lambda e, m=m: e.tensor_tensor(out=tt[:, m, :], in0=O_ps[:, m, :], in1=rden[:, m, :],
                                                                    op=ALU.mult),
                              reads=[("O", m), ("rden", m)], writes=[("tt", m)])
                    ph.op("dve", lambda e: e.scalar_tensor_tensor(tt[:, 0, :], tt[:, 1, :], ls[:, 2:3], tt[:, 0, :],
                                                                  ALU.mult, ALU.add),
                          reads=[("tt", 0), ("tt", 1), "ls"], writes=[("tt", 0)])
                    ph.op("act", lambda e: e.activation(out=sq[:], in_=tt[:, 0, :], func=AF.Square),
                          reads=[("tt", 0)], writes=["sq"])
                    ph.op("pe", lambda e: e.matmul(den_ps[:, 0, :], ones_b[:], sq[:], start=True, stop=True),
                          reads=["sq", "ones"], writes=[("den", 0)])
                    ph.op("dve", lambda e: e.tensor_scalar(rden[:, 0, :], den_ps[:, 0, :], 1.0 / 128, SUBLN_EPS,
                                                           ALU.mult, ALU.add),
                          reads=[("den", 0)], writes=[("rden", 0)])
                    ph.op("act", lambda e: e.activation(out=rden[:, 0, :], in_=rden[:, 0, :], func=AF.Sqrt),
                          reads=[("rden", 0)], writes=[("rden", 0)])
                    ph.op("dve", lambda e: e.reciprocal(rden[:, 0, :], rden[:, 0, :]),
                          reads=[("rden", 0)], writes=[("rden", 0)])
                    ph.op("dve", lambda e, b=b: e.scalar_tensor_tensor(ot[:, b, :], tt[:, 0, :], gs[:, 0:1], rden[:, 0, :],
                                                                       ALU.mult, ALU.mult),
                          reads=[("tt", 0), ("rden", 0), "gs"], writes=[("ot", b)])
                    ph.op("pool", lambda e, b=b, ao=ao, h=h: e.dma_start(
                        out=OT[256 + h * 128:256 + (h + 1) * 128, ao:ao + 512], in_=ot[:, b, :]),
                        reads=[("ot", b)], dma="st_oB%d" % b)
            ph.emit()

    def phase_wout(self, l, jobs):
        nc, dr = self.nc, self.dr
        with contextlib.ExitStack() as st:
            w = st.enter_context(nc.sbuf_tensor(self._nm("pw_w"), [128, 8, D], BF16))
            xt = st.enter_context(nc.sbuf_tensor(self._nm("pw_xt"), [128, 2, 8, 512], F32))
            ot = st.enter_context(nc.sbuf_tensor(self._nm("pw_ot"), [128, 2, 8, 512], BF16))
            ps = st.enter_context(nc.psum_tensor(self._nm("pw_ps"), [128, 4, 512], F32))
            ph = Phase(nc)
            wsrc = dr["w_out"][l].rearrange("(c p) n -> p c n", p=128)
            for c in range(8):
                ph.op("pool", lambda e, c=c: e.dma_start(out=w[:, c, :], in_=wsrc[:, c, :]), writes=[("w", c)],
                      dma="ld_w%d" % (c % 4))
            wres = [("w", c) for c in range(8)]
            k = 0
            for (s, L, own) in jobs:
                if own:
                    XT = dr["XL_o"].rearrange("(c p) t -> p c t", p=128)
                    OT = dr["OT_o"].rearrange("(c p) t -> p c t", p=128)
                    nt = self.LQ // 512
                else:
                    XT = self.X(l, s).rearrange("(c p) t -> p c t", p=128)
                    OT = dr["OT_" + s].rearrange("(c p) t -> p c t", p=128)
                    nt = L // 512
                for t in range(nt):
                    b = t % 2
                    ph.op("sp", lambda e, b=b, t=t, XT=XT: e.dma_start(out=xt[:, b], in_=XT[:, :, t * 512:(t + 1) * 512]),
                          writes=[("xt", b, c) for c in range(8)], dma="ld_xt%d" % b)
                    ph.op("sp", lambda e, b=b, t=t, OT=OT: e.dma_start(out=ot[:, b], in_=OT[:, :, t * 512:(t + 1) * 512]),
                          writes=[("ot", b)], dma="ld_ot%d" % b)
                    for oc in range(8):
                        pb = k % 4
                        k += 1
                        for c in range(8):
                            ph.op("pe", lambda e, pb=pb, c=c, oc=oc, b=b: e.matmul(
                                ps[:, pb, :], w[:, c, oc * 128:(oc + 1) * 128], ot[:, b, c, :], start=(c == 0), stop=(c == 7)),
                                reads=[("ot", b)] + wres if c == 0 else [], writes=[("ps", pb)])
                        ph.op("dve", lambda e, pb=pb, b=b, oc=oc: e.tensor_tensor(out=xt[:, b, oc, :], in0=ps[:, pb, :],
                                                                              in1=xt[:, b, oc, :], op=ALU.add),
                              reads=[("ps", pb)], writes=[("xt", b, oc)])
                    ph.op("pool", lambda e, b=b, t=t, XT=XT: e.dma_start(out=XT[:, :, t * 512:(t + 1) * 512], in_=xt[:, b]),
                          reads=[("xt", b, c) for c in range(8)], dma="st_xt%d" % b)
            ph.emit()

    def phase_mlp(self, l, jobs):
        nc, dr = self.nc, self.dr
        NT = 256
        NC = NT + 2
        any_own = any(j[2] for j in jobs)
        with contextlib.ExitStack() as st:
            wu = st.enter_context(nc.sbuf_tensor(self._nm("pm_wu"), [128, 8, 2 * D_FF], BF16))
            wd = st.enter_context(nc.sbuf_tensor(self._nm("pm_wd"), [128, 22, D], BF16))
            g = st.enter_context(nc.sbuf_tensor(self._nm("pm_g"), [128, 8], F32))
            cw = st.enter_context(nc.sbuf_tensor(self._nm("pm_cw"), [128, 3 * 44], F32))
            cb = st.enter_context(nc.sbuf_tensor(self._nm("pm_cb"), [128, 44], F32))
            ones = st.enter_context(nc.sbuf_tensor(self._nm("pm_ones"), [128, 128], BF16))
            xt = st.enter_context(nc.sbuf_tensor(self._nm("pm_xt"), [128, 2, 8, NC], F32))
            sq = st.enter_context(nc.sbuf_tensor(self._nm("pm_sq"), [128, 8, NC], BF16))
            r = st.enter_context(nc.sbuf_tensor(self._nm("pm_r"), [128, NC], F32))
            h = st.enter_context(nc.sbuf_tensor(self._nm("pm_h"), [128, 8, NC], BF16))
            tmp = st.enter_context(nc.sbuf_tensor(self._nm("pm_tmp"), [128, 2, 2, NT], F32))
            gt = st.enter_context(nc.sbuf_tensor(self._nm("pm_gt"), [128, 22, NT], BF16))
            xo = st.enter_context(nc.sbuf_tensor(self._nm("pm_xo"), [128, 2, 8, NT], F32))
            ss_ps = st.enter_context(nc.psum_tensor(self._nm("pm_ss"), [128, 512], F32))
            u_ps = st.enter_context(nc.psum_tensor(self._nm("pm_u"), [128, 4, 512], F32))
            o_ps = st.enter_context(nc.psum_tensor(self._nm("pm_o"), [128, 2, 512], F32))
            ph = Phase(nc)
            ph.op("pool", lambda e: e.memset(ones[:], 1.0), writes=["ones"])
            ph.op("sp", lambda e: e.dma_start(out=g[:], in_=dr["ln2"][:, l * 8:(l + 1) * 8]), writes=["params"], dma="ld_g")
            ph.op("sp", lambda e: e.dma_start(out=cw[:], in_=dr["convw"][:, l * 132:(l + 1) * 132]), writes=["params"], dma="ld_g")
            ph.op("sp", lambda e: e.dma_start(out=cb[:], in_=dr["convb"][:, l * 44:(l + 1) * 44]), writes=["params"], dma="ld_g")
            flag = st.enter_context(nc.sbuf_tensor(self._nm("pm_flag"), [128, 2], F32))
            if any_own:
                ph.op("sp", lambda e: e.dma_start(out=flag[:], in_=dr["c_flag"][:, :]), writes=["params"], dma="ld_g")
            wsrc = dr["w_up"][l].rearrange("(c p) n -> p c n", p=128)
            for c in range(8):
                for n0 in range(0, 2 * D_FF, 1408):
                    ph.op("pool", lambda e, c=c, n0=n0: e.dma_start(out=wu[:, c, n0:n0 + 1408], in_=wsrc[:, c, n0:n0 + 1408]),
                          writes=[("wu", c, n0)], dma="ld_w%d" % (c % 4))
            wures = [("wu", c, n0) for c in range(8) for n0 in range(0, 2 * D_FF, 1408)]
            wdsrc = dr["w_down"][l].rearrange("(c p) n -> p c n", p=128)
            for c in range(22):
                ph.op("pool", lambda e, c=c: e.dma_start(out=wd[:, c, :], in_=wdsrc[:, c, :]), writes=[("wd", c)],
                      dma="ld_w%d" % (c % 4))
            wdres = [("wd", c) for c in range(22)]
            uk = 0
            ok = 0
            for (s, L, own) in jobs:
                if own:
                    XT = dr["XL_o"].rearrange("(c p) t -> p c t", p=128)
                    XO = dr["XW_o"].rearrange("(c p) t -> p c t", p=128)
                    nt = self.shard // NT
                else:
                    XT = self.X(l, s).rearrange("(c p) t -> p c t", p=128)
                    XO = self.X(l + 1, s).rearrange("(c p) t -> p c t", p=128)
                    nt = L // NT
                cb0 = QM if own else 0
                for t in range(nt):
                    b = t % 2
                    t0 = t * NT
                    lo = 1 if (t == 0 and not own) else 0
                    hi = NC - 1 if (t == nt - 1 and not own) else NC
                    if lo:
                        ph.op("pool", lambda e, b=b: e.memset(xt[:, b, :, 0:1], 0.0), writes=[("xt", b)])
                    if hi != NC:
                        ph.op("pool", lambda e, b=b: e.memset(xt[:, b, :, NC - 1:NC], 0.0), writes=[("xt", b)])
                    ph.op("sp", lambda e, b=b, t0=t0, lo=lo, hi=hi, XT=XT, cb0=cb0: e.dma_start(out=xt[:, b, :, lo:hi],
                                                                           in_=XT[:, :, cb0 + t0 - 1 + lo:cb0 + t0 - 1 + hi]),
                          writes=[("xt", b)] if not (lo or hi != NC) else [("xt", b), ("xtedge", b)], dma="ld_xt%d" % b)
                    if own and t == 0:
                        ph.op("dve", lambda e, b=b: e.tensor_scalar(xt[:, b, :, 0:1], xt[:, b, :, 0:1], flag[:, 0:1], None,
                                                                    ALU.mult), reads=[("xt", b), "params"], writes=[("xt", b)])
                    if own and t == nt - 1:
                        ph.op("dve", lambda e, b=b: e.tensor_scalar(xt[:, b, :, NC - 1:NC], xt[:, b, :, NC - 1:NC],
                                                                    flag[:, 1:2], None, ALU.mult),
                              reads=[("xt", b), "params"], writes=[("xt", b)])
                    self._rmsnorm_tile(ph, xt[:, b], ("xt", b), NC, sq[:], ones[:], ss_ps[:, 0:NC], "ss", r[:], "r",
                                       lambda c: h[:, c, :], lambda c: ("h", c), lambda c: g[:, c:c + 1], D, EPS)
                    hres = [("h", c) for c in range(8)]
                    for p in range(22):
                        for part in range(2):
                            j = p + 22 * part
                            ub = uk % 4
                            uk += 1
                            for c in range(8):
                                ph.op("pe", lambda e, ub=ub, c=c, j=j: e.matmul(
                                    u_ps[:, ub, 0:NC], wu[:, c, j * 128:(j + 1) * 128], h[:, c, :], start=(c == 0), stop=(c == 7)),
                                    reads=hres + wures if c == 0 else [], writes=[("u", ub)])
                            tb = p % 2
                            tm = tmp[:, tb, part, :]
                            tres = ("tmp", tb, part)
                            ph.op("act", lambda e, tm=tm, ub=ub, j=j: e.activation(
                                out=tm, in_=u_ps[:, ub, 1:NT + 1], func=AF.Identity, bias=cb[:, j:j + 1],
                                scale=cw[:, 44 + j:45 + j]), reads=[("u", ub), "params"], writes=[tres])
                            ph.op("dve", lambda e, tm=tm, ub=ub, j=j: e.scalar_tensor_tensor(
                                tm, u_ps[:, ub, 0:NT], cw[:, j:j + 1], tm, ALU.mult, ALU.add),
                                reads=[("u", ub), "params"], writes=[tres])
                            ph.op("dve", lambda e, tm=tm, ub=ub, j=j: e.scalar_tensor_tensor(
                                tm, u_ps[:, ub, 2:NT + 2], cw[:, 88 + j:89 + j], tm, ALU.mult, ALU.add),
                                reads=[("u", ub), "params"], writes=[tres])
                        tb = p % 2
                        ph.op("act", lambda e, tb=tb: e.activation(out=tmp[:, tb, 0, :], in_=tmp[:, tb, 0, :], func=AF.Silu),
                              reads=[("tmp", tb, 0)], writes=[("tmp", tb, 0)])
                        ph.op("pool", lambda e, tb=tb, p=p: e.tensor_tensor(out=gt[:, p, :], in0=tmp[:, tb, 0, :],
                                                                          in1=tmp[:, tb, 1, :], op=ALU.mult),
                              reads=[("tmp", tb, 0), ("tmp", tb, 1)], writes=[("gt", p)])
                    gres = [("gt", p) for p in range(22)]
                    for oc in range(8):
                        ob = ok % 2
                        ok += 1
                        for c in range(22):
                            ph.op("pe", lambda e, ob=ob, c=c, oc=oc: e.matmul(
                                o_ps[:, ob, 0:NT], wd[:, c, oc * 128:(oc + 1) * 128], gt[:, c, :], start=(c == 0), stop=(c == 21)),
                                reads=gres + wdres if c == 0 else [], writes=[("o", ob)])
                        ph.op("dve", lambda e, ob=ob, b=b, oc=oc: e.tensor_tensor(out=xo[:, b, oc, :], in0=o_ps[:, ob, 0:NT],
                                                                              in1=xt[:, b, oc, 1:NT + 1], op=ALU.add),
                              reads=[("o", ob), ("xt", b)], writes=[("xo", b, oc)])
                    ph.op("pool", lambda e, b=b, t0=t0, XO=XO: e.dma_start(out=XO[:, :, t0:t0 + NT], in_=xo[:, b]),
                          reads=[("xo", b, c) for c in range(8)], dma="st_xo%d" % b)
            ph.emit()

    def phase_final(self, s, L, own=False):
        nc, dr = self.nc, self.dr
        if own:
            XT = dr["XW_o"].rearrange("(c p) t -> p c t", p=128)
            y = dr["y_o"]
            nt = self.shard // 512
        else:
            XT = self.X(self.depth, s).rearrange("(c p) t -> p c t", p=128)
            y = dr["y_" + s]
            nt = L // 512
        with contextlib.ExitStack() as st:
            ident = st.enter_context(nc.sbuf_tensor(self._nm("pf_id"), [128, 128], F32))
            g = st.enter_context(nc.sbuf_tensor(self._nm("pf_g"), [128, 8], F32))
            ones = st.enter_context(nc.sbuf_tensor(self._nm("pf_ones"), [128, 128], BF16))
            xt = st.enter_context(nc.sbuf_tensor(self._nm("pf_xt"), [128, 2, 8, 512], F32))
            sq = st.enter_context(nc.sbuf_tensor(self._nm("pf_sq"), [128, 8, 512], BF16))
            r = st.enter_context(nc.sbuf_tensor(self._nm("pf_r"), [128, 512], F32))
            h = st.enter_context(nc.sbuf_tensor(self._nm("pf_h"), [128, 8, 512], F32))
            yo = st.enter_context(nc.sbuf_tensor(self._nm("pf_yo"), [128, 2, 4, D], F32))
            ss_ps = st.enter_context(nc.psum_tensor(self._nm("pf_ss"), [128, 512], F32))
            ps = st.enter_context(nc.psum_tensor(self._nm("pf_ps"), [128, 3, 2, 512], F32))
            ph = Phase(nc)
            ph.op("pool", lambda e: e.memset(ones[:], 1.0), writes=["ones"])
            ph.op("sp", lambda e: e.dma_start(out=ident[:], in_=dr["c_ident"][:, :]), writes=["ident"], dma="ld_g")
            ph.op("sp", lambda e: e.dma_start(out=g[:], in_=dr["lnf"][:, :]), writes=["params"], dma="ld_g")
            k = 0
            for t in range(nt):
                b = t % 2
                ph.op("sp", lambda e, b=b, t=t: e.dma_start(out=xt[:, b], in_=XT[:, :, t * 512:(t + 1) * 512]),
                      writes=[("xt", b)], dma="ld_xt%d" % b)
                self._rmsnorm_tile(ph, xt[:, b], ("xt", b), 512, sq[:], ones[:], ss_ps[:], "ss", r[:], "r",
                                   lambda c: h[:, c, :], lambda c: ("h", c), lambda c: g[:, c:c + 1], D, EPS)
                for sub in range(4):
                    pb = k % 3
                    k += 1
                    for c in range(8):
                        ph.op("pe", lambda e, pb=pb, c=c, sub=sub: e.transpose(
                            ps[:, pb, c // 4, (c % 4) * 128:(c % 4 + 1) * 128], h[:, c, sub * 128:(sub + 1) * 128], ident[:]),
                            reads=[("h", c), "ident"], writes=[("ps", pb)])
                    if sub % 2 == 0:
                        ph.op("act", lambda e, pb=pb, b=b, sub=sub: e.copy(
                            out=yo[:, b, sub, :], in_=ps[:, pb].rearrange("p a n -> p (a n)")),
                            reads=[("ps", pb)], writes=[("yo", b, sub)])
                    else:
                        ph.op("dve", lambda e, pb=pb, b=b, sub=sub: e.tensor_copy(
                            out=yo[:, b, sub, :], in_=ps[:, pb].rearrange("p a n -> p (a n)")),
                            reads=[("ps", pb)], writes=[("yo", b, sub)])
                dst = y[t * 512:(t + 1) * 512, :].rearrange("(s p) f -> p s f", p=128)
                ph.op("pool", lambda e, b=b, dst=dst: e.dma_start(out=dst, in_=yo[:, b]),
                      reads=[("yo", b, sub) for sub in range(4)], dma="st_y%d" % b)
            ph.emit()

    def _nm(self, base):
        self._cnt = getattr(self, "_cnt", 0) + 1
        return "%s_%d" % (base, self._cnt)

    def build(self):
        import os
        lim = int(os.environ.get("KPH", "1000"))
        plist = []
        last = self.depth - 1
        for (s, L) in self.seqs:
            if s == "p":
                plist.append(lambda s=s, L=L: self.phase_init_pads(s, L, last))
            plist.append(lambda s=s, L=L: self.phase_transpose_in(s, L))
        for l in range(self.depth):
            plist.append(lambda l=l: self.phase_proj(l, [(s, L) for (s, L) in self.seqs]))
            for (s, L) in self.seqs:
                own = (s == "p") and l == last
                if own:
                    plist.append(lambda s=s, L=L, l=l: self.phase_localize(s, L, l))
                    plist.append(lambda s=s, L=L, l=l: self.phase_attn_ac(l, s, L, own=True))
                    plist.append(lambda s=s, L=L, l=l: self.phase_attn_b_own(l, s, L))
                else:
                    plist.append(lambda s=s, L=L, l=l: self.phase_attn_ac(l, s, L))
                    plist.append(lambda s=s, L=L, l=l: self.phase_attn_b(l, s, L))
            jobs = [(s, L, (s == "p") and l == last) for (s, L) in self.seqs]
            plist.append(lambda l=l, jobs=jobs: self.phase_wout(l, jobs))
            plist.append(lambda l=l, jobs=jobs: self.phase_mlp(l, jobs))
        for (s, L) in self.seqs:
            plist.append(lambda s=s, L=L: self.phase_final(s, L, own=(s == "p")))
        for f in plist[:lim]:
            f()
        return self.nc


def _toeplitz(width, X, f):
    kk = np.arange(128)[:, None]
    col = np.arange(width)[None, :]
    return f(col - kk - X)


def make_consts(Lmax):
    c = {}
    c["c_ident"] = np.eye(128, dtype=np.float32)
    slA = 2.0 ** (-8.0 * np.arange(1, 5) / 4)
    slC = (2.0 ** (-8.0 * np.arange(1, 13) / 12)).astype(np.float32).astype(np.float64)
    eye = np.eye(128)
    c["c_diagA"] = _bf(np.concatenate([eye * (-8.0 * s) for s in slA], 1))
    c["c_diagB"] = _bf(np.concatenate([eye * (-8.0 * s) for s in slA], 1))
    dC = []
    for gh in range(12):
        v = -8.0 * slC[gh]
        hi = float(_bf(v).astype(np.float32))
        lo = float(_bf(v - hi).astype(np.float32))
        dC += [eye * hi, eye * lo]
    c["c_diagC"] = _bf(np.concatenate(dC, 1))

    def band(W, d):
        def f(delta):
            ad = np.abs(delta)
            ok = (ad <= W) & (delta % d == 0)
            return np.where(ok, ad, BIG).astype(np.float32)
        return f
    c["c_MA"] = _bf(_toeplitz(1152, 512, band(128, 1)))
    c["c_MC0"] = _bf(_toeplitz(1152, 128 * 5 - 128, band(64, 1)))
    c["c_MC1"] = _bf(_toeplitz(1408, 128 * 7 - 256, band(256, 4)))
    c["c_MC2"] = _bf(_toeplitz(2944, 128 * 19 - 1024, band(1024, 16)))
    c["c_MBh"] = _bf(_toeplitz(896, 384, lambda dl: (2 * (np.abs(dl) // 2)).astype(np.float32)))
    c["c_MBl"] = _bf(_toeplitz(896, 384, lambda dl: (np.abs(dl) % 2).astype(np.float32)))
    pos = np.arange(Lmax)
    hi = (pos // 128) * 128.0
    lo = (pos % 128) * 1.0
    kaug = np.zeros((4, 4, Lmax), np.float32)
    qaug = np.zeros((4, 2, 4, Lmax), np.float32)
    for h in range(4):
        s8 = 8.0 * slA[h]
        kaug[h] = np.stack([np.ones(Lmax), np.ones(Lmax), s8 * hi, s8 * lo])
        base = np.stack([-s8 * hi, -s8 * lo, np.ones(Lmax), np.ones(Lmax)])
        qaug[h, 0] = base
        qaug[h, 1] = -base
    c["c_kaug"] = _bf(kaug)
    c["c_qaug"] = _bf(qaug)
    return c


def make_core_consts(core, L, SH):
    LQ, LK = SH + 2 * QM, SH + 2 * KM
    tok0 = core * SH
    c = {}
    kstart = tok0 - KM + 128 * np.arange(LK // 128)
    valid = (kstart >= 0) & (kstart < L)
    c["c_kbias"] = np.ascontiguousarray(np.broadcast_to(np.where(valid, 0.0, KBIAS_NEG).astype(np.float32)[None, :],
                                                        (128, LK // 128)))
    slA = 2.0 ** (-8.0 * np.arange(1, 5) / 4)
    pos = np.arange(L)
    hi = (pos // 128) * 128.0
    lo = (pos % 128) * 1.0
    nch = LQ // 512
    kaug = np.zeros((nch, 4, 4, L), np.float32)
    for ci in range(nch):
        A = tok0 - QM + 512 * ci
        blk0 = (pos // 128) * 128
        left = blk0 + 127 < A
        right = blk0 > A + 511
        sign = np.where(left, 1.0, np.where(right, -1.0, 0.0))
        near = (sign == 0)
        for h in range(4):
            s8 = 8.0 * slA[h]
            rows = np.stack([np.ones(L), np.ones(L), s8 * hi, s8 * lo]) * sign[None, :]
            rows[2, near] = -262144.0
            kaug[ci, h] = rows
    c["c_kaugL"] = _bf(kaug)
    qpos = np.clip(tok0 - QM + np.arange(LQ), 0, L - 1)
    qhi = (qpos // 128) * 128.0
    qlo = (qpos % 128) * 1.0
    qaug = np.zeros((4, 4, LQ), np.float32)
    for h in range(4):
        s8 = 8.0 * slA[h]
        qaug[h] = np.stack([-s8 * qhi, -s8 * qlo, np.ones(LQ), np.ones(LQ)])
    c["c_qaugL"] = _bf(qaug)
    fl = np.array([0.0 if core == 0 else 1.0, 0.0 if (tok0 + SH) >= L else 1.0], np.float32)
    c["c_flag"] = np.ascontiguousarray(np.broadcast_to(fl[None, :], (128, 2)))
    return c


def layout_params(p, depth):
    out = {}

    def chunks(v, n):
        v = np.asarray(v, np.float32).reshape(-1, n, 128)
        return np.ascontiguousarray(v.transpose(2, 0, 1).reshape(128, -1))
    out["ln1"] = chunks(p["ln1"], 8)
    out["ln2"] = chunks(p["ln2"], 8)
    out["lnf"] = chunks(np.asarray(p["ln_f"])[None], 8)
    out["subln"] = np.ascontiguousarray(np.asarray(p["subln"], np.float32).T)
    out["sink"] = np.ascontiguousarray(np.broadcast_to(np.asarray(p["a_sink"], np.float32).reshape(1, -1), (128, depth * 4)))
    lam = np.stack([np.asarray(p[k], np.float32) for k in ("lam_q1", "lam_k1", "lam_q2", "lam_k2")], 1)
    out["lam"] = np.ascontiguousarray(np.broadcast_to(lam.reshape(1, -1), (128, depth * 256)))
    cw = np.asarray(p["conv_w"], np.float32).reshape(depth, 3, 44, 128)
    out["convw"] = np.ascontiguousarray(cw.transpose(3, 0, 1, 2).reshape(128, -1))
    cb = np.asarray(p["conv_b"], np.float32).reshape(depth, 44, 128)
    out["convb"] = np.ascontiguousarray(cb.transpose(2, 0, 1).reshape(128, -1))
    for k in ("w_in", "w_out", "w_up", "w_down"):
        out[k] = np.ascontiguousarray(np.asarray(p[k], np.float32))
    return out


_CACHE = {}


def run(seq_inputs, params, depth=DEPTH, n_cores=8):
    seqs = [(k, v.shape[0]) for k, v in seq_inputs[0].items()]
    key = (tuple(seqs), depth)
    if key not in _CACHE:
        _CACHE[key] = Builder(seqs, depth).build()
    nc = _CACHE[key]
    Lmax = max(L for _, L in seqs)
    shared = dict(make_consts(Lmax))
    shared.update(layout_params(params, depth))
    in_maps = []
    Lp = dict(seqs).get("p")
    for c in range(n_cores):
        m = dict(shared)
        if Lp:
            m.update(make_core_consts(c, Lp, Lp // 8))
        for k, v in seq_inputs[c].items():
            m["x_" + k] = np.ascontiguousarray(np.asarray(v, np.float32))
        in_maps.append(m)
    import os
    if os.environ.get("KTRACE"):
        res = run_bass_kernel_spmd(nc, in_maps, core_ids=list(range(n_cores)), trace=True)
        print("EXEC_TIME_NS", res.exec_time_ns, flush=True)
    else:
        res = run_bass_kernel_spmd(nc, in_maps, core_ids=list(range(n_cores)))
    return res.results


def kernel(x_prompt, x_sample, ln1, w_in, a_sink, lam_q1, lam_k1, lam_q2, lam_k2, subln,
           w_out, ln2, w_up, conv_w, conv_b, w_down, ln_f):
    params = dict(ln1=ln1, w_in=w_in, a_sink=a_sink, lam_q1=lam_q1, lam_k1=lam_k1, lam_q2=lam_q2, lam_k2=lam_k2,
                  subln=subln, w_out=w_out, ln2=ln2, w_up=w_up, conv_w=conv_w, conv_b=conv_b, w_down=w_down, ln_f=ln_f)
    x_prompt = np.asarray(x_prompt, np.float32)
    x_sample = np.asarray(x_sample, np.float32)
    seq_inputs = [{"s": x_sample[c], "p": x_prompt[0]} for c in range(8)]
    res = run(seq_inputs, params)
    y_sample = np.stack([res[c]["y_s"] for c in range(8)], 0)
    y_prompt = np.concatenate([res[c]["y_o"] for c in range(8)], 0)[None]
    return (y_prompt.astype(np.float32), y_sample.astype(np.float32))
```

```python
import contextlib
import numpy as np
import ml_dtypes
import concourse.bass as bass
import concourse.mybir as mybir
from concourse.bass_utils import run_bass_kernel_spmd

F32 = mybir.dt.float32
BF16 = mybir.dt.bfloat16
AF = mybir.ActivationFunctionType
ALU = mybir.AluOpType

D = 1024
DEPTH = 2
HD = 64
D_IN = 4352
D_FF = 2816
EPS = 1e-6
SUBLN_EPS = 1e-5
BIG = float(2 ** 20)
B_SKIP = 110.0
QM = 512
KM = 1536
KBIAS_NEG = -30000.0
SAME_ENG_SYNC = True

W_COLS_T = [(0, 256), (256, 128), (512, 512), (1024, 512), (2048, 768), (2816, 768)]
QT_AQ, QT_AK, QT_BQ, QT_BK, QT_CQ, QT_CK = 0, 256, 384, 896, 1408, 2176
QT_ROWS = 2944
W_COLS_V = [(384, 128), (1536, 512), (3584, 768)]
VT_AV, VT_BV, VT_CV = 0, 128, 640
VT_COLS = 1408


class _Op:
    __slots__ = ("eng", "fn", "deps", "dma_key", "needs_inc", "sem", "val", "idx")

    def __init__(self, eng, fn, dma_key):
        self.eng = eng
        self.fn = fn
        self.deps = []
        self.dma_key = dma_key
        self.needs_inc = dma_key is not None
        self.sem = None
        self.val = 0


class Phase:
    ENGS = ("pe", "act", "dve", "pool", "sp")

    def __init__(self, nc):
        self.nc = nc
        self.ops = {e: [] for e in self.ENGS}
        self.lastw = {}
        self.readers = {}
        self.last_dma = {}

    def pid(self, e):
        c = self.__dict__.setdefault("_pidc", {})
        if id(e) not in c:
            c[id(e)] = e.partition_id()
        return c[id(e)]

    def op(self, eng, fn, reads=(), writes=(), dma=None):
        o = _Op(eng, fn, dma)
        deps = []
        for r in reads:
            w = self.lastw.get(r)
            if w is not None:
                deps.append(w)
        for r in writes:
            w = self.lastw.get(r)
            if w is not None:
                deps.append(w)
            deps.extend(self.readers.get(r, ()))
        if dma is not None:
            prev = self.last_dma.get(dma)
            if prev is not None:
                deps.append(prev)
            self.last_dma[dma] = o
        seen = set()
        for d in deps:
            if d is o or id(d) in seen:
                continue
            seen.add(id(d))
            if d.dma_key is None and d.eng == eng and (eng == "pe" or not SAME_ENG_SYNC):
                continue
            o.deps.append(d)
            d.needs_inc = True
        for r in reads:
            self.readers.setdefault(r, []).append(o)
        for r in writes:
            self.lastw[r] = o
            self.readers[r] = []
        self.ops[eng].append(o)
        return o

    def emit(self):
        nc = self.nc
        reg = SEMREG
        LIM = 30000

        def assign(key, o, inc):
            ent = reg.get(key)
            if ent is None or ent[1] + inc > LIM:
                ent = [nc.alloc_semaphore("k%d" % len(reg.setdefault("_all", []))), 0]
                reg["_all"].append(ent[0])
                reg[key] = ent
            ent[1] += inc
            o.sem = ent[0]
            o.val = ent[1]

        for e in self.ENGS:
            for o in self.ops[e]:
                if o.dma_key is None:
                    if o.needs_inc:
                        assign(("eng", e), o, 1)
                else:
                    assign(("dma", o.dma_key), o, 16)
        final = {}
        for e in self.ENGS:
            for o in self.ops[e]:
                if o.dma_key is not None:
                    final[id(o.sem)] = (o.sem, o.val)
        with nc.Block() as block:
            def make(e):
                def body(eng):
                    waited = {}
                    for o in self.ops[e]:
                        for d in o.deps:
                            if waited.get(id(d.sem), 0) >= d.val:
                                continue
                            waited[id(d.sem)] = d.val
                            eng.wait_ge(d.sem, d.val)
                        ins = o.fn(eng)
                        if o.needs_inc:
                            ins.then_inc(o.sem, 16 if o.dma_key is not None else 1)
                    if e == "sp":
                        for k, (sm, v) in final.items():
                            if waited.get(k, 0) < v:
                                eng.wait_ge(sm, v)
                return body

            block.tensor(make("pe"))
            block.scalar(make("act"))
            block.vector(make("dve"))
            block.gpsimd(make("pool"))
            block.sync(make("sp"))


SEMREG = {}


def _bf(a):
    return np.asarray(a, np.float32).astype(ml_dtypes.bfloat16)


class Builder:
    def __init__(self, seqs, depth=DEPTH):
        self.seqs = seqs
        self.depth = depth
        self.nc = bass.Bass("TRN2", target_bir_lowering=False)
        nc = self.nc
        SEMREG.clear()
        self.dr = {}

        def din(name, shape, dt=F32):
            self.dr[name] = nc.dram_tensor(name, list(shape), dt, kind="ExternalInput").ap()

        def dout(name, shape):
            self.dr[name] = nc.dram_tensor(name, list(shape), F32, kind="ExternalOutput").ap()

        def dtmp(name, shape, dt):
            self.dr[name] = nc.dram_tensor(name, list(shape), dt).ap()

        self.shard = None
        for (s, L) in seqs:
            din("x_" + s, (L, D))
            if s == "p":
                self.shard = L // 8
            else:
                dout("y_" + s, (L, D))
            dtmp("XT_" + s, (D, L + 2 * QM), F32)
            dtmp("XU_" + s, (D, L + 2 * QM), F32)
            dtmp("QT_" + s, (QT_ROWS, L + 2 * KM), BF16)
            dtmp("VT_" + s, (L + 2 * KM, VT_COLS), BF16)
            dtmp("OT_" + s, (D, L), BF16)
        if self.shard:
            SH = self.shard
            self.LQ = SH + 2 * QM
            self.LK = SH + 2 * KM
            dout("y_o", (SH, D))
            dtmp("XL_o", (D, self.LQ), F32)
            dtmp("XW_o", (D, SH), F32)
            dtmp("QT_o", (QT_ROWS, self.LK), BF16)
            dtmp("VT_o", (self.LK, VT_COLS), BF16)
            dtmp("OT_o", (D, self.LQ), BF16)
        Lmax = max(L for _, L in seqs)
        self.Lmax = Lmax
        din("w_in", (depth, D, D_IN))
        din("w_out", (depth, D, D))
        din("w_up", (depth, D, 2 * D_FF))
        din("w_down", (depth, D_FF, D))
        din("ln1", (128, depth * 8))
        din("ln2", (128, depth * 8))
        din("lnf", (128, 8))
        din("subln", (128, depth))
        din("sink", (128, depth * 4))
        din("lam", (128, depth * 4 * 64))
        din("convw", (128, depth * 3 * 44))
        din("convb", (128, depth * 44))
        din("c_ident", (128, 128))
        din("c_diagA", (128, 4 * 128), BF16)
        din("c_diagB", (128, 4 * 128), BF16)
        din("c_diagC", (128, 24 * 128), BF16)
        din("c_MA", (128, 1152), BF16)
        din("c_MC0", (128, 1152), BF16)
        din("c_MC1", (128, 1408), BF16)
        din("c_MC2", (128, 2944), BF16)
        din("c_MBh", (128, 896), BF16)
        din("c_MBl", (128, 896), BF16)
        din("c_kaug", (4, 4, Lmax), BF16)
        din("c_qaug", (4, 2, 4, Lmax), BF16)
        if self.shard:
            din("c_kbias", (128, self.LK // 128))
            din("c_kaugL", (self.LQ // 512, 4, 4, Lmax), BF16)
            din("c_qaugL", (4, 4, self.LQ), BF16)
            din("c_flag", (128, 2))

    def X(self, which, s):
        t = self.dr[("XT_", "XU_")[which % 2] + s]
        return t[:, QM:t.shape[1] - QM]

    def QTv(self, s):
        t = self.dr["QT_" + s]
        return t[:, KM:t.shape[1] - KM]

    def VTv(self, s):
        t = self.dr["VT_" + s]
        return t[KM:t.shape[0] - KM, :]

    def phase_init_pads(self, s, L, which):
        nc, dr = self.nc, self.dr
        with contextlib.ExitStack() as st:
            zb = st.enter_context(nc.sbuf_tensor(self._nm("pi_zb"), [128, 12, VT_COLS], BF16))
            zf = st.enter_context(nc.sbuf_tensor(self._nm("pi_zf"), [128, 8, QM], F32))
            zq = st.enter_context(nc.sbuf_tensor(self._nm("pi_zq"), [128, KM], BF16))
            ph = Phase(nc)
            ph.op("pool", lambda e: e.memset(zq[:], 0.0), writes=["zb"])
            ph.op("pool", lambda e: e.memset(zb[:], 0.0), writes=["zb"])
            ph.op("pool", lambda e: e.memset(zf[:], 0.0), writes=["zf"])
            QTf = dr["QT_" + s]
            VTf = dr["VT_" + s]
            Xf = dr[("XT_", "XU_")[which % 2] + s]
            k = 0
            for c0 in (0, KM + L):
                for j in range(QT_ROWS // 128):
                    ph.op("sp", lambda e, c0=c0, j=j: e.dma_start(out=QTf[j * 128:(j + 1) * 128, c0:c0 + KM],
                                                               in_=zq[:]),
                          reads=["zb"], dma="st_z%d" % (k % 4))
                    k += 1
                ph.op("sp", lambda e, c0=c0: e.dma_start(
                    out=VTf[c0:c0 + KM, :].rearrange("(u p) n -> p u n", p=128), in_=zb[:]), reads=["zb"], dma="st_z%d" % (k % 4))
                k += 1
            for c0 in (0, QM + L):
                ph.op("sp", lambda e, c0=c0: e.dma_start(
                    out=Xf[:, c0:c0 + QM].rearrange("(c p) t -> p c t", p=128), in_=zf[:]), reads=["zf"], dma="st_z%d" % (k % 4))
                k += 1
            ph.emit()

    def phase_localize(self, s, L, which):
        nc, dr = self.nc, self.dr
        SH, LQ, LK = self.shard, self.LQ, self.LK
        QTf = dr["QT_" + s]
        VTf = dr["VT_" + s]
        Xf = dr[("XT_", "XU_")[which % 2] + s]
        ph = Phase(nc)
        k = 0
        RQ = QT_ROWS // 4
        for j in range(4):
            ph.op("sp", lambda e, j=j: e.dma_start(out=dr["QT_o"][j * RQ:(j + 1) * RQ, :],
                                                   in_=QTf[j * RQ:(j + 1) * RQ, bass.ds(ph.pid(e) * SH, LK)]),
                  dma="cp%d" % (k % 4))
            k += 1
        for c0 in range(0, VT_COLS, VT_COLS // 4):
            ph.op("sp", lambda e, c0=c0: e.dma_start(out=dr["VT_o"][:, c0:c0 + VT_COLS // 4],
                                                     in_=VTf[bass.ds(ph.pid(e) * SH, LK), c0:c0 + VT_COLS // 4]),
                  dma="cp%d" % (k % 4))
            k += 1
        for j in range(2):
            ph.op("sp", lambda e, j=j: e.dma_start(out=dr["XL_o"][j * 512:(j + 1) * 512, :],
                                                   in_=Xf[j * 512:(j + 1) * 512, bass.ds(ph.pid(e) * SH, LQ)]),
                  dma="cp%d" % (k % 4))
            k += 1
        ph.emit()

    def phase_transpose_in(self, s, L):
        nc, dr = self.nc, self.dr
        x = dr["x_" + s]
        XT = self.X(0, s).rearrange("(c p) t -> p c t", p=128)
        nt = L // 512
        with contextlib.ExitStack() as st:
            ident = st.enter_context(nc.sbuf_tensor(self._nm("p0_id"), [128, 128], F32))
            xin = st.enter_context(nc.sbuf_tensor(self._nm("p0_xin"), [128, 2, 4, D], F32))
            xt = st.enter_context(nc.sbuf_tensor(self._nm("p0_xt"), [128, 2, 8, 512], F32))
            ps = st.enter_context(nc.psum_tensor(self._nm("p0_ps"), [128, 4, 512], F32))
            ph = Phase(nc)
            ph.op("sp", lambda e: e.dma_start(out=ident[:], in_=dr["c_ident"][:, :]), writes=["ident"], dma="ld_id")
            for t in range(nt):
                b = t % 2
                src = x[t * 512:(t + 1) * 512, :].rearrange("(s p) f -> p s f", p=128)
                ph.op("sp", lambda e, b=b, src=src: e.dma_start(out=xin[:, b], in_=src),
                      writes=[("xin", b)], dma="ld_xin%d" % b)
                for c in range(8):
                    pb = (t * 8 + c) % 4
                    for sub in range(4):
                        ph.op("pe", lambda e, pb=pb, sub=sub, b=b, c=c: e.transpose(
                            ps[:, pb, sub * 128:(sub + 1) * 128], xin[:, b, sub, c * 128:(c + 1) * 128], ident[:]),
                            reads=[("xin", b), "ident"], writes=[("ps", pb, sub)])
                    if c % 2 == 0:
                        ph.op("act", lambda e, pb=pb, b=b, c=c: e.copy(out=xt[:, b, c, :], in_=ps[:, pb, :]),
                              reads=[("ps", pb, q) for q in range(4)], writes=[("xt", b, c)])
                    else:
                        ph.op("dve", lambda e, pb=pb, b=b, c=c: e.tensor_copy(out=xt[:, b, c, :], in_=ps[:, pb, :]),
                              reads=[("ps", pb, q) for q in range(4)], writes=[("xt", b, c)])
                ph.op("pool", lambda e, b=b, t=t: e.dma_start(out=XT[:, :, t * 512:(t + 1) * 512], in_=xt[:, b]),
                      reads=[("xt", b, c) for c in range(8)], dma="st_xt%d" % b)
            ph.emit()

    def _rmsnorm_tile(self, ph, xt_ap, xt_res, ncols, sq, ones_bf, ss_ps, ss_res, r_ap, r_res, h_ap_fn, h_res_fn,
                      g_ap_fn, nfeat, eps):
        ph.op("act", lambda e: e.activation(out=sq, in_=xt_ap, func=AF.Square),
              reads=[xt_res], writes=["sq"])
        for c in range(8):
            ph.op("pe", lambda e, c=c: e.matmul(ss_ps, ones_bf, sq[:, c, :], start=(c == 0), stop=(c == 7)),
                  reads=["sq", "ones"], writes=[ss_res])
        ph.op("dve", lambda e: e.tensor_scalar(r_ap, ss_ps, 1.0 / nfeat, eps, ALU.mult, ALU.add),
              reads=[ss_res], writes=[r_res])
        ph.op("act", lambda e: e.activation(out=r_ap, in_=r_ap, func=AF.Sqrt), reads=[r_res], writes=[r_res])
        ph.op("dve", lambda e: e.reciprocal(r_ap, r_ap), reads=[r_res], writes=[r_res])
        for c in range(8):
            eng = "dve"
            ph.op(eng, lambda e, c=c: e.scalar_tensor_tensor(h_ap_fn(c), xt_ap[:, c, :], g_ap_fn(c), r_ap,
                                                              ALU.mult, ALU.mult),
                  reads=[xt_res, r_res, "params"], writes=[h_res_fn(c)])

    def phase_proj(self, l, jobs):
        nc, dr = self.nc, self.dr
        w_in = dr["w_in"]
        with contextlib.ExitStack() as st:
            w = st.enter_context(nc.sbuf_tensor(self._nm("p1_w"), [128, 8, D_IN], BF16))
            g = st.enter_context(nc.sbuf_tensor(self._nm("p1_g"), [128, 8], F32))
            ones = st.enter_context(nc.sbuf_tensor(self._nm("p1_ones"), [128, 128], BF16))
            xt = st.enter_context(nc.sbuf_tensor(self._nm("p1_xt"), [128, 2, 8, 512], F32))
            sq = st.enter_context(nc.sbuf_tensor(self._nm("p1_sq"), [128, 8, 512], BF16))
            r = st.enter_context(nc.sbuf_tensor(self._nm("p1_r"), [128, 512], F32))
            h = st.enter_context(nc.sbuf_tensor(self._nm("p1_h"), [128, 2, 8, 512], BF16))
            qt = st.enter_context(nc.sbuf_tensor(self._nm("p1_qt"), [128, 2, 23, 512], BF16))
            vt = st.enter_context(nc.sbuf_tensor(self._nm("p1_vt"), [128, 2, 4, VT_COLS], BF16))
            ss_ps = st.enter_context(nc.psum_tensor(self._nm("p1_ss"), [128, 512], F32))
            ps = st.enter_context(nc.psum_tensor(self._nm("p1_ps"), [128, 6, 512], F32))
            ph = Phase(nc)
            ph.op("pool", lambda e: e.memset(ones[:], 1.0), writes=["ones"])
            ph.op("sp", lambda e: e.dma_start(out=g[:], in_=dr["ln1"][:, l * 8:(l + 1) * 8]), writes=["params"], dma="ld_g")
            wsrc = w_in[l].rearrange("(c p) n -> p c n", p=128)
            for c in range(8):
                for (n0, n1) in ((0, 1536), (1536, 3072), (3072, D_IN)):
                    ph.op("pool", lambda e, c=c, n0=n0, n1=n1: e.dma_start(out=w[:, c, n0:n1], in_=wsrc[:, c, n0:n1]),
                          writes=[("w", c, n0)], dma="ld_w%d" % (c % 4))
            wres = [("w", c, n0) for c in range(8) for n0 in (0, 1536, 3072)]
            pcount = [0]

            def next_ps():
                pcount[0] += 1
                return pcount[0] % 6

            ecount = [0]

            def evac(ph, dst, src, reads, writes):
                ecount[0] += 1
                if ecount[0] % 2 == 0:
                    ph.op("act", lambda e: e.copy(out=dst, in_=src), reads=reads, writes=writes)
                else:
                    ph.op("dve", lambda e: e.tensor_copy(out=dst, in_=src), reads=reads, writes=writes)

            for (s, L) in jobs:
                XT = self.X(l, s).rearrange("(c p) t -> p c t", p=128)
                QT = self.QTv(s).rearrange("(j p) t -> p j t", p=128)
                VT = self.VTv(s)
                nt = L // 512
                for t in range(nt):
                    b = t % 2
                    ph.op("sp", lambda e, b=b, t=t, XT=XT: e.dma_start(out=xt[:, b], in_=XT[:, :, t * 512:(t + 1) * 512]),
                          writes=[("xt", b)], dma="ld_xt%d" % b)
                    self._rmsnorm_tile(ph, xt[:, b], ("xt", b), 512, sq[:], ones[:], ss_ps[:], "ss", r[:], "r",
                                       lambda c, b=b: h[:, b, c, :], lambda c, b=b: ("h", b, c),
                                       lambda c: g[:, c:c + 1], D, EPS)
                    hres = [("h", b, c) for c in range(8)]
                    j = 0
                    for (c0, n) in W_COLS_T:
                        for jj in range(n // 128):
                            pb = next_ps()
                            col = c0 + jj * 128
                            for c in range(8):
                                ph.op("pe", lambda e, pb=pb, c=c, col=col, b=b: e.matmul(
                                    ps[:, pb, :], w[:, c, col:col + 128], h[:, b, c, :], start=(c == 0), stop=(c == 7)),
                                    reads=hres + wres if c == 0 else [], writes=[("ps", pb)])
                            evac(ph, qt[:, b, j, :], ps[:, pb, :], [("ps", pb)], [("qt", b, j)])
                            j += 1
                    ph.op("pool", lambda e, b=b, t=t, QT=QT: e.dma_start(out=QT[:, :, t * 512:(t + 1) * 512], in_=qt[:, b]),
                          reads=[("qt", b, j) for j in range(23)], dma="st_qt%d" % b)
                    for sub in range(4):
                        for (c0, n), v0 in zip(W_COLS_V, (VT_AV, VT_BV, VT_CV)):
                            for n0 in range(0, n, 512):
                                nn = min(512, n - n0)
                                pb = next_ps()
                                for c in range(8):
                                    ph.op("pe", lambda e, pb=pb, c=c, col=c0 + n0, nn=nn, b=b, sub=sub: e.matmul(
                                        ps[:, pb, 0:nn], h[:, b, c, sub * 128:(sub + 1) * 128], w[:, c, col:col + nn],
                                        start=(c == 0), stop=(c == 7)),
                                        reads=hres + wres if c == 0 else [], writes=[("ps", pb)])
                                evac(ph, vt[:, b, sub, v0 + n0:v0 + n0 + nn], ps[:, pb, 0:nn], [("ps", pb)],
                                     [("vt", b, sub, v0 + n0)])
                    vsrc = VT[t * 512:(t + 1) * 512, :].rearrange("(s p) n -> p s n", p=128)
                    ph.op("pool", lambda e, b=b, vsrc=vsrc: e.dma_start(out=vsrc, in_=vt[:, b]),
                          reads=[("vt", b, sub, v) for sub in range(4) for v in (0, 128, 640, 1152)], dma="st_vt%d" % b)
            ph.emit()

    def _attend(self, ph, items, nout, S_ps, O_ps, pt, acc, dv, tag):
        n = len(items)
        first = {}
        last = {}
        for i, it in enumerate(items):
            first.setdefault(it["out"], i)
            last[it["out"]] = i
        NS = S_ps.shape[1]
        NP = pt.shape[1]

        def do_s(i):
            it = items[i]
            sb = i % NS
            nm = len(it["masks"])
            ph.op("pe", lambda e: e.matmul(S_ps[:, sb, :], it["k"], it["q"], start=True, stop=(nm == 0)),
                  reads=it["reads"], writes=[("S", sb)])
            for mi, (ml, mr) in enumerate(it["masks"]):
                ph.op("pe", lambda e, ml=ml, mr=mr, mi=mi: e.matmul(S_ps[:, sb, :], ml, mr, start=False,
                                                                    stop=(mi == nm - 1)),
                      reads=["consts"], writes=[("S", sb)])
            if it.get("bias") is not None:
                ph.op("act", lambda e: e.activation(out=pt[:, i % NP, :], in_=S_ps[:, sb, :], func=AF.Exp, scale=0.125,
                                                    bias=it["bias"]),
                      reads=[("S", sb), "consts"], writes=[("pt", i % NP)])
            else:
                ph.op("act", lambda e: e.activation(out=pt[:, i % NP, :], in_=S_ps[:, sb, :], func=AF.Exp, scale=0.125),
                      reads=[("S", sb)], writes=[("pt", i % NP)])

        cnt = {}

        def do_pv(i):
            it = items[i]
            o = it["out"]
            par = cnt.get(o, 0) % 2
            fresh = cnt.get(o, 0) < 2
            cnt[o] = cnt.get(o, 0) + 1
            if fresh:
                ph.op("dve", lambda e: e.tensor_copy(out=acc[:, o, par, :], in_=pt[:, i % NP, :]),
                      reads=[("pt", i % NP)], writes=[("acc", o, par)])
            else:
                ph.op("dve", lambda e: e.tensor_tensor(out=acc[:, o, par, :], in0=acc[:, o, par, :],
                                                      in1=pt[:, i % NP, :], op=ALU.add),
                      reads=[("pt", i % NP)], writes=[("acc", o, par)])
            ph.op("pe", lambda e: e.matmul(O_ps[0:dv, o, :], it["v"], pt[:, i % NP, :], start=(first[o] == i),
                                           stop=(last[o] == i)),
                  reads=[("pt", i % NP)] + it["reads"], writes=[("O", o)])

        LA = 2
        for i in range(n):
            do_s(i)
            if i >= LA:
                do_pv(i - LA)
        for j in range(max(0, n - LA), n):
            do_pv(j)
        for o, c in cnt.items():
            if c >= 2:
                ph.op("dve", lambda e, o=o: e.tensor_tensor(out=acc[:, o, 0, :], in0=acc[:, o, 0, :],
                                                            in1=acc[:, o, 1, :], op=ALU.add),
                      reads=[("acc", o, 1)], writes=[("acc", o, 0)])

    def _attend_b(self, ph, pairs, S_ps, O_ps, pt, acc, NPB, den_ps=None, ones_b=None, pe_every=4):
        n = len(pairs)

        def do_s(i):
            it = pairs[i]
            sb = 2 * (i % 3)
            nm = len(it["masks"])
            for m in range(2):
                ph.op("pe", lambda e, m=m: e.matmul(S_ps[:, sb + m, :], it["k"][m], it["q"][m], start=True, stop=(nm == 0)),
                      reads=it["reads"], writes=[("S", i % 3)])
                for mi, (ml, mr) in enumerate(it["masks"]):
                    ph.op("pe", lambda e, m=m, ml=ml, mr=mr, mi=mi: e.matmul(S_ps[:, sb + m, :], ml, mr, start=False,
                                                                         stop=(mi == nm - 1)),
                          reads=["consts"], writes=[("S", i % 3)])
            ps0 = 2 * (i % NPB)
            ph.op("act", lambda e: e.activation(out=pt[:, ps0:ps0 + 2, :], in_=S_ps[:, sb:sb + 2, :], func=AF.Exp,
                                                scale=0.125),
                  reads=[("S", i % 3)], writes=[("pt", i % NPB)])

        st8 = {"dve": 0, "pe": 0}

        def do_pv(i):
            it = pairs[i]
            ps0 = 2 * (i % NPB)
            if den_ps is not None and (i % pe_every) == pe_every - 1:
                for m in range(2):
                    ph.op("pe", lambda e, m=m, first=(st8["pe"] == 0): e.matmul(den_ps[:, m, :], ones_b, pt[:, ps0 + m, :],
                                                                               start=first, stop=False),
                          reads=[("pt", i % NPB), "ones"], writes=[("den", m)])
                st8["pe"] += 1
            else:
                par = st8["dve"] % 2
                if st8["dve"] < 2:
                    ph.op("dve", lambda e: e.tensor_copy(out=acc[:, par], in_=pt[:, ps0:ps0 + 2, :]),
                          reads=[("pt", i % NPB)], writes=[("acc", par)])
                else:
                    ph.op("dve", lambda e: e.tensor_tensor(out=acc[:, par], in0=acc[:, par], in1=pt[:, ps0:ps0 + 2, :],
                                                          op=ALU.add),
                          reads=[("pt", i % NPB)], writes=[("acc", par)])
                st8["dve"] += 1
            for m in range(2):
                ph.op("pe", lambda e, m=m: e.matmul(O_ps[:, m, :], it["v"], pt[:, ps0 + m, :], start=(i == 0),
                                                    stop=(i == n - 1)),
                      reads=[("pt", i % NPB)] + it["reads"], writes=[("O", m)])

        LA = 2
        for i in range(n):
            do_s(i)
            if i >= LA:
                do_pv(i - LA)
        for j in range(max(0, n - LA), n):
            do_pv(j)
        if st8["dve"] >= 2:
            ph.op("dve", lambda e: e.tensor_tensor(out=acc[:, 0], in0=acc[:, 0], in1=acc[:, 1], op=ALU.add),
                  reads=[("acc", 1)], writes=[("acc", 0)])
        return st8["pe"] > 0

    def _finalize_den(self, ph, acc_ap, acc_res, ones_f, den_ps, den_res, rden_ap, rden_res, rows, extra=None,
                      start=True):
        ph.op("pe", lambda e: e.matmul(den_ps, ones_f, acc_ap, start=start, stop=True),
              reads=[acc_res, "ones"], writes=[den_res])
        if extra is not None:
            ph.op("dve", lambda e: e.tensor_scalar(rden_ap, den_ps[0:rows], extra, None, ALU.add),
                  reads=[den_res, "params"], writes=[rden_res])
            ph.op("dve", lambda e: e.reciprocal(rden_ap, rden_ap), reads=[rden_res], writes=[rden_res])
        else:
            ph.op("dve", lambda e: e.reciprocal(rden_ap, den_ps[0:rows]), reads=[den_res], writes=[rden_res])

    def phase_attn_ac(self, l, s, L, own=False):
        nc, dr = self.nc, self.dr
        if own:
            QT, VT, OT = dr["QT_o"], dr["VT_o"], dr["OT_o"]
            nt = self.LQ // 512
            kshift = KM - QM
            L = self.LK
        else:
            QT = self.QTv(s)
            VT = self.VTv(s)
            OT = dr["OT_" + s]
            nt = L // 512
            kshift = 0
        nblk = L // 128
        GW = (128, 256, 1024)
        GN = (6, 8, 20)
        with contextlib.ExitStack() as st:
            ones_f = st.enter_context(nc.sbuf_tensor(self._nm("pa_onesf"), [128, 128], F32))
            esink = st.enter_context(nc.sbuf_tensor(self._nm("pa_sink"), [128, 4], F32))
            dA = st.enter_context(nc.sbuf_tensor(self._nm("pa_dA"), [128, 4 * 128], BF16))
            dC = st.enter_context(nc.sbuf_tensor(self._nm("pa_dC"), [128, 24 * 128], BF16))
            MA = st.enter_context(nc.sbuf_tensor(self._nm("pa_MA"), [128, 1152], BF16))
            MC0 = st.enter_context(nc.sbuf_tensor(self._nm("pa_MC0"), [128, 1152], BF16))
            MC1 = st.enter_context(nc.sbuf_tensor(self._nm("pa_MC1"), [128, 1408], BF16))
            MC2 = st.enter_context(nc.sbuf_tensor(self._nm("pa_MC2"), [128, 2944], BF16))
            kA = st.enter_context(nc.sbuf_tensor(self._nm("pa_kA"), [128, 2, 768], BF16))
            vA = st.enter_context(nc.sbuf_tensor(self._nm("pa_vA"), [128, 2, 6, 128], BF16))
            qA = st.enter_context(nc.sbuf_tensor(self._nm("pa_qA"), [128, 2, 2, 512], BF16))
            kC = st.enter_context(nc.sbuf_tensor(self._nm("pa_kC"), [128, 2, 2, 34 * 128], BF16))
            vC = st.enter_context(nc.sbuf_tensor(self._nm("pa_vC"), [128, 2, 34, 256], BF16))
            qC = st.enter_context(nc.sbuf_tensor(self._nm("pa_qC"), [128, 2, 3, 2, 512], BF16))
            pt = st.enter_context(nc.sbuf_tensor(self._nm("pa_pt"), [128, 4, 512], BF16))
            acc = st.enter_context(nc.sbuf_tensor(self._nm("pa_acc"), [128, 2, 2, 512], F32))
            rden = st.enter_context(nc.sbuf_tensor(self._nm("pa_rden"), [64, 512], F32))
            ot = st.enter_context(nc.sbuf_tensor(self._nm("pa_ot"), [64, 2, 8, 512], BF16))
            S_ps = st.enter_context(nc.psum_tensor(self._nm("pa_S"), [128, 4, 512], F32))
            O_ps = st.enter_context(nc.psum_tensor(self._nm("pa_O"), [128, 2, 512], F32))
            den_ps = st.enter_context(nc.psum_tensor(self._nm("pa_den"), [128, 512], F32))
            ph = Phase(nc)
            kbias = None
            if own:
                kbias = st.enter_context(nc.sbuf_tensor(self._nm("pa_kbias"), [128, self.LK // 128], F32))
                ph.op("sp", lambda e: e.dma_start(out=kbias[:], in_=dr["c_kbias"][:, :]), writes=["consts"], dma="ld_c1")
            MC = (MC0, MC1, MC2)
            ph.op("pool", lambda e: e.memset(ones_f[:], 1.0), writes=["ones"])
            ph.op("sp", lambda e: e.dma_start(out=esink[:], in_=dr["sink"][:, l * 4:(l + 1) * 4]), writes=["params"], dma="ld_c0")
            ph.op("act", lambda e: e.activation(out=esink[:], in_=esink[:], func=AF.Exp), reads=["params"], writes=["params"])
            for nm, tl in (("c_diagA", dA), ("c_diagC", dC), ("c_MA", MA), ("c_MC0", MC0), ("c_MC1", MC1), ("c_MC2", MC2)):
                ph.op("sp", lambda e, nm=nm, tl=tl: e.dma_start(out=tl[:], in_=dr[nm][:, :]), writes=["consts"], dma="ld_c1")
            goff = (0, 6, 14)
            oc = [0]
            for c in range(nt):
                b = c % 2
                a = c * 512 + kshift
                ao = c * 512
                u_lo = max(0, -((a - 128) // 128))
                u_hi = min(6, (L - (a - 128)) // 128)
                k0 = a - 128 + 128 * u_lo
                k1 = a - 128 + 128 * u_hi
                ph.op("sp", lambda e, b=b, k0=k0, k1=k1, u_lo=u_lo, u_hi=u_hi: e.dma_start(
                    out=kA[:, b, u_lo * 128:u_hi * 128], in_=QT[QT_AK:QT_AK + 128, k0:k1]),
                    writes=[("kA", b)], dma="ld_kA%d" % b)
                ph.op("sp", lambda e, b=b, k0=k0, k1=k1, u_lo=u_lo, u_hi=u_hi: e.dma_start(
                    out=vA[:, b, u_lo:u_hi, :],
                    in_=VT[k0:k1, VT_AV:VT_AV + 128].rearrange("(u p) n -> p u n", p=128)),
                    writes=[("vA", b)], dma="ld_vA%d" % b)
                for kvh in range(2):
                    for j in range(2):
                        r0 = QT_AQ + (2 * kvh + j) * 64
                        ph.op("sp", lambda e, b=b, kvh=kvh, j=j, r0=r0, a=a: e.dma_start(
                            out=qA[kvh * 64:(kvh + 1) * 64, b, j, :], in_=QT[r0:r0 + 64, a:a + 512]),
                            writes=[("qA", b, kvh, j)], dma="ld_qA%d" % b)
                cval = []
                for g in range(3):
                    ulo = max(0, -((a - GW[g]) // 128))
                    uhi = min(GN[g], (L - (a - GW[g])) // 128)
                    cval.append((ulo, uhi))
                    k0 = a - GW[g] + 128 * ulo
                    k1 = a - GW[g] + 128 * uhi
                    for pr in range(2):
                        r0 = QT_CK + g * 256 + pr * 128
                        ph.op("sp", lambda e, b=b, g=g, pr=pr, r0=r0, k0=k0, k1=k1, ulo=ulo, uhi=uhi: e.dma_start(
                            out=kC[:, b, pr, (goff[g] + ulo) * 128:(goff[g] + uhi) * 128], in_=QT[r0:r0 + 128, k0:k1]),
                            writes=[("kC", b, g, pr)], dma="ld_kC%d" % b)
                        r1 = QT_CQ + g * 256 + pr * 128
                        ph.op("sp", lambda e, b=b, g=g, pr=pr, r1=r1, a=a: e.dma_start(
                            out=qC[:, b, g, pr, :], in_=QT[r1:r1 + 128, a:a + 512]),
                            writes=[("qC", b, g, pr)], dma="ld_qC%d" % b)
                    ph.op("sp", lambda e, b=b, g=g, k0=k0, k1=k1, ulo=ulo, uhi=uhi: e.dma_start(
                        out=vC[:, b, goff[g] + ulo:goff[g] + uhi, :],
                        in_=VT[k0:k1, VT_CV + g * 256:VT_CV + (g + 1) * 256].rearrange("(u p) n -> p u n", p=128)),
                        writes=[("vC", b, g)], dma="ld_vC%d" % b)
                for hq in range(4):
                    kvh, j = hq // 2, hq % 2
                    items = []
                    for u in range(u_lo, u_hi):
                        items.append(dict(
                            q=qA[kvh * 64:(kvh + 1) * 64, b, j, :],
                            k=kA[kvh * 64:(kvh + 1) * 64, b, u * 128:(u + 1) * 128],
                            v=vA[:, b, u, kvh * 64:(kvh + 1) * 64],
                            masks=[(dA[:, hq * 128:(hq + 1) * 128], MA[:, 640 - 128 * u:640 - 128 * u + 512])],
                            bias=(kbias[:, (a - 128) // 128 + u:(a - 128) // 128 + u + 1] if own else None),
                            out=oc[0] % 2, reads=[("kA", b), ("vA", b), ("qA", b, kvh, j), "consts"]))
                    o = oc[0] % 2
                    oc[0] += 1
                    self._attend(ph, items, 1, S_ps, O_ps, pt, acc, 64, "A")
                    self._finalize_den(ph, acc[:, o, 0, :], ("acc", o, 0), ones_f[:], den_ps[:], "den", rden[:], "rden", 64,
                                       extra=esink[0:64, hq:hq + 1])
                    ph.op("dve", lambda e, o=o, b=b, hq=hq: e.tensor_tensor(out=ot[:, b, hq, :], in0=O_ps[0:64, o, :],
                                                                           in1=rden[:], op=ALU.mult),
                          reads=[("O", o), "rden"], writes=[("ot", b, hq)])
                for h in range(4):
                    pr, hp = h // 2, h % 2
                    items = []
                    for g in range(3):
                        ulo, uhi = cval[g]
                        for u in range(ulo, uhi):
                            off = 128 * (GN[g] - 1) - 128 * u
                            gi = (g * 4 + h) * 2
                            items.append(dict(
                                q=qC[hp * 64:(hp + 1) * 64, b, g, pr, :],
                                k=kC[hp * 64:(hp + 1) * 64, b, pr, (goff[g] + u) * 128:(goff[g] + u + 1) * 128],
                                v=vC[:, b, goff[g] + u, h * 64:(h + 1) * 64],
                                masks=[(dC[:, gi * 128:(gi + 1) * 128], MC[g][:, off:off + 512])],
                                bias=(kbias[:, (a - GW[g]) // 128 + u:(a - GW[g]) // 128 + u + 1] if own else None),
                                out=oc[0] % 2,
                                reads=[("kC", b, g, pr), ("vC", b, g), ("qC", b, g, pr), "consts"]))
                    o = oc[0] % 2
                    oc[0] += 1
                    self._attend(ph, items, 1, S_ps, O_ps, pt, acc, 64, "C")
                    self._finalize_den(ph, acc[:, o, 0, :], ("acc", o, 0), ones_f[:], den_ps[:], "den", rden[:], "rden", 64)
                    ph.op("dve", lambda e, o=o, b=b, h=h: e.tensor_tensor(out=ot[:, b, 4 + h, :], in0=O_ps[0:64, o, :],
                                                                         in1=rden[:], op=ALU.mult),
                          reads=[("O", o), "rden"], writes=[("ot", b, 4 + h)])
                ph.op("pool", lambda e, b=b, a=ao: e.dma_start(
                    out=OT[0:256, a:a + 512].rearrange("(h p) t -> p h t", p=64), in_=ot[:, b, 0:4, :]),
                    reads=[("ot", b, hh) for hh in range(4)], dma="st_oA%d" % b)
                ph.op("pool", lambda e, b=b, a=ao: e.dma_start(
                    out=OT[768:1024, a:a + 512].rearrange("(h p) t -> p h t", p=64), in_=ot[:, b, 4:8, :]),
                    reads=[("ot", b, 4 + hh) for hh in range(4)], dma="st_oC%d" % b)
            ph.emit()

    def phase_attn_b(self, l, s, L, own=False):
        nc, dr = self.nc, self.dr
        QT = self.QTv(s)
        VT = self.VTv(s)
        OT = dr["OT_" + s]
        nt = L // 512
        nblk = L // 128
        lam_init = 0.8 - 0.6 * float(np.exp(-0.3 * l))
        with contextlib.ExitStack() as st:
            ones_f = st.enter_context(nc.sbuf_tensor(self._nm("pb_onesf"), [128, 128], F32))
            ones_b = st.enter_context(nc.sbuf_tensor(self._nm("pb_onesb"), [128, 128], BF16))
            lamt = st.enter_context(nc.sbuf_tensor(self._nm("pb_lam"), [128, 4 * 64], F32))
            lt = st.enter_context(nc.sbuf_tensor(self._nm("pb_lt"), [128, 2 * 64], F32))
            ls = st.enter_context(nc.sbuf_tensor(self._nm("pb_ls"), [128, 4], F32))
            gs = st.enter_context(nc.sbuf_tensor(self._nm("pb_gs"), [128, 1], F32))
            dB = st.enter_context(nc.sbuf_tensor(self._nm("pb_dB"), [128, 4 * 128], BF16))
            MBh = st.enter_context(nc.sbuf_tensor(self._nm("pb_MBh"), [128, 896], BF16))
            MBl = st.enter_context(nc.sbuf_tensor(self._nm("pb_MBl"), [128, 896], BF16))
            kB = st.enter_context(nc.sbuf_tensor(self._nm("pb_kB"), [68, 2, L], BF16))
            vB = st.enter_context(nc.sbuf_tensor(self._nm("pb_vB"), [128, nblk, 128], BF16))
            qB = st.enter_context(nc.sbuf_tensor(self._nm("pb_qB"), [68, 2, 2, 3, 512], BF16))
            pt = st.enter_context(nc.sbuf_tensor(self._nm("pb_pt"), [128, 6, 512], BF16))
            acc = st.enter_context(nc.sbuf_tensor(self._nm("pb_acc"), [128, 2, 2, 512], F32))
            rden = st.enter_context(nc.sbuf_tensor(self._nm("pb_rden"), [128, 2, 512], F32))
            tt = st.enter_context(nc.sbuf_tensor(self._nm("pb_t"), [128, 2, 512], F32))
            sq = st.enter_context(nc.sbuf_tensor(self._nm("pb_sq"), [128, 512], BF16))
            ot = st.enter_context(nc.sbuf_tensor(self._nm("pb_ot"), [128, 2, 512], BF16))
            S_ps = st.enter_context(nc.psum_tensor(self._nm("pb_S"), [128, 6, 512], F32))
            O_ps = st.enter_context(nc.psum_tensor(self._nm("pb_O"), [128, 2, 512], F32))
            den_ps = S_ps[:, 0:2, :]
            ph = Phase(nc)
            ph.op("pool", lambda e: e.memset(ones_f[:], 1.0), writes=["ones"])
            ph.op("pool", lambda e: e.memset(ones_b[:], 1.0), writes=["ones"])
            ph.op("pool", lambda e: e.memset(qB[:], 0.0), writes=["qinit"])
            for nm, tl in (("c_diagB", dB), ("c_MBh", MBh), ("c_MBl", MBl)):
                ph.op("sp", lambda e, nm=nm, tl=tl: e.dma_start(out=tl[:], in_=dr[nm][:, :]), writes=["consts"], dma="ld_c1")
            ph.op("sp", lambda e: e.dma_start(out=lamt[:], in_=dr["lam"][:, l * 256:(l + 1) * 256]), writes=["lamt"], dma="ld_c0")
            ph.op("sp", lambda e: e.dma_start(out=gs[:], in_=dr["subln"][:, l:l + 1], allow_slow_non_contiguous=True), writes=["gs"], dma="ld_c0")
            ph.op("dve", lambda e: e.tensor_tensor(out=lt[:, 0:64], in0=lamt[:, 0:64], in1=lamt[:, 64:128], op=ALU.mult),
                  reads=["lamt"], writes=["lt"])
            ph.op("dve", lambda e: e.tensor_tensor(out=lt[:, 64:128], in0=lamt[:, 128:192], in1=lamt[:, 192:256], op=ALU.mult),
                  reads=["lamt"], writes=["lt"])
            ph.op("dve", lambda e: e.reduce_sum(ls[:, 0:1], lt[:, 0:64], mybir.AxisListType.X), reads=["lt"], writes=["ls"])
            ph.op("dve", lambda e: e.reduce_sum(ls[:, 1:2], lt[:, 64:128], mybir.AxisListType.X), reads=["lt"], writes=["ls"])
            ph.op("act", lambda e: e.activation(out=ls[:, 0:2], in_=ls[:, 0:2], func=AF.Exp), reads=["ls"], writes=["ls"])
            ph.op("dve", lambda e: e.tensor_tensor(out=ls[:, 2:3], in0=ls[:, 1:2], in1=ls[:, 0:1], op=ALU.subtract),
                  reads=["ls"], writes=["ls"])
            ph.op("dve", lambda e: e.tensor_scalar(ls[:, 2:3], ls[:, 2:3], -lam_init, None, ALU.add),
                  reads=["ls"], writes=["ls"])
            ph.op("dve", lambda e: e.tensor_scalar(gs[:], gs[:], 1.0 - lam_init, None, ALU.mult),
                  reads=["gs"], writes=["gs"])
            HCH = min(4096, L)
            VCH = min(32, nblk)
            for h in range(4):
                for m in range(2):
                    r0 = QT_BK + h * 128 + m * 64
                    for c0 in range(0, L, HCH):
                        ph.op("sp", lambda e, m=m, r0=r0, c0=c0: e.dma_start(out=kB[0:64, m, c0:c0 + HCH],
                                                                         in_=QT[r0:r0 + 64, c0:c0 + HCH]),
                              writes=[("kB", m)], dma="ld_kB%d" % m)
                    ph.op("sp", lambda e, m=m, h=h: e.dma_start(out=kB[64:68, m, :], in_=dr["c_kaug"][h, :, 0:L]),
                          writes=[("kB", m)], dma="ld_kB%d" % m)
                for c0 in range(0, nblk, VCH):
                    ph.op("sp", lambda e, h=h, c0=c0: e.dma_start(
                        out=vB[:, c0:c0 + VCH, :],
                        in_=VT[c0 * 128:(c0 + VCH) * 128, VT_BV + h * 128:VT_BV + (h + 1) * 128].rearrange(
                            "(u p) n -> p u n", p=128)),
                        writes=["vB"], dma="ld_vB")
                for c in range(nt):
                    b = c % 2
                    a = c * 512
                    for m in range(2):
                        r0 = QT_BQ + h * 128 + m * 64
                        for var in range(3):
                            ph.op("sp", lambda e, b=b, m=m, var=var, r0=r0, a=a: e.dma_start(
                                out=qB[0:64, b, m, var, :], in_=QT[r0:r0 + 64, a:a + 512]),
                                reads=["qinit"], writes=[("qB", b, m)], dma="ld_qB%d" % b)
                        for var in range(2):
                            ph.op("sp", lambda e, b=b, m=m, var=var, h=h, a=a: e.dma_start(
                                out=qB[64:68, b, m, var, :], in_=dr["c_qaug"][h, var, :, a:a + 512]),
                                reads=["qinit"], writes=[("qB", b, m)], dma="ld_qB%d" % b)
                    pairs = []
                    slope = 2.0 ** (-2.0 * (h + 1))
                    for kb in range(nblk):
                        k0 = kb * 128
                        dmin = max(0, k0 - (a + 511), a - (k0 + 127))
                        if slope * dmin >= B_SKIP:
                            continue
                        if kb < 4 * c:
                            var, masks = 0, []
                        elif kb > 4 * c + 3:
                            var, masks = 1, []
                        else:
                            u = kb - 4 * c
                            off = 384 - 128 * u
                            var = 2
                            masks = [(dB[:, h * 128:(h + 1) * 128], MBh[:, off:off + 512]),
                                     (dB[:, h * 128:(h + 1) * 128], MBl[:, off:off + 512])]
                        pairs.append(dict(q=[qB[:, b, 0, var, :], qB[:, b, 1, var, :]],
                                          k=[kB[:, 0, k0:k0 + 128], kB[:, 1, k0:k0 + 128]],
                                          v=vB[:, kb, :], masks=masks,
                                          reads=[("kB", 0), ("kB", 1), "vB", ("qB", b, 0), ("qB", b, 1), "consts"]))
                    pe_used = self._attend_b(ph, pairs, S_ps, O_ps, pt, acc, 3)
                    for m in range(2):
                        self._finalize_den(ph, acc[:, 0, m, :], ("acc", 0), ones_f[:], den_ps[:, m, :], ("S", 0),
                                           rden[:, m, :], ("rden", m), 128, start=(not pe_used))
                        ph.op("dve", lambda e, m=m: e.tensor_tensor(out=tt[:, m, :], in0=O_ps[:, m, :], in1=rden[:, m, :],
                                                                    op=ALU.mult),
                              reads=[("O", m), ("rden", m)], writes=[("tt", m)])
                    ph.op("dve", lambda e: e.scalar_tensor_tensor(tt[:, 0, :], tt[:, 1, :], ls[:, 2:3], tt[:, 0, :],
                                                                  ALU.mult, ALU.add),
                          reads=[("tt", 0), ("tt", 1), "ls"], writes=[("tt", 0)])
                    ph.op("act", lambda e: e.activation(out=sq[:], in_=tt[:, 0, :], func=AF.Square),
                          reads=[("tt", 0)], writes=["sq"])
                    ph.op("pe", lambda e: e.matmul(den_ps[:, 0, :], ones_b[:], sq[:], start=True, stop=True),
                          reads=["sq", "ones"], writes=[("S", 0)])
                    ph.op("dve", lambda e: e.tensor_scalar(rden[:, 0, :], den_ps[:, 0, :], 1.0 / 128, SUBLN_EPS,
                                                           ALU.mult, ALU.add),
                          reads=[("S", 0)], writes=[("rden", 0)])
                    ph.op("act", lambda e: e.activation(out=rden[:, 0, :], in_=rden[:, 0, :], func=AF.Sqrt),
                          reads=[("rden", 0)], writes=[("rden", 0)])
                    ph.op("dve", lambda e: e.reciprocal(rden[:, 0, :], rden[:, 0, :]),
                          reads=[("rden", 0)], writes=[("rden", 0)])
                    ph.op("dve", lambda e, b=b: e.scalar_tensor_tensor(ot[:, b, :], tt[:, 0, :], gs[:, 0:1], rden[:, 0, :],
                                                                       ALU.mult, ALU.mult),
                          reads=[("tt", 0), ("rden", 0), "gs"], writes=[("ot", b)])
                    ph.op("pool", lambda e, b=b, a=a, h=h: e.dma_start(
                        out=OT[256 + h * 128:256 + (h + 1) * 128, a:a + 512], in_=ot[:, b, :]),
                        reads=[("ot", b)], dma="st_oB%d" % b)
            ph.emit()

    def phase_attn_b_own(self, l, s, L):
        nc, dr = self.nc, self.dr
        QT = self.QTv(s)
        VT = self.VTv(s)
        QTo, VTo, OT = dr["QT_o"], dr["VT_o"], dr["OT_o"]
        nt = self.LQ // 512
        kshift = KM - QM
        nblk = L // 128
        lam_init = 0.8 - 0.6 * float(np.exp(-0.3 * l))
        with contextlib.ExitStack() as st:
            def sb(name, shape, dt):
                return st.enter_context(nc.sbuf_tensor(self._nm(name), shape, dt))
            ones_f = sb("po_onesf", [128, 128], F32)
            ones_b = sb("po_onesb", [128, 128], BF16)
            lamt = sb("po_lam", [128, 4 * 64], F32)
            lt = sb("po_lt", [128, 2 * 64], F32)
            ls = sb("po_ls", [128, 4], F32)
            gs = sb("po_gs", [128, 1], F32)
            dB = sb("po_dB", [128, 4 * 128], BF16)
            MBh = sb("po_MBh", [128, 896], BF16)
            MBl = sb("po_MBl", [128, 896], BF16)
            kB = sb("po_kB", [68, 2, L], BF16)
            vB = sb("po_vB", [128, nblk, 128], BF16)
            kN = sb("po_kN", [68, 2, 2, 512], BF16)
            vN = sb("po_vN", [128, 2, 4, 128], BF16)
            qB = sb("po_qB", [68, 2, 2, 2, 512], BF16)
            pt = sb("po_pt", [128, 6, 512], BF16)
            acc = sb("po_acc", [128, 2, 2, 512], F32)
            rden = sb("po_rden", [128, 2, 512], F32)
            tt = sb("po_t", [128, 2, 512], F32)
            sq = sb("po_sq", [128, 512], BF16)
            ot = sb("po_ot", [128, 2, 512], BF16)
            S_ps = st.enter_context(nc.psum_tensor(self._nm("po_S"), [128, 6, 512], F32))
            O_ps = st.enter_context(nc.psum_tensor(self._nm("po_O"), [128, 2, 512], F32))
            den_ps = S_ps[:, 0:2, :]
            ph = Phase(nc)
            ph.op("pool", lambda e: e.memset(ones_f[:], 1.0), writes=["ones"])
            ph.op("pool", lambda e: e.memset(ones_b[:], 1.0), writes=["ones"])
            ph.op("pool", lambda e: e.memset(qB[:], 0.0), writes=["qinit"])
            ph.op("pool", lambda e: e.memset(kN[:], 0.0), writes=["qinit"])
            for nm, tl in (("c_diagB", dB), ("c_MBh", MBh), ("c_MBl", MBl)):
                ph.op("sp", lambda e, nm=nm, tl=tl: e.dma_start(out=tl[:], in_=dr[nm][:, :]), writes=["consts"], dma="ld_c1")
            ph.op("sp", lambda e: e.dma_start(out=lamt[:], in_=dr["lam"][:, l * 256:(l + 1) * 256]), writes=["lamt"], dma="ld_c0")
            ph.op("sp", lambda e: e.dma_start(out=gs[:], in_=dr["subln"][:, l:l + 1], allow_slow_non_contiguous=True),
                  writes=["gs"], dma="ld_c0")
            ph.op("dve", lambda e: e.tensor_tensor(out=lt[:, 0:64], in0=lamt[:, 0:64], in1=lamt[:, 64:128], op=ALU.mult),
                  reads=["lamt"], writes=["lt"])
            ph.op("dve", lambda e: e.tensor_tensor(out=lt[:, 64:128], in0=lamt[:, 128:192], in1=lamt[:, 192:256], op=ALU.mult),
                  reads=["lamt"], writes=["lt"])
            ph.op("dve", lambda e: e.reduce_sum(ls[:, 0:1], lt[:, 0:64], mybir.AxisListType.X), reads=["lt"], writes=["ls"])
            ph.op("dve", lambda e: e.reduce_sum(ls[:, 1:2], lt[:, 64:128], mybir.AxisListType.X), reads=["lt"], writes=["ls"])
            ph.op("act", lambda e: e.activation(out=ls[:, 0:2], in_=ls[:, 0:2], func=AF.Exp), reads=["ls"], writes=["ls"])
            ph.op("dve", lambda e: e.tensor_tensor(out=ls[:, 2:3], in0=ls[:, 1:2], in1=ls[:, 0:1], op=ALU.subtract),
                  reads=["ls"], writes=["ls"])
            ph.op("dve", lambda e: e.tensor_scalar(ls[:, 2:3], ls[:, 2:3], -lam_init, None, ALU.add),
                  reads=["ls"], writes=["ls"])
            ph.op("dve", lambda e: e.tensor_scalar(gs[:], gs[:], 1.0 - lam_init, None, ALU.mult),
                  reads=["gs"], writes=["gs"])
            HCH = min(4096, L)
            VCH = min(32, nblk)
            for h in range(4):
                for m in range(2):
                    r0 = QT_BK + h * 128 + m * 64
                    for c0 in range(0, L, HCH):
                        ph.op("sp", lambda e, m=m, r0=r0, c0=c0: e.dma_start(out=kB[0:64, m, c0:c0 + HCH],
                                                                         in_=QT[r0:r0 + 64, c0:c0 + HCH]),
                              writes=[("kB", m)], dma="ld_kB%d" % m)
                for c0 in range(0, nblk, VCH):
                    ph.op("sp", lambda e, h=h, c0=c0: e.dma_start(
                        out=vB[:, c0:c0 + VCH, :],
                        in_=VT[c0 * 128:(c0 + VCH) * 128, VT_BV + h * 128:VT_BV + (h + 1) * 128].rearrange(
                            "(u p) n -> p u n", p=128)),
                        writes=["vB"], dma="ld_vB")
                for c in range(nt):
                    b = c % 2
                    ak = c * 512 + kshift
                    ao = c * 512
                    for m in range(2):
                        ph.op("sp", lambda e, m=m, h=h, c=c: e.dma_start(out=kB[64:68, m, :],
                                                                       in_=dr["c_kaugL"][c, h, :, 0:L]),
                              writes=[("kBa", m)], dma="ld_kBa%d" % m)
                        r0 = QT_BQ + h * 128 + m * 64
                        for var in range(2):
                            ph.op("sp", lambda e, b=b, m=m, var=var, r0=r0, ak=ak: e.dma_start(
                                out=qB[0:64, b, m, var, :], in_=QTo[r0:r0 + 64, ak:ak + 512]),
                                reads=["qinit"], writes=[("qB", b, m)], dma="ld_qB%d" % b)
                        ph.op("sp", lambda e, b=b, m=m, h=h, ao=ao: e.dma_start(
                            out=qB[64:68, b, m, 0, :], in_=dr["c_qaugL"][h, :, ao:ao + 512]),
                            reads=["qinit"], writes=[("qB", b, m)], dma="ld_qB%d" % b)
                        r1 = QT_BK + h * 128 + m * 64
                        ph.op("sp", lambda e, b=b, m=m, r1=r1, ak=ak: e.dma_start(
                            out=kN[0:64, b, m, :], in_=QTo[r1:r1 + 64, ak:ak + 512]),
                            reads=["qinit"], writes=[("kN", b)], dma="ld_kN%d" % b)
                    ph.op("sp", lambda e, b=b, h=h, ak=ak: e.dma_start(
                        out=vN[:, b, :, :],
                        in_=VTo[ak:ak + 512, VT_BV + h * 128:VT_BV + (h + 1) * 128].rearrange("(u p) n -> p u n", p=128)),
                        writes=[("vN", b)], dma="ld_vN%d" % b)
                    pairs = []
                    for kb in range(nblk):
                        k0 = kb * 128
                        pairs.append(dict(q=[qB[:, b, 0, 0, :], qB[:, b, 1, 0, :]],
                                          k=[kB[:, 0, k0:k0 + 128], kB[:, 1, k0:k0 + 128]],
                                          v=vB[:, kb, :], masks=[],
                                          reads=[("kB", 0), ("kB", 1), ("kBa", 0), ("kBa", 1), "vB", ("qB", b, 0),
                                                 ("qB", b, 1)]))
                    for u in range(4):
                        off = 384 - 128 * u
                        masks = [(dB[:, h * 128:(h + 1) * 128], MBh[:, off:off + 512]),
                                 (dB[:, h * 128:(h + 1) * 128], MBl[:, off:off + 512])]
                        pairs.append(dict(q=[qB[:, b, 0, 1, :], qB[:, b, 1, 1, :]],
                                          k=[kN[:, b, 0, u * 128:(u + 1) * 128], kN[:, b, 1, u * 128:(u + 1) * 128]],
                                          v=vN[:, b, u, :], masks=masks,
                                          reads=[("kN", b), ("vN", b), ("qB", b, 0), ("qB", b, 1), "consts"]))
                    pe_used = self._attend_b(ph, pairs, S_ps, O_ps, pt, acc, 3)
                    for m in range(2):
                        self._finalize_den(ph, acc[:, 0, m, :], ("acc", 0), ones_f[:], den_ps[:, m, :], ("S", 0),
                                           rden[:, m, :], ("rden", m), 128, start=(not pe_used))
                        ph.op("dve", lambda e, m=m: e.tensor_tensor(out=tt[:, m, :], in0=O_ps[:, m, :], in1=rden[:, m, :],
                                                                    op=ALU.mult),
                              reads=[("O", m), ("rden", m)], writes=[("tt", m)])
                    ph.op("dve", lambda e: e.scalar_tensor_tensor(tt[:, 0, :], tt[:, 1, :], ls[:, 2:3], tt[:, 0, :],
                                                                  ALU.mult, ALU.add),
                          reads=[("tt", 0), ("tt", 1), "ls"], writes=[("tt", 0)])
                    ph.op("act", lambda e: e.activation(out=sq[:], in_=tt[:, 0, :], func=AF.Square),
                          reads=[("tt", 0)], writes=["sq"])
                    ph.op("pe", lambda e: e.matmul(den_ps[:, 0, :], ones_b[:], sq[:], start=True, stop=True),
                          reads=["sq", "ones"], writes=[("S", 0)])
                    ph.op("dve", lambda e: e.tensor_scalar(rden[:, 0, :], den_ps[:, 0, :], 1.0 / 128, SUBLN_EPS,
                                                           ALU.mult, ALU.add),
                          reads=[("S", 0)], writes=[("rden", 0)])
                    ph.op("act", lambda e: e.activation(out=rden[:, 0, :], in_=rden[:, 0, :], func=AF.Sqrt),
                          reads=[("rden", 0)], writes=[("rden", 0)])
                    ph.op("dve", lambda e: e.reciprocal(rden[:, 0, :], rden[:, 0, :]),
                          reads=[("rden", 0)], writes=[("rden", 0)])
                    ph.op("dve", lambda e, b=b: e.scalar_tensor_tensor(ot[:, b, :], tt[:, 0, :], gs[:, 0:1], rden[:, 0, :],
                                                                       ALU.mult, ALU.mult),
                          reads=[("tt", 0), ("rden", 0), "gs"], writes=[("ot", b)])
                    ph.op("pool", lambda e, b=b, ao=ao, h=h: e.dma_start(
                        out=OT[256 + h * 128:256 + (h + 1) * 128, ao:ao + 512], in_=ot[:, b, :]),
                        reads=[("ot", b)], dma="st_oB%d" % b)
            ph.emit()

    def phase_wout(self, l, jobs):
        nc, dr = self.nc, self.dr
        with contextlib.ExitStack() as st:
            w = st.enter_context(nc.sbuf_tensor(self._nm("pw_w"), [128, 8, D], BF16))
            xt = st.enter_context(nc.sbuf_tensor(self._nm("pw_xt"), [128, 2, 8, 512], F32))
            ot = st.enter_context(nc.sbuf_tensor(self._nm("pw_ot"), [128, 2, 8, 512], BF16))
            ps = st.enter_context(nc.psum_tensor(self._nm("pw_ps"), [128, 4, 512], F32))
            ph = Phase(nc)
            wsrc = dr["w_out"][l].rearrange("(c p) n -> p c n", p=128)
            for c in range(8):
                ph.op("pool", lambda e, c=c: e.dma_start(out=w[:, c, :], in_=wsrc[:, c, :]), writes=[("w", c)],
                      dma="ld_w%d" % (c % 4))
            wres = [("w", c) for c in range(8)]
            k = 0
            for (s, L, own) in jobs:
                if own:
                    XT = dr["XL_o"].rearrange("(c p) t -> p c t", p=128)
                    OT = dr["OT_o"].rearrange("(c p) t -> p c t", p=128)
                    nt = self.LQ // 512
                else:
                    XT = self.X(l, s).rearrange("(c p) t -> p c t", p=128)
                    OT = dr["OT_" + s].rearrange("(c p) t -> p c t", p=128)
                    nt = L // 512
                for t in range(nt):
                    b = t % 2
                    ph.op("sp", lambda e, b=b, t=t, XT=XT: e.dma_start(out=xt[:, b], in_=XT[:, :, t * 512:(t + 1) * 512]),
                          writes=[("xt", b, c) for c in range(8)], dma="ld_xt%d" % b)
                    ph.op("sp", lambda e, b=b, t=t, OT=OT: e.dma_start(out=ot[:, b], in_=OT[:, :, t * 512:(t + 1) * 512]),
                          writes=[("ot", b)], dma="ld_ot%d" % b)
                    for oc in range(8):
                        pb = k % 4
                        k += 1
                        for c in range(8):
                            ph.op("pe", lambda e, pb=pb, c=c, oc=oc, b=b: e.matmul(
                                ps[:, pb, :], w[:, c, oc * 128:(oc + 1) * 128], ot[:, b, c, :], start=(c == 0), stop=(c == 7)),
                                reads=[("ot", b)] + wres if c == 0 else [], writes=[("ps", pb)])
                        ph.op("dve", lambda e, pb=pb, b=b, oc=oc: e.tensor_tensor(out=xt[:, b, oc, :], in0=ps[:, pb, :],
                                                                              in1=xt[:, b, oc, :], op=ALU.add),
                              reads=[("ps", pb)], writes=[("xt", b, oc)])
                    ph.op("pool", lambda e, b=b, t=t, XT=XT: e.dma_start(out=XT[:, :, t * 512:(t + 1) * 512], in_=xt[:, b]),
                          reads=[("xt", b, c) for c in range(8)], dma="st_xt%d" % b)
            ph.emit()

    def phase_mlp(self, l, jobs):
        nc, dr = self.nc, self.dr
        NT = 256
        NC = NT + 2
        any_own = any(j[2] for j in jobs)
        with contextlib.ExitStack() as st:
            wu = st.enter_context(nc.sbuf_tensor(self._nm("pm_wu"), [128, 8, 2 * D_FF], BF16))
            wd = st.enter_context(nc.sbuf_tensor(self._nm("pm_wd"), [128, 22, D], BF16))
            g = st.enter_context(nc.sbuf_tensor(self._nm("pm_g"), [128, 8], F32))
            cw = st.enter_context(nc.sbuf_tensor(self._nm("pm_cw"), [128, 3 * 44], F32))
            cb = st.enter_context(nc.sbuf_tensor(self._nm("pm_cb"), [128, 44], F32))
            ones = st.enter_context(nc.sbuf_tensor(self._nm("pm_ones"), [128, 128], BF16))
            xt = st.enter_context(nc.sbuf_tensor(self._nm("pm_xt"), [128, 2, 8, NC], F32))
            sq = st.enter_context(nc.sbuf_tensor(self._nm("pm_sq"), [128, 8, NC], BF16))
            r = st.enter_context(nc.sbuf_tensor(self._nm("pm_r"), [128, NC], F32))
            h = st.enter_context(nc.sbuf_tensor(self._nm("pm_h"), [128, 8, NC], BF16))
            tmp = st.enter_context(nc.sbuf_tensor(self._nm("pm_tmp"), [128, 2, 2, NT], F32))
            gt = st.enter_context(nc.sbuf_tensor(self._nm("pm_gt"), [128, 22, NT], BF16))
            xo = st.enter_context(nc.sbuf_tensor(self._nm("pm_xo"), [128, 2, 8, NT], F32))
            ss_ps = st.enter_context(nc.psum_tensor(self._nm("pm_ss"), [128, 512], F32))
            u_ps = st.enter_context(nc.psum_tensor(self._nm("pm_u"), [128, 4, 512], F32))
            o_ps = st.enter_context(nc.psum_tensor(self._nm("pm_o"), [128, 2, 512], F32))
            ph = Phase(nc)
            ph.op("pool", lambda e: e.memset(ones[:], 1.0), writes=["ones"])
            ph.op("sp", lambda e: e.dma_start(out=g[:], in_=dr["ln2"][:, l * 8:(l + 1) * 8]), writes=["params"], dma="ld_g")
            ph.op("sp", lambda e: e.dma_start(out=cw[:], in_=dr["convw"][:, l * 132:(l + 1) * 132]), writes=["params"], dma="ld_g")
            ph.op("sp", lambda e: e.dma_start(out=cb[:], in_=dr["convb"][:, l * 44:(l + 1) * 44]), writes=["params"], dma="ld_g")
            flag = st.enter_context(nc.sbuf_tensor(self._nm("pm_flag"), [128, 2], F32))
            if any_own:
                ph.op("sp", lambda e: e.dma_start(out=flag[:], in_=dr["c_flag"][:, :]), writes=["params"], dma="ld_g")
            wsrc = dr["w_up"][l].rearrange("(c p) n -> p c n", p=128)
            for c in range(8):
                for n0 in range(0, 2 * D_FF, 1408):
                    ph.op("pool", lambda e, c=c, n0=n0: e.dma_start(out=wu[:, c, n0:n0 + 1408], in_=wsrc[:, c, n0:n0 + 1408]),
                          writes=[("wu", c, n0)], dma="ld_w%d" % (c % 4))
            wures = [("wu", c, n0) for c in range(8) for n0 in range(0, 2 * D_FF, 1408)]
            wdsrc = dr["w_down"][l].rearrange("(c p) n -> p c n", p=128)
            for c in range(22):
                ph.op("pool", lambda e, c=c: e.dma_start(out=wd[:, c, :], in_=wdsrc[:, c, :]), writes=[("wd", c)],
                      dma="ld_w%d" % (c % 4))
            wdres = [("wd", c) for c in range(22)]
            uk = 0
            ok = 0
            for (s, L, own) in jobs:
                if own:
                    XT = dr["XL_o"].rearrange("(c p) t -> p c t", p=128)
                    XO = dr["XW_o"].rearrange("(c p) t -> p c t", p=128)
                    nt = self.shard // NT
                else:
                    XT = self.X(l, s).rearrange("(c p) t -> p c t", p=128)
                    XO = self.X(l + 1, s).rearrange("(c p) t -> p c t", p=128)
                    nt = L // NT
                cb0 = QM if own else 0
                for t in range(nt):
                    b = t % 2
                    t0 = t * NT
                    lo = 1 if (t == 0 and not own) else 0
                    hi = NC - 1 if (t == nt - 1 and not own) else NC
                    if lo:
                        ph.op("pool", lambda e, b=b: e.memset(xt[:, b, :, 0:1], 0.0), writes=[("xt", b)])
                    if hi != NC:
                        ph.op("pool", lambda e, b=b: e.memset(xt[:, b, :, NC - 1:NC], 0.0), writes=[("xt", b)])
                    ph.op("sp", lambda e, b=b, t0=t0, lo=lo, hi=hi, XT=XT, cb0=cb0: e.dma_start(out=xt[:, b, :, lo:hi],
                                                                           in_=XT[:, :, cb0 + t0 - 1 + lo:cb0 + t0 - 1 + hi]),
                          writes=[("xt", b)] if not (lo or hi != NC) else [("xt", b), ("xtedge", b)], dma="ld_xt%d" % b)
                    if own and t == 0:
                        ph.op("dve", lambda e, b=b: e.tensor_scalar(xt[:, b, :, 0:1], xt[:, b, :, 0:1], flag[:, 0:1], None,
                                                                    ALU.mult), reads=[("xt", b), "params"], writes=[("xt", b)])
                    if own and t == nt - 1:
                        ph.op("dve", lambda e, b=b: e.tensor_scalar(xt[:, b, :, NC - 1:NC], xt[:, b, :, NC - 1:NC],
                                                                    flag[:, 1:2], None, ALU.mult),
                              reads=[("xt", b), "params"], writes=[("xt", b)])
                    self._rmsnorm_tile(ph, xt[:, b], ("xt", b), NC, sq[:], ones[:], ss_ps[:, 0:NC], "ss", r[:], "r",
                                       lambda c: h[:, c, :], lambda c: ("h", c), lambda c: g[:, c:c + 1], D, EPS)
                    hres = [("h", c) for c in range(8)]
                    for p in range(22):
                        for part in range(2):
                            j = p + 22 * part
                            ub = uk % 4
                            uk += 1
                            for c in range(8):
                                ph.op("pe", lambda e, ub=ub, c=c, j=j: e.matmul(
                                    u_ps[:, ub, 0:NC], wu[:, c, j * 128:(j + 1) * 128], h[:, c, :], start=(c == 0), stop=(c == 7)),
                                    reads=hres + wures if c == 0 else [], writes=[("u", ub)])
                            tb = p % 2
                            tm = tmp[:, tb, part, :]
                            tres = ("tmp", tb, part)
                            ph.op("act", lambda e, tm=tm, ub=ub, j=j: e.activation(
                                out=tm, in_=u_ps[:, ub, 1:NT + 1], func=AF.Identity, bias=cb[:, j:j + 1],
                                scale=cw[:, 44 + j:45 + j]), reads=[("u", ub), "params"], writes=[tres])
                            ph.op("dve", lambda e, tm=tm, ub=ub, j=j: e.scalar_tensor_tensor(
                                tm, u_ps[:, ub, 0:NT], cw[:, j:j + 1], tm, ALU.mult, ALU.add),
                                reads=[("u", ub), "params"], writes=[tres])
                            ph.op("dve", lambda e, tm=tm, ub=ub, j=j: e.scalar_tensor_tensor(
                                tm, u_ps[:, ub, 2:NT + 2], cw[:, 88 + j:89 + j], tm, ALU.mult, ALU.add),
                                reads=[("u", ub), "params"], writes=[tres])
                        tb = p % 2
                        ph.op("act", lambda e, tb=tb: e.activation(out=tmp[:, tb, 0, :], in_=tmp[:, tb, 0, :], func=AF.Silu),
                              reads=[("tmp", tb, 0)], writes=[("tmp", tb, 0)])
                        ph.op("pool", lambda e, tb=tb, p=p: e.tensor_tensor(out=gt[:, p, :], in0=tmp[:, tb, 0, :],
                                                                          in1=tmp[:, tb, 1, :], op=ALU.mult),
                              reads=[("tmp", tb, 0), ("tmp", tb, 1)], writes=[("gt", p)])
                    gres = [("gt", p) for p in range(22)]
                    for oc in range(8):
                        ob = ok % 2
                        ok += 1
                        for c in range(22):
                            ph.op("pe", lambda e, ob=ob, c=c, oc=oc: e.matmul(
                                o_ps[:, ob, 0:NT], wd[:, c, oc * 128:(oc + 1) * 128], gt[:, c, :], start=(c == 0), stop=(c == 21)),
                                reads=gres + wdres if c == 0 else [], writes=[("o", ob)])
                        ph.op("dve", lambda e, ob=ob, b=b, oc=oc: e.tensor_tensor(out=xo[:, b, oc, :], in0=o_ps[:, ob, 0:NT],
                                                                              in1=xt[:, b, oc, 1:NT + 1], op=ALU.add),
                              reads=[("o", ob), ("xt", b)], writes=[("xo", b, oc)])
                    ph.op("pool", lambda e, b=b, t0=t0, XO=XO: e.dma_start(out=XO[:, :, t0:t0 + NT], in_=xo[:, b]),
                          reads=[("xo", b, c) for c in range(8)], dma="st_xo%d" % b)
            ph.emit()

    def phase_final(self, s, L, own=False):
        nc, dr = self.nc, self.dr
        if own:
            XT = dr["XW_o"].rearrange("(c p) t -> p c t", p=128)
            y = dr["y_o"]
            nt = self.shard // 512
        else:
            XT = self.X(self.depth, s).rearrange("(c p) t -> p c t", p=128)
            y = dr["y_" + s]
            nt = L // 512
        with contextlib.ExitStack() as st:
            ident = st.enter_context(nc.sbuf_tensor(self._nm("pf_id"), [128, 128], F32))
            g = st.enter_context(nc.sbuf_tensor(self._nm("pf_g"), [128, 8], F32))
            ones = st.enter_context(nc.sbuf_tensor(self._nm("pf_ones"), [128, 128], BF16))
            xt = st.enter_context(nc.sbuf_tensor(self._nm("pf_xt"), [128, 2, 8, 512], F32))
            sq = st.enter_context(nc.sbuf_tensor(self._nm("pf_sq"), [128, 8, 512], BF16))
            r = st.enter_context(nc.sbuf_tensor(self._nm("pf_r"), [128, 512], F32))
            h = st.enter_context(nc.sbuf_tensor(self._nm("pf_h"), [128, 8, 512], F32))
            yo = st.enter_context(nc.sbuf_tensor(self._nm("pf_yo"), [128, 2, 4, D], F32))
            ss_ps = st.enter_context(nc.psum_tensor(self._nm("pf_ss"), [128, 512], F32))
            ps = st.enter_context(nc.psum_tensor(self._nm("pf_ps"), [128, 3, 2, 512], F32))
            ph = Phase(nc)
            ph.op("pool", lambda e: e.memset(ones[:], 1.0), writes=["ones"])
            ph.op("sp", lambda e: e.dma_start(out=ident[:], in_=dr["c_ident"][:, :]), writes=["ident"], dma="ld_g")
            ph.op("sp", lambda e: e.dma_start(out=g[:], in_=dr["lnf"][:, :]), writes=["params"], dma="ld_g")
            k = 0
            for t in range(nt):
                b = t % 2
                ph.op("sp", lambda e, b=b, t=t: e.dma_start(out=xt[:, b], in_=XT[:, :, t * 512:(t + 1) * 512]),
                      writes=[("xt", b)], dma="ld_xt%d" % b)
                self._rmsnorm_tile(ph, xt[:, b], ("xt", b), 512, sq[:], ones[:], ss_ps[:], "ss", r[:], "r",
                                   lambda c: h[:, c, :], lambda c: ("h", c), lambda c: g[:, c:c + 1], D, EPS)
                for sub in range(4):
                    pb = k % 3
                    k += 1
                    for c in range(8):
                        ph.op("pe", lambda e, pb=pb, c=c, sub=sub: e.transpose(
                            ps[:, pb, c // 4, (c % 4) * 128:(c % 4 + 1) * 128], h[:, c, sub * 128:(sub + 1) * 128], ident[:]),
                            reads=[("h", c), "ident"], writes=[("ps", pb)])
                    if sub % 2 == 0:
                        ph.op("act", lambda e, pb=pb, b=b, sub=sub: e.copy(
                            out=yo[:, b, sub, :], in_=ps[:, pb].rearrange("p a n -> p (a n)")),
                            reads=[("ps", pb)], writes=[("yo", b, sub)])
                    else:
                        ph.op("dve", lambda e, pb=pb, b=b, sub=sub: e.tensor_copy(
                            out=yo[:, b, sub, :], in_=ps[:, pb].rearrange("p a n -> p (a n)")),
                            reads=[("ps", pb)], writes=[("yo", b, sub)])
                dst = y[t * 512:(t + 1) * 512, :].rearrange("(s p) f -> p s f", p=128)
                ph.op("pool", lambda e, b=b, dst=dst: e.dma_start(out=dst, in_=yo[:, b]),
                      reads=[("yo", b, sub) for sub in range(4)], dma="st_y%d" % b)
            ph.emit()

    def _nm(self, base):
        self._cnt = getattr(self, "_cnt", 0) + 1
        return "%s_%d" % (base, self._cnt)

    def build(self):
        import os
        lim = int(os.environ.get("KPH", "1000"))
        plist = []
        last = self.depth - 1
        for (s, L) in self.seqs:
            if s == "p":
                plist.append(lambda s=s, L=L: self.phase_init_pads(s, L, last))
            plist.append(lambda s=s, L=L: self.phase_transpose_in(s, L))
        for l in range(self.depth):
            plist.append(lambda l=l: self.phase_proj(l, [(s, L) for (s, L) in self.seqs]))
            for (s, L) in self.seqs:
                own = (s == "p") and l == last
                if own:
                    plist.append(lambda s=s, L=L, l=l: self.phase_localize(s, L, l))
                    plist.append(lambda s=s, L=L, l=l: self.phase_attn_ac(l, s, L, own=True))
                    plist.append(lambda s=s, L=L, l=l: self.phase_attn_b_own(l, s, L))
                else:
                    plist.append(lambda s=s, L=L, l=l: self.phase_attn_ac(l, s, L))
                    plist.append(lambda s=s, L=L, l=l: self.phase_attn_b(l, s, L))
            jobs = [(s, L, (s == "p") and l == last) for (s, L) in self.seqs]
            plist.append(lambda l=l, jobs=jobs: self.phase_wout(l, jobs))
            plist.append(lambda l=l, jobs=jobs: self.phase_mlp(l, jobs))
        for (s, L) in self.seqs:
            plist.append(lambda s=s, L=L: self.phase_final(s, L, own=(s == "p")))
        for f in plist[:lim]:
            f()
        return self.nc


def _toeplitz(width, X, f):
    kk = np.arange(128)[:, None]
    col = np.arange(width)[None, :]
    return f(col - kk - X)


def make_consts(Lmax):
    c = {}
    c["c_ident"] = np.eye(128, dtype=np.float32)
    slA = 2.0 ** (-8.0 * np.arange(1, 5) / 4)
    slC = (2.0 ** (-8.0 * np.arange(1, 13) / 12)).astype(np.float32).astype(np.float64)
    eye = np.eye(128)
    c["c_diagA"] = _bf(np.concatenate([eye * (-8.0 * s) for s in slA], 1))
    c["c_diagB"] = _bf(np.concatenate([eye * (-8.0 * s) for s in slA], 1))
    dC = []
    for gh in range(12):
        v = -8.0 * slC[gh]
        hi = float(_bf(v).astype(np.float32))
        lo = float(_bf(v - hi).astype(np.float32))
        dC += [eye * hi, eye * lo]
    c["c_diagC"] = _bf(np.concatenate(dC, 1))

    def band(W, d):
        def f(delta):
            ad = np.abs(delta)
            ok = (ad <= W) & (delta % d == 0)
            return np.where(ok, ad, BIG).astype(np.float32)
        return f
    c["c_MA"] = _bf(_toeplitz(1152, 512, band(128, 1)))
    c["c_MC0"] = _bf(_toeplitz(1152, 128 * 5 - 128, band(64, 1)))
    c["c_MC1"] = _bf(_toeplitz(1408, 128 * 7 - 256, band(256, 4)))
    c["c_MC2"] = _bf(_toeplitz(2944, 128 * 19 - 1024, band(1024, 16)))
    c["c_MBh"] = _bf(_toeplitz(896, 384, lambda dl: (2 * (np.abs(dl) // 2)).astype(np.float32)))
    c["c_MBl"] = _bf(_toeplitz(896, 384, lambda dl: (np.abs(dl) % 2).astype(np.float32)))
    pos = np.arange(Lmax)
    hi = (pos // 128) * 128.0
    lo = (pos % 128) * 1.0
    kaug = np.zeros((4, 4, Lmax), np.float32)
    qaug = np.zeros((4, 2, 4, Lmax), np.float32)
    for h in range(4):
        s8 = 8.0 * slA[h]
        kaug[h] = np.stack([np.ones(Lmax), np.ones(Lmax), s8 * hi, s8 * lo])
        base = np.stack([-s8 * hi, -s8 * lo, np.ones(Lmax), np.ones(Lmax)])
        qaug[h, 0] = base
        qaug[h, 1] = -base
    c["c_kaug"] = _bf(kaug)
    c["c_qaug"] = _bf(qaug)
    return c


def make_core_consts(core, L, SH):
    LQ, LK = SH + 2 * QM, SH + 2 * KM
    tok0 = core * SH
    c = {}
    kstart = tok0 - KM + 128 * np.arange(LK // 128)
    valid = (kstart >= 0) & (kstart < L)
    c["c_kbias"] = np.ascontiguousarray(np.broadcast_to(np.where(valid, 0.0, KBIAS_NEG).astype(np.float32)[None, :],
                                                        (128, LK // 128)))
    slA = 2.0 ** (-8.0 * np.arange(1, 5) / 4)
    pos = np.arange(L)
    hi = (pos // 128) * 128.0
    lo = (pos % 128) * 1.0
    nch = LQ // 512
    kaug = np.zeros((nch, 4, 4, L), np.float32)
    for ci in range(nch):
        A = tok0 - QM + 512 * ci
        blk0 = (pos // 128) * 128
        left = blk0 + 127 < A
        right = blk0 > A + 511
        sign = np.where(left, 1.0, np.where(right, -1.0, 0.0))
        near = (sign == 0)
        for h in range(4):
            s8 = 8.0 * slA[h]
            rows = np.stack([np.ones(L), np.ones(L), s8 * hi, s8 * lo]) * sign[None, :]
            rows[2, near] = -262144.0
            kaug[ci, h] = rows
    c["c_kaugL"] = _bf(kaug)
    qpos = np.clip(tok0 - QM + np.arange(LQ), 0, L - 1)
    qhi = (qpos // 128) * 128.0
    qlo = (qpos % 128) * 1.0
    qaug = np.zeros((4, 4, LQ), np.float32)
    for h in range(4):
        s8 = 8.0 * slA[h]
        qaug[h] = np.stack([-s8 * qhi, -s8 * qlo, np.ones(LQ), np.ones(LQ)])
    c["c_qaugL"] = _bf(qaug)
    fl = np.array([0.0 if core == 0 else 1.0, 0.0 if (tok0 + SH) >= L else 1.0], np.float32)
    c["c_flag"] = np.ascontiguousarray(np.broadcast_to(fl[None, :], (128, 2)))
    return c


def layout_params(p, depth):
    out = {}

    def chunks(v, n):
        v = np.asarray(v, np.float32).reshape(-1, n, 128)
        return np.ascontiguousarray(v.transpose(2, 0, 1).reshape(128, -1))
    out["ln1"] = chunks(p["ln1"], 8)
    out["ln2"] = chunks(p["ln2"], 8)
    out["lnf"] = chunks(np.asarray(p["ln_f"])[None], 8)
    out["subln"] = np.ascontiguousarray(np.asarray(p["subln"], np.float32).T)
    out["sink"] = np.ascontiguousarray(np.broadcast_to(np.asarray(p["a_sink"], np.float32).reshape(1, -1), (128, depth * 4)))
    lam = np.stack([np.asarray(p[k], np.float32) for k in ("lam_q1", "lam_k1", "lam_q2", "lam_k2")], 1)
    out["lam"] = np.ascontiguousarray(np.broadcast_to(lam.reshape(1, -1), (128, depth * 256)))
    cw = np.asarray(p["conv_w"], np.float32).reshape(depth, 3, 44, 128)
    out["convw"] = np.ascontiguousarray(cw.transpose(3, 0, 1, 2).reshape(128, -1))
    cb = np.asarray(p["conv_b"], np.float32).reshape(depth, 44, 128)
    out["convb"] = np.ascontiguousarray(cb.transpose(2, 0, 1).reshape(128, -1))
    for k in ("w_in", "w_out", "w_up", "w_down"):
        out[k] = np.ascontiguousarray(np.asarray(p[k], np.float32))
    return out


_CACHE = {}


def run(seq_inputs, params, depth=DEPTH, n_cores=8):
    seqs = [(k, v.shape[0]) for k, v in seq_inputs[0].items()]
    key = (tuple(seqs), depth)
    if key not in _CACHE:
        _CACHE[key] = Builder(seqs, depth).build()
    nc = _CACHE[key]
    Lmax = max(L for _, L in seqs)
    shared = dict(make_consts(Lmax))
    shared.update(layout_params(params, depth))
    in_maps = []
    Lp = dict(seqs).get("p")
    for c in range(n_cores):
        m = dict(shared)
        if Lp:
            m.update(make_core_consts(c, Lp, Lp // 8))
        for k, v in seq_inputs[c].items():
            m["x_" + k] = np.ascontiguousarray(np.asarray(v, np.float32))
        in_maps.append(m)
    import os
    if os.environ.get("KTRACE"):
        res = run_bass_kernel_spmd(nc, in_maps, core_ids=list(range(n_cores)), trace=True)
        print("EXEC_TIME_NS", res.exec_time_ns, flush=True)
    else:
        res = run_bass_kernel_spmd(nc, in_maps, core_ids=list(range(n_cores)))
    return res.results


def kernel(x_prompt, x_sample, ln1, w_in, a_sink, lam_q1, lam_k1, lam_q2, lam_k2, subln,
           w_out, ln2, w_up, conv_w, conv_b, w_down, ln_f):
    params = dict(ln1=ln1, w_in=w_in, a_sink=a_sink, lam_q1=lam_q1, lam_k1=lam_k1, lam_q2=lam_q2, lam_k2=lam_k2,
                  subln=subln, w_out=w_out, ln2=ln2, w_up=w_up, conv_w=conv_w, conv_b=conv_b, w_down=w_down, ln_f=ln_f)
    x_prompt = np.asarray(x_prompt, np.float32)
    x_sample = np.asarray(x_sample, np.float32)
    seq_inputs = [{"s": x_sample[c], "p": x_prompt[0]} for c in range(8)]
    res = run(seq_inputs, params)
    y_sample = np.stack([res[c]["y_s"] for c in range(8)], 0)
    y_prompt = np.concatenate([res[c]["y_o"] for c in range(8)], 0)[None]
    return (y_prompt.astype(np.float32), y_sample.astype(np.float32))
```

```python
import contextlib
import numpy as np
import ml_dtypes
import concourse.bass as bass
import concourse.mybir as mybir
from concourse.bass_utils import run_bass_kernel_spmd

F32 = mybir.dt.float32
BF16 = mybir.dt.bfloat16
AF = mybir.ActivationFunctionType
ALU = mybir.AluOpType

D = 1024
DEPTH = 2
HD = 64
D_IN = 4352
D_FF = 2816
EPS = 1e-6
SUBLN_EPS = 1e-5
BIG = float(2 ** 20)
B_SKIP = 80.0
PE_FILL = 1
QM = 512
KM = 1536
KBIAS_NEG = -30000.0
SAME_ENG_SYNC = True

W_COLS_T = [(0, 256), (256, 128), (512, 512), (1024, 512), (2048, 768), (2816, 768)]
QT_AQ, QT_AK, QT_BQ, QT_BK, QT_CQ, QT_CK = 0, 256, 384, 896, 1408, 2176
QT_ROWS = 2944
W_COLS_V = [(384, 128), (1536, 512), (3584, 768)]
VT_AV, VT_BV, VT_CV = 0, 128, 640
VT_COLS = 1408


class _Op:
    __slots__ = ("eng", "fn", "deps", "dma_key", "needs_inc", "sem", "val", "idx")

    def __init__(self, eng, fn, dma_key):
        self.eng = eng
        self.fn = fn
        self.deps = []
        self.dma_key = dma_key
        self.needs_inc = dma_key is not None
        self.sem = None
        self.val = 0


class Phase:
    ENGS = ("pe", "act", "dve", "pool", "sp")

    def __init__(self, nc):
        self.nc = nc
        self.ops = {e: [] for e in self.ENGS}
        self.lastw = {}
        self.readers = {}
        self.last_dma = {}

    def pid(self, e):
        c = self.__dict__.setdefault("_pidc", {})
        if id(e) not in c:
            c[id(e)] = e.partition_id()
        return c[id(e)]

    def op(self, eng, fn, reads=(), writes=(), dma=None):
        o = _Op(eng, fn, dma)
        deps = []
        for r in reads:
            w = self.lastw.get(r)
            if w is not None:
                deps.append(w)
        for r in writes:
            w = self.lastw.get(r)
            if w is not None:
                deps.append(w)
            deps.extend(self.readers.get(r, ()))
        if dma is not None:
            prev = self.last_dma.get(dma)
            if prev is not None:
                deps.append(prev)
            self.last_dma[dma] = o
        seen = set()
        for d in deps:
            if d is o or id(d) in seen:
                continue
            seen.add(id(d))
            if d.dma_key is None and d.eng == eng and (eng == "pe" or not SAME_ENG_SYNC):
                continue
            o.deps.append(d)
            d.needs_inc = True
        for r in reads:
            self.readers.setdefault(r, []).append(o)
        for r in writes:
            self.lastw[r] = o
            self.readers[r] = []
        self.ops[eng].append(o)
        return o

    def emit(self):
        nc = self.nc
        reg = SEMREG
        LIM = 30000

        def assign(key, o, inc):
            ent = reg.get(key)
            if ent is None or ent[1] + inc > LIM:
                ent = [nc.alloc_semaphore("k%d" % len(reg.setdefault("_all", []))), 0]
                reg["_all"].append(ent[0])
                reg[key] = ent
            ent[1] += inc
            o.sem = ent[0]
            o.val = ent[1]

        for e in self.ENGS:
            for o in self.ops[e]:
                if o.dma_key is None:
                    if o.needs_inc:
                        assign(("eng", e), o, 1)
                else:
                    assign(("dma", o.dma_key), o, 16)
        final = {}
        for e in self.ENGS:
            for o in self.ops[e]:
                if o.dma_key is not None:
                    final[id(o.sem)] = (o.sem, o.val)
        with nc.Block() as block:
            def make(e):
                def body(eng):
                    waited = {}
                    for o in self.ops[e]:
                        for d in o.deps:
                            if waited.get(id(d.sem), 0) >= d.val:
                                continue
                            waited[id(d.sem)] = d.val
                            eng.wait_ge(d.sem, d.val)
                        ins = o.fn(eng)
                        if o.needs_inc:
                            ins.then_inc(o.sem, 16 if o.dma_key is not None else 1)
                    if e == "sp":
                        for k, (sm, v) in final.items():
                            if waited.get(k, 0) < v:
                                eng.wait_ge(sm, v)
                return body

            block.tensor(make("pe"))
            block.scalar(make("act"))
            block.vector(make("dve"))
            block.gpsimd(make("pool"))
            block.sync(make("sp"))


SEMREG = {}


def _bf(a):
    return np.asarray(a, np.float32).astype(ml_dtypes.bfloat16)


class Builder:
    def __init__(self, seqs, depth=DEPTH):
        self.seqs = seqs
        self.depth = depth
        self.nc = bass.Bass("TRN2", target_bir_lowering=False)
        nc = self.nc
        SEMREG.clear()
        self.dr = {}

        def din(name, shape, dt=F32):
            self.dr[name] = nc.dram_tensor(name, list(shape), dt, kind="ExternalInput").ap()

        def dout(name, shape):
            self.dr[name] = nc.dram_tensor(name, list(shape), F32, kind="ExternalOutput").ap()

        def dtmp(name, shape, dt):
            self.dr[name] = nc.dram_tensor(name, list(shape), dt).ap()

        self.shard = None
        for (s, L) in seqs:
            din("x_" + s, (L, D))
            if s == "p":
                self.shard = L // 8
            else:
                dout("y_" + s, (L, D))
            dtmp("XT_" + s, (D, L + 2 * QM), F32)
            dtmp("XU_" + s, (D, L + 2 * QM), F32)
            dtmp("QT_" + s, (QT_ROWS, L + 2 * KM), BF16)
            dtmp("VT_" + s, (L + 2 * KM, VT_COLS), BF16)
            dtmp("OT_" + s, (D, L), BF16)
        if self.shard:
            SH = self.shard
            self.LQ = SH + 2 * QM
            self.LK = SH + 2 * KM
            dout("y_o", (SH, D))
            dtmp("XL_o", (D, self.LQ), F32)
            dtmp("XW_o", (D, SH), F32)
            dtmp("QT_o", (QT_ROWS, self.LK), BF16)
            dtmp("VT_o", (self.LK, VT_COLS), BF16)
            dtmp("OT_o", (D, self.LQ), BF16)
        Lmax = max(L for _, L in seqs)
        self.Lmax = Lmax
        din("w_in", (depth, D, D_IN))
        din("w_out", (depth, D, D))
        din("w_up", (depth, D, 2 * D_FF))
        din("w_down", (depth, D_FF, D))
        din("ln1", (128, depth * 8))
        din("ln2", (128, depth * 8))
        din("lnf", (128, 8))
        din("subln", (128, depth))
        din("sink", (128, depth * 4))
        din("lam", (128, depth * 4 * 64))
        din("convw", (128, depth * 3 * 44))
        din("convb", (128, depth * 44))
        din("c_ident", (128, 128))
        din("c_diagA", (128, 4 * 128), BF16)
        din("c_diagB", (128, 4 * 128), BF16)
        din("c_diagC", (128, 24 * 128), BF16)
        din("c_MA", (128, 1152), BF16)
        din("c_MC0", (128, 1152), BF16)
        din("c_MC1", (128, 1408), BF16)
        din("c_MC2", (128, 2944), BF16)
        din("c_MBh", (128, 896), BF16)
        din("c_MBl", (128, 896), BF16)
        din("c_kaug", (4, 4, Lmax), BF16)
        din("c_qaug", (4, 2, 4, Lmax), BF16)
        if self.shard:
            din("c_kbias", (128, self.LK // 128))
            din("c_kaugL", (self.LQ // 512, 4, 4, Lmax), BF16)
            din("c_qaugL", (4, 4, self.LQ), BF16)
            din("c_flag", (128, 2))

    def X(self, which, s):
        t = self.dr[("XT_", "XU_")[which % 2] + s]
        return t[:, QM:t.shape[1] - QM]

    def QTv(self, s):
        t = self.dr["QT_" + s]
        return t[:, KM:t.shape[1] - KM]

    def VTv(self, s):
        t = self.dr["VT_" + s]
        return t[KM:t.shape[0] - KM, :]

    def phase_init_pads(self, s, L, which):
        nc, dr = self.nc, self.dr
        with contextlib.ExitStack() as st:
            zb = st.enter_context(nc.sbuf_tensor(self._nm("pi_zb"), [128, 12, VT_COLS], BF16))
            zf = st.enter_context(nc.sbuf_tensor(self._nm("pi_zf"), [128, 8, QM], F32))
            zq = st.enter_context(nc.sbuf_tensor(self._nm("pi_zq"), [128, KM], BF16))
            ph = Phase(nc)
            ph.op("pool", lambda e: e.memset(zq[:], 0.0), writes=["zb"])
            ph.op("pool", lambda e: e.memset(zb[:], 0.0), writes=["zb"])
            ph.op("pool", lambda e: e.memset(zf[:], 0.0), writes=["zf"])
            QTf = dr["QT_" + s]
            VTf = dr["VT_" + s]
            Xf = dr[("XT_", "XU_")[which % 2] + s]
            k = 0
            for c0 in (0, KM + L):
                for j in range(QT_ROWS // 128):
                    ph.op("sp", lambda e, c0=c0, j=j: e.dma_start(out=QTf[j * 128:(j + 1) * 128, c0:c0 + KM],
                                                               in_=zq[:]),
                          reads=["zb"], dma="st_z%d" % (k % 4))
                    k += 1
                ph.op("sp", lambda e, c0=c0: e.dma_start(
                    out=VTf[c0:c0 + KM, :].rearrange("(u p) n -> p u n", p=128), in_=zb[:]), reads=["zb"], dma="st_z%d" % (k % 4))
                k += 1
            for c0 in (0, QM + L):
                ph.op("sp", lambda e, c0=c0: e.dma_start(
                    out=Xf[:, c0:c0 + QM].rearrange("(c p) t -> p c t", p=128), in_=zf[:]), reads=["zf"], dma="st_z%d" % (k % 4))
                k += 1
            ph.emit()

    def phase_localize(self, s, L, which):
        nc, dr = self.nc, self.dr
        SH, LQ, LK = self.shard, self.LQ, self.LK
        QTf = dr["QT_" + s]
        VTf = dr["VT_" + s]
        Xf = dr[("XT_", "XU_")[which % 2] + s]
        ph = Phase(nc)
        k = 0
        RQ = QT_ROWS // 4
        for j in range(4):
            ph.op("sp", lambda e, j=j: e.dma_start(out=dr["QT_o"][j * RQ:(j + 1) * RQ, :],
                                                   in_=QTf[j * RQ:(j + 1) * RQ, bass.ds(ph.pid(e) * SH, LK)]),
                  dma="cp%d" % (k % 4))
            k += 1
        for c0 in range(0, VT_COLS, VT_COLS // 4):
            ph.op("sp", lambda e, c0=c0: e.dma_start(out=dr["VT_o"][:, c0:c0 + VT_COLS // 4],
                                                     in_=VTf[bass.ds(ph.pid(e) * SH, LK), c0:c0 + VT_COLS // 4]),
                  dma="cp%d" % (k % 4))
            k += 1
        for j in range(2):
            ph.op("sp", lambda e, j=j: e.dma_start(out=dr["XL_o"][j * 512:(j + 1) * 512, :],
                                                   in_=Xf[j * 512:(j + 1) * 512, bass.ds(ph.pid(e) * SH, LQ)]),
                  dma="cp%d" % (k % 4))
            k += 1
        ph.emit()

    def phase_transpose_in(self, s, L):
        nc, dr = self.nc, self.dr
        x = dr["x_" + s]
        XT = self.X(0, s).rearrange("(c p) t -> p c t", p=128)
        nt = L // 512
        with contextlib.ExitStack() as st:
            ident = st.enter_context(nc.sbuf_tensor(self._nm("p0_id"), [128, 128], F32))
            xin = st.enter_context(nc.sbuf_tensor(self._nm("p0_xin"), [128, 2, 4, D], F32))
            xt = st.enter_context(nc.sbuf_tensor(self._nm("p0_xt"), [128, 2, 8, 512], F32))
            ps = st.enter_context(nc.psum_tensor(self._nm("p0_ps"), [128, 4, 512], F32))
            ph = Phase(nc)
            ph.op("sp", lambda e: e.dma_start(out=ident[:], in_=dr["c_ident"][:, :]), writes=["ident"], dma="ld_id")
            for t in range(nt):
                b = t % 2
                src = x[t * 512:(t + 1) * 512, :].rearrange("(s p) f -> p s f", p=128)
                ph.op("sp", lambda e, b=b, src=src: e.dma_start(out=xin[:, b], in_=src),
                      writes=[("xin", b)], dma="ld_xin%d" % b)
                for c in range(8):
                    pb = (t * 8 + c) % 4
                    for sub in range(4):
                        ph.op("pe", lambda e, pb=pb, sub=sub, b=b, c=c: e.transpose(
                            ps[:, pb, sub * 128:(sub + 1) * 128], xin[:, b, sub, c * 128:(c + 1) * 128], ident[:]),
                            reads=[("xin", b), "ident"], writes=[("ps", pb, sub)])
                    if c % 2 == 0:
                        ph.op("act", lambda e, pb=pb, b=b, c=c: e.copy(out=xt[:, b, c, :], in_=ps[:, pb, :]),
                              reads=[("ps", pb, q) for q in range(4)], writes=[("xt", b, c)])
                    else:
                        ph.op("dve", lambda e, pb=pb, b=b, c=c: e.tensor_copy(out=xt[:, b, c, :], in_=ps[:, pb, :]),
                              reads=[("ps", pb, q) for q in range(4)], writes=[("xt", b, c)])
                ph.op("pool", lambda e, b=b, t=t: e.dma_start(out=XT[:, :, t * 512:(t + 1) * 512], in_=xt[:, b]),
                      reads=[("xt", b, c) for c in range(8)], dma="st_xt%d" % b)
            ph.emit()

    def _rmsnorm_tile(self, ph, xt_ap, xt_res, ncols, sq, ones_bf, ss_ps, ss_res, r_ap, r_res, h_ap_fn, h_res_fn,
                      g_ap_fn, nfeat, eps):
        ph.op("act", lambda e: e.activation(out=sq, in_=xt_ap, func=AF.Square),
              reads=[xt_res], writes=["sq"])
        for c in range(8):
            ph.op("pe", lambda e, c=c: e.matmul(ss_ps, ones_bf, sq[:, c, :], start=(c == 0), stop=(c == 7)),
                  reads=["sq", "ones"], writes=[ss_res])
        ph.op("dve", lambda e: e.tensor_scalar(r_ap, ss_ps, 1.0 / nfeat, eps, ALU.mult, ALU.add),
              reads=[ss_res], writes=[r_res])
        ph.op("act", lambda e: e.activation(out=r_ap, in_=r_ap, func=AF.Sqrt), reads=[r_res], writes=[r_res])
        ph.op("dve", lambda e: e.reciprocal(r_ap, r_ap), reads=[r_res], writes=[r_res])
        for c in range(8):
            eng = "dve"
            ph.op(eng, lambda e, c=c: e.scalar_tensor_tensor(h_ap_fn(c), xt_ap[:, c, :], g_ap_fn(c), r_ap,
                                                              ALU.mult, ALU.mult),
                  reads=[xt_res, r_res, "params"], writes=[h_res_fn(c)])

    def phase_proj(self, l, jobs):
        nc, dr = self.nc, self.dr
        w_in = dr["w_in"]
        with contextlib.ExitStack() as st:
            w = st.enter_context(nc.sbuf_tensor(self._nm("p1_w"), [128, 8, D_IN], BF16))
            g = st.enter_context(nc.sbuf_tensor(self._nm("p1_g"), [128, 8], F32))
            ones = st.enter_context(nc.sbuf_tensor(self._nm("p1_ones"), [128, 128], BF16))
            xt = st.enter_context(nc.sbuf_tensor(self._nm("p1_xt"), [128, 2, 8, 512], F32))
            sq = st.enter_context(nc.sbuf_tensor(self._nm("p1_sq"), [128, 8, 512], BF16))
            r = st.enter_context(nc.sbuf_tensor(self._nm("p1_r"), [128, 512], F32))
            h = st.enter_context(nc.sbuf_tensor(self._nm("p1_h"), [128, 2, 8, 512], BF16))
            qt = st.enter_context(nc.sbuf_tensor(self._nm("p1_qt"), [128, 2, 23, 512], BF16))
            vt = st.enter_context(nc.sbuf_tensor(self._nm("p1_vt"), [128, 2, 4, VT_COLS], BF16))
            ss_ps = st.enter_context(nc.psum_tensor(self._nm("p1_ss"), [128, 512], F32))
            ps = st.enter_context(nc.psum_tensor(self._nm("p1_ps"), [128, 6, 512], F32))
            ph = Phase(nc)
            ph.op("pool", lambda e: e.memset(ones[:], 1.0), writes=["ones"])
            ph.op("sp", lambda e: e.dma_start(out=g[:], in_=dr["ln1"][:, l * 8:(l + 1) * 8]), writes=["params"], dma="ld_g")
            wsrc = w_in[l].rearrange("(c p) n -> p c n", p=128)
            for c in range(8):
                for (n0, n1) in ((0, 1536), (1536, 3072), (3072, D_IN)):
                    ph.op("pool", lambda e, c=c, n0=n0, n1=n1: e.dma_start(out=w[:, c, n0:n1], in_=wsrc[:, c, n0:n1]),
                          writes=[("w", c, n0)], dma="ld_w%d" % (c % 4))
            wres = [("w", c, n0) for c in range(8) for n0 in (0, 1536, 3072)]
            pcount = [0]

            def next_ps():
                pcount[0] += 1
                return pcount[0] % 6

            ecount = [0]

            def evac(ph, dst, src, reads, writes):
                ecount[0] += 1
                if ecount[0] % 2 == 0:
                    ph.op("act", lambda e: e.copy(out=dst, in_=src), reads=reads, writes=writes)
                else:
                    ph.op("dve", lambda e: e.tensor_copy(out=dst, in_=src), reads=reads, writes=writes)

            for (s, L) in jobs:
                XT = self.X(l, s).rearrange("(c p) t -> p c t", p=128)
                QT = self.QTv(s).rearrange("(j p) t -> p j t", p=128)
                VT = self.VTv(s)
                nt = L // 512
                for t in range(nt):
                    b = t % 2
                    ph.op("sp", lambda e, b=b, t=t, XT=XT: e.dma_start(out=xt[:, b], in_=XT[:, :, t * 512:(t + 1) * 512]),
                          writes=[("xt", b)], dma="ld_xt%d" % b)
                    self._rmsnorm_tile(ph, xt[:, b], ("xt", b), 512, sq[:], ones[:], ss_ps[:], "ss", r[:], "r",
                                       lambda c, b=b: h[:, b, c, :], lambda c, b=b: ("h", b, c),
                                       lambda c: g[:, c:c + 1], D, EPS)
                    hres = [("h", b, c) for c in range(8)]
                    j = 0
                    for (c0, n) in W_COLS_T:
                        for jj in range(n // 128):
                            pb = next_ps()
                            col = c0 + jj * 128
                            for c in range(8):
                                ph.op("pe", lambda e, pb=pb, c=c, col=col, b=b: e.matmul(
                                    ps[:, pb, :], w[:, c, col:col + 128], h[:, b, c, :], start=(c == 0), stop=(c == 7)),
                                    reads=hres + wres if c == 0 else [], writes=[("ps", pb)])
                            evac(ph, qt[:, b, j, :], ps[:, pb, :], [("ps", pb)], [("qt", b, j)])
                            j += 1
                    ph.op("pool", lambda e, b=b, t=t, QT=QT: e.dma_start(out=QT[:, :, t * 512:(t + 1) * 512], in_=qt[:, b]),
                          reads=[("qt", b, j) for j in range(23)], dma="st_qt%d" % b)
                    for sub in range(4):
                        for (c0, n), v0 in zip(W_COLS_V, (VT_AV, VT_BV, VT_CV)):
                            for n0 in range(0, n, 512):
                                nn = min(512, n - n0)
                                pb = next_ps()
                                for c in range(8):
                                    ph.op("pe", lambda e, pb=pb, c=c, col=c0 + n0, nn=nn, b=b, sub=sub: e.matmul(
                                        ps[:, pb, 0:nn], h[:, b, c, sub * 128:(sub + 1) * 128], w[:, c, col:col + nn],
                                        start=(c == 0), stop=(c == 7)),
                                        reads=hres + wres if c == 0 else [], writes=[("ps", pb)])
                                evac(ph, vt[:, b, sub, v0 + n0:v0 + n0 + nn], ps[:, pb, 0:nn], [("ps", pb)],
                                     [("vt", b, sub, v0 + n0)])
                    vsrc = VT[t * 512:(t + 1) * 512, :].rearrange("(s p) n -> p s n", p=128)
                    ph.op("pool", lambda e, b=b, vsrc=vsrc: e.dma_start(out=vsrc, in_=vt[:, b]),
                          reads=[("vt", b, sub, v) for sub in range(4) for v in (0, 128, 640, 1152)], dma="st_vt%d" % b)
            ph.emit()

    def _attend(self, ph, items, nout, S_ps, O_ps, pt, acc, dv, tag, fill=None):
        n = len(items)
        first = {}
        last = {}
        for i, it in enumerate(items):
            first.setdefault(it["out"], i)
            last[it["out"]] = i
        NS = S_ps.shape[1]
        NP = pt.shape[1]

        def do_s(i):
            it = items[i]
            sb = i % NS
            nm = len(it["masks"])
            ph.op("pe", lambda e: e.matmul(S_ps[:, sb, :], it["k"], it["q"], start=True, stop=(nm == 0)),
                  reads=it["reads"], writes=[("S", sb)])
            for mi, (ml, mr) in enumerate(it["masks"]):
                ph.op("pe", lambda e, ml=ml, mr=mr, mi=mi: e.matmul(S_ps[:, sb, :], ml, mr, start=False,
                                                                    stop=(mi == nm - 1)),
                      reads=["consts"], writes=[("S", sb)])
            if it.get("bias") is not None:
                ph.op("act", lambda e: e.activation(out=pt[:, i % NP, :], in_=S_ps[:, sb, :], func=AF.Exp, scale=0.125,
                                                    bias=it["bias"]),
                      reads=[("S", sb), "consts"], writes=[("pt", i % NP)])
            else:
                ph.op("act", lambda e: e.activation(out=pt[:, i % NP, :], in_=S_ps[:, sb, :], func=AF.Exp, scale=0.125),
                      reads=[("S", sb)], writes=[("pt", i % NP)])

        cnt = {}

        def do_pv(i):
            it = items[i]
            o = it["out"]
            par = cnt.get(o, 0) % 2
            fresh = cnt.get(o, 0) < 2
            cnt[o] = cnt.get(o, 0) + 1
            if fresh:
                ph.op("dve", lambda e: e.tensor_copy(out=acc[:, o, par, :], in_=pt[:, i % NP, :]),
                      reads=[("pt", i % NP)], writes=[("acc", o, par)])
            else:
                ph.op("dve", lambda e: e.tensor_tensor(out=acc[:, o, par, :], in0=acc[:, o, par, :],
                                                      in1=pt[:, i % NP, :], op=ALU.add),
                      reads=[("pt", i % NP)], writes=[("acc", o, par)])
            if fill is not None:
                for _ in range(PE_FILL):
                    ph.op("pe", lambda e: e.matmul(fill[0], fill[1], fill[2], start=True, stop=True))
            ph.op("pe", lambda e: e.matmul(O_ps[0:dv, o, :], it["v"], pt[:, i % NP, :], start=(first[o] == i),
                                           stop=(last[o] == i)),
                  reads=[("pt", i % NP)] + it["reads"], writes=[("O", o)])

        LA = 2
        for i in range(n):
            do_s(i)
            if i >= LA:
                do_pv(i - LA)
        for j in range(max(0, n - LA), n):
            do_pv(j)
        for o, c in cnt.items():
            if c >= 2:
                ph.op("dve", lambda e, o=o: e.tensor_tensor(out=acc[:, o, 0, :], in0=acc[:, o, 0, :],
                                                            in1=acc[:, o, 1, :], op=ALU.add),
                      reads=[("acc", o, 1)], writes=[("acc", o, 0)])

    def _attend_b(self, ph, pairs, S_ps, O_ps, pt, acc, NPB, den_ps=None, ones_b=None, pe_every=4, fill=None):
        NSB = S_ps.shape[1] // 2
        n = len(pairs)

        def do_s(i):
            it = pairs[i]
            sb = 2 * (i % NSB)
            nm = len(it["masks"])
            for m in range(2):
                ph.op("pe", lambda e, m=m: e.matmul(S_ps[:, sb + m, :], it["k"][m], it["q"][m], start=True, stop=(nm == 0)),
                      reads=it["reads"], writes=[("S", i % NSB)])
                for mi, (ml, mr) in enumerate(it["masks"]):
                    ph.op("pe", lambda e, m=m, ml=ml, mr=mr, mi=mi: e.matmul(S_ps[:, sb + m, :], ml, mr, start=False,
                                                                         stop=(mi == nm - 1)),
                          reads=["consts"], writes=[("S", i % NSB)])
            ps0 = 2 * (i % NPB)
            ph.op("act", lambda e: e.activation(out=pt[:, ps0:ps0 + 2, :], in_=S_ps[:, sb:sb + 2, :], func=AF.Exp,
                                                scale=0.125),
                  reads=[("S", i % NSB)], writes=[("pt", i % NPB)])

        st8 = {"dve": 0, "pe": 0}

        def do_pv(i):
            it = pairs[i]
            ps0 = 2 * (i % NPB)
            if den_ps is not None and (i % pe_every) == pe_every - 1:
                for m in range(2):
                    ph.op("pe", lambda e, m=m, first=(st8["pe"] == 0): e.matmul(den_ps[:, m, :], ones_b, pt[:, ps0 + m, :],
                                                                               start=first, stop=False),
                          reads=[("pt", i % NPB), "ones"], writes=[("den", m)])
                st8["pe"] += 1
            else:
                par = st8["dve"] % 2
                if st8["dve"] < 2:
                    ph.op("dve", lambda e: e.tensor_copy(out=acc[:, par], in_=pt[:, ps0:ps0 + 2, :]),
                          reads=[("pt", i % NPB)], writes=[("acc", par)])
                else:
                    ph.op("dve", lambda e: e.tensor_tensor(out=acc[:, par], in0=acc[:, par], in1=pt[:, ps0:ps0 + 2, :],
                                                          op=ALU.add),
                          reads=[("pt", i % NPB)], writes=[("acc", par)])
                st8["dve"] += 1
            if fill is not None:
                for _ in range(PE_FILL):
                    ph.op("pe", lambda e: e.matmul(fill[0], fill[1], fill[2], start=True, stop=True))
            for m in range(2):
                ph.op("pe", lambda e, m=m: e.matmul(O_ps[:, m, :], it["v"], pt[:, ps0 + m, :], start=(i == 0),
                                                    stop=(i == n - 1)),
                      reads=[("pt", i % NPB)] + it["reads"], writes=[("O", m)])

        LA = NSB - 1
        for i in range(n):
            do_s(i)
            if i >= LA:
                do_pv(i - LA)
        for j in range(max(0, n - LA), n):
            do_pv(j)
        if st8["dve"] >= 2:
            ph.op("dve", lambda e: e.tensor_tensor(out=acc[:, 0], in0=acc[:, 0], in1=acc[:, 1], op=ALU.add),
                  reads=[("acc", 1)], writes=[("acc", 0)])
        return st8["pe"] > 0

    def _finalize_den(self, ph, acc_ap, acc_res, ones_f, den_ps, den_res, rden_ap, rden_res, rows, extra=None,
                      start=True):
        ph.op("pe", lambda e: e.matmul(den_ps, ones_f, acc_ap, start=start, stop=True),
              reads=[acc_res, "ones"], writes=[den_res])
        if extra is not None:
            ph.op("dve", lambda e: e.tensor_scalar(rden_ap, den_ps[0:rows], extra, None, ALU.add),
                  reads=[den_res, "params"], writes=[rden_res])
            ph.op("dve", lambda e: e.reciprocal(rden_ap, rden_ap), reads=[rden_res], writes=[rden_res])
        else:
            ph.op("dve", lambda e: e.reciprocal(rden_ap, den_ps[0:rows]), reads=[den_res], writes=[rden_res])

    def phase_attn_ac(self, l, s, L, own=False):
        nc, dr = self.nc, self.dr
        if own:
            QT, VT, OT = dr["QT_o"], dr["VT_o"], dr["OT_o"]
            nt = self.LQ // 512
            kshift = KM - QM
            L = self.LK
        else:
            QT = self.QTv(s)
            VT = self.VTv(s)
            OT = dr["OT_" + s]
            nt = L // 512
            kshift = 0
        nblk = L // 128
        GW = (128, 256, 1024)
        GN = (6, 8, 20)
        with contextlib.ExitStack() as st:
            ones_f = st.enter_context(nc.sbuf_tensor(self._nm("pa_onesf"), [128, 128], F32))
            esink = st.enter_context(nc.sbuf_tensor(self._nm("pa_sink"), [128, 4], F32))
            dA = st.enter_context(nc.sbuf_tensor(self._nm("pa_dA"), [128, 4 * 128], BF16))
            dC = st.enter_context(nc.sbuf_tensor(self._nm("pa_dC"), [128, 24 * 128], BF16))
            MA = st.enter_context(nc.sbuf_tensor(self._nm("pa_MA"), [128, 1152], BF16))
            MC0 = st.enter_context(nc.sbuf_tensor(self._nm("pa_MC0"), [128, 1152], BF16))
            MC1 = st.enter_context(nc.sbuf_tensor(self._nm("pa_MC1"), [128, 1408], BF16))
            MC2 = st.enter_context(nc.sbuf_tensor(self._nm("pa_MC2"), [128, 2944], BF16))
            kA = st.enter_context(nc.sbuf_tensor(self._nm("pa_kA"), [128, 2, 768], BF16))
            vA = st.enter_context(nc.sbuf_tensor(self._nm("pa_vA"), [128, 2, 6, 128], BF16))
            qA = st.enter_context(nc.sbuf_tensor(self._nm("pa_qA"), [128, 2, 2, 512], BF16))
            kC = st.enter_context(nc.sbuf_tensor(self._nm("pa_kC"), [128, 2, 2, 34 * 128], BF16))
            vC = st.enter_context(nc.sbuf_tensor(self._nm("pa_vC"), [128, 2, 34, 256], BF16))
            qC = st.enter_context(nc.sbuf_tensor(self._nm("pa_qC"), [128, 2, 3, 2, 512], BF16))
            pt = st.enter_context(nc.sbuf_tensor(self._nm("pa_pt"), [128, 4, 512], BF16))
            acc = st.enter_context(nc.sbuf_tensor(self._nm("pa_acc"), [128, 2, 2, 512], F32))
            rden = st.enter_context(nc.sbuf_tensor(self._nm("pa_rden"), [64, 512], F32))
            ot = st.enter_context(nc.sbuf_tensor(self._nm("pa_ot"), [64, 2, 8, 512], BF16))
            S_ps = st.enter_context(nc.psum_tensor(self._nm("pa_S"), [128, 4, 512], F32))
            O_ps = st.enter_context(nc.psum_tensor(self._nm("pa_O"), [128, 2, 512], F32))
            den_ps = st.enter_context(nc.psum_tensor(self._nm("pa_den"), [128, 512], F32))
            fill_ps = st.enter_context(nc.psum_tensor(self._nm("pa_fill"), [128, 512], F32))
            fillsrc = st.enter_context(nc.sbuf_tensor(self._nm("pa_fsrc"), [128, 512], BF16))
            ph = Phase(nc)
            kbias = None
            if own:
                kbias = st.enter_context(nc.sbuf_tensor(self._nm("pa_kbias"), [128, self.LK // 128], F32))
                ph.op("sp", lambda e: e.dma_start(out=kbias[:], in_=dr["c_kbias"][:, :]), writes=["consts"], dma="ld_c1")
            MC = (MC0, MC1, MC2)
            ph.op("pool", lambda e: e.memset(ones_f[:], 1.0), writes=["ones"])
            ph.op("pool", lambda e: e.memset(fillsrc[:], 0.5), writes=["consts"])
            fill = (fill_ps[:], dA[:, 0:128], fillsrc[:])
            ph.op("sp", lambda e: e.dma_start(out=esink[:], in_=dr["sink"][:, l * 4:(l + 1) * 4]), writes=["params"], dma="ld_c0")
            ph.op("act", lambda e: e.activation(out=esink[:], in_=esink[:], func=AF.Exp), reads=["params"], writes=["params"])
            for nm, tl in (("c_diagA", dA), ("c_diagC", dC), ("c_MA", MA), ("c_MC0", MC0), ("c_MC1", MC1), ("c_MC2", MC2)):
                ph.op("sp", lambda e, nm=nm, tl=tl: e.dma_start(out=tl[:], in_=dr[nm][:, :]), writes=["consts"], dma="ld_c1")
            goff = (0, 6, 14)
            oc = [0]
            for c in range(nt):
                b = c % 2
                a = c * 512 + kshift
                ao = c * 512
                u_lo = max(0, -((a - 128) // 128))
                u_hi = min(6, (L - (a - 128)) // 128)
                k0 = a - 128 + 128 * u_lo
                k1 = a - 128 + 128 * u_hi
                ph.op("sp", lambda e, b=b, k0=k0, k1=k1, u_lo=u_lo, u_hi=u_hi: e.dma_start(
                    out=kA[:, b, u_lo * 128:u_hi * 128], in_=QT[QT_AK:QT_AK + 128, k0:k1]),
                    writes=[("kA", b)], dma="ld_kA%d" % b)
                ph.op("sp", lambda e, b=b, k0=k0, k1=k1, u_lo=u_lo, u_hi=u_hi: e.dma_start(
                    out=vA[:, b, u_lo:u_hi, :],
                    in_=VT[k0:k1, VT_AV:VT_AV + 128].rearrange("(u p) n -> p u n", p=128)),
                    writes=[("vA", b)], dma="ld_vA%d" % b)
                for kvh in range(2):
                    for j in range(2):
                        r0 = QT_AQ + (2 * kvh + j) * 64
                        ph.op("sp", lambda e, b=b, kvh=kvh, j=j, r0=r0, a=a: e.dma_start(
                            out=qA[kvh * 64:(kvh + 1) * 64, b, j, :], in_=QT[r0:r0 + 64, a:a + 512]),
                            writes=[("qA", b, kvh, j)], dma="ld_qA%d" % b)
                cval = []
                for g in range(3):
                    ulo = max(0, -((a - GW[g]) // 128))
                    uhi = min(GN[g], (L - (a - GW[g])) // 128)
                    cval.append((ulo, uhi))
                    k0 = a - GW[g] + 128 * ulo
                    k1 = a - GW[g] + 128 * uhi
                    for pr in range(2):
                        r0 = QT_CK + g * 256 + pr * 128
                        ph.op("sp", lambda e, b=b, g=g, pr=pr, r0=r0, k0=k0, k1=k1, ulo=ulo, uhi=uhi: e.dma_start(
                            out=kC[:, b, pr, (goff[g] + ulo) * 128:(goff[g] + uhi) * 128], in_=QT[r0:r0 + 128, k0:k1]),
                            writes=[("kC", b, g, pr)], dma="ld_kC%d" % b)
                        r1 = QT_CQ + g * 256 + pr * 128
                        ph.op("sp", lambda e, b=b, g=g, pr=pr, r1=r1, a=a: e.dma_start(
                            out=qC[:, b, g, pr, :], in_=QT[r1:r1 + 128, a:a + 512]),
                            writes=[("qC", b, g, pr)], dma="ld_qC%d" % b)
                    ph.op("sp", lambda e, b=b, g=g, k0=k0, k1=k1, ulo=ulo, uhi=uhi: e.dma_start(
                        out=vC[:, b, goff[g] + ulo:goff[g] + uhi, :],
                        in_=VT[k0:k1, VT_CV + g * 256:VT_CV + (g + 1) * 256].rearrange("(u p) n -> p u n", p=128)),
                        writes=[("vC", b, g)], dma="ld_vC%d" % b)
                for hq in range(4):
                    kvh, j = hq // 2, hq % 2
                    items = []
                    for u in range(u_lo, u_hi):
                        items.append(dict(
                            q=qA[kvh * 64:(kvh + 1) * 64, b, j, :],
                            k=kA[kvh * 64:(kvh + 1) * 64, b, u * 128:(u + 1) * 128],
                            v=vA[:, b, u, kvh * 64:(kvh + 1) * 64],
                            masks=[(dA[:, hq * 128:(hq + 1) * 128], MA[:, 640 - 128 * u:640 - 128 * u + 512])],
                            bias=(kbias[:, (a - 128) // 128 + u:(a - 128) // 128 + u + 1] if own else None),
                            out=oc[0] % 2, reads=[("kA", b), ("vA", b), ("qA", b, kvh, j), "consts"]))
                    o = oc[0] % 2
                    oc[0] += 1
                    self._attend(ph, items, 1, S_ps, O_ps, pt, acc, 64, "A", fill=fill)
                    self._finalize_den(ph, acc[:, o, 0, :], ("acc", o, 0), ones_f[:], den_ps[:], "den", rden[:], "rden", 64,
                                       extra=esink[0:64, hq:hq + 1])
                    ph.op("dve", lambda e, o=o, b=b, hq=hq: e.tensor_tensor(out=ot[:, b, hq, :], in0=O_ps[0:64, o, :],
                                                                           in1=rden[:], op=ALU.mult),
                          reads=[("O", o), "rden"], writes=[("ot", b, hq)])
                for h in range(4):
                    pr, hp = h // 2, h % 2
                    items = []
                    for g in range(3):
                        ulo, uhi = cval[g]
                        for u in range(ulo, uhi):
                            off = 128 * (GN[g] - 1) - 128 * u
                            gi = (g * 4 + h) * 2
                            items.append(dict(
                                q=qC[hp * 64:(hp + 1) * 64, b, g, pr, :],
                                k=kC[hp * 64:(hp + 1) * 64, b, pr, (goff[g] + u) * 128:(goff[g] + u + 1) * 128],
                                v=vC[:, b, goff[g] + u, h * 64:(h + 1) * 64],
                                masks=[(dC[:, gi * 128:(gi + 1) * 128], MC[g][:, off:off + 512])],
                                bias=(kbias[:, (a - GW[g]) // 128 + u:(a - GW[g]) // 128 + u + 1] if own else None),
                                out=oc[0] % 2,
                                reads=[("kC", b, g, pr), ("vC", b, g), ("qC", b, g, pr), "consts"]))
                    o = oc[0] % 2
                    oc[0] += 1
                    self._attend(ph, items, 1, S_ps, O_ps, pt, acc, 64, "C", fill=fill)
                    self._finalize_den(ph, acc[:, o, 0, :], ("acc", o, 0), ones_f[:], den_ps[:], "den", rden[:], "rden", 64)
                    ph.op("dve", lambda e, o=o, b=b, h=h: e.tensor_tensor(out=ot[:, b, 4 + h, :], in0=O_ps[0:64, o, :],
                                                                         in1=rden[:], op=ALU.mult),
                          reads=[("O", o), "rden"], writes=[("ot", b, 4 + h)])
                ph.op("pool", lambda e, b=b, a=ao: e.dma_start(
                    out=OT[0:256, a:a + 512].rearrange("(h p) t -> p h t", p=64), in_=ot[:, b, 0:4, :]),
                    reads=[("ot", b, hh) for hh in range(4)], dma="st_oA%d" % b)
                ph.op("pool", lambda e, b=b, a=ao: e.dma_start(
                    out=OT[768:1024, a:a + 512].rearrange("(h p) t -> p h t", p=64), in_=ot[:, b, 4:8, :]),
                    reads=[("ot", b, 4 + hh) for hh in range(4)], dma="st_oC%d" % b)
            ph.emit()

    def phase_attn_b(self, l, s, L, own=False):
        nc, dr = self.nc, self.dr
        QT = self.QTv(s)
        VT = self.VTv(s)
        OT = dr["OT_" + s]
        nt = L // 512
        nblk = L // 128
        lam_init = 0.8 - 0.6 * float(np.exp(-0.3 * l))
        with contextlib.ExitStack() as st:
            ones_f = st.enter_context(nc.sbuf_tensor(self._nm("pb_onesf"), [128, 128], F32))
            ones_b = st.enter_context(nc.sbuf_tensor(self._nm("pb_onesb"), [128, 128], BF16))
            lamt = st.enter_context(nc.sbuf_tensor(self._nm("pb_lam"), [128, 4 * 64], F32))
            lt = st.enter_context(nc.sbuf_tensor(self._nm("pb_lt"), [128, 2 * 64], F32))
            ls = st.enter_context(nc.sbuf_tensor(self._nm("pb_ls"), [128, 4], F32))
            gs = st.enter_context(nc.sbuf_tensor(self._nm("pb_gs"), [128, 1], F32))
            dB = st.enter_context(nc.sbuf_tensor(self._nm("pb_dB"), [128, 4 * 128], BF16))
            MBh = st.enter_context(nc.sbuf_tensor(self._nm("pb_MBh"), [128, 896], BF16))
            MBl = st.enter_context(nc.sbuf_tensor(self._nm("pb_MBl"), [128, 896], BF16))
            kB = st.enter_context(nc.sbuf_tensor(self._nm("pb_kB"), [68, 2, L], BF16))
            vB = st.enter_context(nc.sbuf_tensor(self._nm("pb_vB"), [128, nblk, 128], BF16))
            qB = st.enter_context(nc.sbuf_tensor(self._nm("pb_qB"), [68, 2, 2, 3, 512], BF16))
            pt = st.enter_context(nc.sbuf_tensor(self._nm("pb_pt"), [128, 6, 512], BF16))
            acc = st.enter_context(nc.sbuf_tensor(self._nm("pb_acc"), [128, 2, 2, 512], F32))
            rden = st.enter_context(nc.sbuf_tensor(self._nm("pb_rden"), [128, 2, 512], F32))
            tt = st.enter_context(nc.sbuf_tensor(self._nm("pb_t"), [128, 2, 512], F32))
            sq = st.enter_context(nc.sbuf_tensor(self._nm("pb_sq"), [128, 512], BF16))
            ot = st.enter_context(nc.sbuf_tensor(self._nm("pb_ot"), [128, 2, 512], BF16))
            S_ps = st.enter_context(nc.psum_tensor(self._nm("pb_S"), [128, 4, 512], F32))
            fill_ps = st.enter_context(nc.psum_tensor(self._nm("pb_fill"), [128, 512], F32))
            O_ps = st.enter_context(nc.psum_tensor(self._nm("pb_O"), [128, 2, 512], F32))
            den_ps = S_ps[:, 0:2, :]
            ph = Phase(nc)
            ph.op("pool", lambda e: e.memset(ones_f[:], 1.0), writes=["ones"])
            ph.op("pool", lambda e: e.memset(ones_b[:], 1.0), writes=["ones"])
            ph.op("pool", lambda e: e.memset(qB[:], 0.0), writes=["qinit"])
            for nm, tl in (("c_diagB", dB), ("c_MBh", MBh), ("c_MBl", MBl)):
                ph.op("sp", lambda e, nm=nm, tl=tl: e.dma_start(out=tl[:], in_=dr[nm][:, :]), writes=["consts"], dma="ld_c1")
            ph.op("sp", lambda e: e.dma_start(out=lamt[:], in_=dr["lam"][:, l * 256:(l + 1) * 256]), writes=["lamt"], dma="ld_c0")
            ph.op("sp", lambda e: e.dma_start(out=gs[:], in_=dr["subln"][:, l:l + 1], allow_slow_non_contiguous=True), writes=["gs"], dma="ld_c0")
            ph.op("dve", lambda e: e.tensor_tensor(out=lt[:, 0:64], in0=lamt[:, 0:64], in1=lamt[:, 64:128], op=ALU.mult),
                  reads=["lamt"], writes=["lt"])
            ph.op("dve", lambda e: e.tensor_tensor(out=lt[:, 64:128], in0=lamt[:, 128:192], in1=lamt[:, 192:256], op=ALU.mult),
                  reads=["lamt"], writes=["lt"])
            ph.op("dve", lambda e: e.reduce_sum(ls[:, 0:1], lt[:, 0:64], mybir.AxisListType.X), reads=["lt"], writes=["ls"])
            ph.op("dve", lambda e: e.reduce_sum(ls[:, 1:2], lt[:, 64:128], mybir.AxisListType.X), reads=["lt"], writes=["ls"])
            ph.op("act", lambda e: e.activation(out=ls[:, 0:2], in_=ls[:, 0:2], func=AF.Exp), reads=["ls"], writes=["ls"])
            ph.op("dve", lambda e: e.tensor_tensor(out=ls[:, 2:3], in0=ls[:, 1:2], in1=ls[:, 0:1], op=ALU.subtract),
                  reads=["ls"], writes=["ls"])
            ph.op("dve", lambda e: e.tensor_scalar(ls[:, 2:3], ls[:, 2:3], -lam_init, None, ALU.add),
                  reads=["ls"], writes=["ls"])
            ph.op("dve", lambda e: e.tensor_scalar(gs[:], gs[:], 1.0 - lam_init, None, ALU.mult),
                  reads=["gs"], writes=["gs"])
            HCH = min(4096, L)
            VCH = min(32, nblk)
            for h in range(4):
                for m in range(2):
                    r0 = QT_BK + h * 128 + m * 64
                    for c0 in range(0, L, HCH):
                        ph.op("sp", lambda e, m=m, r0=r0, c0=c0: e.dma_start(out=kB[0:64, m, c0:c0 + HCH],
                                                                         in_=QT[r0:r0 + 64, c0:c0 + HCH]),
                              writes=[("kB", m)], dma="ld_kB%d" % m)
                    ph.op("sp", lambda e, m=m, h=h: e.dma_start(out=kB[64:68, m, :], in_=dr["c_kaug"][h, :, 0:L]),
                          writes=[("kB", m)], dma="ld_kB%d" % m)
                for c0 in range(0, nblk, VCH):
                    ph.op("sp", lambda e, h=h, c0=c0: e.dma_start(
                        out=vB[:, c0:c0 + VCH, :],
                        in_=VT[c0 * 128:(c0 + VCH) * 128, VT_BV + h * 128:VT_BV + (h + 1) * 128].rearrange(
                            "(u p) n -> p u n", p=128)),
                        writes=["vB"], dma="ld_vB")
                for c in range(nt):
                    b = c % 2
                    a = c * 512
                    for m in range(2):
                        r0 = QT_BQ + h * 128 + m * 64
                        for var in range(3):
                            ph.op("sp", lambda e, b=b, m=m, var=var, r0=r0, a=a: e.dma_start(
                                out=qB[0:64, b, m, var, :], in_=QT[r0:r0 + 64, a:a + 512]),
                                reads=["qinit"], writes=[("qB", b, m)], dma="ld_qB%d" % b)
                        for var in range(2):
                            ph.op("sp", lambda e, b=b, m=m, var=var, h=h, a=a: e.dma_start(
                                out=qB[64:68, b, m, var, :], in_=dr["c_qaug"][h, var, :, a:a + 512]),
                                reads=["qinit"], writes=[("qB", b, m)], dma="ld_qB%d" % b)
                    pairs = []
                    slope = 2.0 ** (-2.0 * (h + 1))
                    for kb in range(nblk):
                        k0 = kb * 128
                        dmin = max(0, k0 - (a + 511), a - (k0 + 127))
                        if slope * dmin >= B_SKIP:
                            continue
                        if kb < 4 * c:
                            var, masks = 0, []
                        elif kb > 4 * c + 3:
                            var, masks = 1, []
                        else:
                            u = kb - 4 * c
                            off = 384 - 128 * u
                            var = 2
                            masks = [(dB[:, h * 128:(h + 1) * 128], MBh[:, off:off + 512]),
                                     (dB[:, h * 128:(h + 1) * 128], MBl[:, off:off + 512])]
                        pairs.append(dict(q=[qB[:, b, 0, var, :], qB[:, b, 1, var, :]],
                                          k=[kB[:, 0, k0:k0 + 128], kB[:, 1, k0:k0 + 128]],
                                          v=vB[:, kb, :], masks=masks,
                                          reads=[("kB", 0), ("kB", 1), "vB", ("qB", b, 0), ("qB", b, 1), "consts"]))
                    pe_used = self._attend_b(ph, pairs, S_ps, O_ps, pt, acc, 3, fill=(fill_ps[:], ones_b[:], sq[:]))
                    for m in range(2):
                        self._finalize_den(ph, acc[:, 0, m, :], ("acc", 0), ones_f[:], den_ps[:, m, :], ("S", 0),
                                           rden[:, m, :], ("rden", m), 128, start=(not pe_used))
                        ph.op("dve", lambda e, m=m: e.tensor_tensor(out=tt[:, m, :], in0=O_ps[:, m, :], in1=rden[:, m, :],
                                                                    op=ALU.mult),
                              reads=[("O", m), ("rden", m)], writes=[("tt", m)])
                    ph.op("dve", lambda e: e.scalar_tensor_tensor(tt[:, 0, :], tt[:, 1, :], ls[:, 2:3], tt[:, 0, :],
                                                                  ALU.mult, ALU.add),
                          reads=[("tt", 0), ("tt", 1), "ls"], writes=[("tt", 0)])
                    ph.op("act", lambda e: e.activation(out=sq[:], in_=tt[:, 0, :], func=AF.Square),
                          reads=[("tt", 0)], writes=["sq"])
                    ph.op("pe", lambda e: e.matmul(den_ps[:, 0, :], ones_b[:], sq[:], start=True, stop=True),
                          reads=["sq", "ones"], writes=[("S", 0)])
                    ph.op("dve", lambda e: e.tensor_scalar(rden[:, 0, :], den_ps[:, 0, :], 1.0 / 128, SUBLN_EPS,
                                                           ALU.mult, ALU.add),
                          reads=[("S", 0)], writes=[("rden", 0)])
                    ph.op("act", lambda e: e.activation(out=rden[:, 0, :], in_=rden[:, 0, :], func=AF.Sqrt),
                          reads=[("rden", 0)], writes=[("rden", 0)])
                    ph.op("dve", lambda e: e.reciprocal(rden[:, 0, :], rden[:, 0, :]),
                          reads=[("rden", 0)], writes=[("rden", 0)])
                    ph.op("dve", lambda e, b=b: e.scalar_tensor_tensor(ot[:, b, :], tt[:, 0, :], gs[:, 0:1], rden[:, 0, :],
                                                                       ALU.mult, ALU.mult),
                          reads=[("tt", 0), ("rden", 0), "gs"], writes=[("ot", b)])
                    ph.op("pool", lambda e, b=b, a=a, h=h: e.dma_start(
                        out=OT[256 + h * 128:256 + (h + 1) * 128, a:a + 512], in_=ot[:, b, :]),
                        reads=[("ot", b)], dma="st_oB%d" % b)
            ph.emit()

    def phase_attn_b_own(self, l, s, L):
        nc, dr = self.nc, self.dr
        QT = self.QTv(s)
        VT = self.VTv(s)
        QTo, VTo, OT = dr["QT_o"], dr["VT_o"], dr["OT_o"]
        nt = self.LQ // 512
        kshift = KM - QM
        nblk = L // 128
        lam_init = 0.8 - 0.6 * float(np.exp(-0.3 * l))
        with contextlib.ExitStack() as st:
            def sb(name, shape, dt):
                return st.enter_context(nc.sbuf_tensor(self._nm(name), shape, dt))
            ones_f = sb("po_onesf", [128, 128], F32)
            ones_b = sb("po_onesb", [128, 128], BF16)
            lamt = sb("po_lam", [128, 4 * 64], F32)
            lt = sb("po_lt", [128, 2 * 64], F32)
            ls = sb("po_ls", [128, 4], F32)
            gs = sb("po_gs", [128, 1], F32)
            dB = sb("po_dB", [128, 4 * 128], BF16)
            MBh = sb("po_MBh", [128, 896], BF16)
            MBl = sb("po_MBl", [128, 896], BF16)
            kB = sb("po_kB", [68, 2, L], BF16)
            vB = sb("po_vB", [128, nblk, 128], BF16)
            kN = sb("po_kN", [68, 2, 2, 512], BF16)
            vN = sb("po_vN", [128, 2, 4, 128], BF16)
            qB = sb("po_qB", [68, 2, 2, 2, 512], BF16)
            pt = sb("po_pt", [128, 6, 512], BF16)
            acc = sb("po_acc", [128, 2, 2, 512], F32)
            rden = sb("po_rden", [128, 2, 512], F32)
            tt = sb("po_t", [128, 2, 512], F32)
            sq = sb("po_sq", [128, 512], BF16)
            ot = sb("po_ot", [128, 2, 512], BF16)
            S_ps = st.enter_context(nc.psum_tensor(self._nm("po_S"), [128, 4, 512], F32))
            fill_ps = st.enter_context(nc.psum_tensor(self._nm("po_fill"), [128, 512], F32))
            O_ps = st.enter_context(nc.psum_tensor(self._nm("po_O"), [128, 2, 512], F32))
            den_ps = S_ps[:, 0:2, :]
            ph = Phase(nc)
            ph.op("pool", lambda e: e.memset(ones_f[:], 1.0), writes=["ones"])
            ph.op("pool", lambda e: e.memset(ones_b[:], 1.0), writes=["ones"])
            ph.op("pool", lambda e: e.memset(qB[:], 0.0), writes=["qinit"])
            ph.op("pool", lambda e: e.memset(kN[:], 0.0), writes=["qinit"])
            for nm, tl in (("c_diagB", dB), ("c_MBh", MBh), ("c_MBl", MBl)):
                ph.op("sp", lambda e, nm=nm, tl=tl: e.dma_start(out=tl[:], in_=dr[nm][:, :]), writes=["consts"], dma="ld_c1")
            ph.op("sp", lambda e: e.dma_start(out=lamt[:], in_=dr["lam"][:, l * 256:(l + 1) * 256]), writes=["lamt"], dma="ld_c0")
            ph.op("sp", lambda e: e.dma_start(out=gs[:], in_=dr["subln"][:, l:l + 1], allow_slow_non_contiguous=True),
                  writes=["gs"], dma="ld_c0")
            ph.op("dve", lambda e: e.tensor_tensor(out=lt[:, 0:64], in0=lamt[:, 0:64], in1=lamt[:, 64:128], op=ALU.mult),
                  reads=["lamt"], writes=["lt"])
            ph.op("dve", lambda e: e.tensor_tensor(out=lt[:, 64:128], in0=lamt[:, 128:192], in1=lamt[:, 192:256], op=ALU.mult),
                  reads=["lamt"], writes=["lt"])
            ph.op("dve", lambda e: e.reduce_sum(ls[:, 0:1], lt[:, 0:64], mybir.AxisListType.X), reads=["lt"], writes=["ls"])
            ph.op("dve", lambda e: e.reduce_sum(ls[:, 1:2], lt[:, 64:128], mybir.AxisListType.X), reads=["lt"], writes=["ls"])
            ph.op("act", lambda e: e.activation(out=ls[:, 0:2], in_=ls[:, 0:2], func=AF.Exp), reads=["ls"], writes=["ls"])
            ph.op("dve", lambda e: e.tensor_tensor(out=ls[:, 2:3], in0=ls[:, 1:2], in1=ls[:, 0:1], op=ALU.subtract),
                  reads=["ls"], writes=["ls"])
            ph.op("dve", lambda e: e.tensor_scalar(ls[:, 2:3], ls[:, 2:3], -lam_init, None, ALU.add),
                  reads=["ls"], writes=["ls"])
            ph.op("dve", lambda e: e.tensor_scalar(gs[:], gs[:], 1.0 - lam_init, None, ALU.mult),
                  reads=["gs"], writes=["gs"])
            HCH = min(4096, L)
            VCH = min(32, nblk)
            for h in range(4):
                for m in range(2):
                    r0 = QT_BK + h * 128 + m * 64
                    for c0 in range(0, L, HCH):
                        ph.op("sp", lambda e, m=m, r0=r0, c0=c0: e.dma_start(out=kB[0:64, m, c0:c0 + HCH],
                                                                         in_=QT[r0:r0 + 64, c0:c0 + HCH]),
                              writes=[("kB", m)], dma="ld_kB%d" % m)
                for c0 in range(0, nblk, VCH):
                    ph.op("sp", lambda e, h=h, c0=c0: e.dma_start(
                        out=vB[:, c0:c0 + VCH, :],
                        in_=VT[c0 * 128:(c0 + VCH) * 128, VT_BV + h * 128:VT_BV + (h + 1) * 128].rearrange(
                            "(u p) n -> p u n", p=128)),
                        writes=["vB"], dma="ld_vB")
                for c in range(nt):
                    b = c % 2
                    ak = c * 512 + kshift
                    ao = c * 512
                    for m in range(2):
                        ph.op("sp", lambda e, m=m, h=h, c=c: e.dma_start(out=kB[64:68, m, :],
                                                                       in_=dr["c_kaugL"][c, h, :, 0:L]),
                              writes=[("kBa", m)], dma="ld_kBa%d" % m)
                        r0 = QT_BQ + h * 128 + m * 64
                        for var in range(2):
                            ph.op("sp", lambda e, b=b, m=m, var=var, r0=r0, ak=ak: e.dma_start(
                                out=qB[0:64, b, m, var, :], in_=QTo[r0:r0 + 64, ak:ak + 512]),
                                reads=["qinit"], writes=[("qB", b, m)], dma="ld_qB%d" % b)
                        ph.op("sp", lambda e, b=b, m=m, h=h, ao=ao: e.dma_start(
                            out=qB[64:68, b, m, 0, :], in_=dr["c_qaugL"][h, :, ao:ao + 512]),
                            reads=["qinit"], writes=[("qB", b, m)], dma="ld_qB%d" % b)
                        r1 = QT_BK + h * 128 + m * 64
                        ph.op("sp", lambda e, b=b, m=m, r1=r1, ak=ak: e.dma_start(
                            out=kN[0:64, b, m, :], in_=QTo[r1:r1 + 64, ak:ak + 512]),
                            reads=["qinit"], writes=[("kN", b)], dma="ld_kN%d" % b)
                    ph.op("sp", lambda e, b=b, h=h, ak=ak: e.dma_start(
                        out=vN[:, b, :, :],
                        in_=VTo[ak:ak + 512, VT_BV + h * 128:VT_BV + (h + 1) * 128].rearrange("(u p) n -> p u n", p=128)),
                        writes=[("vN", b)], dma="ld_vN%d" % b)
                    pairs = []
                    for kb in range(nblk):
                        k0 = kb * 128
                        pairs.append(dict(q=[qB[:, b, 0, 0, :], qB[:, b, 1, 0, :]],
                                          k=[kB[:, 0, k0:k0 + 128], kB[:, 1, k0:k0 + 128]],
                                          v=vB[:, kb, :], masks=[],
                                          reads=[("kB", 0), ("kB", 1), ("kBa", 0), ("kBa", 1), "vB", ("qB", b, 0),
                                                 ("qB", b, 1)]))
                    for u in range(4):
                        off = 384 - 128 * u
                        masks = [(dB[:, h * 128:(h + 1) * 128], MBh[:, off:off + 512]),
                                 (dB[:, h * 128:(h + 1) * 128], MBl[:, off:off + 512])]
                        pairs.append(dict(q=[qB[:, b, 0, 1, :], qB[:, b, 1, 1, :]],
                                          k=[kN[:, b, 0, u * 128:(u + 1) * 128], kN[:, b, 1, u * 128:(u + 1) * 128]],
                                          v=vN[:, b, u, :], masks=masks,
                                          reads=[("kN", b), ("vN", b), ("qB", b, 0), ("qB", b, 1), "consts"]))
                    pe_used = self._attend_b(ph, pairs, S_ps, O_ps, pt, acc, 3, fill=(fill_ps[:], ones_b[:], sq[:]))
                    for m in range(2):
                        self._finalize_den(ph, acc[:, 0, m, :], ("acc", 0), ones_f[:], den_ps[:, m, :], ("S", 0),
                                           rden[:, m, :], ("rden", m), 128, start=(not pe_used))
                        ph.op("dve", lambda e, m=m: e.tensor_tensor(out=tt[:, m, :], in0=O_ps[:, m, :], in1=rden[:, m, :],
                                                                    op=ALU.mult),
                              reads=[("O", m), ("rden", m)], writes=[("tt", m)])
                    ph.op("dve", lambda e: e.scalar_tensor_tensor(tt[:, 0, :], tt[:, 1, :], ls[:, 2:3], tt[:, 0, :],
                                                                  ALU.mult, ALU.add),
                          reads=[("tt", 0), ("tt", 1), "ls"], writes=[("tt", 0)])
                    ph.op("act", lambda e: e.activation(out=sq[:], in_=tt[:, 0, :], func=AF.Square),
                          reads=[("tt", 0)], writes=["sq"])
                    ph.op("pe", lambda e: e.matmul(den_ps[:, 0, :], ones_b[:], sq[:], start=True, stop=True),
                          reads=["sq", "ones"], writes=[("S", 0)])
                    ph.op("dve", lambda e: e.tensor_scalar(rden[:, 0, :], den_ps[:, 0, :], 1.0 / 128, SUBLN_EPS,
                                                           ALU.mult, ALU.add),
                          reads=[("S", 0)], writes=[("rden", 0)])
                    ph.op("act", lambda e: e.activation(out=rden[:, 0, :], in_=rden[:, 0, :], func=AF.Sqrt),
                          reads=[("rden", 0)], writes=[("rden", 0)])
                    ph.op("dve", lambda e: e.reciprocal(rden[:, 0, :], rden[:, 0, :]),
                          reads=[("rden", 0)], writes=[("rden", 0)])
                    ph.op("dve", lambda e, b=b: e.scalar_tensor_tensor(ot[:, b, :], tt[:, 0, :], gs[:, 0:1], rden[:, 0, :],
                                                                       ALU.mult, ALU.mult),
                          reads=[("tt", 0), ("rden", 0), "gs"], writes=[("ot", b)])
                    ph.op("pool", lambda e, b=b, ao=ao, h=h: e.dma_start(
                        out=OT[256 + h * 128:256 + (h + 1) * 128, ao:ao + 512], in_=ot[:, b, :]),
                        reads=[("ot", b)], dma="st_oB%d" % b)
            ph.emit()

    def phase_wout(self, l, jobs):
        nc, dr = self.nc, self.dr
        with contextlib.ExitStack() as st:
            w = st.enter_context(nc.sbuf_tensor(self._nm("pw_w"), [128, 8, D], BF16))
            xt = st.enter_context(nc.sbuf_tensor(self._nm("pw_xt"), [128, 2, 8, 512], F32))
            ot = st.enter_context(nc.sbuf_tensor(self._nm("pw_ot"), [128, 2, 8, 512], BF16))
            ps = st.enter_context(nc.psum_tensor(self._nm("pw_ps"), [128, 4, 512], F32))
            ph = Phase(nc)
            wsrc = dr["w_out"][l].rearrange("(c p) n -> p c n", p=128)
            for c in range(8):
                ph.op("pool", lambda e, c=c: e.dma_start(out=w[:, c, :], in_=wsrc[:, c, :]), writes=[("w", c)],
                      dma="ld_w%d" % (c % 4))
            wres = [("w", c) for c in range(8)]
            k = 0
            for (s, L, own) in jobs:
                if own:
                    XT = dr["XL_o"].rearrange("(c p) t -> p c t", p=128)
                    OT = dr["OT_o"].rearrange("(c p) t -> p c t", p=128)
                    nt = self.LQ // 512
                else:
                    XT = self.X(l, s).rearrange("(c p) t -> p c t", p=128)
                    OT = dr["OT_" + s].rearrange("(c p) t -> p c t", p=128)
                    nt = L // 512
                for t in range(nt):
                    b = t % 2
                    ph.op("sp", lambda e, b=b, t=t, XT=XT: e.dma_start(out=xt[:, b], in_=XT[:, :, t * 512:(t + 1) * 512]),
                          writes=[("xt", b, c) for c in range(8)], dma="ld_xt%d" % b)
                    ph.op("sp", lambda e, b=b, t=t, OT=OT: e.dma_start(out=ot[:, b], in_=OT[:, :, t * 512:(t + 1) * 512]),
                          writes=[("ot", b)], dma="ld_ot%d" % b)
                    for oc in range(8):
                        pb = k % 4
                        k += 1
                        for c in range(8):
                            ph.op("pe", lambda e, pb=pb, c=c, oc=oc, b=b: e.matmul(
                                ps[:, pb, :], w[:, c, oc * 128:(oc + 1) * 128], ot[:, b, c, :], start=(c == 0), stop=(c == 7)),
                                reads=[("ot", b)] + wres if c == 0 else [], writes=[("ps", pb)])
                        ph.op("dve", lambda e, pb=pb, b=b, oc=oc: e.tensor_tensor(out=xt[:, b, oc, :], in0=ps[:, pb, :],
                                                                              in1=xt[:, b, oc, :], op=ALU.add),
                              reads=[("ps", pb)], writes=[("xt", b, oc)])
                    ph.op("pool", lambda e, b=b, t=t, XT=XT: e.dma_start(out=XT[:, :, t * 512:(t + 1) * 512], in_=xt[:, b]),
                          reads=[("xt", b, c) for c in range(8)], dma="st_xt%d" % b)
            ph.emit()

    def phase_mlp(self, l, jobs):
        nc, dr = self.nc, self.dr
        NT = 256
        NC = NT + 2
        any_own = any(j[2] for j in jobs)
        with contextlib.ExitStack() as st:
            wu = st.enter_context(nc.sbuf_tensor(self._nm("pm_wu"), [128, 8, 2 * D_FF], BF16))
            wd = st.enter_context(nc.sbuf_tensor(self._nm("pm_wd"), [128, 22, D], BF16))
            g = st.enter_context(nc.sbuf_tensor(self._nm("pm_g"), [128, 8], F32))
            cw = st.enter_context(nc.sbuf_tensor(self._nm("pm_cw"), [128, 3 * 44], F32))
            cb = st.enter_context(nc.sbuf_tensor(self._nm("pm_cb"), [128, 44], F32))
            ones = st.enter_context(nc.sbuf_tensor(self._nm("pm_ones"), [128, 128], BF16))
            xt = st.enter_context(nc.sbuf_tensor(self._nm("pm_xt"), [128, 2, 8, NC], F32))
            sq = st.enter_context(nc.sbuf_tensor(self._nm("pm_sq"), [128, 8, NC], BF16))
            r = st.enter_context(nc.sbuf_tensor(self._nm("pm_r"), [128, NC], F32))
            h = st.enter_context(nc.sbuf_tensor(self._nm("pm_h"), [128, 8, NC], BF16))
            tmp = st.enter_context(nc.sbuf_tensor(self._nm("pm_tmp"), [128, 2, 2, NT], F32))
            gt = st.enter_context(nc.sbuf_tensor(self._nm("pm_gt"), [128, 22, NT], BF16))
            xo = st.enter_context(nc.sbuf_tensor(self._nm("pm_xo"), [128, 2, 8, NT], F32))
            ss_ps = st.enter_context(nc.psum_tensor(self._nm("pm_ss"), [128, 512], F32))
            u_ps = st.enter_context(nc.psum_tensor(self._nm("pm_u"), [128, 4, 512], F32))
            o_ps = st.enter_context(nc.psum_tensor(self._nm("pm_o"), [128, 2, 512], F32))
            ph = Phase(nc)
            ph.op("pool", lambda e: e.memset(ones[:], 1.0), writes=["ones"])
            ph.op("sp", lambda e: e.dma_start(out=g[:], in_=dr["ln2"][:, l * 8:(l + 1) * 8]), writes=["params"], dma="ld_g")
            ph.op("sp", lambda e: e.dma_start(out=cw[:], in_=dr["convw"][:, l * 132:(l + 1) * 132]), writes=["params"], dma="ld_g")
            ph.op("sp", lambda e: e.dma_start(out=cb[:], in_=dr["convb"][:, l * 44:(l + 1) * 44]), writes=["params"], dma="ld_g")
            flag = st.enter_context(nc.sbuf_tensor(self._nm("pm_flag"), [128, 2], F32))
            if any_own:
                ph.op("sp", lambda e: e.dma_start(out=flag[:], in_=dr["c_flag"][:, :]), writes=["params"], dma="ld_g")
            wsrc = dr["w_up"][l].rearrange("(c p) n -> p c n", p=128)
            for c in range(8):
                for n0 in range(0, 2 * D_FF, 1408):
                    ph.op("pool", lambda e, c=c, n0=n0: e.dma_start(out=wu[:, c, n0:n0 + 1408], in_=wsrc[:, c, n0:n0 + 1408]),
                          writes=[("wu", c, n0)], dma="ld_w%d" % (c % 4))
            wures = [("wu", c, n0) for c in range(8) for n0 in range(0, 2 * D_FF, 1408)]
            wdsrc = dr["w_down"][l].rearrange("(c p) n -> p c n", p=128)
            for c in range(22):
                ph.op("pool", lambda e, c=c: e.dma_start(out=wd[:, c, :], in_=wdsrc[:, c, :]), writes=[("wd", c)],
                      dma="ld_w%d" % (c % 4))
            wdres = [("wd", c) for c in range(22)]
            uk = 0
            ok = 0
            for (s, L, own) in jobs:
                if own:
                    XT = dr["XL_o"].rearrange("(c p) t -> p c t", p=128)
                    XO = dr["XW_o"].rearrange("(c p) t -> p c t", p=128)
                    nt = self.shard // NT
                else:
                    XT = self.X(l, s).rearrange("(c p) t -> p c t", p=128)
                    XO = self.X(l + 1, s).rearrange("(c p) t -> p c t", p=128)
                    nt = L // NT
                cb0 = QM if own else 0
                for t in range(nt):
                    b = t % 2
                    t0 = t * NT
                    lo = 1 if (t == 0 and not own) else 0
                    hi = NC - 1 if (t == nt - 1 and not own) else NC
                    if lo:
                        ph.op("pool", lambda e, b=b: e.memset(xt[:, b, :, 0:1], 0.0), writes=[("xt", b)])
                    if hi != NC:
                        ph.op("pool", lambda e, b=b: e.memset(xt[:, b, :, NC - 1:NC], 0.0), writes=[("xt", b)])
                    ph.op("sp", lambda e, b=b, t0=t0, lo=lo, hi=hi, XT=XT, cb0=cb0: e.dma_start(out=xt[:, b, :, lo:hi],
                                                                           in_=XT[:, :, cb0 + t0 - 1 + lo:cb0 + t0 - 1 + hi]),
                          writes=[("xt", b)] if not (lo or hi != NC) else [("xt", b), ("xtedge", b)], dma="ld_xt%d" % b)
                    if own and t == 0:
                        ph.op("dve", lambda e, b=b: e.tensor_scalar(xt[:, b, :, 0:1], xt[:, b, :, 0:1], flag[:, 0:1], None,
                                                                    ALU.mult), reads=[("xt", b), "params"], writes=[("xt", b)])
                    if own and t == nt - 1:
                        ph.op("dve", lambda e, b=b: e.tensor_scalar(xt[:, b, :, NC - 1:NC], xt[:, b, :, NC - 1:NC],
                                                                    flag[:, 1:2], None, ALU.mult),
                              reads=[("xt", b), "params"], writes=[("xt", b)])
                    self._rmsnorm_tile(ph, xt[:, b], ("xt", b), NC, sq[:], ones[:], ss_ps[:, 0:NC], "ss", r[:], "r",
                                       lambda c: h[:, c, :], lambda c: ("h", c), lambda c: g[:, c:c + 1], D, EPS)
                    hres = [("h", c) for c in range(8)]
                    for p in range(22):
                        for part in range(2):
                            j = p + 22 * part
                            ub = uk % 4
                            uk += 1
                            for c in range(8):
                                ph.op("pe", lambda e, ub=ub, c=c, j=j: e.matmul(
                                    u_ps[:, ub, 0:NC], wu[:, c, j * 128:(j + 1) * 128], h[:, c, :], start=(c == 0), stop=(c == 7)),
                                    reads=hres + wures if c == 0 else [], writes=[("u", ub)])
                            tb = p % 2
                            tm = tmp[:, tb, part, :]
                            tres = ("tmp", tb, part)
                            ph.op("act", lambda e, tm=tm, ub=ub, j=j: e.activation(
                                out=tm, in_=u_ps[:, ub, 1:NT + 1], func=AF.Identity, bias=cb[:, j:j + 1],
                                scale=cw[:, 44 + j:45 + j]), reads=[("u", ub), "params"], writes=[tres])
                            ph.op("dve", lambda e, tm=tm, ub=ub, j=j: e.scalar_tensor_tensor(
                                tm, u_ps[:, ub, 0:NT], cw[:, j:j + 1], tm, ALU.mult, ALU.add),
                                reads=[("u", ub), "params"], writes=[tres])
                            ph.op("dve", lambda e, tm=tm, ub=ub, j=j: e.scalar_tensor_tensor(
                                tm, u_ps[:, ub, 2:NT + 2], cw[:, 88 + j:89 + j], tm, ALU.mult, ALU.add),
                                reads=[("u", ub), "params"], writes=[tres])
                        tb = p % 2
                        ph.op("act", lambda e, tb=tb: e.activation(out=tmp[:, tb, 0, :], in_=tmp[:, tb, 0, :], func=AF.Silu),
                              reads=[("tmp", tb, 0)], writes=[("tmp", tb, 0)])
                        ph.op("pool", lambda e, tb=tb, p=p: e.tensor_tensor(out=gt[:, p, :], in0=tmp[:, tb, 0, :],
                                                                          in1=tmp[:, tb, 1, :], op=ALU.mult),
                              reads=[("tmp", tb, 0), ("tmp", tb, 1)], writes=[("gt", p)])
                    gres = [("gt", p) for p in range(22)]
                    for oc in range(8):
                        ob = ok % 2
                        ok += 1
                        for c in range(22):
                            ph.op("pe", lambda e, ob=ob, c=c, oc=oc: e.matmul(
                                o_ps[:, ob, 0:NT], wd[:, c, oc * 128:(oc + 1) * 128], gt[:, c, :], start=(c == 0), stop=(c == 21)),
                                reads=gres + wdres if c == 0 else [], writes=[("o", ob)])
                        ph.op("dve", lambda e, ob=ob, b=b, oc=oc: e.tensor_tensor(out=xo[:, b, oc, :], in0=o_ps[:, ob, 0:NT],
                                                                              in1=xt[:, b, oc, 1:NT + 1], op=ALU.add),
                              reads=[("o", ob), ("xt", b)], writes=[("xo", b, oc)])
                    ph.op("pool", lambda e, b=b, t0=t0, XO=XO: e.dma_start(out=XO[:, :, t0:t0 + NT], in_=xo[:, b]),
                          reads=[("xo", b, c) for c in range(8)], dma="st_xo%d" % b)
            ph.emit()

    def phase_final(self, s, L, own=False):
        nc, dr = self.nc, self.dr
        if own:
            XT = dr["XW_o"].rearrange("(c p) t -> p c t", p=128)
            y = dr["y_o"]
            nt = self.shard // 512
        else:
            XT = self.X(self.depth, s).rearrange("(c p) t -> p c t", p=128)
            y = dr["y_" + s]
            nt = L // 512
        with contextlib.ExitStack() as st:
            ident = st.enter_context(nc.sbuf_tensor(self._nm("pf_id"), [128, 128], F32))
            g = st.enter_context(nc.sbuf_tensor(self._nm("pf_g"), [128, 8], F32))
            ones = st.enter_context(nc.sbuf_tensor(self._nm("pf_ones"), [128, 128], BF16))
            xt = st.enter_context(nc.sbuf_tensor(self._nm("pf_xt"), [128, 2, 8, 512], F32))
            sq = st.enter_context(nc.sbuf_tensor(self._nm("pf_sq"), [128, 8, 512], BF16))
            r = st.enter_context(nc.sbuf_tensor(self._nm("pf_r"), [128, 512], F32))
            h = st.enter_context(nc.sbuf_tensor(self._nm("pf_h"), [128, 8, 512], F32))
            yo = st.enter_context(nc.sbuf_tensor(self._nm("pf_yo"), [128, 2, 4, D], F32))
            ss_ps = st.enter_context(nc.psum_tensor(self._nm("pf_ss"), [128, 512], F32))
            ps = st.enter_context(nc.psum_tensor(self._nm("pf_ps"), [128, 3, 2, 512], F32))
            ph = Phase(nc)
            ph.op("pool", lambda e: e.memset(ones[:], 1.0), writes=["ones"])
            ph.op("sp", lambda e: e.dma_start(out=ident[:], in_=dr["c_ident"][:, :]), writes=["ident"], dma="ld_g")
            ph.op("sp", lambda e: e.dma_start(out=g[:], in_=dr["lnf"][:, :]), writes=["params"], dma="ld_g")
            k = 0
            for t in range(nt):
                b = t % 2
                ph.op("sp", lambda e, b=b, t=t: e.dma_start(out=xt[:, b], in_=XT[:, :, t * 512:(t + 1) * 512]),
                      writes=[("xt", b)], dma="ld_xt%d" % b)
                self._rmsnorm_tile(ph, xt[:, b], ("xt", b), 512, sq[:], ones[:], ss_ps[:], "ss", r[:], "r",
                                   lambda c: h[:, c, :], lambda c: ("h", c), lambda c: g[:, c:c + 1], D, EPS)
                for sub in range(4):
                    pb = k % 3
                    k += 1
                    for c in range(8):
                        ph.op("pe", lambda e, pb=pb, c=c, sub=sub: e.transpose(
                            ps[:, pb, c // 4, (c % 4) * 128:(c % 4 + 1) * 128], h[:, c, sub * 128:(sub + 1) * 128], ident[:]),
                            reads=[("h", c), "ident"], writes=[("ps", pb)])
                    if sub % 2 == 0:
                        ph.op("act", lambda e, pb=pb, b=b, sub=sub: e.copy(
                            out=yo[:, b, sub, :], in_=ps[:, pb].rearrange("p a n -> p (a n)")),
                            reads=[("ps", pb)], writes=[("yo", b, sub)])
                    else:
                        ph.op("dve", lambda e, pb=pb, b=b, sub=sub: e.tensor_copy(
                            out=yo[:, b, sub, :], in_=ps[:, pb].rearrange("p a n -> p (a n)")),
                            reads=[("ps", pb)], writes=[("yo", b, sub)])
                dst = y[t * 512:(t + 1) * 512, :].rearrange("(s p) f -> p s f", p=128)
                ph.op("pool", lambda e, b=b, dst=dst: e.dma_start(out=dst, in_=yo[:, b]),
                      reads=[("yo", b, sub) for sub in range(4)], dma="st_y%d" % b)
            ph.emit()

    def _nm(self, base):
        self._cnt = getattr(self, "_cnt", 0) + 1
        return "%s_%d" % (base, self._cnt)

    def build(self):
        import os
        lim = int(os.environ.get("KPH", "1000"))
        plist = []
        last = self.depth - 1
        for (s, L) in self.seqs:
            if s == "p":
                plist.append(lambda s=s, L=L: self.phase_init_pads(s, L, last))
            plist.append(lambda s=s, L=L: self.phase_transpose_in(s, L))
        for l in range(self.depth):
            plist.append(lambda l=l: self.phase_proj(l, [(s, L) for (s, L) in self.seqs]))
            for (s, L) in self.seqs:
                own = (s == "p") and l == last
                if own:
                    plist.append(lambda s=s, L=L, l=l: self.phase_localize(s, L, l))
                    plist.append(lambda s=s, L=L, l=l: self.phase_attn_ac(l, s, L, own=True))
                    plist.append(lambda s=s, L=L, l=l: self.phase_attn_b_own(l, s, L))
                else:
                    plist.append(lambda s=s, L=L, l=l: self.phase_attn_ac(l, s, L))
                    plist.append(lambda s=s, L=L, l=l: self.phase_attn_b(l, s, L))
            jobs = [(s, L, (s == "p") and l == last) for (s, L) in self.seqs]
            plist.append(lambda l=l, jobs=jobs: self.phase_wout(l, jobs))
            plist.append(lambda l=l, jobs=jobs: self.phase_mlp(l, jobs))
        for (s, L) in self.seqs:
            plist.append(lambda s=s, L=L: self.phase_final(s, L, own=(s == "p")))
        for f in plist[:lim]:
            f()
        return self.nc


def _toeplitz(width, X, f):
    kk = np.arange(128)[:, None]
    col = np.arange(width)[None, :]
    return f(col - kk - X)


def make_consts(Lmax):
    c = {}
    c["c_ident"] = np.eye(128, dtype=np.float32)
    slA = 2.0 ** (-8.0 * np.arange(1, 5) / 4)
    slC = (2.0 ** (-8.0 * np.arange(1, 13) / 12)).astype(np.float32).astype(np.float64)
    eye = np.eye(128)
    c["c_diagA"] = _bf(np.concatenate([eye * (-8.0 * s) for s in slA], 1))
    c["c_diagB"] = _bf(np.concatenate([eye * (-8.0 * s) for s in slA], 1))
    dC = []
    for gh in range(12):
        v = -8.0 * slC[gh]
        hi = float(_bf(v).astype(np.float32))
        lo = float(_bf(v - hi).astype(np.float32))
        dC += [eye * hi, eye * lo]
    c["c_diagC"] = _bf(np.concatenate(dC, 1))

    def band(W, d):
        def f(delta):
            ad = np.abs(delta)
            ok = (ad <= W) & (delta % d == 0)
            return np.where(ok, ad, BIG).astype(np.float32)
        return f
    c["c_MA"] = _bf(_toeplitz(1152, 512, band(128, 1)))
    c["c_MC0"] = _bf(_toeplitz(1152, 128 * 5 - 128, band(64, 1)))
    c["c_MC1"] = _bf(_toeplitz(1408, 128 * 7 - 256, band(256, 4)))
    c["c_MC2"] = _bf(_toeplitz(2944, 128 * 19 - 1024, band(1024, 16)))
    c["c_MBh"] = _bf(_toeplitz(896, 384, lambda dl: (2 * (np.abs(dl) // 2)).astype(np.float32)))
    c["c_MBl"] = _bf(_toeplitz(896, 384, lambda dl: (np.abs(dl) % 2).astype(np.float32)))
    pos = np.arange(Lmax)
    hi = (pos // 128) * 128.0
    lo = (pos % 128) * 1.0
    kaug = np.zeros((4, 4, Lmax), np.float32)
    qaug = np.zeros((4, 2, 4, Lmax), np.float32)
    for h in range(4):
        s8 = 8.0 * slA[h]
        kaug[h] = np.stack([np.ones(Lmax), np.ones(Lmax), s8 * hi, s8 * lo])
        base = np.stack([-s8 * hi, -s8 * lo, np.ones(Lmax), np.ones(Lmax)])
        qaug[h, 0] = base
        qaug[h, 1] = -base
    c["c_kaug"] = _bf(kaug)
    c["c_qaug"] = _bf(qaug)
    return c


def make_core_consts(core, L, SH):
    LQ, LK = SH + 2 * QM, SH + 2 * KM
    tok0 = core * SH
    c = {}
    kstart = tok0 - KM + 128 * np.arange(LK // 128)
    valid = (kstart >= 0) & (kstart < L)
    c["c_kbias"] = np.ascontiguousarray(np.broadcast_to(np.where(valid, 0.0, KBIAS_NEG).astype(np.float32)[None, :],
                                                        (128, LK // 128)))
    slA = 2.0 ** (-8.0 * np.arange(1, 5) / 4)
    pos = np.arange(L)
    hi = (pos // 128) * 128.0
    lo = (pos % 128) * 1.0
    nch = LQ // 512
    kaug = np.zeros((nch, 4, 4, L), np.float32)
    for ci in range(nch):
        A = tok0 - QM + 512 * ci
        blk0 = (pos // 128) * 128
        left = blk0 + 127 < A
        right = blk0 > A + 511
        sign = np.where(left, 1.0, np.where(right, -1.0, 0.0))
        near = (sign == 0)
        for h in range(4):
            s8 = 8.0 * slA[h]
            rows = np.stack([np.ones(L), np.ones(L), s8 * hi, s8 * lo]) * sign[None, :]
            rows[2, near] = -262144.0
            kaug[ci, h] = rows
    c["c_kaugL"] = _bf(kaug)
    qpos = np.clip(tok0 - QM + np.arange(LQ), 0, L - 1)
    qhi = (qpos // 128) * 128.0
    qlo = (qpos % 128) * 1.0
    qaug = np.zeros((4, 4, LQ), np.float32)
    for h in range(4):
        s8 = 8.0 * slA[h]
        qaug[h] = np.stack([-s8 * qhi, -s8 * qlo, np.ones(LQ), np.ones(LQ)])
    c["c_qaugL"] = _bf(qaug)
    fl = np.array([0.0 if core == 0 else 1.0, 0.0 if (tok0 + SH) >= L else 1.0], np.float32)
    c["c_flag"] = np.ascontiguousarray(np.broadcast_to(fl[None, :], (128, 2)))
    return c


def layout_params(p, depth):
    out = {}

    def chunks(v, n):
        v = np.asarray(v, np.float32).reshape(-1, n, 128)
        return np.ascontiguousarray(v.transpose(2, 0, 1).reshape(128, -1))
    out["ln1"] = chunks(p["ln1"], 8)
    out["ln2"] = chunks(p["ln2"], 8)
    out["lnf"] = chunks(np.asarray(p["ln_f"])[None], 8)
    out["subln"] = np.ascontiguousarray(np.asarray(p["subln"], np.float32).T)
    out["sink"] = np.ascontiguousarray(np.broadcast_to(np.asarray(p["a_sink"], np.float32).reshape(1, -1), (128, depth * 4)))
    lam = np.stack([np.asarray(p[k], np.float32) for k in ("lam_q1", "lam_k1", "lam_q2", "lam_k2")], 1)
    out["lam"] = np.ascontiguousarray(np.broadcast_to(lam.reshape(1, -1), (128, depth * 256)))
    cw = np.asarray(p["conv_w"], np.float32).reshape(depth, 3, 44, 128)
    out["convw"] = np.ascontiguousarray(cw.transpose(3, 0, 1, 2).reshape(128, -1))
    cb = np.asarray(p["conv_b"], np.float32).reshape(depth, 44, 128)
    out["convb"] = np.ascontiguousarray(cb.transpose(2, 0, 1).reshape(128, -1))
    for k in ("w_in", "w_out", "w_up", "w_down"):
        out[k] = np.ascontiguousarray(np.asarray(p[k], np.float32))
    return out


_CACHE = {}


def run(seq_inputs, params, depth=DEPTH, n_cores=8):
    seqs = [(k, v.shape[0]) for k, v in seq_inputs[0].items()]
    key = (tuple(seqs), depth)
    if key not in _CACHE:
        _CACHE[key] = Builder(seqs, depth).build()
    nc = _CACHE[key]
    Lmax = max(L for _, L in seqs)
    shared = dict(make_consts(Lmax))
    shared.update(layout_params(params, depth))
    in_maps = []
    Lp = dict(seqs).get("p")
    for c in range(n_cores):
        m = dict(shared)
        if Lp:
            m.update(make_core_consts(c, Lp, Lp // 8))
        for k, v in seq_inputs[c].items():
            m["x_" + k] = np.ascontiguousarray(np.asarray(v, np.float32))
        in_maps.append(m)
    import os
    if os.environ.get("KTRACE"):
        res = run_bass_kernel_spmd(nc, in_maps, core_ids=list(range(n_cores)), trace=True)
        print("EXEC_TIME_NS", res.exec_time_ns, flush=True)
    else:
        res = run_bass_kernel_spmd(nc, in_maps, core_ids=list(range(n_cores)))
    return res.results


def kernel(x_prompt, x_sample, ln1, w_in, a_sink, lam_q1, lam_k1, lam_q2, lam_k2, subln,
           w_out, ln2, w_up, conv_w, conv_b, w_down, ln_f):
    params = dict(ln1=ln1, w_in=w_in, a_sink=a_sink, lam_q1=lam_q1, lam_k1=lam_k1, lam_q2=lam_q2, lam_k2=lam_k2,
                  subln=subln, w_out=w_out, ln2=ln2, w_up=w_up, conv_w=conv_w, conv_b=conv_b, w_down=w_down, ln_f=ln_f)
    x_prompt = np.asarray(x_prompt, np.float32)
    x_sample = np.asarray(x_sample, np.float32)
    seq_inputs = [{"s": x_sample[c], "p": x_prompt[0]} for c in range(8)]
    res = run(seq_inputs, params)
    y_sample = np.stack([res[c]["y_s"] for c in range(8)], 0)
    y_prompt = np.concatenate([res[c]["y_o"] for c in range(8)], 0)[None]
    return (y_prompt.astype(np.float32), y_sample.astype(np.float32))
```

```python
import contextlib
import numpy as np
import ml_dtypes
import concourse.bass as bass
import concourse.mybir as mybir
from concourse.bass_utils import run_bass_kernel_spmd

F32 = mybir.dt.float32
BF16 = mybir.dt.bfloat16
AF = mybir.ActivationFunctionType
ALU = mybir.AluOpType

D = 1024
DEPTH = 2
HD = 64
D_IN = 4352
D_FF = 2816
EPS = 1e-6
SUBLN_EPS = 1e-5
BIG = float(2 ** 20)
B_SKIP = 60.0
PE_FILL = 1
QM = 512
KM = 1536
KBIAS_NEG = -30000.0
SAME_ENG_SYNC = True

W_COLS_T = [(0, 256), (256, 128), (512, 512), (1024, 512), (2048, 768), (2816, 768)]
QT_AQ, QT_AK, QT_BQ, QT_BK, QT_CQ, QT_CK = 0, 256, 384, 896, 1408, 2176
QT_ROWS = 2944
W_COLS_V = [(384, 128), (1536, 512), (3584, 768)]
VT_AV, VT_BV, VT_CV = 0, 128, 640
VT_COLS = 1408


class _Op:
    __slots__ = ("eng", "fn", "deps", "dma_key", "needs_inc", "sem", "val", "idx")

    def __init__(self, eng, fn, dma_key):
        self.eng = eng
        self.fn = fn
        self.deps = []
        self.dma_key = dma_key
        self.needs_inc = dma_key is not None
        self.sem = None
        self.val = 0


class Phase:
    ENGS = ("pe", "act", "dve", "pool", "sp")

    def __init__(self, nc):
        self.nc = nc
        self.ops = {e: [] for e in self.ENGS}
        self.lastw = {}
        self.readers = {}
        self.last_dma = {}

    def pid(self, e):
        c = self.__dict__.setdefault("_pidc", {})
        if id(e) not in c:
            c[id(e)] = e.partition_id()
        return c[id(e)]

    def op(self, eng, fn, reads=(), writes=(), dma=None):
        o = _Op(eng, fn, dma)
        deps = []
        for r in reads:
            w = self.lastw.get(r)
            if w is not None:
                deps.append(w)
        for r in writes:
            w = self.lastw.get(r)
            if w is not None:
                deps.append(w)
            deps.extend(self.readers.get(r, ()))
        if dma is not None:
            prev = self.last_dma.get(dma)
            if prev is not None:
                deps.append(prev)
            self.last_dma[dma] = o
        seen = set()
        for d in deps:
            if d is o or id(d) in seen:
                continue
            seen.add(id(d))
            if d.dma_key is None and d.eng == eng and (eng == "pe" or not SAME_ENG_SYNC):
                continue
            o.deps.append(d)
            d.needs_inc = True
        for r in reads:
            self.readers.setdefault(r, []).append(o)
        for r in writes:
            self.lastw[r] = o
            self.readers[r] = []
        self.ops[eng].append(o)
        return o

    def emit(self):
        nc = self.nc
        reg = SEMREG
        LIM = 30000

        def assign(key, o, inc):
            ent = reg.get(key)
            if ent is None or ent[1] + inc > LIM:
                ent = [nc.alloc_semaphore("k%d" % len(reg.setdefault("_all", []))), 0]
                reg["_all"].append(ent[0])
                reg[key] = ent
            ent[1] += inc
            o.sem = ent[0]
            o.val = ent[1]

        for e in self.ENGS:
            for o in self.ops[e]:
                if o.dma_key is None:
                    if o.needs_inc:
                        assign(("eng", e), o, 1)
                else:
                    assign(("dma", o.dma_key), o, 16)
        final = {}
        for e in self.ENGS:
            for o in self.ops[e]:
                if o.dma_key is not None:
                    final[id(o.sem)] = (o.sem, o.val)
        with nc.Block() as block:
            def make(e):
                def body(eng):
                    waited = {}
                    for o in self.ops[e]:
                        for d in o.deps:
                            if waited.get(id(d.sem), 0) >= d.val:
                                continue
                            waited[id(d.sem)] = d.val
                            eng.wait_ge(d.sem, d.val)
                        ins = o.fn(eng)
                        if o.needs_inc:
                            ins.then_inc(o.sem, 16 if o.dma_key is not None else 1)
                    if e == "sp":
                        for k, (sm, v) in final.items():
                            if waited.get(k, 0) < v:
                                eng.wait_ge(sm, v)
                return body

            block.tensor(make("pe"))
            block.scalar(make("act"))
            block.vector(make("dve"))
            block.gpsimd(make("pool"))
            block.sync(make("sp"))


SEMREG = {}


def _bf(a):
    return np.asarray(a, np.float32).astype(ml_dtypes.bfloat16)


class Builder:
    def __init__(self, seqs, depth=DEPTH):
        self.seqs = seqs
        self.depth = depth
        self.nc = bass.Bass("TRN2", target_bir_lowering=False)
        nc = self.nc
        SEMREG.clear()
        self.dr = {}

        def din(name, shape, dt=F32):
            self.dr[name] = nc.dram_tensor(name, list(shape), dt, kind="ExternalInput").ap()

        def dout(name, shape):
            self.dr[name] = nc.dram_tensor(name, list(shape), F32, kind="ExternalOutput").ap()

        def dtmp(name, shape, dt):
            self.dr[name] = nc.dram_tensor(name, list(shape), dt).ap()

        self.shard = None
        for (s, L) in seqs:
            din("x_" + s, (L, D))
            if s == "p":
                self.shard = L // 8
            else:
                dout("y_" + s, (L, D))
            dtmp("XT_" + s, (D, L + 2 * QM), F32)
            dtmp("XU_" + s, (D, L + 2 * QM), F32)
            dtmp("QT_" + s, (QT_ROWS, L + 2 * KM), BF16)
            dtmp("VT_" + s, (L + 2 * KM, VT_COLS), BF16)
            dtmp("OT_" + s, (D, L), BF16)
        if self.shard:
            SH = self.shard
            self.LQ = SH + 2 * QM
            self.LK = SH + 2 * KM
            dout("y_o", (SH, D))
            dtmp("XL_o", (D, self.LQ), F32)
            dtmp("XW_o", (D, SH), F32)
            dtmp("QT_o", (QT_ROWS, self.LK), BF16)
            dtmp("VT_o", (self.LK, VT_COLS), BF16)
            dtmp("OT_o", (D, self.LQ), BF16)
        Lmax = max(L for _, L in seqs)
        self.Lmax = Lmax
        din("w_in", (depth, D, D_IN))
        din("w_out", (depth, D, D))
        din("w_up", (depth, D, 2 * D_FF))
        din("w_down", (depth, D_FF, D))
        din("ln1", (128, depth * 8))
        din("ln2", (128, depth * 8))
        din("lnf", (128, 8))
        din("subln", (128, depth))
        din("sink", (128, depth * 4))
        din("lam", (128, depth * 4 * 64))
        din("convw", (128, depth * 3 * 44))
        din("convb", (128, depth * 44))
        din("c_ident", (128, 128))
        din("c_diagA", (128, 4 * 128), BF16)
        din("c_diagB", (128, 4 * 128), BF16)
        din("c_diagC", (128, 24 * 128), BF16)
        din("c_MA", (128, 1152), BF16)
        din("c_MC0", (128, 1152), BF16)
        din("c_MC1", (128, 1408), BF16)
        din("c_MC2", (128, 2944), BF16)
        din("c_MBh", (128, 896), BF16)
        din("c_MBl", (128, 896), BF16)
        din("c_kaug", (4, 4, Lmax), BF16)
        din("c_qaug", (4, 2, 4, Lmax), BF16)
        if self.shard:
            din("c_kbias", (128, self.LK // 128))
            din("c_kaugL", (self.LQ // 512, 4, 4, Lmax), BF16)
            din("c_qaugL", (4, 4, self.LQ), BF16)
            din("c_flag", (128, 2))

    def X(self, which, s):
        t = self.dr[("XT_", "XU_")[which % 2] + s]
        return t[:, QM:t.shape[1] - QM]

    def QTv(self, s):
        t = self.dr["QT_" + s]
        return t[:, KM:t.shape[1] - KM]

    def VTv(self, s):
        t = self.dr["VT_" + s]
        return t[KM:t.shape[0] - KM, :]

    def phase_init_pads(self, s, L, which):
        nc, dr = self.nc, self.dr
        with contextlib.ExitStack() as st:
            zb = st.enter_context(nc.sbuf_tensor(self._nm("pi_zb"), [128, 12, VT_COLS], BF16))
            zf = st.enter_context(nc.sbuf_tensor(self._nm("pi_zf"), [128, 8, QM], F32))
            zq = st.enter_context(nc.sbuf_tensor(self._nm("pi_zq"), [128, KM], BF16))
            ph = Phase(nc)
            ph.op("pool", lambda e: e.memset(zq[:], 0.0), writes=["zb"])
            ph.op("pool", lambda e: e.memset(zb[:], 0.0), writes=["zb"])
            ph.op("pool", lambda e: e.memset(zf[:], 0.0), writes=["zf"])
            QTf = dr["QT_" + s]
            VTf = dr["VT_" + s]
            Xf = dr[("XT_", "XU_")[which % 2] + s]
            k = 0
            for c0 in (0, KM + L):
                for j in range(QT_ROWS // 128):
                    ph.op("sp", lambda e, c0=c0, j=j: e.dma_start(out=QTf[j * 128:(j + 1) * 128, c0:c0 + KM],
                                                               in_=zq[:]),
                          reads=["zb"], dma="st_z%d" % (k % 4))
                    k += 1
                ph.op("sp", lambda e, c0=c0: e.dma_start(
                    out=VTf[c0:c0 + KM, :].rearrange("(u p) n -> p u n", p=128), in_=zb[:]), reads=["zb"], dma="st_z%d" % (k % 4))
                k += 1
            for c0 in (0, QM + L):
                ph.op("sp", lambda e, c0=c0: e.dma_start(
                    out=Xf[:, c0:c0 + QM].rearrange("(c p) t -> p c t", p=128), in_=zf[:]), reads=["zf"], dma="st_z%d" % (k % 4))
                k += 1
            ph.emit()

    def phase_localize(self, s, L, which):
        nc, dr = self.nc, self.dr
        SH, LQ, LK = self.shard, self.LQ, self.LK
        QTf = dr["QT_" + s]
        VTf = dr["VT_" + s]
        Xf = dr[("XT_", "XU_")[which % 2] + s]
        ph = Phase(nc)
        k = 0
        RQ = QT_ROWS // 4
        for j in range(4):
            ph.op("sp", lambda e, j=j: e.dma_start(out=dr["QT_o"][j * RQ:(j + 1) * RQ, :],
                                                   in_=QTf[j * RQ:(j + 1) * RQ, bass.ds(ph.pid(e) * SH, LK)]),
                  dma="cp%d" % (k % 4))
            k += 1
        for c0 in range(0, VT_COLS, VT_COLS // 4):
            ph.op("sp", lambda e, c0=c0: e.dma_start(out=dr["VT_o"][:, c0:c0 + VT_COLS // 4],
                                                     in_=VTf[bass.ds(ph.pid(e) * SH, LK), c0:c0 + VT_COLS // 4]),
                  dma="cp%d" % (k % 4))
            k += 1
        for j in range(2):
            ph.op("sp", lambda e, j=j: e.dma_start(out=dr["XL_o"][j * 512:(j + 1) * 512, :],
                                                   in_=Xf[j * 512:(j + 1) * 512, bass.ds(ph.pid(e) * SH, LQ)]),
                  dma="cp%d" % (k % 4))
            k += 1
        ph.emit()

    def phase_transpose_in(self, s, L):
        nc, dr = self.nc, self.dr
        x = dr["x_" + s]
        XT = self.X(0, s).rearrange("(c p) t -> p c t", p=128)
        nt = L // 512
        with contextlib.ExitStack() as st:
            ident = st.enter_context(nc.sbuf_tensor(self._nm("p0_id"), [128, 128], F32))
            xin = st.enter_context(nc.sbuf_tensor(self._nm("p0_xin"), [128, 2, 4, D], F32))
            xt = st.enter_context(nc.sbuf_tensor(self._nm("p0_xt"), [128, 2, 8, 512], F32))
            ps = st.enter_context(nc.psum_tensor(self._nm("p0_ps"), [128, 4, 512], F32))
            ph = Phase(nc)
            ph.op("sp", lambda e: e.dma_start(out=ident[:], in_=dr["c_ident"][:, :]), writes=["ident"], dma="ld_id")
            for t in range(nt):
                b = t % 2
                src = x[t * 512:(t + 1) * 512, :].rearrange("(s p) f -> p s f", p=128)
                ph.op("sp", lambda e, b=b, src=src: e.dma_start(out=xin[:, b], in_=src),
                      writes=[("xin", b)], dma="ld_xin%d" % b)
                for c in range(8):
                    pb = (t * 8 + c) % 4
                    for sub in range(4):
                        ph.op("pe", lambda e, pb=pb, sub=sub, b=b, c=c: e.transpose(
                            ps[:, pb, sub * 128:(sub + 1) * 128], xin[:, b, sub, c * 128:(c + 1) * 128], ident[:]),
                            reads=[("xin", b), "ident"], writes=[("ps", pb, sub)])
                    if c % 2 == 0:
                        ph.op("act", lambda e, pb=pb, b=b, c=c: e.copy(out=xt[:, b, c, :], in_=ps[:, pb, :]),
                              reads=[("ps", pb, q) for q in range(4)], writes=[("xt", b, c)])
                    else:
                        ph.op("dve", lambda e, pb=pb, b=b, c=c: e.tensor_copy(out=xt[:, b, c, :], in_=ps[:, pb, :]),
                              reads=[("ps", pb, q) for q in range(4)], writes=[("xt", b, c)])
                ph.op("pool", lambda e, b=b, t=t: e.dma_start(out=XT[:, :, t * 512:(t + 1) * 512], in_=xt[:, b]),
                      reads=[("xt", b, c) for c in range(8)], dma="st_xt%d" % b)
            ph.emit()

    def _rmsnorm_tile(self, ph, xt_ap, xt_res, ncols, sq, ones_bf, ss_ps, ss_res, r_ap, r_res, h_ap_fn, h_res_fn,
                      g_ap_fn, nfeat, eps):
        ph.op("act", lambda e: e.activation(out=sq, in_=xt_ap, func=AF.Square),
              reads=[xt_res], writes=["sq"])
        for c in range(8):
            ph.op("pe", lambda e, c=c: e.matmul(ss_ps, ones_bf, sq[:, c, :], start=(c == 0), stop=(c == 7)),
                  reads=["sq", "ones"], writes=[ss_res])
        ph.op("dve", lambda e: e.tensor_scalar(r_ap, ss_ps, 1.0 / nfeat, eps, ALU.mult, ALU.add),
              reads=[ss_res], writes=[r_res])
        ph.op("act", lambda e: e.activation(out=r_ap, in_=r_ap, func=AF.Sqrt), reads=[r_res], writes=[r_res])
        ph.op("dve", lambda e: e.reciprocal(r_ap, r_ap), reads=[r_res], writes=[r_res])
        for c in range(8):
            eng = "dve"
            ph.op(eng, lambda e, c=c: e.scalar_tensor_tensor(h_ap_fn(c), xt_ap[:, c, :], g_ap_fn(c), r_ap,
                                                              ALU.mult, ALU.mult),
                  reads=[xt_res, r_res, "params"], writes=[h_res_fn(c)])

    def phase_proj(self, l, jobs):
        nc, dr = self.nc, self.dr
        w_in = dr["w_in"]
        with contextlib.ExitStack() as st:
            w = st.enter_context(nc.sbuf_tensor(self._nm("p1_w"), [128, 8, D_IN], BF16))
            g = st.enter_context(nc.sbuf_tensor(self._nm("p1_g"), [128, 8], F32))
            ones = st.enter_context(nc.sbuf_tensor(self._nm("p1_ones"), [128, 128], BF16))
            xt = st.enter_context(nc.sbuf_tensor(self._nm("p1_xt"), [128, 2, 8, 512], F32))
            sq = st.enter_context(nc.sbuf_tensor(self._nm("p1_sq"), [128, 8, 512], BF16))
            r = st.enter_context(nc.sbuf_tensor(self._nm("p1_r"), [128, 512], F32))
            h = st.enter_context(nc.sbuf_tensor(self._nm("p1_h"), [128, 2, 8, 512], BF16))
            qt = st.enter_context(nc.sbuf_tensor(self._nm("p1_qt"), [128, 2, 23, 512], BF16))
            vt = st.enter_context(nc.sbuf_tensor(self._nm("p1_vt"), [128, 2, 4, VT_COLS], BF16))
            ss_ps = st.enter_context(nc.psum_tensor(self._nm("p1_ss"), [128, 512], F32))
            ps = st.enter_context(nc.psum_tensor(self._nm("p1_ps"), [128, 6, 512], F32))
            ph = Phase(nc)
            ph.op("pool", lambda e: e.memset(ones[:], 1.0), writes=["ones"])
            ph.op("sp", lambda e: e.dma_start(out=g[:], in_=dr["ln1"][:, l * 8:(l + 1) * 8]), writes=["params"], dma="ld_g")
            wsrc = w_in[l].rearrange("(c p) n -> p c n", p=128)
            for c in range(8):
                for (n0, n1) in ((0, 1536), (1536, 3072), (3072, D_IN)):
                    ph.op("pool", lambda e, c=c, n0=n0, n1=n1: e.dma_start(out=w[:, c, n0:n1], in_=wsrc[:, c, n0:n1]),
                          writes=[("w", c, n0)], dma="ld_w%d" % (c % 4))
            wres = [("w", c, n0) for c in range(8) for n0 in (0, 1536, 3072)]
            pcount = [0]

            def next_ps():
                pcount[0] += 1
                return pcount[0] % 6

            ecount = [0]

            def evac(ph, dst, src, reads, writes):
                ecount[0] += 1
                if ecount[0] % 2 == 0:
                    ph.op("act", lambda e: e.copy(out=dst, in_=src), reads=reads, writes=writes)
                else:
                    ph.op("dve", lambda e: e.tensor_copy(out=dst, in_=src), reads=reads, writes=writes)

            for (s, L) in jobs:
                XT = self.X(l, s).rearrange("(c p) t -> p c t", p=128)
                QT = self.QTv(s).rearrange("(j p) t -> p j t", p=128)
                VT = self.VTv(s)
                nt = L // 512
                for t in range(nt):
                    b = t % 2
                    ph.op("sp", lambda e, b=b, t=t, XT=XT: e.dma_start(out=xt[:, b], in_=XT[:, :, t * 512:(t + 1) * 512]),
                          writes=[("xt", b)], dma="ld_xt%d" % b)
                    self._rmsnorm_tile(ph, xt[:, b], ("xt", b), 512, sq[:], ones[:], ss_ps[:], "ss", r[:], "r",
                                       lambda c, b=b: h[:, b, c, :], lambda c, b=b: ("h", b, c),
                                       lambda c: g[:, c:c + 1], D, EPS)
                    hres = [("h", b, c) for c in range(8)]
                    j = 0
                    for (c0, n) in W_COLS_T:
                        for jj in range(n // 128):
                            pb = next_ps()
                            col = c0 + jj * 128
                            for c in range(8):
                                ph.op("pe", lambda e, pb=pb, c=c, col=col, b=b: e.matmul(
                                    ps[:, pb, :], w[:, c, col:col + 128], h[:, b, c, :], start=(c == 0), stop=(c == 7)),
                                    reads=hres + wres if c == 0 else [], writes=[("ps", pb)])
                            evac(ph, qt[:, b, j, :], ps[:, pb, :], [("ps", pb)], [("qt", b, j)])
                            j += 1
                    ph.op("pool", lambda e, b=b, t=t, QT=QT: e.dma_start(out=QT[:, :, t * 512:(t + 1) * 512], in_=qt[:, b]),
                          reads=[("qt", b, j) for j in range(23)], dma="st_qt%d" % b)
                    for sub in range(4):
                        for (c0, n), v0 in zip(W_COLS_V, (VT_AV, VT_BV, VT_CV)):
                            for n0 in range(0, n, 512):
                                nn = min(512, n - n0)
                                pb = next_ps()
                                for c in range(8):
                                    ph.op("pe", lambda e, pb=pb, c=c, col=c0 + n0, nn=nn, b=b, sub=sub: e.matmul(
                                        ps[:, pb, 0:nn], h[:, b, c, sub * 128:(sub + 1) * 128], w[:, c, col:col + nn],
                                        start=(c == 0), stop=(c == 7)),
                                        reads=hres + wres if c == 0 else [], writes=[("ps", pb)])
                                evac(ph, vt[:, b, sub, v0 + n0:v0 + n0 + nn], ps[:, pb, 0:nn], [("ps", pb)],
                                     [("vt", b, sub, v0 + n0)])
                    vsrc = VT[t * 512:(t + 1) * 512, :].rearrange("(s p) n -> p s n", p=128)
                    ph.op("pool", lambda e, b=b, vsrc=vsrc: e.dma_start(out=vsrc, in_=vt[:, b]),
                          reads=[("vt", b, sub, v) for sub in range(4) for v in (0, 128, 640, 1152)], dma="st_vt%d" % b)
            ph.emit()

    def _attend(self, ph, items, nout, S_ps, O_ps, pt, acc, dv, tag, fill=None):
        n = len(items)
        first = {}
        last = {}
        for i, it in enumerate(items):
            first.setdefault(it["out"], i)
            last[it["out"]] = i
        NS = S_ps.shape[1]
        NP = pt.shape[1]

        def do_s(i):
            it = items[i]
            sb = i % NS
            nm = len(it["masks"])
            ph.op("pe", lambda e: e.matmul(S_ps[:, sb, :], it["k"], it["q"], start=True, stop=(nm == 0)),
                  reads=it["reads"], writes=[("S", sb)])
            for mi, (ml, mr) in enumerate(it["masks"]):
                ph.op("pe", lambda e, ml=ml, mr=mr, mi=mi: e.matmul(S_ps[:, sb, :], ml, mr, start=False,
                                                                    stop=(mi == nm - 1)),
                      reads=["consts"], writes=[("S", sb)])
            if it.get("bias") is not None:
                ph.op("act", lambda e: e.activation(out=pt[:, i % NP, :], in_=S_ps[:, sb, :], func=AF.Exp, scale=0.125,
                                                    bias=it["bias"]),
                      reads=[("S", sb), "consts"], writes=[("pt", i % NP)])
            else:
                ph.op("act", lambda e: e.activation(out=pt[:, i % NP, :], in_=S_ps[:, sb, :], func=AF.Exp, scale=0.125),
                      reads=[("S", sb)], writes=[("pt", i % NP)])

        cnt = {}

        def do_pv(i):
            it = items[i]
            o = it["out"]
            par = cnt.get(o, 0) % 2
            fresh = cnt.get(o, 0) < 2
            cnt[o] = cnt.get(o, 0) + 1
            if fresh:
                ph.op("dve", lambda e: e.tensor_copy(out=acc[:, o, par, :], in_=pt[:, i % NP, :]),
                      reads=[("pt", i % NP)], writes=[("acc", o, par)])
            else:
                ph.op("dve", lambda e: e.tensor_tensor(out=acc[:, o, par, :], in0=acc[:, o, par, :],
                                                      in1=pt[:, i % NP, :], op=ALU.add),
                      reads=[("pt", i % NP)], writes=[("acc", o, par)])
            if fill is not None:
                for _ in range(PE_FILL):
                    ph.op("pe", lambda e: e.matmul(fill[0], fill[1], fill[2], start=True, stop=True))
            ph.op("pe", lambda e: e.matmul(O_ps[0:dv, o, :], it["v"], pt[:, i % NP, :], start=(first[o] == i),
                                           stop=(last[o] == i)),
                  reads=[("pt", i % NP)] + it["reads"], writes=[("O", o)])

        LA = 2
        for i in range(n):
            do_s(i)
            if i >= LA:
                do_pv(i - LA)
        for j in range(max(0, n - LA), n):
            do_pv(j)
        for o, c in cnt.items():
            if c >= 2:
                ph.op("dve", lambda e, o=o: e.tensor_tensor(out=acc[:, o, 0, :], in0=acc[:, o, 0, :],
                                                            in1=acc[:, o, 1, :], op=ALU.add),
                      reads=[("acc", o, 1)], writes=[("acc", o, 0)])

    def _attend_b(self, ph, pairs, S_ps, O_ps, pt, acc, NPB, den_ps=None, ones_b=None, pe_every=4, fill=None):
        NSB = S_ps.shape[1] // 2
        n = len(pairs)

        def do_s(i):
            it = pairs[i]
            sb = 2 * (i % NSB)
            nm = len(it["masks"])
            for m in range(2):
                ph.op("pe", lambda e, m=m: e.matmul(S_ps[:, sb + m, :], it["k"][m], it["q"][m], start=True, stop=(nm == 0)),
                      reads=it["reads"], writes=[("S", i % NSB)])
                for mi, (ml, mr) in enumerate(it["masks"]):
                    ph.op("pe", lambda e, m=m, ml=ml, mr=mr, mi=mi: e.matmul(S_ps[:, sb + m, :], ml, mr, start=False,
                                                                         stop=(mi == nm - 1)),
                          reads=["consts"], writes=[("S", i % NSB)])
            ps0 = 2 * (i % NPB)
            ph.op("act", lambda e: e.activation(out=pt[:, ps0:ps0 + 2, :], in_=S_ps[:, sb:sb + 2, :], func=AF.Exp,
                                                scale=0.125),
                  reads=[("S", i % NSB)], writes=[("pt", i % NPB)])

        st8 = {"dve": 0, "pe": 0}

        def do_pv(i):
            it = pairs[i]
            ps0 = 2 * (i % NPB)
            if den_ps is not None and (i % pe_every) == pe_every - 1:
                for m in range(2):
                    ph.op("pe", lambda e, m=m, first=(st8["pe"] == 0): e.matmul(den_ps[:, m, :], ones_b, pt[:, ps0 + m, :],
                                                                               start=first, stop=False),
                          reads=[("pt", i % NPB), "ones"], writes=[("den", m)])
                st8["pe"] += 1
            else:
                par = st8["dve"] % 2
                if st8["dve"] < 2:
                    ph.op("dve", lambda e: e.tensor_copy(out=acc[:, par], in_=pt[:, ps0:ps0 + 2, :]),
                          reads=[("pt", i % NPB)], writes=[("acc", par)])
                else:
                    ph.op("dve", lambda e: e.tensor_tensor(out=acc[:, par], in0=acc[:, par], in1=pt[:, ps0:ps0 + 2, :],
                                                          op=ALU.add),
                          reads=[("pt", i % NPB)], writes=[("acc", par)])
                st8["dve"] += 1
            if fill is not None:
                for _ in range(PE_FILL):
                    ph.op("pe", lambda e: e.matmul(fill[0], fill[1], fill[2], start=True, stop=True))
            for m in range(2):
                ph.op("pe", lambda e, m=m: e.matmul(O_ps[:, m, :], it["v"], pt[:, ps0 + m, :], start=(i == 0),
                                                    stop=(i == n - 1)),
                      reads=[("pt", i % NPB)] + it["reads"], writes=[("O", m)])

        LA = NSB - 1
        for i in range(n):
            do_s(i)
            if i >= LA:
                do_pv(i - LA)
        for j in range(max(0, n - LA), n):
            do_pv(j)
        if st8["dve"] >= 2:
            ph.op("dve", lambda e: e.tensor_tensor(out=acc[:, 0], in0=acc[:, 0], in1=acc[:, 1], op=ALU.add),
                  reads=[("acc", 1)], writes=[("acc", 0)])
        return st8["pe"] > 0

    def _finalize_den(self, ph, acc_ap, acc_res, ones_f, den_ps, den_res, rden_ap, rden_res, rows, extra=None,
                      start=True):
        ph.op("pe", lambda e: e.matmul(den_ps, ones_f, acc_ap, start=start, stop=True),
              reads=[acc_res, "ones"], writes=[den_res])
        if extra is not None:
            ph.op("dve", lambda e: e.tensor_scalar(rden_ap, den_ps[0:rows], extra, None, ALU.add),
                  reads=[den_res, "params"], writes=[rden_res])
            ph.op("dve", lambda e: e.reciprocal(rden_ap, rden_ap), reads=[rden_res], writes=[rden_res])
        else:
            ph.op("dve", lambda e: e.reciprocal(rden_ap, den_ps[0:rows]), reads=[den_res], writes=[rden_res])

    def phase_attn_ac(self, l, s, L, own=False):
        nc, dr = self.nc, self.dr
        if own:
            QT, VT, OT = dr["QT_o"], dr["VT_o"], dr["OT_o"]
            nt = self.LQ // 512
            kshift = KM - QM
            L = self.LK
        else:
            QT = self.QTv(s)
            VT = self.VTv(s)
            OT = dr["OT_" + s]
            nt = L // 512
            kshift = 0
        nblk = L // 128
        GW = (128, 256, 1024)
        GN = (6, 8, 20)
        with contextlib.ExitStack() as st:
            ones_f = st.enter_context(nc.sbuf_tensor(self._nm("pa_onesf"), [128, 128], F32))
            esink = st.enter_context(nc.sbuf_tensor(self._nm("pa_sink"), [128, 4], F32))
            dA = st.enter_context(nc.sbuf_tensor(self._nm("pa_dA"), [128, 4 * 128], BF16))
            dC = st.enter_context(nc.sbuf_tensor(self._nm("pa_dC"), [128, 24 * 128], BF16))
            MA = st.enter_context(nc.sbuf_tensor(self._nm("pa_MA"), [128, 1152], BF16))
            MC0 = st.enter_context(nc.sbuf_tensor(self._nm("pa_MC0"), [128, 1152], BF16))
            MC1 = st.enter_context(nc.sbuf_tensor(self._nm("pa_MC1"), [128, 1408], BF16))
            MC2 = st.enter_context(nc.sbuf_tensor(self._nm("pa_MC2"), [128, 2944], BF16))
            kA = st.enter_context(nc.sbuf_tensor(self._nm("pa_kA"), [128, 2, 768], BF16))
            vA = st.enter_context(nc.sbuf_tensor(self._nm("pa_vA"), [128, 2, 6, 128], BF16))
            qA = st.enter_context(nc.sbuf_tensor(self._nm("pa_qA"), [128, 2, 2, 512], BF16))
            kC = st.enter_context(nc.sbuf_tensor(self._nm("pa_kC"), [128, 2, 2, 34 * 128], BF16))
            vC = st.enter_context(nc.sbuf_tensor(self._nm("pa_vC"), [128, 2, 34, 256], BF16))
            qC = st.enter_context(nc.sbuf_tensor(self._nm("pa_qC"), [128, 2, 3, 2, 512], BF16))
            pt = st.enter_context(nc.sbuf_tensor(self._nm("pa_pt"), [128, 4, 512], BF16))
            acc = st.enter_context(nc.sbuf_tensor(self._nm("pa_acc"), [128, 2, 2, 512], F32))
            rden = st.enter_context(nc.sbuf_tensor(self._nm("pa_rden"), [64, 512], F32))
            ot = st.enter_context(nc.sbuf_tensor(self._nm("pa_ot"), [64, 2, 8, 512], BF16))
            S_ps = st.enter_context(nc.psum_tensor(self._nm("pa_S"), [128, 4, 512], F32))
            O_ps = st.enter_context(nc.psum_tensor(self._nm("pa_O"), [128, 2, 512], F32))
            den_ps = st.enter_context(nc.psum_tensor(self._nm("pa_den"), [128, 512], F32))
            fill_ps = st.enter_context(nc.psum_tensor(self._nm("pa_fill"), [128, 512], F32))
            fillsrc = st.enter_context(nc.sbuf_tensor(self._nm("pa_fsrc"), [128, 512], BF16))
            ph = Phase(nc)
            kbias = None
            if own:
                kbias = st.enter_context(nc.sbuf_tensor(self._nm("pa_kbias"), [128, self.LK // 128], F32))
                ph.op("sp", lambda e: e.dma_start(out=kbias[:], in_=dr["c_kbias"][:, :]), writes=["consts"], dma="ld_c1")
            MC = (MC0, MC1, MC2)
            ph.op("pool", lambda e: e.memset(ones_f[:], 1.0), writes=["ones"])
            ph.op("pool", lambda e: e.memset(fillsrc[:], 0.5), writes=["consts"])
            fill = (fill_ps[:], dA[:, 0:128], fillsrc[:])
            ph.op("sp", lambda e: e.dma_start(out=esink[:], in_=dr["sink"][:, l * 4:(l + 1) * 4]), writes=["params"], dma="ld_c0")
            ph.op("act", lambda e: e.activation(out=esink[:], in_=esink[:], func=AF.Exp), reads=["params"], writes=["params"])
            for nm, tl in (("c_diagA", dA), ("c_diagC", dC), ("c_MA", MA), ("c_MC0", MC0), ("c_MC1", MC1), ("c_MC2", MC2)):
                ph.op("sp", lambda e, nm=nm, tl=tl: e.dma_start(out=tl[:], in_=dr[nm][:, :]), writes=["consts"], dma="ld_c1")
            goff = (0, 6, 14)
            oc = [0]
            for c in range(nt):
                b = c % 2
                a = c * 512 + kshift
                ao = c * 512
                u_lo = max(0, -((a - 128) // 128))
                u_hi = min(6, (L - (a - 128)) // 128)
                k0 = a - 128 + 128 * u_lo
                k1 = a - 128 + 128 * u_hi
                ph.op("sp", lambda e, b=b, k0=k0, k1=k1, u_lo=u_lo, u_hi=u_hi: e.dma_start(
                    out=kA[:, b, u_lo * 128:u_hi * 128], in_=QT[QT_AK:QT_AK + 128, k0:k1]),
                    writes=[("kA", b)], dma="ld_kA%d" % b)
                ph.op("sp", lambda e, b=b, k0=k0, k1=k1, u_lo=u_lo, u_hi=u_hi: e.dma_start(
                    out=vA[:, b, u_lo:u_hi, :],
                    in_=VT[k0:k1, VT_AV:VT_AV + 128].rearrange("(u p) n -> p u n", p=128)),
                    writes=[("vA", b)], dma="ld_vA%d" % b)
                for kvh in range(2):
                    for j in range(2):
                        r0 = QT_AQ + (2 * kvh + j) * 64
                        ph.op("sp", lambda e, b=b, kvh=kvh, j=j, r0=r0, a=a: e.dma_start(
                            out=qA[kvh * 64:(kvh + 1) * 64, b, j, :], in_=QT[r0:r0 + 64, a:a + 512]),
                            writes=[("qA", b, kvh, j)], dma="ld_qA%d" % b)
                cval = []
                for g in range(3):
                    ulo = max(0, -((a - GW[g]) // 128))
                    uhi = min(GN[g], (L - (a - GW[g])) // 128)
                    cval.append((ulo, uhi))
                    k0 = a - GW[g] + 128 * ulo
                    k1 = a - GW[g] + 128 * uhi
                    for pr in range(2):
                        r0 = QT_CK + g * 256 + pr * 128
                        ph.op("sp", lambda e, b=b, g=g, pr=pr, r0=r0, k0=k0, k1=k1, ulo=ulo, uhi=uhi: e.dma_start(
                            out=kC[:, b, pr, (goff[g] + ulo) * 128:(goff[g] + uhi) * 128], in_=QT[r0:r0 + 128, k0:k1]),
                            writes=[("kC", b, g, pr)], dma="ld_kC%d" % b)
                        r1 = QT_CQ + g * 256 + pr * 128
                        ph.op("sp", lambda e, b=b, g=g, pr=pr, r1=r1, a=a: e.dma_start(
                            out=qC[:, b, g, pr, :], in_=QT[r1:r1 + 128, a:a + 512]),
                            writes=[("qC", b, g, pr)], dma="ld_qC%d" % b)
                    ph.op("sp", lambda e, b=b, g=g, k0=k0, k1=k1, ulo=ulo, uhi=uhi: e.dma_start(
                        out=vC[:, b, goff[g] + ulo:goff[g] + uhi, :],
                        in_=VT[k0:k1, VT_CV + g * 256:VT_CV + (g + 1) * 256].rearrange("(u p) n -> p u n", p=128)),
                        writes=[("vC", b, g)], dma="ld_vC%d" % b)
                for hq in range(4):
                    kvh, j = hq // 2, hq % 2
                    items = []
                    for u in range(u_lo, u_hi):
                        items.append(dict(
                            q=qA[kvh * 64:(kvh + 1) * 64, b, j, :],
                            k=kA[kvh * 64:(kvh + 1) * 64, b, u * 128:(u + 1) * 128],
                            v=vA[:, b, u, kvh * 64:(kvh + 1) * 64],
                            masks=[(dA[:, hq * 128:(hq + 1) * 128], MA[:, 640 - 128 * u:640 - 128 * u + 512])],
                            bias=(kbias[:, (a - 128) // 128 + u:(a - 128) // 128 + u + 1] if own else None),
                            out=oc[0] % 2, reads=[("kA", b), ("vA", b), ("qA", b, kvh, j), "consts"]))
                    o = oc[0] % 2
                    oc[0] += 1
                    self._attend(ph, items, 1, S_ps, O_ps, pt, acc, 64, "A", fill=fill)
                    self._finalize_den(ph, acc[:, o, 0, :], ("acc", o, 0), ones_f[:], den_ps[:], "den", rden[:], "rden", 64,
                                       extra=esink[0:64, hq:hq + 1])
                    ph.op("dve", lambda e, o=o, b=b, hq=hq: e.tensor_tensor(out=ot[:, b, hq, :], in0=O_ps[0:64, o, :],
                                                                           in1=rden[:], op=ALU.mult),
                          reads=[("O", o), "rden"], writes=[("ot", b, hq)])
                for h in range(4):
                    pr, hp = h // 2, h % 2
                    items = []
                    for g in range(3):
                        ulo, uhi = cval[g]
                        for u in range(ulo, uhi):
                            off = 128 * (GN[g] - 1) - 128 * u
                            gi = (g * 4 + h) * 2
                            items.append(dict(
                                q=qC[hp * 64:(hp + 1) * 64, b, g, pr, :],
                                k=kC[hp * 64:(hp + 1) * 64, b, pr, (goff[g] + u) * 128:(goff[g] + u + 1) * 128],
                                v=vC[:, b, goff[g] + u, h * 64:(h + 1) * 64],
                                masks=[(dC[:, gi * 128:(gi + 1) * 128], MC[g][:, off:off + 512])],
                                bias=(kbias[:, (a - GW[g]) // 128 + u:(a - GW[g]) // 128 + u + 1] if own else None),
                                out=oc[0] % 2,
                                reads=[("kC", b, g, pr), ("vC", b, g), ("qC", b, g, pr), "consts"]))
                    o = oc[0] % 2
                    oc[0] += 1
                    self._attend(ph, items, 1, S_ps, O_ps, pt, acc, 64, "C", fill=fill)
                    self._finalize_den(ph, acc[:, o, 0, :], ("acc", o, 0), ones_f[:], den_ps[:], "den", rden[:], "rden", 64)
                    ph.op("dve", lambda e, o=o, b=b, h=h: e.tensor_tensor(out=ot[:, b, 4 + h, :], in0=O_ps[0:64, o, :],
                                                                         in1=rden[:], op=ALU.mult),
                          reads=[("O", o), "rden"], writes=[("ot", b, 4 + h)])
                ph.op("pool", lambda e, b=b, a=ao: e.dma_start(
                    out=OT[0:256, a:a + 512].rearrange("(h p) t -> p h t", p=64), in_=ot[:, b, 0:4, :]),
                    reads=[("ot", b, hh) for hh in range(4)], dma="st_oA%d" % b)
                ph.op("pool", lambda e, b=b, a=ao: e.dma_start(
                    out=OT[768:1024, a:a + 512].rearrange("(h p) t -> p h t", p=64), in_=ot[:, b, 4:8, :]),
                    reads=[("ot", b, 4 + hh) for hh in range(4)], dma="st_oC%d" % b)
            ph.emit()

    def phase_attn_b(self, l, s, L, own=False):
        nc, dr = self.nc, self.dr
        QT = self.QTv(s)
        VT = self.VTv(s)
        OT = dr["OT_" + s]
        nt = L // 512
        nblk = L // 128
        lam_init = 0.8 - 0.6 * float(np.exp(-0.3 * l))
        with contextlib.ExitStack() as st:
            ones_f = st.enter_context(nc.sbuf_tensor(self._nm("pb_onesf"), [128, 128], F32))
            ones_b = st.enter_context(nc.sbuf_tensor(self._nm("pb_onesb"), [128, 128], BF16))
            lamt = st.enter_context(nc.sbuf_tensor(self._nm("pb_lam"), [128, 4 * 64], F32))
            lt = st.enter_context(nc.sbuf_tensor(self._nm("pb_lt"), [128, 2 * 64], F32))
            ls = st.enter_context(nc.sbuf_tensor(self._nm("pb_ls"), [128, 4], F32))
            gs = st.enter_context(nc.sbuf_tensor(self._nm("pb_gs"), [128, 1], F32))
            dB = st.enter_context(nc.sbuf_tensor(self._nm("pb_dB"), [128, 4 * 128], BF16))
            MBh = st.enter_context(nc.sbuf_tensor(self._nm("pb_MBh"), [128, 896], BF16))
            MBl = st.enter_context(nc.sbuf_tensor(self._nm("pb_MBl"), [128, 896], BF16))
            kB = st.enter_context(nc.sbuf_tensor(self._nm("pb_kB"), [68, 2, L], BF16))
            vB = st.enter_context(nc.sbuf_tensor(self._nm("pb_vB"), [128, nblk, 128], BF16))
            qB = st.enter_context(nc.sbuf_tensor(self._nm("pb_qB"), [68, 2, 2, 3, 512], BF16))
            pt = st.enter_context(nc.sbuf_tensor(self._nm("pb_pt"), [128, 6, 512], BF16))
            acc = st.enter_context(nc.sbuf_tensor(self._nm("pb_acc"), [128, 2, 2, 512], F32))
            rden = st.enter_context(nc.sbuf_tensor(self._nm("pb_rden"), [128, 2, 512], F32))
            tt = st.enter_context(nc.sbuf_tensor(self._nm("pb_t"), [128, 2, 512], F32))
            sq = st.enter_context(nc.sbuf_tensor(self._nm("pb_sq"), [128, 512], BF16))
            ot = st.enter_context(nc.sbuf_tensor(self._nm("pb_ot"), [128, 2, 512], BF16))
            S_ps = st.enter_context(nc.psum_tensor(self._nm("pb_S"), [128, 4, 512], F32))
            fill_ps = st.enter_context(nc.psum_tensor(self._nm("pb_fill"), [128, 512], F32))
            O_ps = st.enter_context(nc.psum_tensor(self._nm("pb_O"), [128, 2, 512], F32))
            den_ps = S_ps[:, 0:2, :]
            ph = Phase(nc)
            ph.op("pool", lambda e: e.memset(ones_f[:], 1.0), writes=["ones"])
            ph.op("pool", lambda e: e.memset(ones_b[:], 1.0), writes=["ones"])
            ph.op("pool", lambda e: e.memset(qB[:], 0.0), writes=["qinit"])
            for nm, tl in (("c_diagB", dB), ("c_MBh", MBh), ("c_MBl", MBl)):
                ph.op("sp", lambda e, nm=nm, tl=tl: e.dma_start(out=tl[:], in_=dr[nm][:, :]), writes=["consts"], dma="ld_c1")
            ph.op("sp", lambda e: e.dma_start(out=lamt[:], in_=dr["lam"][:, l * 256:(l + 1) * 256]), writes=["lamt"], dma="ld_c0")
            ph.op("sp", lambda e: e.dma_start(out=gs[:], in_=dr["subln"][:, l:l + 1], allow_slow_non_contiguous=True), writes=["gs"], dma="ld_c0")
            ph.op("dve", lambda e: e.tensor_tensor(out=lt[:, 0:64], in0=lamt[:, 0:64], in1=lamt[:, 64:128], op=ALU.mult),
                  reads=["lamt"], writes=["lt"])
            ph.op("dve", lambda e: e.tensor_tensor(out=lt[:, 64:128], in0=lamt[:, 128:192], in1=lamt[:, 192:256], op=ALU.mult),
                  reads=["lamt"], writes=["lt"])
            ph.op("dve", lambda e: e.reduce_sum(ls[:, 0:1], lt[:, 0:64], mybir.AxisListType.X), reads=["lt"], writes=["ls"])
            ph.op("dve", lambda e: e.reduce_sum(ls[:, 1:2], lt[:, 64:128], mybir.AxisListType.X), reads=["lt"], writes=["ls"])
            ph.op("act", lambda e: e.activation(out=ls[:, 0:2], in_=ls[:, 0:2], func=AF.Exp), reads=["ls"], writes=["ls"])
            ph.op("dve", lambda e: e.tensor_tensor(out=ls[:, 2:3], in0=ls[:, 1:2], in1=ls[:, 0:1], op=ALU.subtract),
                  reads=["ls"], writes=["ls"])
            ph.op("dve", lambda e: e.tensor_scalar(ls[:, 2:3], ls[:, 2:3], -lam_init, None, ALU.add),
                  reads=["ls"], writes=["ls"])
            ph.op("dve", lambda e: e.tensor_scalar(gs[:], gs[:], 1.0 - lam_init, None, ALU.mult),
                  reads=["gs"], writes=["gs"])
            HCH = min(4096, L)
            VCH = min(32, nblk)
            for h in range(4):
                for m in range(2):
                    r0 = QT_BK + h * 128 + m * 64
                    for c0 in range(0, L, HCH):
                        ph.op("sp", lambda e, m=m, r0=r0, c0=c0: e.dma_start(out=kB[0:64, m, c0:c0 + HCH],
                                                                         in_=QT[r0:r0 + 64, c0:c0 + HCH]),
                              writes=[("kB", m)], dma="ld_kB%d" % m)
                    ph.op("sp", lambda e, m=m, h=h: e.dma_start(out=kB[64:68, m, :], in_=dr["c_kaug"][h, :, 0:L]),
                          writes=[("kB", m)], dma="ld_kB%d" % m)
                for c0 in range(0, nblk, VCH):
                    ph.op("sp", lambda e, h=h, c0=c0: e.dma_start(
                        out=vB[:, c0:c0 + VCH, :],
                        in_=VT[c0 * 128:(c0 + VCH) * 128, VT_BV + h * 128:VT_BV + (h + 1) * 128].rearrange(
                            "(u p) n -> p u n", p=128)),
                        writes=["vB"], dma="ld_vB")
                for c in range(nt):
                    b = c % 2
                    a = c * 512
                    for m in range(2):
                        r0 = QT_BQ + h * 128 + m * 64
                        for var in range(3):
                            ph.op("sp", lambda e, b=b, m=m, var=var, r0=r0, a=a: e.dma_start(
                                out=qB[0:64, b, m, var, :], in_=QT[r0:r0 + 64, a:a + 512]),
                                reads=["qinit"], writes=[("qB", b, m)], dma="ld_qB%d" % b)
                        for var in range(2):
                            ph.op("sp", lambda e, b=b, m=m, var=var, h=h, a=a: e.dma_start(
                                out=qB[64:68, b, m, var, :], in_=dr["c_qaug"][h, var, :, a:a + 512]),
                                reads=["qinit"], writes=[("qB", b, m)], dma="ld_qB%d" % b)
                    pairs = []
                    slope = 2.0 ** (-2.0 * (h + 1))
                    for kb in range(nblk):
                        k0 = kb * 128
                        dmin = max(0, k0 - (a + 511), a - (k0 + 127))
                        if slope * dmin >= B_SKIP:
                            continue
                        if kb < 4 * c:
                            var, masks = 0, []
                        elif kb > 4 * c + 3:
                            var, masks = 1, []
                        else:
                            u = kb - 4 * c
                            off = 384 - 128 * u
                            var = 2
                            masks = [(dB[:, h * 128:(h + 1) * 128], MBh[:, off:off + 512]),
                                     (dB[:, h * 128:(h + 1) * 128], MBl[:, off:off + 512])]
                        pairs.append(dict(q=[qB[:, b, 0, var, :], qB[:, b, 1, var, :]],
                                          k=[kB[:, 0, k0:k0 + 128], kB[:, 1, k0:k0 + 128]],
                                          v=vB[:, kb, :], masks=masks,
                                          reads=[("kB", 0), ("kB", 1), "vB", ("qB", b, 0), ("qB", b, 1), "consts"]))
                    pe_used = self._attend_b(ph, pairs, S_ps, O_ps, pt, acc, 3, fill=(fill_ps[:], ones_b[:], sq[:]))
                    for m in range(2):
                        self._finalize_den(ph, acc[:, 0, m, :], ("acc", 0), ones_f[:], den_ps[:, m, :], ("S", 0),
                                           rden[:, m, :], ("rden", m), 128, start=(not pe_used))
                        ph.op("dve", lambda e, m=m: e.tensor_tensor(out=tt[:, m, :], in0=O_ps[:, m, :], in1=rden[:, m, :],
                                                                    op=ALU.mult),
                              reads=[("O", m), ("rden", m)], writes=[("tt", m)])
                    ph.op("dve", lambda e: e.scalar_tensor_tensor(tt[:, 0, :], tt[:, 1, :], ls[:, 2:3], tt[:, 0, :],
                                                                  ALU.mult, ALU.add),
                          reads=[("tt", 0), ("tt", 1), "ls"], writes=[("tt", 0)])
                    ph.op("act", lambda e: e.activation(out=sq[:], in_=tt[:, 0, :], func=AF.Square),
                          reads=[("tt", 0)], writes=["sq"])
                    ph.op("pe", lambda e: e.matmul(den_ps[:, 0, :], ones_b[:], sq[:], start=True, stop=True),
                          reads=["sq", "ones"], writes=[("S", 0)])
                    ph.op("dve", lambda e: e.tensor_scalar(rden[:, 0, :], den_ps[:, 0, :], 1.0 / 128, SUBLN_EPS,
                                                           ALU.mult, ALU.add),
                          reads=[("S", 0)], writes=[("rden", 0)])
                    ph.op("act", lambda e: e.activation(out=rden[:, 0, :], in_=rden[:, 0, :], func=AF.Sqrt),
                          reads=[("rden", 0)], writes=[("rden", 0)])
                    ph.op("dve", lambda e: e.reciprocal(rden[:, 0, :], rden[:, 0, :]),
                          reads=[("rden", 0)], writes=[("rden", 0)])
                    ph.op("dve", lambda e, b=b: e.scalar_tensor_tensor(ot[:, b, :], tt[:, 0, :], gs[:, 0:1], rden[:, 0, :],
                                                                       ALU.mult, ALU.mult),
                          reads=[("tt", 0), ("rden", 0), "gs"], writes=[("ot", b)])
                    ph.op("pool", lambda e, b=b, a=a, h=h: e.dma_start(
                        out=OT[256 + h * 128:256 + (h + 1) * 128, a:a + 512], in_=ot[:, b, :]),
                        reads=[("ot", b)], dma="st_oB%d" % b)
            ph.emit()

    def phase_attn_b_own(self, l, s, L):
        nc, dr = self.nc, self.dr
        QT = self.QTv(s)
        VT = self.VTv(s)
        QTo, VTo, OT = dr["QT_o"], dr["VT_o"], dr["OT_o"]
        nt = self.LQ // 512
        kshift = KM - QM
        nblk = L // 128
        lam_init = 0.8 - 0.6 * float(np.exp(-0.3 * l))
        with contextlib.ExitStack() as st:
            def sb(name, shape, dt):
                return st.enter_context(nc.sbuf_tensor(self._nm(name), shape, dt))
            ones_f = sb("po_onesf", [128, 128], F32)
            ones_b = sb("po_onesb", [128, 128], BF16)
            lamt = sb("po_lam", [128, 4 * 64], F32)
            lt = sb("po_lt", [128, 2 * 64], F32)
            ls = sb("po_ls", [128, 4], F32)
            gs = sb("po_gs", [128, 1], F32)
            dB = sb("po_dB", [128, 4 * 128], BF16)
            MBh = sb("po_MBh", [128, 896], BF16)
            MBl = sb("po_MBl", [128, 896], BF16)
            kB = sb("po_kB", [68, 2, L], BF16)
            vB = sb("po_vB", [128, nblk, 128], BF16)
            kN = sb("po_kN", [68, 2, 2, 512], BF16)
            vN = sb("po_vN", [128, 2, 4, 128], BF16)
            qB = sb("po_qB", [68, 2, 2, 2, 512], BF16)
            pt = sb("po_pt", [128, 6, 512], BF16)
            acc = sb("po_acc", [128, 2, 2, 512], F32)
            rden = sb("po_rden", [128, 2, 512], F32)
            tt = sb("po_t", [128, 2, 512], F32)
            sq = sb("po_sq", [128, 512], BF16)
            ot = sb("po_ot", [128, 2, 512], BF16)
            S_ps = st.enter_context(nc.psum_tensor(self._nm("po_S"), [128, 4, 512], F32))
            fill_ps = st.enter_context(nc.psum_tensor(self._nm("po_fill"), [128, 512], F32))
            O_ps = st.enter_context(nc.psum_tensor(self._nm("po_O"), [128, 2, 512], F32))
            den_ps = S_ps[:, 0:2, :]
            ph = Phase(nc)
            ph.op("pool", lambda e: e.memset(ones_f[:], 1.0), writes=["ones"])
            ph.op("pool", lambda e: e.memset(ones_b[:], 1.0), writes=["ones"])
            ph.op("pool", lambda e: e.memset(qB[:], 0.0), writes=["qinit"])
            ph.op("pool", lambda e: e.memset(kN[:], 0.0), writes=["qinit"])
            for nm, tl in (("c_diagB", dB), ("c_MBh", MBh), ("c_MBl", MBl)):
                ph.op("sp", lambda e, nm=nm, tl=tl: e.dma_start(out=tl[:], in_=dr[nm][:, :]), writes=["consts"], dma="ld_c1")
            ph.op("sp", lambda e: e.dma_start(out=lamt[:], in_=dr["lam"][:, l * 256:(l + 1) * 256]), writes=["lamt"], dma="ld_c0")
            ph.op("sp", lambda e: e.dma_start(out=gs[:], in_=dr["subln"][:, l:l + 1], allow_slow_non_contiguous=True),
                  writes=["gs"], dma="ld_c0")
            ph.op("dve", lambda e: e.tensor_tensor(out=lt[:, 0:64], in0=lamt[:, 0:64], in1=lamt[:, 64:128], op=ALU.mult),
                  reads=["lamt"], writes=["lt"])
            ph.op("dve", lambda e: e.tensor_tensor(out=lt[:, 64:128], in0=lamt[:, 128:192], in1=lamt[:, 192:256], op=ALU.mult),
                  reads=["lamt"], writes=["lt"])
            ph.op("dve", lambda e: e.reduce_sum(ls[:, 0:1], lt[:, 0:64], mybir.AxisListType.X), reads=["lt"], writes=["ls"])
            ph.op("dve", lambda e: e.reduce_sum(ls[:, 1:2], lt[:, 64:128], mybir.AxisListType.X), reads=["lt"], writes=["ls"])
            ph.op("act", lambda e: e.activation(out=ls[:, 0:2], in_=ls[:, 0:2], func=AF.Exp), reads=["ls"], writes=["ls"])
            ph.op("dve", lambda e: e.tensor_tensor(out=ls[:, 2:3], in0=ls[:, 1:2], in1=ls[:, 0:1], op=ALU.subtract),
                  reads=["ls"], writes=["ls"])
            ph.op("dve", lambda e: e.tensor_scalar(ls[:, 2:3], ls[:, 2:3], -lam_init, None, ALU.add),
                  reads=["ls"], writes=["ls"])
            ph.op("dve", lambda e: e.tensor_scalar(gs[:], gs[:], 1.0 - lam_init, None, ALU.mult),
                  reads=["gs"], writes=["gs"])
            HCH = min(4096, L)
            VCH = min(32, nblk)
            for h in range(4):
                for m in range(2):
                    r0 = QT_BK + h * 128 + m * 64
                    for c0 in range(0, L, HCH):
                        ph.op("sp", lambda e, m=m, r0=r0, c0=c0: e.dma_start(out=kB[0:64, m, c0:c0 + HCH],
                                                                         in_=QT[r0:r0 + 64, c0:c0 + HCH]),
                              writes=[("kB", m)], dma="ld_kB%d" % m)
                for c0 in range(0, nblk, VCH):
                    ph.op("sp", lambda e, h=h, c0=c0: e.dma_start(
                        out=vB[:, c0:c0 + VCH, :],
                        in_=VT[c0 * 128:(c0 + VCH) * 128, VT_BV + h * 128:VT_BV + (h + 1) * 128].rearrange(
                            "(u p) n -> p u n", p=128)),
                        writes=["vB"], dma="ld_vB")
                for c in range(nt):
                    b = c % 2
                    ak = c * 512 + kshift
                    ao = c * 512
                    for m in range(2):
                        ph.op("sp", lambda e, m=m, h=h, c=c: e.dma_start(out=kB[64:68, m, :],
                                                                       in_=dr["c_kaugL"][c, h, :, 0:L]),
                              writes=[("kBa", m)], dma="ld_kBa%d" % m)
                        r0 = QT_BQ + h * 128 + m * 64
                        for var in range(2):
                            ph.op("sp", lambda e, b=b, m=m, var=var, r0=r0, ak=ak: e.dma_start(
                                out=qB[0:64, b, m, var, :], in_=QTo[r0:r0 + 64, ak:ak + 512]),
                                reads=["qinit"], writes=[("qB", b, m)], dma="ld_qB%d" % b)
                        ph.op("sp", lambda e, b=b, m=m, h=h, ao=ao: e.dma_start(
                            out=qB[64:68, b, m, 0, :], in_=dr["c_qaugL"][h, :, ao:ao + 512]),
                            reads=["qinit"], writes=[("qB", b, m)], dma="ld_qB%d" % b)
                        r1 = QT_BK + h * 128 + m * 64
                        ph.op("sp", lambda e, b=b, m=m, r1=r1, ak=ak: e.dma_start(
                            out=kN[0:64, b, m, :], in_=QTo[r1:r1 + 64, ak:ak + 512]),
                            reads=["qinit"], writes=[("kN", b)], dma="ld_kN%d" % b)
                    ph.op("sp", lambda e, b=b, h=h, ak=ak: e.dma_start(
                        out=vN[:, b, :, :],
                        in_=VTo[ak:ak + 512, VT_BV + h * 128:VT_BV + (h + 1) * 128].rearrange("(u p) n -> p u n", p=128)),
                        writes=[("vN", b)], dma="ld_vN%d" % b)
                    pairs = []
                    for kb in range(nblk):
                        k0 = kb * 128
                        pairs.append(dict(q=[qB[:, b, 0, 0, :], qB[:, b, 1, 0, :]],
                                          k=[kB[:, 0, k0:k0 + 128], kB[:, 1, k0:k0 + 128]],
                                          v=vB[:, kb, :], masks=[],
                                          reads=[("kB", 0), ("kB", 1), ("kBa", 0), ("kBa", 1), "vB", ("qB", b, 0),
                                                 ("qB", b, 1)]))
                    for u in range(4):
                        off = 384 - 128 * u
                        masks = [(dB[:, h * 128:(h + 1) * 128], MBh[:, off:off + 512]),
                                 (dB[:, h * 128:(h + 1) * 128], MBl[:, off:off + 512])]
                        pairs.append(dict(q=[qB[:, b, 0, 1, :], qB[:, b, 1, 1, :]],
                                          k=[kN[:, b, 0, u * 128:(u + 1) * 128], kN[:, b, 1, u * 128:(u + 1) * 128]],
                                          v=vN[:, b, u, :], masks=masks,
                                          reads=[("kN", b), ("vN", b), ("qB", b, 0), ("qB", b, 1), "consts"]))
                    pe_used = self._attend_b(ph, pairs, S_ps, O_ps, pt, acc, 3, fill=(fill_ps[:], ones_b[:], sq[:]))
                    for m in range(2):
                        self._finalize_den(ph, acc[:, 0, m, :], ("acc", 0), ones_f[:], den_ps[:, m, :], ("S", 0),
                                           rden[:, m, :], ("rden", m), 128, start=(not pe_used))
                        ph.op("dve", lambda e, m=m: e.tensor_tensor(out=tt[:, m, :], in0=O_ps[:, m, :], in1=rden[:, m, :],
                                                                    op=ALU.mult),
                              reads=[("O", m), ("rden", m)], writes=[("tt", m)])
                    ph.op("dve", lambda e: e.scalar_tensor_tensor(tt[:, 0, :], tt[:, 1, :], ls[:, 2:3], tt[:, 0, :],
                                                                  ALU.mult, ALU.add),
                          reads=[("tt", 0), ("tt", 1), "ls"], writes=[("tt", 0)])
                    ph.op("act", lambda e: e.activation(out=sq[:], in_=tt[:, 0, :], func=AF.Square),
                          reads=[("tt", 0)], writes=["sq"])
                    ph.op("pe", lambda e: e.matmul(den_ps[:, 0, :], ones_b[:], sq[:], start=True, stop=True),
                          reads=["sq", "ones"], writes=[("S", 0)])
                    ph.op("dve", lambda e: e.tensor_scalar(rden[:, 0, :], den_ps[:, 0, :], 1.0 / 128, SUBLN_EPS,
                                                           ALU.mult, ALU.add),
                          reads=[("S", 0)], writes=[("rden", 0)])
                    ph.op("act", lambda e: e.activation(out=rden[:, 0, :], in_=rden[:, 0, :], func=AF.Sqrt),
                          reads=[("rden", 0)], writes=[("rden", 0)])
                    ph.op("dve", lambda e: e.reciprocal(rden[:, 0, :], rden[:, 0, :]),
                          reads=[("rden", 0)], writes=[("rden", 0)])
                    ph.op("dve", lambda e, b=b: e.scalar_tensor_tensor(ot[:, b, :], tt[:, 0, :], gs[:, 0:1], rden[:, 0, :],
                                                                       ALU.mult, ALU.mult),
                          reads=[("tt", 0), ("rden", 0), "gs"], writes=[("ot", b)])
                    ph.op("pool", lambda e, b=b, ao=ao, h=h: e.dma_start(
                        out=OT[256 + h * 128:256 + (h + 1) * 128, ao:ao + 512], in_=ot[:, b, :]),
                        reads=[("ot", b)], dma="st_oB%d" % b)
            ph.emit()

    def phase_wout(self, l, jobs):
        nc, dr = self.nc, self.dr
        with contextlib.ExitStack() as st:
            w = st.enter_context(nc.sbuf_tensor(self._nm("pw_w"), [128, 8, D], BF16))
            xt = st.enter_context(nc.sbuf_tensor(self._nm("pw_xt"), [128, 2, 8, 512], F32))
            ot = st.enter_context(nc.sbuf_tensor(self._nm("pw_ot"), [128, 2, 8, 512], BF16))
            ps = st.enter_context(nc.psum_tensor(self._nm("pw_ps"), [128, 4, 512], F32))
            ph = Phase(nc)
            wsrc = dr["w_out"][l].rearrange("(c p) n -> p c n", p=128)
            for c in range(8):
                ph.op("pool", lambda e, c=c: e.dma_start(out=w[:, c, :], in_=wsrc[:, c, :]), writes=[("w", c)],
                      dma="ld_w%d" % (c % 4))
            wres = [("w", c) for c in range(8)]
            k = 0
            for (s, L, own) in jobs:
                if own:
                    XT = dr["XL_o"].rearrange("(c p) t -> p c t", p=128)
                    OT = dr["OT_o"].rearrange("(c p) t -> p c t", p=128)
                    nt = self.LQ // 512
                else:
                    XT = self.X(l, s).rearrange("(c p) t -> p c t", p=128)
                    OT = dr["OT_" + s].rearrange("(c p) t -> p c t", p=128)
                    nt = L // 512
                for t in range(nt):
                    b = t % 2
                    ph.op("sp", lambda e, b=b, t=t, XT=XT: e.dma_start(out=xt[:, b], in_=XT[:, :, t * 512:(t + 1) * 512]),
                          writes=[("xt", b, c) for c in range(8)], dma="ld_xt%d" % b)
                    ph.op("sp", lambda e, b=b, t=t, OT=OT: e.dma_start(out=ot[:, b], in_=OT[:, :, t * 512:(t + 1) * 512]),
                          writes=[("ot", b)], dma="ld_ot%d" % b)
                    for oc in range(8):
                        pb = k % 4
                        k += 1
                        for c in range(8):
                            ph.op("pe", lambda e, pb=pb, c=c, oc=oc, b=b: e.matmul(
                                ps[:, pb, :], w[:, c, oc * 128:(oc + 1) * 128], ot[:, b, c, :], start=(c == 0), stop=(c == 7)),
                                reads=[("ot", b)] + wres if c == 0 else [], writes=[("ps", pb)])
                        ph.op("dve", lambda e, pb=pb, b=b, oc=oc: e.tensor_tensor(out=xt[:, b, oc, :], in0=ps[:, pb, :],
                                                                              in1=xt[:, b, oc, :], op=ALU.add),
                              reads=[("ps", pb)], writes=[("xt", b, oc)])
                    ph.op("pool", lambda e, b=b, t=t, XT=XT: e.dma_start(out=XT[:, :, t * 512:(t + 1) * 512], in_=xt[:, b]),
                          reads=[("xt", b, c) for c in range(8)], dma="st_xt%d" % b)
            ph.emit()

    def phase_mlp(self, l, jobs):
        nc, dr = self.nc, self.dr
        NT = 256
        NC = NT + 2
        any_own = any(j[2] for j in jobs)
        with contextlib.ExitStack() as st:
            wu = st.enter_context(nc.sbuf_tensor(self._nm("pm_wu"), [128, 8, 2 * D_FF], BF16))
            wd = st.enter_context(nc.sbuf_tensor(self._nm("pm_wd"), [128, 22, D], BF16))
            g = st.enter_context(nc.sbuf_tensor(self._nm("pm_g"), [128, 8], F32))
            cw = st.enter_context(nc.sbuf_tensor(self._nm("pm_cw"), [128, 3 * 44], F32))
            cb = st.enter_context(nc.sbuf_tensor(self._nm("pm_cb"), [128, 44], F32))
            ones = st.enter_context(nc.sbuf_tensor(self._nm("pm_ones"), [128, 128], BF16))
            xt = st.enter_context(nc.sbuf_tensor(self._nm("pm_xt"), [128, 2, 8, NC], F32))
            sq = st.enter_context(nc.sbuf_tensor(self._nm("pm_sq"), [128, 8, NC], BF16))
            r = st.enter_context(nc.sbuf_tensor(self._nm("pm_r"), [128, NC], F32))
            h = st.enter_context(nc.sbuf_tensor(self._nm("pm_h"), [128, 8, NC], BF16))
            tmp = st.enter_context(nc.sbuf_tensor(self._nm("pm_tmp"), [128, 2, 2, NT], F32))
            gt = st.enter_context(nc.sbuf_tensor(self._nm("pm_gt"), [128, 22, NT], BF16))
            xo = st.enter_context(nc.sbuf_tensor(self._nm("pm_xo"), [128, 2, 8, NT], F32))
            ss_ps = st.enter_context(nc.psum_tensor(self._nm("pm_ss"), [128, 512], F32))
            u_ps = st.enter_context(nc.psum_tensor(self._nm("pm_u"), [128, 4, 512], F32))
            o_ps = st.enter_context(nc.psum_tensor(self._nm("pm_o"), [128, 2, 512], F32))
            ph = Phase(nc)
            ph.op("pool", lambda e: e.memset(ones[:], 1.0), writes=["ones"])
            ph.op("sp", lambda e: e.dma_start(out=g[:], in_=dr["ln2"][:, l * 8:(l + 1) * 8]), writes=["params"], dma="ld_g")
            ph.op("sp", lambda e: e.dma_start(out=cw[:], in_=dr["convw"][:, l * 132:(l + 1) * 132]), writes=["params"], dma="ld_g")
            ph.op("sp", lambda e: e.dma_start(out=cb[:], in_=dr["convb"][:, l * 44:(l + 1) * 44]), writes=["params"], dma="ld_g")
            flag = st.enter_context(nc.sbuf_tensor(self._nm("pm_flag"), [128, 2], F32))
            if any_own:
                ph.op("sp", lambda e: e.dma_start(out=flag[:], in_=dr["c_flag"][:, :]), writes=["params"], dma="ld_g")
            wsrc = dr["w_up"][l].rearrange("(c p) n -> p c n", p=128)
            for c in range(8):
                for n0 in range(0, 2 * D_FF, 1408):
                    ph.op("pool", lambda e, c=c, n0=n0: e.dma_start(out=wu[:, c, n0:n0 + 1408], in_=wsrc[:, c, n0:n0 + 1408]),
                          writes=[("wu", c, n0)], dma="ld_w%d" % (c % 4))
            wures = [("wu", c, n0) for c in range(8) for n0 in range(0, 2 * D_FF, 1408)]
            wdsrc = dr["w_down"][l].rearrange("(c p) n -> p c n", p=128)
            for c in range(22):
                ph.op("pool", lambda e, c=c: e.dma_start(out=wd[:, c, :], in_=wdsrc[:, c, :]), writes=[("wd", c)],
                      dma="ld_w%d" % (c % 4))
            wdres = [("wd", c) for c in range(22)]
            uk = 0
            ok = 0
            for (s, L, own) in jobs:
                if own:
                    XT = dr["XL_o"].rearrange("(c p) t -> p c t", p=128)
                    XO = dr["XW_o"].rearrange("(c p) t -> p c t", p=128)
                    nt = self.shard // NT
                else:
                    XT = self.X(l, s).rearrange("(c p) t -> p c t", p=128)
                    XO = self.X(l + 1, s).rearrange("(c p) t -> p c t", p=128)
                    nt = L // NT
                cb0 = QM if own else 0
                for t in range(nt):
                    b = t % 2
                    t0 = t * NT
                    lo = 1 if (t == 0 and not own) else 0
                    hi = NC - 1 if (t == nt - 1 and not own) else NC
                    if lo:
                        ph.op("pool", lambda e, b=b: e.memset(xt[:, b, :, 0:1], 0.0), writes=[("xt", b)])
                    if hi != NC:
                        ph.op("pool", lambda e, b=b: e.memset(xt[:, b, :, NC - 1:NC], 0.0), writes=[("xt", b)])
                    ph.op("sp", lambda e, b=b, t0=t0, lo=lo, hi=hi, XT=XT, cb0=cb0: e.dma_start(out=xt[:, b, :, lo:hi],
                                                                           in_=XT[:, :, cb0 + t0 - 1 + lo:cb0 + t0 - 1 + hi]),
                          writes=[("xt", b)] if not (lo or hi != NC) else [("xt", b), ("xtedge", b)], dma="ld_xt%d" % b)
                    if own and t == 0:
                        ph.op("dve", lambda e, b=b: e.tensor_scalar(xt[:, b, :, 0:1], xt[:, b, :, 0:1], flag[:, 0:1], None,
                                                                    ALU.mult), reads=[("xt", b), "params"], writes=[("xt", b)])
                    if own and t == nt - 1:
                        ph.op("dve", lambda e, b=b: e.tensor_scalar(xt[:, b, :, NC - 1:NC], xt[:, b, :, NC - 1:NC],
                                                                    flag[:, 1:2], None, ALU.mult),
                              reads=[("xt", b), "params"], writes=[("xt", b)])
                    self._rmsnorm_tile(ph, xt[:, b], ("xt", b), NC, sq[:], ones[:], ss_ps[:, 0:NC], "ss", r[:], "r",
                                       lambda c: h[:, c, :], lambda c: ("h", c), lambda c: g[:, c:c + 1], D, EPS)
                    hres = [("h", c) for c in range(8)]
                    for p in range(22):
                        for part in range(2):
                            j = p + 22 * part
                            ub = uk % 4
                            uk += 1
                            for c in range(8):
                                ph.op("pe", lambda e, ub=ub, c=c, j=j: e.matmul(
                                    u_ps[:, ub, 0:NC], wu[:, c, j * 128:(j + 1) * 128], h[:, c, :], start=(c == 0), stop=(c == 7)),
                                    reads=hres + wures if c == 0 else [], writes=[("u", ub)])
                            tb = p % 2
                            tm = tmp[:, tb, part, :]
                            tres = ("tmp", tb, part)
                            ph.op("act", lambda e, tm=tm, ub=ub, j=j: e.activation(
                                out=tm, in_=u_ps[:, ub, 1:NT + 1], func=AF.Identity, bias=cb[:, j:j + 1],
                                scale=cw[:, 44 + j:45 + j]), reads=[("u", ub), "params"], writes=[tres])
                            ph.op("dve", lambda e, tm=tm, ub=ub, j=j: e.scalar_tensor_tensor(
                                tm, u_ps[:, ub, 0:NT], cw[:, j:j + 1], tm, ALU.mult, ALU.add),
                                reads=[("u", ub), "params"], writes=[tres])
                            ph.op("dve", lambda e, tm=tm, ub=ub, j=j: e.scalar_tensor_tensor(
                                tm, u_ps[:, ub, 2:NT + 2], cw[:, 88 + j:89 + j], tm, ALU.mult, ALU.add),
                                reads=[("u", ub), "params"], writes=[tres])
                        tb = p % 2
                        ph.op("act", lambda e, tb=tb: e.activation(out=tmp[:, tb, 0, :], in_=tmp[:, tb, 0, :], func=AF.Silu),
                              reads=[("tmp", tb, 0)], writes=[("tmp", tb, 0)])
                        ph.op("pool", lambda e, tb=tb, p=p: e.tensor_tensor(out=gt[:, p, :], in0=tmp[:, tb, 0, :],
                                                                          in1=tmp[:, tb, 1, :], op=ALU.mult),
                              reads=[("tmp", tb, 0), ("tmp", tb, 1)], writes=[("gt", p)])
                    gres = [("gt", p) for p in range(22)]
                    for oc in range(8):
                        ob = ok % 2
                        ok += 1
                        for c in range(22):
                            ph.op("pe", lambda e, ob=ob, c=c, oc=oc: e.matmul(
                                o_ps[:, ob, 0:NT], wd[:, c, oc * 128:(oc + 1) * 128], gt[:, c, :], start=(c == 0), stop=(c == 21)),
                                reads=gres + wdres if c == 0 else [], writes=[("o", ob)])
                        ph.op("dve", lambda e, ob=ob, b=b, oc=oc: e.tensor_tensor(out=xo[:, b, oc, :], in0=o_ps[:, ob, 0:NT],
                                                                              in1=xt[:, b, oc, 1:NT + 1], op=ALU.add),
                              reads=[("o", ob), ("xt", b)], writes=[("xo", b, oc)])
                    ph.op("pool", lambda e, b=b, t0=t0, XO=XO: e.dma_start(out=XO[:, :, t0:t0 + NT], in_=xo[:, b]),
                          reads=[("xo", b, c) for c in range(8)], dma="st_xo%d" % b)
            ph.emit()

    def phase_final(self, s, L, own=False):
        nc, dr = self.nc, self.dr
        if own:
            XT = dr["XW_o"].rearrange("(c p) t -> p c t", p=128)
            y = dr["y_o"]
            nt = self.shard // 512
        else:
            XT = self.X(self.depth, s).rearrange("(c p) t -> p c t", p=128)
            y = dr["y_" + s]
            nt = L // 512
        with contextlib.ExitStack() as st:
            ident = st.enter_context(nc.sbuf_tensor(self._nm("pf_id"), [128, 128], F32))
            g = st.enter_context(nc.sbuf_tensor(self._nm("pf_g"), [128, 8], F32))
            ones = st.enter_context(nc.sbuf_tensor(self._nm("pf_ones"), [128, 128], BF16))
            xt = st.enter_context(nc.sbuf_tensor(self._nm("pf_xt"), [128, 2, 8, 512], F32))
            sq = st.enter_context(nc.sbuf_tensor(self._nm("pf_sq"), [128, 8, 512], BF16))
            r = st.enter_context(nc.sbuf_tensor(self._nm("pf_r"), [128, 512], F32))
            h = st.enter_context(nc.sbuf_tensor(self._nm("pf_h"), [128, 8, 512], F32))
            yo = st.enter_context(nc.sbuf_tensor(self._nm("pf_yo"), [128, 2, 4, D], F32))
            ss_ps = st.enter_context(nc.psum_tensor(self._nm("pf_ss"), [128, 512], F32))
            ps = st.enter_context(nc.psum_tensor(self._nm("pf_ps"), [128, 3, 2, 512], F32))
            ph = Phase(nc)
            ph.op("pool", lambda e: e.memset(ones[:], 1.0), writes=["ones"])
            ph.op("sp", lambda e: e.dma_start(out=ident[:], in_=dr["c_ident"][:, :]), writes=["ident"], dma="ld_g")
            ph.op("sp", lambda e: e.dma_start(out=g[:], in_=dr["lnf"][:, :]), writes=["params"], dma="ld_g")
            k = 0
            for t in range(nt):
                b = t % 2
                ph.op("sp", lambda e, b=b, t=t: e.dma_start(out=xt[:, b], in_=XT[:, :, t * 512:(t + 1) * 512]),
                      writes=[("xt", b)], dma="ld_xt%d" % b)
                self._rmsnorm_tile(ph, xt[:, b], ("xt", b), 512, sq[:], ones[:], ss_ps[:], "ss", r[:], "r",
                                   lambda c: h[:, c, :], lambda c: ("h", c), lambda c: g[:, c:c + 1], D, EPS)
                for sub in range(4):
                    pb = k % 3
                    k += 1
                    for c in range(8):
                        ph.op("pe", lambda e, pb=pb, c=c, sub=sub: e.transpose(
                            ps[:, pb, c // 4, (c % 4) * 128:(c % 4 + 1) * 128], h[:, c, sub * 128:(sub + 1) * 128], ident[:]),
                            reads=[("h", c), "ident"], writes=[("ps", pb)])
                    if sub % 2 == 0:
                        ph.op("act", lambda e, pb=pb, b=b, sub=sub: e.copy(
                            out=yo[:, b, sub, :], in_=ps[:, pb].rearrange("p a n -> p (a n)")),
                            reads=[("ps", pb)], writes=[("yo", b, sub)])
                    else:
                        ph.op("dve", lambda e, pb=pb, b=b, sub=sub: e.tensor_copy(
                            out=yo[:, b, sub, :], in_=ps[:, pb].rearrange("p a n -> p (a n)")),
                            reads=[("ps", pb)], writes=[("yo", b, sub)])
                dst = y[t * 512:(t + 1) * 512, :].rearrange("(s p) f -> p s f", p=128)
                ph.op("pool", lambda e, b=b, dst=dst: e.dma_start(out=dst, in_=yo[:, b]),
                      reads=[("yo", b, sub) for sub in range(4)], dma="st_y%d" % b)
            ph.emit()

    def _nm(self, base):
        self._cnt = getattr(self, "_cnt", 0) + 1
        return "%s_%d" % (base, self._cnt)

    def build(self):
        import os
        lim = int(os.environ.get("KPH", "1000"))
        plist = []
        last = self.depth - 1
        for (s, L) in self.seqs:
            if s == "p":
                plist.append(lambda s=s, L=L: self.phase_init_pads(s, L, last))
            plist.append(lambda s=s, L=L: self.phase_transpose_in(s, L))
        for l in range(self.depth):
            plist.append(lambda l=l: self.phase_proj(l, [(s, L) for (s, L) in self.seqs]))
            for (s, L) in self.seqs:
                own = (s == "p") and l == last
                if own:
                    plist.append(lambda s=s, L=L, l=l: self.phase_localize(s, L, l))
                    plist.append(lambda s=s, L=L, l=l: self.phase_attn_ac(l, s, L, own=True))
                    plist.append(lambda s=s, L=L, l=l: self.phase_attn_b_own(l, s, L))
                else:
                    plist.append(lambda s=s, L=L, l=l: self.phase_attn_ac(l, s, L))
                    plist.append(lambda s=s, L=L, l=l: self.phase_attn_b(l, s, L))
            jobs = [(s, L, (s == "p") and l == last) for (s, L) in self.seqs]
            plist.append(lambda l=l, jobs=jobs: self.phase_wout(l, jobs))
            plist.append(lambda l=l, jobs=jobs: self.phase_mlp(l, jobs))
        for (s, L) in self.seqs:
            plist.append(lambda s=s, L=L: self.phase_final(s, L, own=(s == "p")))
        for f in plist[:lim]:
            f()
        return self.nc


def _toeplitz(width, X, f):
    kk = np.arange(128)[:, None]
    col = np.arange(width)[None, :]
    return f(col - kk - X)


def make_consts(Lmax):
    c = {}
    c["c_ident"] = np.eye(128, dtype=np.float32)
    slA = 2.0 ** (-8.0 * np.arange(1, 5) / 4)
    slC = (2.0 ** (-8.0 * np.arange(1, 13) / 12)).astype(np.float32).astype(np.float64)
    eye = np.eye(128)
    c["c_diagA"] = _bf(np.concatenate([eye * (-8.0 * s) for s in slA], 1))
    c["c_diagB"] = _bf(np.concatenate([eye * (-8.0 * s) for s in slA], 1))
    dC = []
    for gh in range(12):
        v = -8.0 * slC[gh]
        hi = float(_bf(v).astype(np.float32))
        lo = float(_bf(v - hi).astype(np.float32))
        dC += [eye * hi, eye * lo]
    c["c_diagC"] = _bf(np.concatenate(dC, 1))

    def band(W, d):
        def f(delta):
            ad = np.abs(delta)
            ok = (ad <= W) & (delta % d == 0)
            return np.where(ok, ad, BIG).astype(np.float32)
        return f
    c["c_MA"] = _bf(_toeplitz(1152, 512, band(128, 1)))
    c["c_MC0"] = _bf(_toeplitz(1152, 128 * 5 - 128, band(64, 1)))
    c["c_MC1"] = _bf(_toeplitz(1408, 128 * 7 - 256, band(256, 4)))
    c["c_MC2"] = _bf(_toeplitz(2944, 128 * 19 - 1024, band(1024, 16)))
    c["c_MBh"] = _bf(_toeplitz(896, 384, lambda dl: (2 * (np.abs(dl) // 2)).astype(np.float32)))
    c["c_MBl"] = _bf(_toeplitz(896, 384, lambda dl: (np.abs(dl) % 2).astype(np.float32)))
    pos = np.arange(Lmax)
    hi = (pos // 128) * 128.0
    lo = (pos % 128) * 1.0
    kaug = np.zeros((4, 4, Lmax), np.float32)
    qaug = np.zeros((4, 2, 4, Lmax), np.float32)
    for h in range(4):
        s8 = 8.0 * slA[h]
        kaug[h] = np.stack([np.ones(Lmax), np.ones(Lmax), s8 * hi, s8 * lo])
        base = np.stack([-s8 * hi, -s8 * lo, np.ones(Lmax), np.ones(Lmax)])
        qaug[h, 0] = base
        qaug[h, 1] = -base
    c["c_kaug"] = _bf(kaug)
    c["c_qaug"] = _bf(qaug)
    return c


def make_core_consts(core, L, SH):
    LQ, LK = SH + 2 * QM, SH + 2 * KM
    tok0 = core * SH
    c = {}
    kstart = tok0 - KM + 128 * np.arange(LK // 128)
    valid = (kstart >= 0) & (kstart < L)
    c["c_kbias"] = np.ascontiguousarray(np.broadcast_to(np.where(valid, 0.0, KBIAS_NEG).astype(np.float32)[None, :],
                                                        (128, LK // 128)))
    slA = 2.0 ** (-8.0 * np.arange(1, 5) / 4)
    pos = np.arange(L)
    hi = (pos // 128) * 128.0
    lo = (pos % 128) * 1.0
    nch = LQ // 512
    kaug = np.zeros((nch, 4, 4, L), np.float32)
    for ci in range(nch):
        A = tok0 - QM + 512 * ci
        blk0 = (pos // 128) * 128
        left = blk0 + 127 < A
        right = blk0 > A + 511
        sign = np.where(left, 1.0, np.where(right, -1.0, 0.0))
        near = (sign == 0)
        for h in range(4):
            s8 = 8.0 * slA[h]
            rows = np.stack([np.ones(L), np.ones(L), s8 * hi, s8 * lo]) * sign[None, :]
            rows[2, near] = -262144.0
            kaug[ci, h] = rows
    c["c_kaugL"] = _bf(kaug)
    qpos = np.clip(tok0 - QM + np.arange(LQ), 0, L - 1)
    qhi = (qpos // 128) * 128.0
    qlo = (qpos % 128) * 1.0
    qaug = np.zeros((4, 4, LQ), np.float32)
    for h in range(4):
        s8 = 8.0 * slA[h]
        qaug[h] = np.stack([-s8 * qhi, -s8 * qlo, np.ones(LQ), np.ones(LQ)])
    c["c_qaugL"] = _bf(qaug)
    fl = np.array([0.0 if core == 0 else 1.0, 0.0 if (tok0 + SH) >= L else 1.0], np.float32)
    c["c_flag"] = np.ascontiguousarray(np.broadcast_to(fl[None, :], (128, 2)))
    return c


def layout_params(p, depth):
    out = {}

    def chunks(v, n):
        v = np.asarray(v, np.float32).reshape(-1, n, 128)
        return np.ascontiguousarray(v.transpose(2, 0, 1).reshape(128, -1))
    out["ln1"] = chunks(p["ln1"], 8)
    out["ln2"] = chunks(p["ln2"], 8)
    out["lnf"] = chunks(np.asarray(p["ln_f"])[None], 8)
    out["subln"] = np.ascontiguousarray(np.asarray(p["subln"], np.float32).T)
    out["sink"] = np.ascontiguousarray(np.broadcast_to(np.asarray(p["a_sink"], np.float32).reshape(1, -1), (128, depth * 4)))
    lam = np.stack([np.asarray(p[k], np.float32) for k in ("lam_q1", "lam_k1", "lam_q2", "lam_k2")], 1)
    out["lam"] = np.ascontiguousarray(np.broadcast_to(lam.reshape(1, -1), (128, depth * 256)))
    cw = np.asarray(p["conv_w"], np.float32).reshape(depth, 3, 44, 128)
    out["convw"] = np.ascontiguousarray(cw.transpose(3, 0, 1, 2).reshape(128, -1))
    cb = np.asarray(p["conv_b"], np.float32).reshape(depth, 44, 128)
    out["convb"] = np.ascontiguousarray(cb.transpose(2, 0, 1).reshape(128, -1))
    for k in ("w_in", "w_out", "w_up", "w_down"):
        out[k] = np.ascontiguousarray(np.asarray(p[k], np.float32))
    return out


_CACHE = {}


def run(seq_inputs, params, depth=DEPTH, n_cores=8):
    seqs = [(k, v.shape[0]) for k, v in seq_inputs[0].items()]
    key = (tuple(seqs), depth)
    if key not in _CACHE:
        _CACHE[key] = Builder(seqs, depth).build()
    nc = _CACHE[key]
    Lmax = max(L for _, L in seqs)
    shared = dict(make_consts(Lmax))
    shared.update(layout_params(params, depth))
    in_maps = []
    Lp = dict(seqs).get("p")
    for c in range(n_cores):
        m = dict(shared)
        if Lp:
            m.update(make_core_consts(c, Lp, Lp // 8))
        for k, v in seq_inputs[c].items():
            m["x_" + k] = np.ascontiguousarray(np.asarray(v, np.float32))
        in_maps.append(m)
    import os
    if os.environ.get("KTRACE"):
        res = run_bass_kernel_spmd(nc, in_maps, core_ids=list(range(n_cores)), trace=True)
        print("EXEC_TIME_NS", res.exec_time_ns, flush=True)
    else:
        res = run_bass_kernel_spmd(nc, in_maps, core_ids=list(range(n_cores)))
    return res.results


def kernel(x_prompt, x_sample, ln1, w_in, a_sink, lam_q1, lam_k1, lam_q2, lam_k2, subln,
           w_out, ln2, w_up, conv_w, conv_b, w_down, ln_f):
    params = dict(ln1=ln1, w_in=w_in, a_sink=a_sink, lam_q1=lam_q1, lam_k1=lam_k1, lam_q2=lam_q2, lam_k2=lam_k2,
                  subln=subln, w_out=w_out, ln2=ln2, w_up=w_up, conv_w=conv_w, conv_b=conv_b, w_down=w_down, ln_f=ln_f)
    x_prompt = np.asarray(x_prompt, np.float32)
    x_sample = np.asarray(x_sample, np.float32)
    seq_inputs = [{"s": x_sample[c], "p": x_prompt[0]} for c in range(8)]
    res = run(seq_inputs, params)
    y_sample = np.stack([res[c]["y_s"] for c in range(8)], 0)
    y_prompt = np.concatenate([res[c]["y_o"] for c in range(8)], 0)[None]
    return (y_prompt.astype(np.float32), y_sample.astype(np.float32))
```
